# Optimizing a Trainium2 kernel written in Bass

```python
import math
import jax, jax.numpy as jnp
from jax import lax
import numpy as np

D_MODEL = 1024
BATCH = 2
SEQ = 16384
DEPTH = 2

CHUNK = 64
Q_BLOCK = 128
A_HEADS = 8
A_HEAD_DIM = 64
A_WIDTH = A_HEADS * A_HEAD_DIM
B_EXPAND = 128
B_WIDTH = D_MODEL // 2
B_HEADS = B_WIDTH // B_EXPAND
B_DK = B_EXPAND
B_DV = B_WIDTH // B_HEADS
FFN_HIDDEN = int(math.ceil((8 * D_MODEL / 3) / 256) * 256)
ALPHA = (2 * DEPTH) ** 0.25
BETA = (8 * DEPTH) ** -0.25
LN_EPS = 1e-5
RMS_EPS = 1e-6

IN_WIDTHS = [A_WIDTH, A_WIDTH, A_WIDTH, A_HEADS,
             B_WIDTH, B_WIDTH, B_WIDTH, B_WIDTH,
             D_MODEL, D_MODEL]
IN_TOTAL = int(sum(IN_WIDTHS))
IN_SPLITS = [int(s) for s in np.cumsum(IN_WIDTHS)[:-1]]

kernel_name = "fox_hgrn2_gated_hybrid_deepnorm"


def layer_norm(x, g, b):
    xf = x.astype(jnp.float32)
    mu = jnp.mean(xf, axis=-1, keepdims=True)
    var = jnp.mean(jnp.square(xf - mu), axis=-1, keepdims=True)
    y = (xf - mu) * lax.rsqrt(var + LN_EPS) * g.astype(jnp.float32) + b.astype(jnp.float32)
    return y.astype(x.dtype)


def fox_attention(q, k, v, log_f):
    B, S, H, dh = q.shape
    nb = S // Q_BLOCK
    F = jnp.cumsum(log_f, axis=1).transpose(0, 2, 1)
    kh = k.transpose(0, 2, 1, 3)
    vh = v.transpose(0, 2, 1, 3)
    qb = q.reshape(B, nb, Q_BLOCK, H, dh).transpose(1, 0, 3, 2, 4)
    fq = F.reshape(B, H, nb, Q_BLOCK).transpose(2, 0, 1, 3)
    starts = jnp.arange(nb, dtype=jnp.int32) * Q_BLOCK
    key_pos = jnp.arange(S, dtype=jnp.int32)
    scale = dh ** -0.5

    def one_block(args):
        q_i, fq_i, start = args
        s = jnp.einsum('bhqd,bhkd->bhqk', q_i, kh).astype(jnp.float32) * scale
        s = s + fq_i[..., None] - F[:, :, None, :]
        q_pos = start + jnp.arange(Q_BLOCK, dtype=jnp.int32)
        causal = key_pos[None, :] <= q_pos[:, None]
        p = jax.nn.softmax(jnp.where(causal, s, -jnp.inf), axis=-1)
        return jnp.einsum('bhqk,bhkd->bhqd', p.astype(vh.dtype), vh)

    o = lax.map(one_block, (qb, fq, starts))
    return o.transpose(1, 0, 3, 2, 4).reshape(B, S, H * dh)


def hgrn2_chunkwise(q, k, v, log_f):
    B, S, H, dk = q.shape
    dv = v.shape[-1]
    nc = S // CHUNK

    def to_chunks(t):
        return t.reshape(B, nc, CHUNK, H, t.shape[-1]).transpose(1, 0, 3, 2, 4)

    qc, kc, vc, gc = to_chunks(q), to_chunks(k), to_chunks(v), to_chunks(log_f)
    incl = jnp.tril(jnp.ones((CHUNK, CHUNK), dtype=bool))

    def step(state, inp):
        q_i, k_i, v_i, g_i = inp
        b = jnp.cumsum(g_i, axis=2)
        o_inter = jnp.einsum('bhtd,bhde->bhte', q_i * jnp.exp(b), state)
        diff = b[:, :, :, None, :] - b[:, :, None, :, :]
        decay = jnp.exp(jnp.where(incl[None, None, :, :, None], diff, -jnp.inf))
        attn = jnp.einsum('bhtd,bhtsd,bhsd->bhts', q_i, decay, k_i)
        o_intra = jnp.einsum('bhts,bhse->bhte', attn, v_i)
        b_last = b[:, :, -1:, :]
        k_dec = k_i * jnp.exp(b_last - b)
        new_state = (jnp.exp(b_last[:, :, 0, :])[..., None] * state
                     + jnp.einsum('bhsd,bhse->bhde', k_dec, v_i))
        return new_state, o_inter + o_intra

    s0 = jnp.zeros((B, H, dk, dv), jnp.float32)
    _, o = lax.scan(step, s0, (qc, kc, vc, gc))
    return o.transpose(1, 0, 3, 2, 4).reshape(B, S, H, dv)


def hybrid_mixer(x, w_in, b_f, lb, norm_g, w_pa, w_pb, w_o):
    B, S, _ = x.shape
    proj = x @ w_in
    q_a, k_a, v_a, f_a, q_b, f_b, i_b, g_b, gate_a, gate_b = jnp.split(proj, IN_SPLITS, axis=-1)

    log_fa = jax.nn.log_sigmoid(f_a.astype(jnp.float32) + b_f.astype(jnp.float32))
    shp_a = (B, S, A_HEADS, A_HEAD_DIM)
    y_a = fox_attention(q_a.reshape(shp_a), k_a.reshape(shp_a), v_a.reshape(shp_a), log_fa)

    lbh = lb.astype(jnp.float32).reshape(B_HEADS, B_DK)
    f = lbh + (1.0 - lbh) * jax.nn.sigmoid(f_b.astype(jnp.float32).reshape(B, S, B_HEADS, B_DK))
    qh = jax.nn.silu(q_b.astype(jnp.float32)).reshape(B, S, B_HEADS, B_DK)
    vh = i_b.astype(jnp.float32).reshape(B, S, B_HEADS, B_DV)
    o_b = hgrn2_chunkwise(qh, 1.0 - f, vh, jnp.log(f))
    o_b = o_b * lax.rsqrt(jnp.mean(jnp.square(o_b), axis=-1, keepdims=True) + RMS_EPS)
    o_b = o_b * norm_g.astype(jnp.float32)
    y_b = (o_b.reshape(B, S, B_WIDTH) * jax.nn.sigmoid(g_b.astype(jnp.float32))).astype(x.dtype)

    p_a = y_a @ w_pa
    p_b = y_b @ w_pb
    merged = jax.nn.sigmoid(gate_a) * p_a + jax.nn.sigmoid(gate_b) * p_b
    return merged @ w_o


def swiglu_ffn(x, w_ff_in, w_ff_out):
    h = x @ w_ff_in
    u, g = jnp.split(h, 2, axis=-1)
    return (jax.nn.silu(g) * u) @ w_ff_out


def setup_inputs(seed: int = 0) -> dict:
    key = jax.random.key(seed)
    ks = jax.random.split(key, 16)
    f32 = jnp.float32
    x = jax.random.normal(ks[0], (BATCH, SEQ, D_MODEL), f32)
    col_scale = np.ones((IN_TOTAL,), np.float32)
    col_scale[2 * A_WIDTH:3 * A_WIDTH] = BETA
    v_b_start = 3 * A_WIDTH + A_HEADS + 2 * B_WIDTH
    col_scale[v_b_start:v_b_start + B_WIDTH] = BETA
    w_in = jax.random.normal(ks[1], (DEPTH, D_MODEL, IN_TOTAL), f32) * (D_MODEL ** -0.5) * jnp.asarray(col_scale)
    b_fgate = jnp.linspace(1.0, 5.0, A_HEADS, dtype=f32)[None, :] + 0.1 * jax.random.normal(ks[2], (DEPTH, A_HEADS), f32)
    hgrn_lb_logits = 0.1 * jax.random.normal(ks[3], (DEPTH, B_WIDTH), f32)
    hgrn_norm_g = 1.0 + 0.02 * jax.random.normal(ks[4], (DEPTH, B_DV), f32)
    w_branch_a = jax.random.normal(ks[5], (DEPTH, A_WIDTH, D_MODEL), f32) * (A_WIDTH ** -0.5) * BETA
    w_branch_b = jax.random.normal(ks[6], (DEPTH, B_WIDTH, D_MODEL), f32) * (B_WIDTH ** -0.5) * BETA
    w_out = jax.random.normal(ks[7], (DEPTH, D_MODEL, D_MODEL), f32) * (D_MODEL ** -0.5) * BETA
    ln1_g = 1.0 + 0.02 * jax.random.normal(ks[8], (DEPTH, D_MODEL), f32)
    ln1_b = 0.02 * jax.random.normal(ks[9], (DEPTH, D_MODEL), f32)
    w_ff_in = jax.random.normal(ks[10], (DEPTH, D_MODEL, 2 * FFN_HIDDEN), f32) * (D_MODEL ** -0.5)
    w_ff_out = jax.random.normal(ks[11], (DEPTH, FFN_HIDDEN, D_MODEL), f32) * (FFN_HIDDEN ** -0.5) * BETA
    ln2_g = 1.0 + 0.02 * jax.random.normal(ks[12], (DEPTH, D_MODEL), f32)
    ln2_b = 0.02 * jax.random.normal(ks[13], (DEPTH, D_MODEL), f32)
    return {"x": x, "w_in": w_in, "b_fgate": b_fgate, "hgrn_lb_logits": hgrn_lb_logits,
            "hgrn_norm_g": hgrn_norm_g, "w_branch_a": w_branch_a, "w_branch_b": w_branch_b,
            "w_out": w_out, "ln1_g": ln1_g, "ln1_b": ln1_b, "w_ff_in": w_ff_in,
            "w_ff_out": w_ff_out, "ln2_g": ln2_g, "ln2_b": ln2_b}


def reference(x, w_in, b_fgate, hgrn_lb_logits, hgrn_norm_g, w_branch_a, w_branch_b,
              w_out, ln1_g, ln1_b, w_ff_in, w_ff_out, ln2_g, ln2_b):
    sm = jax.nn.softmax(hgrn_lb_logits.astype(jnp.float32), axis=0)
    lower_bounds = jnp.cumsum(sm, axis=0) - sm[0:1]
    for l in range(DEPTH):
        mix = hybrid_mixer(x, w_in[l], b_fgate[l], lower_bounds[l], hgrn_norm_g[l],
                           w_branch_a[l], w_branch_b[l], w_out[l])
        x = layer_norm(ALPHA * x + mix, ln1_g[l], ln1_b[l])
        ffn = swiglu_ffn(x, w_ff_in[l], w_ff_out[l])
        x = layer_norm(ALPHA * x + ffn, ln2_g[l], ln2_b[l])
    return x
```

```python
import contextlib
import os
import numpy as np
import ml_dtypes
import concourse.bass as bass
import concourse.mybir as mybir
from concourse.bass_utils import run_bass_kernel_spmd

F32 = mybir.dt.float32
BF16 = mybir.dt.bfloat16
AF = mybir.ActivationFunctionType
ALU = mybir.AluOpType
NPBF = ml_dtypes.bfloat16

D = 1024
SEQ = 16384
DEPTH = 2
FFH = 2816
ALPHA = (2 * DEPTH) ** 0.25
LN_EPS = 1e-5
RMS_EPS = 1e-6
NDMA = 6
GSEM = [None]
GD = {}


class Buf:
    __slots__ = ("name", "writers", "readers")

    def __init__(self, name=""):
        self.name = name
        self.writers = []
        self.readers = []


class Op:
    __slots__ = ("eng", "fn", "deps", "dma", "is_ms", "ms", "extra_waits", "prog")

    def __init__(self, eng, fn):
        self.eng = eng
        self.fn = fn
        self.deps = []
        self.dma = None
        self.is_ms = False
        self.ms = 0
        self.extra_waits = []


class Prog:
    ENGS = ("pe", "act", "dve", "pool", "sp")

    def __init__(self, nc, stack, tag):
        self.nc = nc
        self.tag = tag
        self.q = {e: [] for e in self.ENGS}
        stack = GSEM[0]
        self.psem = {e: stack.enter_context(nc.semaphore(f"{tag}_p_{e}")) for e in self.ENGS}
        if "dsem" not in GD:
            GD["dsem"] = {e: [stack.enter_context(nc.semaphore(f"gd_{e}{i}")) for i in range(NDMA)] for e in ("sp", "pool")}
            GD["dcount"] = {"sp": 0, "pool": 0}
            GD["cc"] = stack.enter_context(nc.semaphore("gcc"))
            GD["ccn"] = 0
        self.dsem = GD["dsem"]
        self.dcount = GD["dcount"]

    def _track(self, op, r, w):
        deps = []
        for b in r:
            deps += b.writers
        for b in w:
            deps += b.writers
            deps += b.readers
        op.deps = [d for d in deps if d.prog is self]
        for b in r:
            if op.dma is None:
                b.readers = [x for x in b.readers if not (x.eng == op.eng and x.dma is None)]
            b.readers.append(op)
        for b in w:
            b.writers = [op]
            b.readers = []

    def op(self, eng, name, r=(), w=(), **kw):
        o = Op(eng, (name, kw))
        o.prog = self
        self._track(o, r, w)
        self.q[eng].append(o)
        return o

    def dma(self, eng, out, in_, r=(), w=()):
        o = Op(eng, ("dma_start", dict(out=out, in_=in_)))
        o.prog = self
        n = self.dcount[eng]
        self.dcount[eng] += 1
        sem = self.dsem[eng][n % NDMA]
        gen = n // NDMA
        o.dma = (sem, 16 * (gen + 1), 16)
        if gen > 0:
            o.extra_waits.append((sem, 16 * gen))
        self._track(o, r, w)
        self.q[eng].append(o)
        return o

    def coll(self, kind, ins, outs, groups, r=(), w=()):
        o = Op("pool", ("collective_compute", dict(kind=kind, op=ALU.bypass, replica_groups=groups, ins=ins, outs=outs)))
        o.prog = self
        GD["ccn"] += 1
        o.dma = (GD["cc"], GD["ccn"], 1)
        self._track(o, r, w)
        self.q["pool"].append(o)
        return o

    def finish(self):
        nc = self.nc
        lasts = []
        for e in self.ENGS:
            if self.q[e]:
                for o in reversed(self.q[e]):
                    if o.dma is None:
                        lasts.append(o)
                        break
        dma_final = []
        for e in ("sp", "pool"):
            n = self.dcount[e]
            for i in range(min(n, NDMA)):
                cnt = (n - 1 - i) // NDMA + 1
                dma_final.append((self.dsem[e][i], 16 * cnt))
        for e in self.ENGS:
            for o in self.q[e]:
                for d in o.deps:
                    if d.dma is None and (d.eng != e or e != "pe"):
                        d.is_ms = True
        for o in lasts:
            o.is_ms = True
        for e in self.ENGS:
            c = 0
            for o in self.q[e]:
                if o.dma is None and o.is_ms:
                    c += 1
                    o.ms = c
        psem = self.psem

        def mk(eng):
            def body(e):
                waited = {}

                def wait(sem, v):
                    k = id(sem)
                    if waited.get(k, 0) >= v:
                        return
                    waited[k] = v
                    e.wait_ge(sem, v)

                for o in self.q[eng]:
                    need = {}
                    for d in o.deps:
                        if d.dma is not None:
                            s, v = d.dma[0], d.dma[1]
                        elif d.eng == eng and eng == "pe":
                            continue
                        else:
                            s, v = psem[d.eng], d.ms
                        k = id(s)
                        if k not in need or need[k][1] < v:
                            need[k] = (s, v)
                    for s, v in o.extra_waits:
                        k = id(s)
                        if k not in need or need[k][1] < v:
                            need[k] = (s, v)
                    for s, v in need.values():
                        wait(s, v)
                    kw = {k: (v(e) if callable(v) else v) for k, v in o.fn[1].items()}
                    ins = getattr(e, o.fn[0])(**kw)
                    if o.dma is not None:
                        ins.then_inc(o.dma[0], o.dma[2])
                    elif o.is_ms:
                        ins.then_inc(psem[eng], 1)
                for o in lasts:
                    if o.eng != eng:
                        wait(psem[o.eng], o.ms)
                for s, v in dma_final:
                    wait(s, v)
                if GD["ccn"] > 0:
                    wait(GD["cc"], GD["ccn"])

            return body

        with nc.Block() as block:
            block.tensor(mk("pe"))
            block.scalar(mk("act"))
            block.vector(mk("dve"))
            block.gpsimd(mk("pool"))
            block.sync(mk("sp"))


class TileAlloc:
    def __init__(self, nc, stack, tag):
        self.nc = nc
        self.stack = stack
        self.tag = tag
        self.n = 0

    def sb(self, shape, dt, name="t"):
        self.n += 1
        t = self.stack.enter_context(self.nc.sbuf_tensor(f"{self.tag}_{name}{self.n}", list(shape), dt))
        return t, Buf(name)

    def ps(self, shape, dt, name="p"):
        self.n += 1
        t = self.stack.enter_context(self.nc.psum_tensor(f"{self.tag}_{name}{self.n}", list(shape), dt))
        return t, Buf(name)


WA_COLS = 898
C_QA, C_KA, C_QB, C_FB, C_GB, C_FA, C_VA = 0, 128, 256, 384, 512, 640, 642
CST_COLS = 1408
K_ID, K_NEG, K_BD, K_RESET, K_ONES = 0, 128, 256, 384, 896


def phase_A(nc, S, xg, wA, vecA, cst, Yw, scr, tag, ycb=None):
    NT = S // 512
    R4 = S // 4
    QT, KT, VA, QE, KE, KD, VB, GS = (scr[k] for k in ("QT", "KT", "VA", "QE", "KE", "KD", "VB", "GS"))

    with contextlib.ExitStack() as st0:
        TA0 = TileAlloc(nc, st0, tag + "g")
        A_sb, bA = TA0.sb([128, S // 64], F32, "Adec")
        cid, bcid = TA0.sb([128, 128], BF16, "ident")
        cneg, bcneg = TA0.sb([128, 128], BF16, "neg")
        cbd, bcbd = TA0.sb([128, 128], BF16, "bdtri")
        cones, bcones = TA0.sb([128, 128], BF16, "ones")
        conesf, bconesf = TA0.sb([128, 64], F32, "onesf")
        vec, bvec = TA0.sb([128, 8], F32, "vec")
        lbv, blbv = TA0.sb([128, 8], F32, "lbv")

        with contextlib.ExitStack() as st:
            P = Prog(nc, st, tag + "1")
            TA = TileAlloc(nc, st, tag + "1")
            dQT = [[Buf() for _ in range(NT)] for _ in range(2)]
            dKT = [[Buf() for _ in range(NT)] for _ in range(2)]
            dVA = [Buf() for _ in range(NT)]
            dH = [Buf() for _ in range(NT)]
            wsb, bw = TA.sb([128, 8, WA_COLS], BF16, "wA")
            creset, bcreset = TA.sb([128, 512], F32, "reset")
            xt = [TA.sb([128, 8, 512], BF16, "xt") for _ in range(2)]
            psF = [TA.ps([128, 512], F32, "psF") for _ in range(5)]
            psT = [TA.ps([128, 512], F32, "psT") for _ in range(2)]
            psX = TA.ps([128, 512], BF16, "psX")
            nF = [0]

            def nextF():
                nF[0] += 1
                return psF[nF[0] % 5]
            stq = [TA.sb([128, 512], BF16, "stq") for _ in range(2)]
            stk = [TA.sb([128, 512], BF16, "stk") for _ in range(2)]
            stv = [TA.sb([128, 4, 256], BF16, "stv") for _ in range(2)]
            sqe = [TA.sb([128, 512], BF16, "sqe") for _ in range(2)]
            ske = [TA.sb([128, 512], BF16, "ske") for _ in range(2)]
            skdT = [TA.sb([128, 512], BF16, "skdT") for _ in range(2)]
            skd = [TA.sb([128, 4, 128], BF16, "skd") for _ in range(2)]
            sgs = [TA.sb([128, 512], BF16, "sgs") for _ in range(2)]
            wf = [TA.sb([128, 512], F32, "wf") for _ in range(24)]
            fz = [TA.sb([2, 512], F32, "fz") for _ in range(6)]
            Fc = [TA.sb([2, 512], F32, "Fc") for _ in range(2)]
            fst = [TA.sb([2, 3, 512], BF16, "fst") for _ in range(2)]
            negrow, bnegrow = TA.sb([2, 3, 512], BF16, "negrow")
            fr = [TA.sb([2, 512], F32, "fr") for _ in range(4)]
            onesrow, bonesrow = TA.sb([2, 3, 512], BF16, "onesrow")

            wv = wA.rearrange("(kc p) n -> p kc n", p=128)
            for kc0 in range(0, 8, 2):
                P.dma("pool", wsb[:, kc0:kc0 + 2, :], wv[:, kc0:kc0 + 2, :], w=[bw])
            P.dma("pool", cid[:], cst[:, K_ID:K_ID + 128], w=[bcid])
            P.dma("pool", cneg[:], cst[:, K_NEG:K_NEG + 128], w=[bcneg])
            P.dma("pool", cbd[:], cst[:, K_BD:K_BD + 128], w=[bcbd])
            P.dma("sp", creset[:], cst[:, K_RESET:K_RESET + 512], w=[bcreset])
            P.dma("pool", cones[:], cst[:, K_ONES:K_ONES + 128], w=[bcones])
            P.dma("sp", conesf[:], cst[:, K_ONES:K_ONES + 64], w=[bconesf])
            P.dma("sp", vec[:], vecA, w=[bvec])
            P.op("pool", "memset", ap=onesrow[:], constant=1.0, w=[bonesrow])
            P.op("pool", "memset", ap=negrow[:], constant=-1.0, w=[bnegrow])
            c_ = lambda i: lbv[:, i:i + 1]
            rw = dict(r=[blbv, bvec], w=[blbv])
            P.op("dve", "tensor_tensor", out=c_(0), in0=vec[:, 0:1], in1=vec[:, 1:2], op=ALU.max, **rw)
            P.op("dve", "tensor_tensor", out=c_(1), in0=vec[:, 0:1], in1=c_(0), op=ALU.subtract, **rw)
            P.op("dve", "tensor_tensor", out=c_(2), in0=vec[:, 1:2], in1=c_(0), op=ALU.subtract, **rw)
            P.op("act", "activation", out=lbv[:, 1:3], in_=lbv[:, 1:3], func=AF.Exp, **rw)
            P.op("dve", "tensor_tensor", out=c_(3), in0=c_(1), in1=c_(2), op=ALU.add, **rw)
            P.op("dve", "reciprocal", out=c_(4), in_=c_(3), **rw)
            P.op("dve", "tensor_tensor", out=c_(5), in0=c_(1), in1=c_(4), op=ALU.mult, **rw)
            P.op("dve", "tensor_tensor", out=c_(6), in0=c_(2), in1=c_(4), op=ALU.mult, **rw)
            P.op("dve", "scalar_tensor_tensor", out=c_(7), in0=c_(6), scalar=vec[:, 4:5], in1=c_(5), op0=ALU.mult, op1=ALU.add, **rw)
            P.op("dve", "tensor_tensor", out=c_(0), in0=c_(7), in1=c_(5), op=ALU.subtract, **rw)
            P.op("dve", "tensor_scalar", out=c_(1), in0=c_(0), scalar1=-1.0, scalar2=1.0, op0=ALU.mult, op1=ALU.add, **rw)
            P.op("dve", "tensor_scalar", out=c_(2), in0=vec[:, 3:4], scalar1=-1.0, scalar2=None, op0=ALU.mult, **rw)
            LB, OML, NBF = lbv[:, 0:1], lbv[:, 1:2], lbv[0:2, 2:3]

            for h in range(2):
                for i in range(NT):
                    tok = slice(i * 512, (i + 1) * 512)
                    P.dma("sp", QT[h:h + 1, 67:70, tok], negrow[0:1, :, :], r=[bnegrow], w=[dQT[h][i]])
                    P.dma("sp", KT[h:h + 1, 64:67, tok], onesrow[0:1, :, :], r=[bonesrow], w=[dKT[h][i]])

            def load_x(i):
                t, b = xt[i % 2]
                r_, c0 = divmod(i * 512, R4)
                src = xg(r_, c0) if callable(xg) else xg[r_].rearrange("(kc p) t -> p kc t", p=128)[:, :, c0:c0 + 512]
                for kc0 in range(0, 8, 4):
                    P.dma("pool", t[:, kc0:kc0 + 4, :], src[:, kc0:kc0 + 4, :], w=[b])

            load_x(0)
            Fprev = [None]

            def tiles(i):
                s3 = i % 3
                return wf[8 * s3:8 * s3 + 8], fz[2 * s3:2 * s3 + 2]

            def stA(i):
                if i + 1 < NT:
                    load_x(i + 1)
                xs, bx = xt[i % 2]
                tok = slice(i * 512, (i + 1) * 512)
                s2 = i % 2
                ((sig, bsig), (g_, bg), (bb, bbb), (eb, beb), (enb, benb), (ebr, bebr), (kk, bkk), (qs, bqs)), ((z0, bz0), (z1, bz1)) = tiles(i)

                def fgroup(col, M):
                    pt, pb = nextF()
                    for kc in range(8):
                        P.op("pe", "matmul", out=pt[0:M, :], lhsT=wsb[:, kc, col:col + M], rhs=xs[:, kc, :], start=(kc == 0), stop=(kc == 7),
                             r=[bw, bx], w=[pb])
                    return pt, pb
                pt, pb = fgroup(C_QA, 128)
                t, b = stq[s2]
                P.op("dve", "tensor_scalar", out=t[:], in0=pt[:], scalar1=0.125, scalar2=None, op0=ALU.mult, r=[pb], w=[b])
                for h in range(2):
                    P.dma("sp", QT[h, 0:64, tok], t[h * 64:(h + 1) * 64, :], r=[b], w=[dQT[h][i]])
                pt, pb = fgroup(C_KA, 128)
                t, b = stk[s2]
                P.op("dve", "tensor_copy", out=t[:], in_=pt[:], r=[pb], w=[b])
                for h in range(2):
                    P.dma("sp", KT[h, 0:64, tok], t[h * 64:(h + 1) * 64, :], r=[b], w=[dKT[h][i]])
                pt, pb = fgroup(C_FB, 128)
                P.op("act", "activation", out=sig[:], in_=pt[:], func=AF.Sigmoid, r=[pb], w=[bsig])
                pt, pb = fgroup(C_GB, 128)
                t, b = sgs[s2]
                P.op("act", "activation", out=t[:], in_=pt[:], func=AF.Sigmoid, r=[pb], w=[b])
                P.dma("sp", GS[:, tok], t[:], r=[b], w=[dH[i]])
                pt, pb = fgroup(C_QB, 128)
                P.op("act", "activation", out=qs[:], in_=pt[:], func=AF.Silu, r=[pb], w=[bqs])
                pt, pb = fgroup(C_FA, 2)
                P.op("act", "activation", out=z0[:], in_=pt[0:2, :], func=AF.Exp, scale=-1.0, bias=NBF, r=[pb, blbv], w=[bz0])
                t, b = stv[s2]
                for sb_ in range(4):
                    pt, pb = psT[sb_ % 2]
                    for kc in range(8):
                        P.op("pe", "matmul", out=pt[:, 0:256], lhsT=xs[:, kc, sb_ * 128:(sb_ + 1) * 128], rhs=wsb[:, kc, C_VA:C_VA + 256],
                             start=(kc == 0), stop=(kc == 7), r=[bw, bx], w=[pb])
                    P.op("act", "copy", out=t[:, sb_, :], in_=pt[:, 0:256], r=[pb], w=[b])
                P.dma("sp", VA[tok, :].rearrange("(a p) d -> p a d", p=128), t[:, :, 0:128], r=[b], w=[dVA[i]])
                P.dma("sp", VB[tok, :].rearrange("(a p) d -> p a d", p=128), t[:, :, 128:256], r=[b], w=[dH[i]])
                P.op("dve", "tensor_scalar", out=sig[:], in0=sig[:], scalar1=OML, scalar2=LB, op0=ALU.mult, op1=ALU.add, r=[bsig, blbv], w=[bsig])

            def stB1(i):
                tok = slice(i * 512, (i + 1) * 512)
                s2 = i % 2
                ((sig, bsig), (g_, bg), (bb, bbb), (eb, beb), (enb, benb), (ebr, bebr), (kk, bkk), (qs, bqs)), ((z0, bz0), (z1, bz1)) = tiles(i)
                P.op("act", "activation", out=g_[:], in_=sig[:], func=AF.Ln, r=[bsig], w=[bg])
                P.op("act", "activation", out=z1[:], in_=z0[:], func=AF.Ln, bias=1.0, r=[bz0], w=[bz1])
                P.op("pool", "tensor_scalar", out=kk[:], in0=sig[:], scalar1=-1.0, scalar2=1.0, op0=ALU.mult, op1=ALU.add, r=[bsig], w=[bkk])
                for c in range(8):
                    cs = slice(c * 64, (c + 1) * 64)
                    P.op("dve", "tensor_tensor_scan", out=bb[:, cs], data0=g_[:, cs], data1=g_[:, cs], initial=0.0, op0=ALU.add, op1=ALU.bypass, r=[bg], w=[bbb])
                Fc_t, Fc_b = Fc[s2]
                Fp = Fprev[0]
                init = 0.0 if Fp is None else Fp[0][:, 511:512]
                rr = [bz1] + ([] if Fp is None else [Fp[1]])
                P.op("dve", "tensor_scalar", out=z1[:], in0=z1[:], scalar1=-1.0, scalar2=None, op0=ALU.mult, r=[bz1], w=[bz1])
                P.op("dve", "tensor_tensor_scan", out=Fc_t[:], data0=z1[:], data1=z1[:], initial=init, op0=ALU.add, op1=ALU.bypass, r=rr, w=[Fc_b])
                Fprev[0] = (Fc_t, Fc_b)
                ft, fb = fst[s2]
                (r1, br1), (r2, br2) = fr[2 * s2:2 * s2 + 2]
                P.op("dve", "tensor_copy", out=ft[:, 0, :], in_=Fc_t[:], r=[Fc_b], w=[fb])
                P.op("dve", "tensor_tensor", out=r1[:], in0=Fc_t[:], in1=ft[:, 0, :], op=ALU.subtract, r=[Fc_b, fb], w=[br1])
                P.op("dve", "tensor_copy", out=ft[:, 1, :], in_=r1[:], r=[br1], w=[fb])
                P.op("dve", "tensor_tensor", out=r2[:], in0=r1[:], in1=ft[:, 1, :], op=ALU.subtract, r=[br1, fb], w=[br2])
                P.op("dve", "tensor_copy", out=ft[:, 2, :], in_=r2[:], r=[br2], w=[fb])
                for h in range(2):
                    P.dma("sp", QT[h:h + 1, 64:67, tok], ft[h:h + 1, 0:3, :], r=[fb], w=[dQT[h][i]])
                    P.dma("sp", KT[h:h + 1, 67:70, tok], ft[h:h + 1, 0:3, :], r=[fb], w=[dKT[h][i]])

            def stB2(i):
                tok = slice(i * 512, (i + 1) * 512)
                s2 = i % 2
                ((sig, bsig), (g_, bg), (bb, bbb), (eb, beb), (enb, benb), (ebr, bebr), (kk, bkk), (qs, bqs)), _ = tiles(i)
                P.op("act", "activation", out=eb[:], in_=bb[:], func=AF.Exp, r=[bbb], w=[beb])
                P.op("act", "activation", out=enb[:], in_=bb[:], func=AF.Exp, scale=-1.0, r=[bbb], w=[benb])
                for c in range(8):
                    cs = slice(c * 64, (c + 1) * 64)
                    last = slice(c * 64 + 63, c * 64 + 64)
                    P.op("act", "activation", out=ebr[:, cs], in_=bb[:, cs], func=AF.Exp, scale=-1.0, bias=bb[:, last], r=[bbb], w=[bebr])
                    P.op("pool", "tensor_copy", out=A_sb[:, i * 8 + c:i * 8 + c + 1], in_=eb[:, last], r=[beb], w=[bA])
                t, b = ske[s2]
                P.op("dve", "tensor_tensor", out=t[:], in0=kk[:], in1=enb[:], op=ALU.mult, r=[bkk, benb], w=[b])
                P.dma("sp", KE[:, tok], t[:], r=[b], w=[dH[i]])
                t, b = sqe[s2]
                P.op("pool", "tensor_tensor", out=t[:], in0=qs[:], in1=eb[:], op=ALU.mult, r=[bqs, beb], w=[b])
                P.dma("sp", QE[:, tok], t[:], r=[b], w=[dH[i]])
                tT, bT = skdT[s2]
                P.op("dve", "tensor_tensor", out=tT[:], in0=kk[:], in1=ebr[:], op=ALU.mult, r=[bkk, bebr], w=[bT])

            def stB3(i):
                tok = slice(i * 512, (i + 1) * 512)
                s2 = i % 2
                tT, bT = skdT[s2]
                for sb_ in range(4):
                    bs = slice(sb_ * 128, (sb_ + 1) * 128)
                    P.op("pe", "transpose", out=psX[0][:, bs], in_=tT[:, bs], identity=cid[:], r=[bT, bcid], w=[psX[1]])
                t, b = skd[s2]
                P.op("dve", "tensor_copy", out=t[:].rearrange("p a d -> p (a d)"), in_=psX[0][:], r=[psX[1]], w=[b])
                P.dma("sp", KD[tok, :].rearrange("(a p) d -> p a d", p=128), t[:], r=[b], w=[dH[i]])

            for t_ in range(NT + 3):
                if t_ < NT:
                    stA(t_)
                if 0 <= t_ - 1 < NT:
                    stB1(t_ - 1)
                if 0 <= t_ - 2 < NT:
                    stB2(t_ - 2)
                if 0 <= t_ - 3 < NT:
                    stB3(t_ - 3)
            P.finish()
        if os.environ.get("STOP_AFTER") == "1":
            return

        with contextlib.ExitStack() as st:
            P = Prog(nc, st, tag + "2")
            TA = TileAlloc(nc, st, tag + "2")
            NB = S // 128
            dYp = {(p_, r_): Buf() for p_ in range(3) for r_ in range(4)}
            qt, bqt = TA.sb([70, S], BF16, "qt")
            kt, bkt = TA.sb([70, S], BF16, "kt")
            vv, bvv = TA.sb([128, NB, 65], BF16, "vv")
            NS, NP_ = 3, 4
            psS = [TA.ps([128, 512], F32, "psS") for _ in range(NS)]
            psO = [TA.ps([128, 512], F32, "psO") for _ in range(2)]
            psX = TA.ps([128, 512], F32, "psX")
            psU = TA.ps([128, 512], F32, "psU")
            psOo = TA.ps([128, 512], F32, "psOo")
            pT = [TA.sb([128, 512], BF16, "pT") for _ in range(NP_)]
            osb = [TA.sb([64, 512], F32, "osb") for _ in range(2)]
            rden = [TA.sb([128, 512], F32, "rden") for _ in range(2)]
            yst = [TA.sb([64, 512], BF16, "yst") for _ in range(2)]
            SEG = min(2048, S)
            NSEG = S // SEG
            seg = []
            for k in range(2):
                seg.append(dict(
                    qe=TA.sb([128, SEG], BF16, "qe"), ke=TA.sb([128, SEG], BF16, "ke"),
                    kd=TA.sb([64, SEG // 64, 128], BF16, "kd"), vb=TA.sb([64, SEG // 64, 128], BF16, "vb"),
                    gs=TA.sb([128, SEG], BF16, "gs")))
            state = [TA.sb([128, 128], F32, "state") for _ in range(2)]
            sbf = [TA.sb([128, 8, 128], BF16, "sbf") for _ in range(2)]
            at = [TA.sb([128, 512], BF16, "at") for _ in range(2)]
            o_sb = [TA.sb([128, 512], F32, "o_sb") for _ in range(2)]
            sq = [TA.sb([128, 512], BF16, "sq") for _ in range(2)]
            rstd = [TA.sb([128, 512], F32, "rstd") for _ in range(2)]
            yb = [TA.sb([128, 512], BF16, "yb") for _ in range(2)]
            P.op("dve", "memset", ap=state[0][0][:], constant=0.0, w=[state[0][1]])
            P.op("dve", "memset", ap=sbf[0][0][:, 0, :], constant=0.0, w=[sbf[0][1]])

            def load_seg(k):
                sgm = seg[k % 2]
                c0 = k * SEG
                P.dma("sp", sgm["qe"][0][:], QE[:, c0:c0 + SEG], w=[sgm["qe"][1]])
                P.dma("pool", sgm["ke"][0][:], KE[:, c0:c0 + SEG], w=[sgm["ke"][1]])
                P.dma("sp", sgm["gs"][0][:], GS[:, c0:c0 + SEG], w=[sgm["gs"][1]])
                P.dma("pool", sgm["kd"][0][:], KD[c0:c0 + SEG, :].rearrange("(a p) d -> p a d", p=64), w=[sgm["kd"][1]])
                P.dma("sp", sgm["vb"][0][:], VB[c0:c0 + SEG, :].rearrange("(a p) d -> p a d", p=64), w=[sgm["vb"][1]])

            cur = [0]

            def hgrn_stages(i):
                k, ti = divmod(i * 512, SEG)
                sgm = seg[k % 2]
                qe, bqe = sgm["qe"]
                ke, bke = sgm["ke"]
                kd, bkd = sgm["kd"]
                vb, bvb = sgm["vb"]
                gs, bgs = sgm["gs"]
                s2 = i % 2
                sb_t, sb_b = sbf[s2]
                nsb_t, nsb_b = sbf[1 - s2]
                pu, pub = psU
                pa, pab = psU
                po, pob = psOo
                at_t, at_b = at[s2]
                o_t, o_b = o_sb[s2]
                sq_t, sq_b = sq[s2]
                rs_t, rs_b = rstd[s2]
                yb_t, yb_b = yb[s2]

                def st_load():
                    if ti == 0:
                        load_seg(k)

                def st_u(c0):
                    def f():
                        for c in range(c0, c0 + 4):
                            ch = (ti + c * 64) // 64
                            P.op("pe", "matmul", out=pu[:, (c % 4) * 128:(c % 4 + 1) * 128], lhsT=kd[:, ch, :], rhs=vb[:, ch, :],
                                 start=True, stop=True, r=[bkd, bvb], w=[pub])
                    return f

                def st_state(c0):
                    def f():
                        for c in range(c0, c0 + 4):
                            so_t, so_b = state[cur[0]]
                            sn_t, sn_b = state[1 - cur[0]]
                            P.op("dve", "scalar_tensor_tensor", out=sn_t[:], in0=so_t[:], scalar=A_sb[:, i * 8 + c:i * 8 + c + 1],
                                 in1=pu[:, (c % 4) * 128:(c % 4 + 1) * 128], op0=ALU.mult, op1=ALU.add, r=[so_b, bA, pub], w=[sn_b])
                            if c < 7:
                                P.op("pool", "tensor_copy", out=sb_t[:, c + 1, :], in_=sn_t[:], r=[sn_b], w=[sb_b])
                            else:
                                P.op("pool", "tensor_copy", out=nsb_t[:, 0, :], in_=sn_t[:], r=[sn_b], w=[nsb_b])
                            cur[0] = 1 - cur[0]
                    return f

                def st_attn():
                    for c in range(8):
                        tk = slice(ti + c * 64, ti + (c + 1) * 64)
                        P.op("pe", "matmul", out=pa[0:64, c * 64:(c + 1) * 64], lhsT=ke[:, tk], rhs=qe[:, tk], start=True, stop=True, r=[bke, bqe], w=[pab])

                def st_mask():
                    for c in range(8):
                        cs = slice(c * 64, (c + 1) * 64)
                        P.op("dve", "tensor_tensor", out=at_t[0:64, cs], in0=pa[0:64, cs], in1=cbd[0:64, 0:64], op=ALU.mult, r=[pab, bcbd], w=[at_b])

                def st_o():
                    for c in range(8):
                        ch = (ti + c * 64) // 64
                        cs = slice(c * 64, (c + 1) * 64)
                        P.op("pe", "matmul", out=po[:, cs], lhsT=vb[:, ch, :], rhs=at_t[0:64, cs], start=True, stop=False, r=[bvb, at_b], w=[pob])
                        P.op("pe", "matmul", out=po[:, cs], lhsT=sb_t[:, c, :], rhs=qe[:, ti + c * 64:ti + (c + 1) * 64],
                             start=False, stop=True, r=[sb_b, bqe], w=[pob])

                def st_sq():
                    P.op("dve", "tensor_copy", out=o_t[:], in_=po[:], r=[pob], w=[o_b])
                    P.op("pool", "tensor_tensor", out=sq_t[:], in0=o_t[:], in1=o_t[:], op=ALU.mult, r=[o_b], w=[sq_b])

                def st_m():
                    P.op("pe", "matmul", out=pa[:], lhsT=cones[:], rhs=sq_t[:], start=True, stop=True, r=[bcones, sq_b], w=[pab])

                def st_rs():
                    P.op("dve", "tensor_scalar", out=rs_t[:], in0=pa[:], scalar1=1.0 / 128.0, scalar2=RMS_EPS, op0=ALU.mult, op1=ALU.add, r=[pab], w=[rs_b])

                def st_sqrt():
                    P.op("act", "activation", out=rs_t[:], in_=rs_t[:], func=AF.Sqrt, r=[rs_b], w=[rs_b])

                def st_fin():
                    P.op("dve", "reciprocal", out=rs_t[:], in_=rs_t[:], r=[rs_b], w=[rs_b])
                    P.op("pool", "tensor_tensor", out=o_t[:], in0=o_t[:], in1=rs_t[:], op=ALU.mult, r=[o_b, rs_b], w=[o_b])
                    P.op("dve", "scalar_tensor_tensor", out=yb_t[:], in0=o_t[:], scalar=vec[:, 2:3], in1=gs[:, ti:ti + 512], op0=ALU.mult, op1=ALU.mult,
                         r=[o_b, bvec, bgs], w=[yb_b])
                    TPC = (S // 4) // 512
                    P.dma("sp", Yw(slice(128, 256), i), yb_t[:], r=[yb_b], w=[dYp[(2, i // TPC)]])
                    if ycb is not None and i % TPC == TPC - 1:
                        ycb(P, 2, i // TPC, dYp[(2, i // TPC)])
                return [st_load, st_u(0), st_state(0), st_u(4), st_state(4), st_attn, st_mask, st_o, st_sq, st_m, st_rs, st_sqrt, st_fin]

            allblocks = []
            for h in range(2):
                for I in range(NT):
                    nk = 4 * I + 4
                    for jb in range(nk):
                        allblocks.append((h, I, jb, nk))
            nblk = len(allblocks)
            per_tile = max(1, int(nblk * 0.85) // NT)
            GAP = max(1, min(6, (per_tile - 2) // 13))
            sched = {}
            for i in range(NT):
                for si in range(13):
                    sched.setdefault(i * per_tile + 2 + si * GAP, []).append((i, si))
            stage_cache = {}
            pending = []
            ndone = [0]
            CH = min(2048, S)

            NPC = S // CH
            bqtp = [Buf() for _ in range(NPC)]
            bktp = [Buf() for _ in range(NPC)]
            bvvp = [Buf() for _ in range(NPC)]

            def load_head(h):
                for c0 in range(0, S, CH):
                    pc = c0 // CH
                    P.dma("pool", kt[:, c0:c0 + CH], KT[h, :, c0:c0 + CH], w=[bktp[pc]])
                    P.dma("sp", qt[:, c0:c0 + CH], QT[h, :, c0:c0 + CH], w=[bqtp[pc]])
                    P.dma("sp", vv[:, c0 // 128:(c0 + CH) // 128, 0:64],
                          VA[c0:c0 + CH, h * 64:(h + 1) * 64].rearrange("(a p) d -> p a d", p=128), w=[bvvp[pc]])

            P.op("pool", "memset", ap=vv[:, :, 64:65], constant=1.0, w=bvvp)
            loaded = set()

            def s_mm(n):
                h, I, jb, nk = allblocks[n]
                d = jb - 4 * I
                qlo = 128 * d if d > 0 else 0
                pt, pb = psS[n % NS]
                P.op("pe", "matmul", out=pt[:, qlo:512], lhsT=kt[0:70, jb * 128:(jb + 1) * 128], rhs=qt[0:70, I * 512 + qlo:I * 512 + 512],
                     start=True, stop=(d < 0), r=[bqtp[(I * 512) // CH], bktp[(jb * 128) // CH]], w=[pb])
                if d >= 0:
                    P.op("pe", "matmul", out=pt[:, qlo:qlo + 128], lhsT=cid[:], rhs=cneg[:], start=False, stop=True, r=[bcid, bcneg], w=[pb])

            qcnt = [0]

            def fin(h, I):
                slot = qcnt[0] % 2
                qcnt[0] += 1
                for pe_ in [p_ for p_ in pending if p_[3] == slot]:
                    pending.remove(pe_)
                    fin2(pe_[0], pe_[1], pe_[3])
                ot, ob = psO[I % 2]
                o_, bo_ = osb[slot]
                rd, brd = rden[slot]
                P.op("dve", "reciprocal", out=rd[64:65, :], in_=ot[64:65, :], r=[ob], w=[brd])
                P.op("dve", "tensor_copy", out=o_[:], in_=ot[0:64, :], r=[ob], w=[bo_])
                pending.append((h, I, ndone[0] + 8, slot))

            def fin2(h, I, slot):
                o_, bo_ = osb[slot]
                rd, brd = rden[slot]
                y_, by_ = yst[slot]
                P.op("pe", "matmul", out=psX[0][0:64, :], lhsT=conesf[64:65, 0:64], rhs=rd[64:65, :], start=True, stop=True, r=[brd, bconesf], w=[psX[1]])
                P.op("dve", "tensor_tensor", out=y_[:], in0=o_[:], in1=psX[0][0:64, :], op=ALU.mult, r=[bo_, psX[1]], w=[by_])
                TPC = (S // 4) // 512
                P.dma("sp", Yw(slice(h * 64, (h + 1) * 64), I), y_[:], r=[by_], w=[dYp[(h, I // TPC)]])
                if ycb is not None and I % TPC == TPC - 1:
                    ycb(P, h, I // TPC, dYp[(h, I // TPC)])

            def exp_pv(n):
                h, I, jb, nk = allblocks[n]
                d = jb - 4 * I
                qlo = 128 * d if d > 0 else 0
                pt, pb = psS[n % NS]
                t, b = pT[n % NP_]
                P.op("act", "activation", out=t[:, qlo:512], in_=pt[:, qlo:512], func=AF.Exp, r=[pb], w=[b])
                ot, ob = psO[I % 2]
                P.op("pe", "matmul", out=ot[0:65, qlo:512], lhsT=vv[:, jb, 0:65], rhs=t[:, qlo:512], start=(jb == 0), stop=(jb == nk - 1),
                     r=[bvvp[(jb * 128) // CH], b], w=[ob])
                if jb == nk - 1:
                    fin(h, I)

            def run_stage(i, si):
                if i not in stage_cache:
                    stage_cache[i] = hgrn_stages(i)
                stage_cache[i][si]()

            LOOK = 2
            for n in range(nblk):
                hcur = allblocks[n][0]
                if n == 0 or allblocks[n - 1][0] != hcur:
                    load_head(hcur)
                    for m in range(n, min(n + LOOK, nblk)):
                        s_mm(m)
                if n + LOOK < nblk and allblocks[n + LOOK][0] == hcur:
                    s_mm(n + LOOK)
                exp_pv(n)
                ndone[0] += 1
                while pending and pending[0][2] <= ndone[0]:
                    h_, I_, _, sl_ = pending.pop(0)
                    fin2(h_, I_, sl_)
                for (i, si) in sched.pop(n, []):
                    run_stage(i, si)
            while pending:
                h_, I_, _, sl_ = pending.pop(0)
                fin2(h_, I_, sl_)
            for n in sorted(sched.keys()):
                for (i, si) in sched[n]:
                    run_stage(i, si)
            P.finish()


def alloc_scratch_A(nc, S, tag):
    def dt(name, shape, dty=BF16):
        return nc.dram_tensor(f"{tag}_{name}", shape, dty, kind="Internal").ap()
    return dict(QT=dt("QT", [2, 70, S]), KT=dt("KT", [2, 70, S]), VA=dt("VA", [S, 128]),
                QE=dt("QE", [128, S]), KE=dt("KE", [128, S]), KD=dt("KD", [S, 128]),
                VB=dt("VB", [S, 128]), GS=dt("GS", [128, S]))


def build_A(S, x_f32=True):
    nc = bass.Bass("TRN2", target_bir_lowering=False)
    R4 = S // 4
    xg = nc.dram_tensor("xg", [4, 1024, R4], F32 if x_f32 else BF16, kind="ExternalInput").ap()
    wA = nc.dram_tensor("wA", [1024, WA_COLS], F32, kind="ExternalInput").ap()
    vecA = nc.dram_tensor("vecA", [128, 8], F32, kind="ExternalInput").ap()
    cst = nc.dram_tensor("cst", [128, CST_COLS], F32, kind="ExternalInput").ap()
    Y = nc.dram_tensor("Y", [256, S], BF16, kind="ExternalOutput").ap()
    scr = alloc_scratch_A(nc, S, "a")
    with contextlib.ExitStack() as gst:
        GSEM[0] = gst
        GD.clear()
        phase_A(nc, S, xg, wA, vecA, cst, lambda rows, i: Y[rows, i * 512:(i + 1) * 512], scr, "A")
    return nc


def ln_alloc(TA, nb=2):
    return dict(hb=[TA.sb([128, 512], BF16, "hb") for _ in range(nb)], hq=[TA.sb([128, 512], BF16, "hq") for _ in range(nb)],
                mean=TA.sb([128, 512], F32, "mean"), rs=TA.sb([128, 512], F32, "rs"))


def layer_norm_T(P, LT, nextP, h, bh, gcol, bcol, bvecB, conesK, bconesK, out_writer):
    hb, hq = LT["hb"], LT["hq"]
    mean, bmean = LT["mean"]
    rs, brs = LT["rs"]
    pm, pmb = nextP()
    pq, pqb = nextP()
    for c in range(8):
        t, b = hb[c % len(hb)]
        q, bq = hq[c % len(hq)]
        P.op("act", "copy", out=t[:], in_=h[:, c, :], r=[bh[c]], w=[b])
        P.op("dve", "tensor_tensor", out=q[:], in0=h[:, c, :], in1=h[:, c, :], op=ALU.mult, r=[bh[c]], w=[bq])
        P.op("pe", "matmul", out=pm[:], lhsT=conesK[:], rhs=t[:], start=(c == 0), stop=(c == 7), r=[bconesK, b], w=[pmb])
        P.op("pe", "matmul", out=pq[:], lhsT=conesK[:], rhs=q[:], start=(c == 0), stop=(c == 7), r=[bconesK, bq], w=[pqb])
    P.op("act", "copy", out=mean[:], in_=pm[:], r=[pmb], w=[bmean])
    P.op("dve", "tensor_tensor", out=rs[:], in0=mean[:], in1=mean[:], op=ALU.mult, r=[bmean], w=[brs])
    P.op("dve", "tensor_tensor", out=rs[:], in0=pq[:], in1=rs[:], op=ALU.subtract, r=[pqb, brs], w=[brs])
    P.op("dve", "tensor_scalar", out=rs[:], in0=rs[:], scalar1=LN_EPS, scalar2=None, op0=ALU.add, r=[brs], w=[brs])
    P.op("act", "activation", out=rs[:], in_=rs[:], func=AF.Sqrt, r=[brs], w=[brs])
    P.op("dve", "reciprocal", out=rs[:], in_=rs[:], r=[brs], w=[brs])
    for c in range(8):
        eng = "dve" if c % 2 == 0 else "pool"
        P.op(eng, "tensor_tensor", out=h[:, c, :], in0=h[:, c, :], in1=mean[:], op=ALU.subtract, r=[bh[c], bmean], w=[bh[c]])
        P.op(eng, "tensor_tensor", out=h[:, c, :], in0=h[:, c, :], in1=rs[:], op=ALU.mult, r=[bh[c], brs], w=[bh[c]])
        P.op(eng, "tensor_scalar", out=h[:, c, :], in0=h[:, c, :], scalar1=gcol(c), scalar2=bcol(c), op0=ALU.mult, op1=ALU.add, r=[bh[c], bvecB], w=[bh[c]])
    out_writer()


def phase_B(nc, R4, Yg, xres, wg, wab, wo, wfi, wfo, vecB, cst, X1, XO, XOB, tag, xcb=None):
    NT = R4 // 512
    with contextlib.ExitStack() as st0:
        TA0 = TileAlloc(nc, st0, tag + "g")
        vB, bvB = TA0.sb([128, 32], F32, "vecB")
        conesK, bconesK = TA0.sb([128, 128], BF16, "onesK")
        onesf, bonesf = TA0.sb([128, 128], F32, "onesf")
        dX1 = [Buf() for _ in range(NT)]

        with contextlib.ExitStack() as st:
            P = Prog(nc, st, tag + "1")
            TA = TileAlloc(nc, st, tag + "1")
            wg_sb, bwg = TA.sb([128, 8, 2048], BF16, "wg")
            wab_sb, bwab = TA.sb([128, 8, 1024], BF16, "wab")
            wo_sb, bwo = TA.sb([128, 8, 1024], BF16, "wo")
            xr = [TA.sb([128, 8, 512], F32, "xr") for _ in range(2)]
            xb = [TA.sb([128, 8, 512], BF16, "xb") for _ in range(2)]
            yt = [TA.sb([128, 8, 512], BF16, "yt") for _ in range(2)]
            mg, bmg = TA.sb([128, 8, 512], BF16, "mg")
            sga = [TA.sb([128, 512], F32, "sga") for _ in range(2)]
            sgb = [TA.sb([128, 512], F32, "sgb") for _ in range(2)]
            t1 = [TA.sb([128, 512], F32, "t1") for _ in range(2)]
            t2 = [TA.sb([128, 512], F32, "t2") for _ in range(2)]
            hh = [(TA.sb([128, 8, 512], F32, "hh")[0], [Buf() for _ in range(8)]) for _ in range(2)]
            psl = [TA.ps([128, 512], F32, "ps") for _ in range(8)]
            LT = ln_alloc(TA, 4)
            npz = [0]

            def nextP():
                npz[0] += 1
                return psl[npz[0] % 8]
            P.dma("sp", vB[:], vecB, w=[bvB])
            P.dma("sp", onesf[:], cst[:, K_ONES:K_ONES + 128], w=[bonesf])
            P.op("dve", "tensor_scalar", out=conesK[:], in0=onesf[:], scalar1=1.0 / 1024.0, scalar2=None, op0=ALU.mult, r=[bonesf], w=[bconesK])
            wgv = wg.rearrange("(kc p) n -> p kc n", p=128)
            bwg_m = [Buf() for _ in range(8)]
            PRE_LOAD0 = True

            def load(i):
                tok = slice(i * 512, (i + 1) * 512)
                xv = xres.rearrange("(kc p) t -> p kc t", p=128)[:, :, tok]
                yv = Yg.rearrange("(kc p) t -> p kc t", p=128)[:, :, tok]
                P.dma("sp", xr[i % 2][0][:], xv, w=[xr[i % 2][1]])
                P.dma("pool", xb[i % 2][0][:], xv, w=[xb[i % 2][1]])
                P.dma("sp", yt[i % 2][0][:], yv, w=[yt[i % 2][1]])

            load(0)
            deferred = []
            for m0 in range(0, 8, 2):
                for n0 in (0, 1024):
                    P.dma("pool", wg_sb[:, :, n0 + m0 * 128:n0 + (m0 + 2) * 128], wgv[:, :, n0 + m0 * 128:n0 + (m0 + 2) * 128], w=[bwg_m[m0], bwg_m[m0 + 1]])
                if m0 == 0:
                    for kc0 in range(0, 8, 4):
                        P.dma("pool", wab_sb[:, kc0:kc0 + 4, :], wab.rearrange("(kc p) n -> p kc n", p=128)[:, kc0:kc0 + 4, :], w=[bwab])
            for kc0 in range(0, 8, 4):
                P.dma("pool", wo_sb[:, kc0:kc0 + 4, :], wo.rearrange("(kc p) n -> p kc n", p=128)[:, kc0:kc0 + 4, :], w=[bwo])
            for i in range(NT):
                if i + 1 < NT:
                    load(i + 1)
                tok = slice(i * 512, (i + 1) * 512)
                xr_t, xr_b = xr[i % 2]
                xb_t, xb_b = xb[i % 2]
                yt_t, yt_b = yt[i % 2]
                h_t, h_b = hh[i % 2]
                for m in range(8):
                    if m == 2 and deferred:
                        deferred.pop(0)()
                    ms = slice(m * 128, (m + 1) * 128)
                    pga, pgab = nextP()
                    for kc in range(8):
                        P.op("pe", "matmul", out=pga[:], lhsT=wg_sb[:, kc, ms], rhs=xb_t[:, kc, :], start=(kc == 0), stop=(kc == 7), r=[bwg_m[m], xb_b], w=[pgab])
                    pgb, pgbb = nextP()
                    for kc in range(8):
                        P.op("pe", "matmul", out=pgb[:], lhsT=wg_sb[:, kc, 1024 + m * 128:1024 + (m + 1) * 128], rhs=xb_t[:, kc, :], start=(kc == 0), stop=(kc == 7), r=[bwg_m[m], xb_b], w=[pgbb])
                    ppa, ppab = nextP()
                    for r_ in range(4):
                        P.op("pe", "matmul", out=ppa[:], lhsT=wab_sb[:, 2 * r_, ms], rhs=yt_t[:, 2 * r_, :], start=(r_ == 0), stop=(r_ == 3), r=[bwab, yt_b], w=[ppab])
                    ppb, ppbb = nextP()
                    for r_ in range(4):
                        P.op("pe", "matmul", out=ppb[:], lhsT=wab_sb[:, 2 * r_ + 1, ms], rhs=yt_t[:, 2 * r_ + 1, :], start=(r_ == 0), stop=(r_ == 3), r=[bwab, yt_b], w=[ppbb])
                    sa, bsa = sga[m % 2]
                    sb_, bsb = sgb[m % 2]
                    a1, ba1 = t1[m % 2]
                    a2, ba2 = t2[m % 2]
                    P.op("act", "activation", out=sa[:], in_=pga[:], func=AF.Sigmoid, r=[pgab], w=[bsa])
                    P.op("act", "activation", out=sb_[:], in_=pgb[:], func=AF.Sigmoid, r=[pgbb], w=[bsb])
                    P.op("dve", "tensor_tensor", out=a1[:], in0=sa[:], in1=ppa[:], op=ALU.mult, r=[bsa, ppab], w=[ba1])
                    P.op("dve", "tensor_tensor", out=a2[:], in0=sb_[:], in1=ppb[:], op=ALU.mult, r=[bsb, ppbb], w=[ba2])
                    P.op("pool", "tensor_tensor", out=mg[:, m, :], in0=a1[:], in1=a2[:], op=ALU.add, r=[ba1, ba2], w=[bmg])
                for mo in range(8):
                    pw, pwb = nextP()
                    for m in range(8):
                        P.op("pe", "matmul", out=pw[:], lhsT=wo_sb[:, m, mo * 128:(mo + 1) * 128], rhs=mg[:, m, :], start=(m == 0), stop=(m == 7), r=[bwo, bmg], w=[pwb])
                    P.op("dve", "scalar_tensor_tensor", out=h_t[:, mo, :], in0=xr_t[:, mo, :], scalar=float(ALPHA), in1=pw[:], op0=ALU.mult, op1=ALU.add,
                         r=[xr_b, pwb], w=[h_b[mo]])

                def wr(i=i, h_t=h_t, h_b=h_b, tok=tok):
                    P.dma("sp", X1.rearrange("(kc p) t -> p kc t", p=128)[:, :, tok], h_t[:], r=h_b, w=[dX1[i]])
                deferred.append(lambda h_t=h_t, h_b=h_b, wr=wr: layer_norm_T(P, LT, nextP, h_t, h_b, lambda c: vB[:, c:c + 1], lambda c: vB[:, 8 + c:9 + c], bvB, conesK, bconesK, wr))
            while deferred:
                deferred.pop(0)()
            P.finish()

        with contextlib.ExitStack() as st:
            P = Prog(nc, st, tag + "2")
            TA = TileAlloc(nc, st, tag + "2")
            NM = FFH // 128
            wfi_sb, bwfi = TA.sb([128, 8, 2 * FFH], BF16, "wfi")
            wfo_sb, bwfo = TA.sb([128, NM, 1024], BF16, "wfo")
            xb = [TA.sb([128, 8, 512], BF16, "xb") for _ in range(2)]
            xrc = [TA.sb([128, 512], F32, "xrc") for _ in range(2)]
            aa, baa = TA.sb([128, NM, 512], BF16, "aa")
            sg = [TA.sb([128, 512], F32, "sg") for _ in range(2)]
            hh = TA.sb([128, 8, 512], F32, "hh")[0]
            bhh = [Buf() for _ in range(8)]
            psl = [TA.ps([128, 512], F32, "ps") for _ in range(8)]
            LT = ln_alloc(TA)
            npz = [0]

            def nextP():
                npz[0] += 1
                return psl[npz[0] % 8]
            wfv = wfi.rearrange("(kc p) n -> p kc n", p=128)
            X1v = X1.rearrange("(kc p) t -> p kc t", p=128)

            def load(i):
                tok = slice(i * 512, (i + 1) * 512)
                P.dma("pool", xb[i % 2][0][:], X1v[:, :, tok], r=[dX1[i]], w=[xb[i % 2][1]])

            deferred = []
            bwfi_m = [Buf() for _ in range(NM)]
            for m0 in range(0, NM, 2):
                for n0 in (0, FFH):
                    P.dma("pool", wfi_sb[:, :, n0 + m0 * 128:n0 + (m0 + 2) * 128], wfv[:, :, n0 + m0 * 128:n0 + (m0 + 2) * 128], w=[bwfi_m[m0], bwfi_m[m0 + 1]])
                if m0 == 0:
                    load(0)
            wov = wfo.rearrange("(kc p) n -> p kc n", p=128)
            for k0 in range(0, NM, 2):
                P.dma("pool", wfo_sb[:, k0:k0 + 2, :], wov[:, k0:k0 + 2, :], w=[bwfo])
            for i in range(NT):
                if i + 1 < NT:
                    load(i + 1)
                tok = slice(i * 512, (i + 1) * 512)
                xb_t, xb_b = xb[i % 2]
                for m in range(NM):
                    if m == 3 and deferred:
                        deferred.pop(0)()
                    pu, pub = nextP()
                    for kc in range(8):
                        P.op("pe", "matmul", out=pu[:], lhsT=wfi_sb[:, kc, m * 128:(m + 1) * 128], rhs=xb_t[:, kc, :], start=(kc == 0), stop=(kc == 7), r=[bwfi_m[m], xb_b], w=[pub])
                    pg, pgb = nextP()
                    for kc in range(8):
                        P.op("pe", "matmul", out=pg[:], lhsT=wfi_sb[:, kc, FFH + m * 128:FFH + (m + 1) * 128], rhs=xb_t[:, kc, :], start=(kc == 0), stop=(kc == 7), r=[bwfi_m[m], xb_b], w=[pgb])
                    s_, bs_ = sg[m % 2]
                    P.op("act", "activation", out=s_[:], in_=pg[:], func=AF.Silu, r=[pgb], w=[bs_])
                    P.op("dve", "tensor_tensor", out=aa[:, m, :], in0=s_[:], in1=pu[:], op=ALU.mult, r=[bs_, pub], w=[baa])
                for mo in range(8):
                    xc, bxc = xrc[mo % 2]
                    P.dma("sp", xc[:], X1v[:, mo, tok], r=[dX1[i]], w=[bxc])
                    po, pob = nextP()
                    for m in range(NM):
                        P.op("pe", "matmul", out=po[:], lhsT=wfo_sb[:, m, mo * 128:(mo + 1) * 128], rhs=aa[:, m, :], start=(m == 0), stop=(m == NM - 1), r=[bwfo, baa], w=[pob])
                    P.op("dve", "scalar_tensor_tensor", out=hh[:, mo, :], in0=xc[:], scalar=float(ALPHA), in1=po[:], op0=ALU.mult, op1=ALU.add, r=[bxc, pob], w=[bhh[mo]])

                def wr(tok=tok, i=i):
                    P.dma("sp", XO.rearrange("(kc p) t -> p kc t", p=128)[:, :, tok], hh[:], r=bhh, w=[Buf()])
                    if XOB is not None:
                        bx_ = Buf()
                        if xcb is not None:
                            P.dma("pool", XOB[i].rearrange("(kc p) t -> p kc t", p=128), hh[:], r=bhh, w=[bx_])
                            xcb(P, i, bx_)
                        else:
                            P.dma("pool", XOB.rearrange("(kc p) t -> p kc t", p=128)[:, :, tok], hh[:], r=bhh, w=[bx_])
                deferred.append(lambda wr=wr: layer_norm_T(P, LT, nextP, hh, bhh, lambda c: vB[:, 16 + c:17 + c], lambda c: vB[:, 24 + c:25 + c], bvB, conesK, bconesK, wr))
            while deferred:
                deferred.pop(0)()
            P.finish()


def build_B(R4, with_bf16_out=False):
    nc = bass.Bass("TRN2", target_bir_lowering=False)
    Yg = nc.dram_tensor("Yg", [1024, R4], BF16, kind="ExternalInput").ap()
    xres = nc.dram_tensor("xres", [1024, R4], F32, kind="ExternalInput").ap()
    wg = nc.dram_tensor("wg", [1024, 2048], F32, kind="ExternalInput").ap()
    wab = nc.dram_tensor("wab", [1024, 1024], F32, kind="ExternalInput").ap()
    wo = nc.dram_tensor("wo", [1024, 1024], F32, kind="ExternalInput").ap()
    wfi = nc.dram_tensor("wfi", [1024, 2 * FFH], F32, kind="ExternalInput").ap()
    wfo = nc.dram_tensor("wfo", [FFH, 1024], F32, kind="ExternalInput").ap()
    vecB = nc.dram_tensor("vecB", [128, 32], F32, kind="ExternalInput").ap()
    cst = nc.dram_tensor("cst", [128, CST_COLS], F32, kind="ExternalInput").ap()
    XO = nc.dram_tensor("XO", [1024, R4], F32, kind="ExternalOutput").ap()
    XOB = nc.dram_tensor("XOB", [1024, R4], BF16, kind="ExternalOutput").ap() if with_bf16_out else None
    X1 = nc.dram_tensor("b_X1", [1024, R4], F32, kind="Internal").ap()
    with contextlib.ExitStack() as gst:
        GSEM[0] = gst
        GD.clear()
        phase_B(nc, R4, Yg, xres, wg, wab, wo, wfi, wfo, vecB, cst, X1, XO, XOB, "B")
    return nc


def make_cst():
    c = np.zeros((128, CST_COLS), np.float32)
    p = np.arange(128)[:, None]
    f = np.arange(128)[None, :]
    c[:, K_ID:K_ID + 128] = (p == f)
    c[:, K_NEG:K_NEG + 128] = np.where(f < p, -30000.0, 0.0)
    c[:, K_BD:K_BD + 128] = (f >= p) & ((p // 64) == (f // 64))
    r = np.ones(512, np.float32)
    r[::64] = 0.0
    c[:, K_RESET:K_RESET + 512] = r[None, :]
    c[:, K_ONES:K_ONES + 512] = 1.0
    return c


def make_wA(w_in_l, j):
    B0 = 3 * 512 + 8
    sl = lambda base: w_in_l[:, base + 128 * j: base + 128 * (j + 1)]
    qa, ka, va = sl(0), sl(512), sl(1024)
    fa = w_in_l[:, 1536 + 2 * j: 1536 + 2 * (j + 1)]
    qb, fb, ib, gb = sl(B0), sl(B0 + 512), sl(B0 + 1024), sl(B0 + 1536)
    return np.ascontiguousarray(np.concatenate([qa, ka, qb, fb, gb, fa, va, ib], axis=1))


def make_vecA(l, j, b_fgate, hgrn_lb_logits, hgrn_norm_g):
    v = np.zeros((128, 8), np.float32)
    v[:, 0] = hgrn_lb_logits[0, 128 * j:128 * (j + 1)]
    v[:, 1] = hgrn_lb_logits[1, 128 * j:128 * (j + 1)]
    v[:, 2] = hgrn_norm_g[l]
    v[0:2, 3] = b_fgate[l, 2 * j:2 * j + 2]
    v[:, 4] = float(l)
    return v


def make_wab(wa_l, wb_l):
    out = np.empty((1024, 1024), np.float32)
    for r in range(4):
        out[256 * r:256 * r + 128] = wa_l[128 * r:128 * (r + 1)]
        out[256 * r + 128:256 * r + 256] = wb_l[128 * r:128 * (r + 1)]
    return out


def make_vecB(l, ln1_g, ln1_b, ln2_g, ln2_b):
    v = np.empty((128, 32), np.float32)
    for k, a in enumerate((ln1_g[l], ln1_b[l], ln2_g[l], ln2_b[l])):
        v[:, 8 * k:8 * k + 8] = a.reshape(8, 128).T
    return v


I32 = mybir.dt.int32
GROUPS = [[0, 1, 2, 3], [4, 5, 6, 7]]


def make_ycb(Ysc, GA, GB, GH):
    def ycb(P, part, r_, buf):
        if part < 2:
            src = Ysc[r_, part * 64:(part + 1) * 64, :]
            dst = (GA, GB)[part][r_].rearrange("s p t -> (s p) t")
        else:
            src = Ysc[r_, 128:256, :]
            dst = GH[r_].rearrange("s p t -> (s p) t")
        P.coll("AllGather", [src], [dst], GROUPS, r=[buf], w=[Buf()])
    return ycb


def exchange_Y(nc, R4, GA, GB, GH, Ygl, jidx, jt, bjt, jr, tag):
    with contextlib.ExitStack() as st:
        P = Prog(nc, st, tag)
        P.dma("sp", jt[:], jidx, w=[bjt])

        def src(Gt):
            def f(e):
                e.reg_load(jr, jt[0:1, 0:1])
                val = e.snap(jr, min_val=0, max_val=3)
                return Gt[bass.ds(val, 1)].rearrange("o s p t -> (o s) p t")
            return f
        Yv = Ygl.rearrange("(s p) t -> s p t", s=4)
        P.dma("sp", Yv[:, 0:64, :], src(GA), r=[bjt], w=[Buf()])
        P.dma("sp", Yv[:, 64:128, :], src(GB), r=[bjt], w=[Buf()])
        P.dma("sp", Yv[:, 128:256, :], src(GH), r=[bjt], w=[Buf()])
        P.finish()


def build_fused(S):
    nc = bass.Bass("TRN2", target_bir_lowering=False)
    R4 = S // 4
    ext = lambda name, shape, dt=F32: nc.dram_tensor(name, shape, dt, kind="ExternalInput").ap()
    itn = lambda name, shape, dt=BF16: nc.dram_tensor(name, shape, dt, kind="Internal").ap()
    xg = ext("xg", [4, 1024, R4])
    xres = ext("xres", [1024, R4])
    cst = ext("cst", [128, CST_COLS])
    jidx = ext("jidx", [1, 4], I32)
    W = []
    for l in range(DEPTH):
        W.append(dict(wA=ext(f"wA{l}", [1024, WA_COLS]), vecA=ext(f"vecA{l}", [128, 8]), wg=ext(f"wg{l}", [1024, 2048]),
                      wab=ext(f"wab{l}", [1024, 1024]), wo=ext(f"wo{l}", [1024, 1024]), wfi=ext(f"wfi{l}", [1024, 2 * FFH]),
                      wfo=ext(f"wfo{l}", [FFH, 1024]), vecB=ext(f"vecB{l}", [128, 32])))
    OUT = nc.dram_tensor("OUT", [1024, R4], F32, kind="ExternalOutput").ap()
    scr = alloc_scratch_A(nc, S, "a")
    Ysc = itn("Ysc", [4, 256, R4])
    GA = itn("GA", [4, 4, 64, R4])
    GB = itn("GB", [4, 4, 64, R4])
    GH = itn("GH", [4, 4, 128, R4])
    NTB = R4 // 512
    XOBt = itn("XOBt", [NTB, 1024, 512])
    XG1t = itn("XG1t", [NTB, 4, 1024, 512])
    Ygl = itn("Ygl", [1024, R4])
    X1 = itn("X1", [1024, R4], F32)
    XO0 = itn("XO0", [1024, R4], F32)
    xsrc1 = lambda r_, c0: XG1t[c0 // 512, r_].rearrange("(kc p) t -> p kc t", p=128)
    Yw = lambda rows, i: Ysc[(i * 512) // R4, rows, (i * 512) % R4:(i * 512) % R4 + 512]
    with contextlib.ExitStack() as gst:
        GSEM[0] = gst
        GD.clear()
        jt = gst.enter_context(nc.sbuf_tensor("jt", [1, 4], I32))
        bjt = Buf()
        jr = gst.enter_context(nc.sync.register("jr"))
        ycb = make_ycb(Ysc, GA, GB, GH)

        def xcb(P, i, buf):
            P.coll("AllGather", [XOBt[i]], [XG1t[i].rearrange("s f t -> (s f) t")], GROUPS, r=[buf], w=[Buf()])
        for l in range(DEPTH):
            w = W[l]
            phase_A(nc, S, xg if l == 0 else xsrc1, w["wA"], w["vecA"], cst, Yw, scr, f"A{l}", ycb=ycb)
            exchange_Y(nc, R4, GA, GB, GH, Ygl, jidx, jt, bjt, jr, f"E{l}")
            if os.environ.get("FUSE_STOP") == "1":
                break
            last = (l == DEPTH - 1)
            phase_B(nc, R4, Ygl, xres if l == 0 else XO0, w["wg"], w["wab"], w["wo"], w["wfi"], w["wfo"], w["vecB"], cst, X1,
                    OUT if last else XO0, None if last else XOBt, f"B{l}", xcb=None if last else xcb)
    return nc


def kernel(x, w_in, b_fgate, hgrn_lb_logits, hgrn_norm_g, w_branch_a, w_branch_b, w_out,
           ln1_g, ln1_b, w_ff_in, w_ff_out, ln2_g, ln2_b):
    x = np.asarray(x, np.float32)
    Bn, S, _ = x.shape
    R4 = S // 4
    f = lambda a: np.asarray(a, np.float32)
    w_in, b_fgate, hgrn_lb_logits, hgrn_norm_g = f(w_in), f(b_fgate), f(hgrn_lb_logits), f(hgrn_norm_g)
    w_branch_a, w_branch_b, w_out = f(w_branch_a), f(w_branch_b), f(w_out)
    ln1_g, ln1_b, w_ff_in, w_ff_out, ln2_g, ln2_b = f(ln1_g), f(ln1_b), f(w_ff_in), f(w_ff_out), f(ln2_g), f(ln2_b)
    cst = make_cst()
    cores = list(range(8))
    xg = [np.ascontiguousarray(x[b].reshape(4, R4, D).transpose(0, 2, 1)) for b in range(Bn)]
    B0 = 3 * 512 + 8 + 4 * 512
    shared = {}
    for l in range(DEPTH):
        shared[f"wg{l}"] = np.ascontiguousarray(w_in[l][:, B0:B0 + 2048])
        shared[f"wab{l}"] = make_wab(w_branch_a[l], w_branch_b[l])
        shared[f"wo{l}"] = w_out[l]
        shared[f"wfi{l}"] = w_ff_in[l]
        shared[f"wfo{l}"] = w_ff_out[l]
        shared[f"vecB{l}"] = make_vecB(l, ln1_g, ln1_b, ln2_g, ln2_b)
    wAs = {(l, j): make_wA(w_in[l], j) for l in range(DEPTH) for j in range(4)}
    in_maps = []
    for c in cores:
        b, j = divmod(c, 4)
        m = dict(xg=xg[b], xres=np.ascontiguousarray(xg[b][j]), cst=cst, jidx=np.array([[j, 0, 0, 0]], np.int32))
        for l in range(DEPTH):
            m[f"wA{l}"] = wAs[(l, j)]
            m[f"vecA{l}"] = make_vecA(l, j, b_fgate, hgrn_lb_logits, hgrn_norm_g)
        m.update(shared)
        in_maps.append(m)
    nc = build_fused(S)
    res = run_bass_kernel_spmd(nc, in_maps, core_ids=cores)
    out = np.empty((Bn, S, D), np.float32)
    for c in cores:
        b, j = divmod(c, 4)
        out[b, j * R4:(j + 1) * R4, :] = np.asarray(res.results[c]["OUT"]).T
    return out
```

```python
import contextlib
import os
import numpy as np
import ml_dtypes
import concourse.bass as bass
import concourse.mybir as mybir
from concourse.bass_utils import run_bass_kernel_spmd

F32 = mybir.dt.float32
BF16 = mybir.dt.bfloat16
AF = mybir.ActivationFunctionType
ALU = mybir.AluOpType
NPBF = ml_dtypes.bfloat16

D = 1024
SEQ = 16384
DEPTH = 2
FFH = 2816
ALPHA = (2 * DEPTH) ** 0.25
LN_EPS = 1e-5
RMS_EPS = 1e-6
NDMA = 6
GSEM = [None]
GD = {}


class Buf:
    __slots__ = ("name", "writers", "readers")

    def __init__(self, name=""):
        self.name = name
        self.writers = []
        self.readers = []


class Op:
    __slots__ = ("eng", "fn", "deps", "dma", "is_ms", "ms", "extra_waits", "prog")

    def __init__(self, eng, fn):
        self.eng = eng
        self.fn = fn
        self.deps = []
        self.dma = None
        self.is_ms = False
        self.ms = 0
        self.extra_waits = []


class Prog:
    ENGS = ("pe", "act", "dve", "pool", "sp")

    def __init__(self, nc, stack, tag):
        self.nc = nc
        self.tag = tag
        self.q = {e: [] for e in self.ENGS}
        stack = GSEM[0]
        self.psem = {e: stack.enter_context(nc.semaphore(f"{tag}_p_{e}")) for e in self.ENGS}
        if "dsem" not in GD:
            GD["dsem"] = {e: [stack.enter_context(nc.semaphore(f"gd_{e}{i}")) for i in range(NDMA)] for e in ("sp", "pool")}
            GD["dcount"] = {"sp": 0, "pool": 0}
            GD["cc"] = stack.enter_context(nc.semaphore("gcc"))
            GD["ccn"] = 0
        self.dsem = GD["dsem"]
        self.dcount = GD["dcount"]

    def _track(self, op, r, w):
        deps = []
        for b in r:
            deps += b.writers
        for b in w:
            deps += b.writers
            deps += b.readers
        op.deps = [d for d in deps if d.prog is self]
        for b in r:
            if op.dma is None:
                b.readers = [x for x in b.readers if not (x.eng == op.eng and x.dma is None)]
            b.readers.append(op)
        for b in w:
            b.writers = [op]
            b.readers = []

    def op(self, eng, name, r=(), w=(), **kw):
        o = Op(eng, (name, kw))
        o.prog = self
        self._track(o, r, w)
        self.q[eng].append(o)
        return o

    def dma(self, eng, out, in_, r=(), w=()):
        o = Op(eng, ("dma_start", dict(out=out, in_=in_)))
        o.prog = self
        n = self.dcount[eng]
        self.dcount[eng] += 1
        sem = self.dsem[eng][n % NDMA]
        gen = n // NDMA
        o.dma = (sem, 16 * (gen + 1), 16)
        if gen > 0:
            o.extra_waits.append((sem, 16 * gen))
        self._track(o, r, w)
        self.q[eng].append(o)
        return o

    def coll(self, kind, ins, outs, groups, r=(), w=()):
        o = Op("pool", ("collective_compute", dict(kind=kind, op=ALU.bypass, replica_groups=groups, ins=ins, outs=outs)))
        o.prog = self
        GD["ccn"] += 1
        o.dma = (GD["cc"], GD["ccn"], 1)
        self._track(o, r, w)
        self.q["pool"].append(o)
        return o

    def finish(self):
        nc = self.nc
        lasts = []
        for e in self.ENGS:
            if self.q[e]:
                for o in reversed(self.q[e]):
                    if o.dma is None:
                        lasts.append(o)
                        break
        dma_final = []
        for e in ("sp", "pool"):
            n = self.dcount[e]
            for i in range(min(n, NDMA)):
                cnt = (n - 1 - i) // NDMA + 1
                dma_final.append((self.dsem[e][i], 16 * cnt))
        for e in self.ENGS:
            for o in self.q[e]:
                for d in o.deps:
                    if d.dma is None and (d.eng != e or e != "pe"):
                        d.is_ms = True
        for o in lasts:
            o.is_ms = True
        for e in self.ENGS:
            c = 0
            for o in self.q[e]:
                if o.dma is None and o.is_ms:
                    c += 1
                    o.ms = c
        psem = self.psem

        def mk(eng):
            def body(e):
                waited = {}

                def wait(sem, v):
                    k = id(sem)
                    if waited.get(k, 0) >= v:
                        return
                    waited[k] = v
                    e.wait_ge(sem, v)

                for o in self.q[eng]:
                    need = {}
                    for d in o.deps:
                        if d.dma is not None:
                            s, v = d.dma[0], d.dma[1]
                        elif d.eng == eng and eng == "pe":
                            continue
                        else:
                            s, v = psem[d.eng], d.ms
                        k = id(s)
                        if k not in need or need[k][1] < v:
                            need[k] = (s, v)
                    for s, v in o.extra_waits:
                        k = id(s)
                        if k not in need or need[k][1] < v:
                            need[k] = (s, v)
                    for s, v in need.values():
                        wait(s, v)
                    kw = {k: (v(e) if callable(v) else v) for k, v in o.fn[1].items()}
                    ins = getattr(e, o.fn[0])(**kw)
                    if o.dma is not None:
                        ins.then_inc(o.dma[0], o.dma[2])
                    elif o.is_ms:
                        ins.then_inc(psem[eng], 1)
                for o in lasts:
                    if o.eng != eng:
                        wait(psem[o.eng], o.ms)
                for s, v in dma_final:
                    wait(s, v)
                if GD["ccn"] > 0:
                    wait(GD["cc"], GD["ccn"])

            return body

        with nc.Block() as block:
            block.tensor(mk("pe"))
            block.scalar(mk("act"))
            block.vector(mk("dve"))
            block.gpsimd(mk("pool"))
            block.sync(mk("sp"))


class TileAlloc:
    def __init__(self, nc, stack, tag):
        self.nc = nc
        self.stack = stack
        self.tag = tag
        self.n = 0

    def sb(self, shape, dt, name="t"):
        self.n += 1
        t = self.stack.enter_context(self.nc.sbuf_tensor(f"{self.tag}_{name}{self.n}", list(shape), dt))
        return t, Buf(name)

    def ps(self, shape, dt, name="p"):
        self.n += 1
        t = self.stack.enter_context(self.nc.psum_tensor(f"{self.tag}_{name}{self.n}", list(shape), dt))
        return t, Buf(name)


WA_COLS = 898
C_QA, C_KA, C_QB, C_FB, C_GB, C_FA, C_VA = 0, 128, 256, 384, 512, 640, 642
CST_COLS = 1408
K_ID, K_NEG, K_BD, K_RESET, K_ONES = 0, 128, 256, 384, 896


def phase_A(nc, S, xg, wA, vecA, cst, Yw, scr, tag, ycb=None):
    NT = S // 512
    R4 = S // 4
    QT, KT, VA, QE, KE, KD, VB, GS = (scr[k] for k in ("QT", "KT", "VA", "QE", "KE", "KD", "VB", "GS"))

    with contextlib.ExitStack() as st0:
        TA0 = TileAlloc(nc, st0, tag + "g")
        A_sb, bA = TA0.sb([128, S // 64], F32, "Adec")
        cid, bcid = TA0.sb([128, 128], BF16, "ident")
        cneg, bcneg = TA0.sb([128, 128], BF16, "neg")
        cbd, bcbd = TA0.sb([128, 128], BF16, "bdtri")
        cones, bcones = TA0.sb([128, 128], BF16, "ones")
        conesf, bconesf = TA0.sb([128, 64], F32, "onesf")
        vec, bvec = TA0.sb([128, 8], F32, "vec")
        lbv, blbv = TA0.sb([128, 8], F32, "lbv")

        with contextlib.ExitStack() as st:
            P = Prog(nc, st, tag + "1")
            TA = TileAlloc(nc, st, tag + "1")
            dQT = [[Buf() for _ in range(NT)] for _ in range(2)]
            dKT = [[Buf() for _ in range(NT)] for _ in range(2)]
            dVA = [Buf() for _ in range(NT)]
            dH = [Buf() for _ in range(NT)]
            wsb, bw = TA.sb([128, 8, WA_COLS], BF16, "wA")
            creset, bcreset = TA.sb([128, 512], F32, "reset")
            xt = [TA.sb([128, 8, 512], BF16, "xt") for _ in range(2)]
            psF = [TA.ps([128, 512], F32, "psF") for _ in range(5)]
            psT = [TA.ps([128, 512], F32, "psT") for _ in range(2)]
            psX = TA.ps([128, 512], BF16, "psX")
            nF = [0]

            def nextF():
                nF[0] += 1
                return psF[nF[0] % 5]
            stq = [TA.sb([128, 512], BF16, "stq") for _ in range(2)]
            stk = [TA.sb([128, 512], BF16, "stk") for _ in range(2)]
            stv = [TA.sb([128, 4, 256], BF16, "stv") for _ in range(2)]
            sqe = [TA.sb([128, 512], BF16, "sqe") for _ in range(2)]
            ske = [TA.sb([128, 512], BF16, "ske") for _ in range(2)]
            skdT = [TA.sb([128, 512], BF16, "skdT") for _ in range(2)]
            skd = [TA.sb([128, 4, 128], BF16, "skd") for _ in range(2)]
            sgs = [TA.sb([128, 512], BF16, "sgs") for _ in range(2)]
            wf = [TA.sb([128, 512], F32, "wf") for _ in range(24)]
            fz = [TA.sb([2, 512], F32, "fz") for _ in range(6)]
            Fc = [TA.sb([2, 512], F32, "Fc") for _ in range(2)]
            fst = [TA.sb([2, 3, 512], BF16, "fst") for _ in range(2)]
            negrow, bnegrow = TA.sb([2, 3, 512], BF16, "negrow")
            fr = [TA.sb([2, 512], F32, "fr") for _ in range(4)]
            onesrow, bonesrow = TA.sb([2, 3, 512], BF16, "onesrow")

            wv = wA.rearrange("(kc p) n -> p kc n", p=128)
            for kc0 in range(0, 8, 2):
                P.dma("pool", wsb[:, kc0:kc0 + 2, :], wv[:, kc0:kc0 + 2, :], w=[bw])
            P.dma("pool", cid[:], cst[:, K_ID:K_ID + 128], w=[bcid])
            P.dma("pool", cneg[:], cst[:, K_NEG:K_NEG + 128], w=[bcneg])
            P.dma("pool", cbd[:], cst[:, K_BD:K_BD + 128], w=[bcbd])
            P.dma("sp", creset[:], cst[:, K_RESET:K_RESET + 512], w=[bcreset])
            P.dma("pool", cones[:], cst[:, K_ONES:K_ONES + 128], w=[bcones])
            P.dma("sp", conesf[:], cst[:, K_ONES:K_ONES + 64], w=[bconesf])
            P.dma("sp", vec[:], vecA, w=[bvec])
            P.op("pool", "memset", ap=onesrow[:], constant=1.0, w=[bonesrow])
            P.op("pool", "memset", ap=negrow[:], constant=-1.0, w=[bnegrow])
            c_ = lambda i: lbv[:, i:i + 1]
            rw = dict(r=[blbv, bvec], w=[blbv])
            P.op("dve", "tensor_tensor", out=c_(0), in0=vec[:, 0:1], in1=vec[:, 1:2], op=ALU.max, **rw)
            P.op("dve", "tensor_tensor", out=c_(1), in0=vec[:, 0:1], in1=c_(0), op=ALU.subtract, **rw)
            P.op("dve", "tensor_tensor", out=c_(2), in0=vec[:, 1:2], in1=c_(0), op=ALU.subtract, **rw)
            P.op("act", "activation", out=lbv[:, 1:3], in_=lbv[:, 1:3], func=AF.Exp, **rw)
            P.op("dve", "tensor_tensor", out=c_(3), in0=c_(1), in1=c_(2), op=ALU.add, **rw)
            P.op("dve", "reciprocal", out=c_(4), in_=c_(3), **rw)
            P.op("dve", "tensor_tensor", out=c_(5), in0=c_(1), in1=c_(4), op=ALU.mult, **rw)
            P.op("dve", "tensor_tensor", out=c_(6), in0=c_(2), in1=c_(4), op=ALU.mult, **rw)
            P.op("dve", "scalar_tensor_tensor", out=c_(7), in0=c_(6), scalar=vec[:, 4:5], in1=c_(5), op0=ALU.mult, op1=ALU.add, **rw)
            P.op("dve", "tensor_tensor", out=c_(0), in0=c_(7), in1=c_(5), op=ALU.subtract, **rw)
            P.op("dve", "tensor_scalar", out=c_(1), in0=c_(0), scalar1=-1.0, scalar2=1.0, op0=ALU.mult, op1=ALU.add, **rw)
            P.op("dve", "tensor_scalar", out=c_(2), in0=vec[:, 3:4], scalar1=-1.0, scalar2=None, op0=ALU.mult, **rw)
            LB, OML, NBF = lbv[:, 0:1], lbv[:, 1:2], lbv[0:2, 2:3]

            for h in range(2):
                for i in range(NT):
                    tok = slice(i * 512, (i + 1) * 512)
                    P.dma("sp", QT[h:h + 1, 67:70, tok], negrow[0:1, :, :], r=[bnegrow], w=[dQT[h][i]])
                    P.dma("sp", KT[h:h + 1, 64:67, tok], onesrow[0:1, :, :], r=[bonesrow], w=[dKT[h][i]])

            def load_x(i):
                t, b = xt[i % 2]
                r_, c0 = divmod(i * 512, R4)
                src = xg(r_, c0) if callable(xg) else xg[r_].rearrange("(kc p) t -> p kc t", p=128)[:, :, c0:c0 + 512]
                for kc0 in range(0, 8, 4):
                    P.dma("pool", t[:, kc0:kc0 + 4, :], src[:, kc0:kc0 + 4, :], w=[b])

            load_x(0)
            Fprev = [None]

            def tiles(i):
                s3 = i % 3
                return wf[8 * s3:8 * s3 + 8], fz[2 * s3:2 * s3 + 2]

            def stA(i):
                if i + 1 < NT:
                    load_x(i + 1)
                xs, bx = xt[i % 2]
                tok = slice(i * 512, (i + 1) * 512)
                s2 = i % 2
                ((sig, bsig), (g_, bg), (bb, bbb), (eb, beb), (enb, benb), (ebr, bebr), (kk, bkk), (qs, bqs)), ((z0, bz0), (z1, bz1)) = tiles(i)

                def fgroup(col, M):
                    pt, pb = nextF()
                    for kc in range(8):
                        P.op("pe", "matmul", out=pt[0:M, :], lhsT=wsb[:, kc, col:col + M], rhs=xs[:, kc, :], start=(kc == 0), stop=(kc == 7),
                             r=[bw, bx], w=[pb])
                    return pt, pb
                pt, pb = fgroup(C_QA, 128)
                t, b = stq[s2]
                P.op("dve", "tensor_scalar", out=t[:], in0=pt[:], scalar1=0.125, scalar2=None, op0=ALU.mult, r=[pb], w=[b])
                for h in range(2):
                    P.dma("sp", QT[h, 0:64, tok], t[h * 64:(h + 1) * 64, :], r=[b], w=[dQT[h][i]])
                pt, pb = fgroup(C_KA, 128)
                t, b = stk[s2]
                P.op("dve", "tensor_copy", out=t[:], in_=pt[:], r=[pb], w=[b])
                for h in range(2):
                    P.dma("sp", KT[h, 0:64, tok], t[h * 64:(h + 1) * 64, :], r=[b], w=[dKT[h][i]])
                pt, pb = fgroup(C_FB, 128)
                P.op("act", "activation", out=sig[:], in_=pt[:], func=AF.Sigmoid, r=[pb], w=[bsig])
                pt, pb = fgroup(C_GB, 128)
                t, b = sgs[s2]
                P.op("act", "activation", out=t[:], in_=pt[:], func=AF.Sigmoid, r=[pb], w=[b])
                P.dma("sp", GS[:, tok], t[:], r=[b], w=[dH[i]])
                pt, pb = fgroup(C_QB, 128)
                P.op("act", "activation", out=qs[:], in_=pt[:], func=AF.Silu, r=[pb], w=[bqs])
                pt, pb = fgroup(C_FA, 2)
                P.op("act", "activation", out=z0[:], in_=pt[0:2, :], func=AF.Exp, scale=-1.0, bias=NBF, r=[pb, blbv], w=[bz0])
                t, b = stv[s2]
                for sb_ in range(4):
                    pt, pb = psT[sb_ % 2]
                    for kc in range(8):
                        P.op("pe", "matmul", out=pt[:, 0:256], lhsT=xs[:, kc, sb_ * 128:(sb_ + 1) * 128], rhs=wsb[:, kc, C_VA:C_VA + 256],
                             start=(kc == 0), stop=(kc == 7), r=[bw, bx], w=[pb])
                    P.op("act", "copy", out=t[:, sb_, :], in_=pt[:, 0:256], r=[pb], w=[b])
                P.dma("sp", VA[tok, :].rearrange("(a p) d -> p a d", p=128), t[:, :, 0:128], r=[b], w=[dVA[i]])
                P.dma("sp", VB[tok, :].rearrange("(a p) d -> p a d", p=128), t[:, :, 128:256], r=[b], w=[dH[i]])
                P.op("dve", "tensor_scalar", out=sig[:], in0=sig[:], scalar1=OML, scalar2=LB, op0=ALU.mult, op1=ALU.add, r=[bsig, blbv], w=[bsig])

            def stB1(i):
                tok = slice(i * 512, (i + 1) * 512)
                s2 = i % 2
                ((sig, bsig), (g_, bg), (bb, bbb), (eb, beb), (enb, benb), (ebr, bebr), (kk, bkk), (qs, bqs)), ((z0, bz0), (z1, bz1)) = tiles(i)
                P.op("act", "activation", out=g_[:], in_=sig[:], func=AF.Ln, r=[bsig], w=[bg])
                P.op("act", "activation", out=z1[:], in_=z0[:], func=AF.Ln, bias=1.0, r=[bz0], w=[bz1])
                P.op("pool", "tensor_scalar", out=kk[:], in0=sig[:], scalar1=-1.0, scalar2=1.0, op0=ALU.mult, op1=ALU.add, r=[bsig], w=[bkk])
                for c in range(8):
                    cs = slice(c * 64, (c + 1) * 64)
                    P.op("dve", "tensor_tensor_scan", out=bb[:, cs], data0=g_[:, cs], data1=g_[:, cs], initial=0.0, op0=ALU.add, op1=ALU.bypass, r=[bg], w=[bbb])
                Fc_t, Fc_b = Fc[s2]
                Fp = Fprev[0]
                init = 0.0 if Fp is None else Fp[0][:, 511:512]
                rr = [bz1] + ([] if Fp is None else [Fp[1]])
                P.op("dve", "tensor_scalar", out=z1[:], in0=z1[:], scalar1=-1.0, scalar2=None, op0=ALU.mult, r=[bz1], w=[bz1])
                P.op("dve", "tensor_tensor_scan", out=Fc_t[:], data0=z1[:], data1=z1[:], initial=init, op0=ALU.add, op1=ALU.bypass, r=rr, w=[Fc_b])
                Fprev[0] = (Fc_t, Fc_b)
                ft, fb = fst[s2]
                (r1, br1), (r2, br2) = fr[2 * s2:2 * s2 + 2]
                P.op("dve", "tensor_copy", out=ft[:, 0, :], in_=Fc_t[:], r=[Fc_b], w=[fb])
                P.op("dve", "tensor_tensor", out=r1[:], in0=Fc_t[:], in1=ft[:, 0, :], op=ALU.subtract, r=[Fc_b, fb], w=[br1])
                P.op("dve", "tensor_copy", out=ft[:, 1, :], in_=r1[:], r=[br1], w=[fb])
                P.op("dve", "tensor_tensor", out=r2[:], in0=r1[:], in1=ft[:, 1, :], op=ALU.subtract, r=[br1, fb], w=[br2])
                P.op("dve", "tensor_copy", out=ft[:, 2, :], in_=r2[:], r=[br2], w=[fb])
                for h in range(2):
                    P.dma("sp", QT[h:h + 1, 64:67, tok], ft[h:h + 1, 0:3, :], r=[fb], w=[dQT[h][i]])
                    P.dma("sp", KT[h:h + 1, 67:70, tok], ft[h:h + 1, 0:3, :], r=[fb], w=[dKT[h][i]])

            def stB2(i):
                tok = slice(i * 512, (i + 1) * 512)
                s2 = i % 2
                ((sig, bsig), (g_, bg), (bb, bbb), (eb, beb), (enb, benb), (ebr, bebr), (kk, bkk), (qs, bqs)), _ = tiles(i)
                P.op("act", "activation", out=eb[:], in_=bb[:], func=AF.Exp, r=[bbb], w=[beb])
                P.op("act", "activation", out=enb[:], in_=bb[:], func=AF.Exp, scale=-1.0, r=[bbb], w=[benb])
                for c in range(8):
                    cs = slice(c * 64, (c + 1) * 64)
                    last = slice(c * 64 + 63, c * 64 + 64)
                    P.op("act", "activation", out=ebr[:, cs], in_=bb[:, cs], func=AF.Exp, scale=-1.0, bias=bb[:, last], r=[bbb], w=[bebr])
                    P.op("pool", "tensor_copy", out=A_sb[:, i * 8 + c:i * 8 + c + 1], in_=eb[:, last], r=[beb], w=[bA])
                t, b = ske[s2]
                P.op("dve", "tensor_tensor", out=t[:], in0=kk[:], in1=enb[:], op=ALU.mult, r=[bkk, benb], w=[b])
                P.dma("sp", KE[:, tok], t[:], r=[b], w=[dH[i]])
                t, b = sqe[s2]
                P.op("pool", "tensor_tensor", out=t[:], in0=qs[:], in1=eb[:], op=ALU.mult, r=[bqs, beb], w=[b])
                P.dma("sp", QE[:, tok], t[:], r=[b], w=[dH[i]])
                tT, bT = skdT[s2]
                P.op("dve", "tensor_tensor", out=tT[:], in0=kk[:], in1=ebr[:], op=ALU.mult, r=[bkk, bebr], w=[bT])

            def stB3(i):
                tok = slice(i * 512, (i + 1) * 512)
                s2 = i % 2
                tT, bT = skdT[s2]
                for sb_ in range(4):
                    bs = slice(sb_ * 128, (sb_ + 1) * 128)
                    P.op("pe", "transpose", out=psX[0][:, bs], in_=tT[:, bs], identity=cid[:], r=[bT, bcid], w=[psX[1]])
                t, b = skd[s2]
                P.op("dve", "tensor_copy", out=t[:].rearrange("p a d -> p (a d)"), in_=psX[0][:], r=[psX[1]], w=[b])
                P.dma("sp", KD[tok, :].rearrange("(a p) d -> p a d", p=128), t[:], r=[b], w=[dH[i]])

            for t_ in range(NT + 3):
                if t_ < NT:
                    stA(t_)
                if 0 <= t_ - 1 < NT:
                    stB1(t_ - 1)
                if 0 <= t_ - 2 < NT:
                    stB2(t_ - 2)
                if 0 <= t_ - 3 < NT:
                    stB3(t_ - 3)
            P.finish()
        if os.environ.get("STOP_AFTER") == "1":
            return

        with contextlib.ExitStack() as st:
            P = Prog(nc, st, tag + "2")
            TA = TileAlloc(nc, st, tag + "2")
            NB = S // 128
            dYp = {(p_, r_): Buf() for p_ in range(3) for r_ in range(4)}
            qt, bqt = TA.sb([70, S], BF16, "qt")
            kt, bkt = TA.sb([70, S], BF16, "kt")
            vv, bvv = TA.sb([128, NB, 65], BF16, "vv")
            NS, NP_ = 3, 4
            psS = [TA.ps([128, 512], F32, "psS") for _ in range(NS)]
            psO = [TA.ps([128, 512], F32, "psO") for _ in range(2)]
            psX = TA.ps([128, 512], F32, "psX")
            psU = TA.ps([128, 512], F32, "psU")
            psOo = TA.ps([128, 512], F32, "psOo")
            pT = [TA.sb([128, 512], BF16, "pT") for _ in range(NP_)]
            osb = [TA.sb([64, 512], F32, "osb") for _ in range(2)]
            rden = [TA.sb([128, 512], F32, "rden") for _ in range(2)]
            yst = [TA.sb([64, 512], BF16, "yst") for _ in range(2)]
            SEG = min(2048, S)
            NSEG = S // SEG
            seg = []
            for k in range(2):
                seg.append(dict(
                    qe=TA.sb([128, SEG], BF16, "qe"), ke=TA.sb([128, SEG], BF16, "ke"),
                    kd=TA.sb([64, SEG // 64, 128], BF16, "kd"), vb=TA.sb([64, SEG // 64, 128], BF16, "vb"),
                    gs=TA.sb([128, SEG], BF16, "gs")))
            state = [TA.sb([128, 128], F32, "state") for _ in range(2)]
            sbf = [TA.sb([128, 8, 128], BF16, "sbf") for _ in range(2)]
            at = [TA.sb([128, 512], BF16, "at") for _ in range(2)]
            o_sb = [TA.sb([128, 512], F32, "o_sb") for _ in range(2)]
            sq = [TA.sb([128, 512], BF16, "sq") for _ in range(2)]
            rstd = [TA.sb([128, 512], F32, "rstd") for _ in range(2)]
            yb = [TA.sb([128, 512], BF16, "yb") for _ in range(2)]
            P.op("dve", "memset", ap=state[0][0][:], constant=0.0, w=[state[0][1]])
            P.op("dve", "memset", ap=sbf[0][0][:, 0, :], constant=0.0, w=[sbf[0][1]])

            def load_seg(k):
                sgm = seg[k % 2]
                c0 = k * SEG
                P.dma("sp", sgm["qe"][0][:], QE[:, c0:c0 + SEG], w=[sgm["qe"][1]])
                P.dma("pool", sgm["ke"][0][:], KE[:, c0:c0 + SEG], w=[sgm["ke"][1]])
                P.dma("sp", sgm["gs"][0][:], GS[:, c0:c0 + SEG], w=[sgm["gs"][1]])
                P.dma("pool", sgm["kd"][0][:], KD[c0:c0 + SEG, :].rearrange("(a p) d -> p a d", p=64), w=[sgm["kd"][1]])
                P.dma("sp", sgm["vb"][0][:], VB[c0:c0 + SEG, :].rearrange("(a p) d -> p a d", p=64), w=[sgm["vb"][1]])

            cur = [0]

            def hgrn_stages(i):
                k, ti = divmod(i * 512, SEG)
                sgm = seg[k % 2]
                qe, bqe = sgm["qe"]
                ke, bke = sgm["ke"]
                kd, bkd = sgm["kd"]
                vb, bvb = sgm["vb"]
                gs, bgs = sgm["gs"]
                s2 = i % 2
                sb_t, sb_b = sbf[s2]
                nsb_t, nsb_b = sbf[1 - s2]
                pu, pub = psU
                pa, pab = psU
                po, pob = psOo
                at_t, at_b = at[s2]
                o_t, o_b = o_sb[s2]
                sq_t, sq_b = sq[s2]
                rs_t, rs_b = rstd[s2]
                yb_t, yb_b = yb[s2]

                def st_load():
                    if ti == 0:
                        load_seg(k)

                def st_u(c0):
                    def f():
                        for c in range(c0, c0 + 4):
                            ch = (ti + c * 64) // 64
                            P.op("pe", "matmul", out=pu[:, (c % 4) * 128:(c % 4 + 1) * 128], lhsT=kd[:, ch, :], rhs=vb[:, ch, :],
                                 start=True, stop=True, r=[bkd, bvb], w=[pub])
                    return f

                def st_state(c0):
                    def f():
                        for c in range(c0, c0 + 4):
                            so_t, so_b = state[cur[0]]
                            sn_t, sn_b = state[1 - cur[0]]
                            P.op("dve", "scalar_tensor_tensor", out=sn_t[:], in0=so_t[:], scalar=A_sb[:, i * 8 + c:i * 8 + c + 1],
                                 in1=pu[:, (c % 4) * 128:(c % 4 + 1) * 128], op0=ALU.mult, op1=ALU.add, r=[so_b, bA, pub], w=[sn_b])
                            if c < 7:
                                P.op("pool", "tensor_copy", out=sb_t[:, c + 1, :], in_=sn_t[:], r=[sn_b], w=[sb_b])
                            else:
                                P.op("pool", "tensor_copy", out=nsb_t[:, 0, :], in_=sn_t[:], r=[sn_b], w=[nsb_b])
                            cur[0] = 1 - cur[0]
                    return f

                def st_attn():
                    for c in range(8):
                        tk = slice(ti + c * 64, ti + (c + 1) * 64)
                        P.op("pe", "matmul", out=pa[0:64, c * 64:(c + 1) * 64], lhsT=ke[:, tk], rhs=qe[:, tk], start=True, stop=True, r=[bke, bqe], w=[pab])

                def st_mask():
                    for c in range(8):
                        cs = slice(c * 64, (c + 1) * 64)
                        P.op("dve", "tensor_tensor", out=at_t[0:64, cs], in0=pa[0:64, cs], in1=cbd[0:64, 0:64], op=ALU.mult, r=[pab, bcbd], w=[at_b])

                def st_o():
                    for c in range(8):
                        ch = (ti + c * 64) // 64
                        cs = slice(c * 64, (c + 1) * 64)
                        P.op("pe", "matmul", out=po[:, cs], lhsT=vb[:, ch, :], rhs=at_t[0:64, cs], start=True, stop=False, r=[bvb, at_b], w=[pob])
                        P.op("pe", "matmul", out=po[:, cs], lhsT=sb_t[:, c, :], rhs=qe[:, ti + c * 64:ti + (c + 1) * 64],
                             start=False, stop=True, r=[sb_b, bqe], w=[pob])

                def st_sq():
                    P.op("dve", "tensor_copy", out=o_t[:], in_=po[:], r=[pob], w=[o_b])
                    P.op("pool", "tensor_tensor", out=sq_t[:], in0=o_t[:], in1=o_t[:], op=ALU.mult, r=[o_b], w=[sq_b])

                def st_m():
                    P.op("pe", "matmul", out=pa[:], lhsT=cones[:], rhs=sq_t[:], start=True, stop=True, r=[bcones, sq_b], w=[pab])

                def st_rs():
                    P.op("dve", "tensor_scalar", out=rs_t[:], in0=pa[:], scalar1=1.0 / 128.0, scalar2=RMS_EPS, op0=ALU.mult, op1=ALU.add, r=[pab], w=[rs_b])

                def st_sqrt():
                    P.op("act", "activation", out=rs_t[:], in_=rs_t[:], func=AF.Sqrt, r=[rs_b], w=[rs_b])

                def st_fin():
                    P.op("dve", "reciprocal", out=rs_t[:], in_=rs_t[:], r=[rs_b], w=[rs_b])
                    P.op("pool", "tensor_tensor", out=o_t[:], in0=o_t[:], in1=rs_t[:], op=ALU.mult, r=[o_b, rs_b], w=[o_b])
                    P.op("dve", "scalar_tensor_tensor", out=yb_t[:], in0=o_t[:], scalar=vec[:, 2:3], in1=gs[:, ti:ti + 512], op0=ALU.mult, op1=ALU.mult,
                         r=[o_b, bvec, bgs], w=[yb_b])
                    TPC = (S // 4) // 512
                    P.dma("sp", Yw(slice(128, 256), i), yb_t[:], r=[yb_b], w=[dYp[(2, i // TPC)]])
                    if ycb is not None and i % TPC == TPC - 1:
                        ycb(P, 2, i // TPC, dYp[(2, i // TPC)])
                return [st_load, st_u(0), st_state(0), st_u(4), st_state(4), st_attn, st_mask, st_o, st_sq, st_m, st_rs, st_sqrt, st_fin]

            allblocks = []
            for h in range(2):
                for I in range(NT):
                    nk = 4 * I + 4
                    for jb in range(nk):
                        allblocks.append((h, I, jb, nk))
            nblk = len(allblocks)
            per_tile = max(1, int(nblk * 0.85) // NT)
            GAP = max(1, min(6, (per_tile - 2) // 13))
            sched = {}
            for i in range(NT):
                for si in range(13):
                    sched.setdefault(i * per_tile + 2 + si * GAP, []).append((i, si))
            stage_cache = {}
            pending = []
            ndone = [0]
            CH = min(2048, S)

            NPC = S // CH
            bqtp = [Buf() for _ in range(NPC)]
            bktp = [Buf() for _ in range(NPC)]
            bvvp = [Buf() for _ in range(NPC)]

            def load_head(h):
                for c0 in range(0, S, CH):
                    pc = c0 // CH
                    P.dma("pool", kt[:, c0:c0 + CH], KT[h, :, c0:c0 + CH], w=[bktp[pc]])
                    P.dma("sp", qt[:, c0:c0 + CH], QT[h, :, c0:c0 + CH], w=[bqtp[pc]])
                    P.dma("sp", vv[:, c0 // 128:(c0 + CH) // 128, 0:64],
                          VA[c0:c0 + CH, h * 64:(h + 1) * 64].rearrange("(a p) d -> p a d", p=128), w=[bvvp[pc]])

            P.op("pool", "memset", ap=vv[:, :, 64:65], constant=1.0, w=bvvp)
            loaded = set()

            def s_mm(n):
                h, I, jb, nk = allblocks[n]
                d = jb - 4 * I
                qlo = 128 * d if d > 0 else 0
                pt, pb = psS[n % NS]
                P.op("pe", "matmul", out=pt[:, qlo:512], lhsT=kt[0:70, jb * 128:(jb + 1) * 128], rhs=qt[0:70, I * 512 + qlo:I * 512 + 512],
                     start=True, stop=(d < 0), r=[bqtp[(I * 512) // CH], bktp[(jb * 128) // CH]], w=[pb])
                if d >= 0:
                    P.op("pe", "matmul", out=pt[:, qlo:qlo + 128], lhsT=cid[:], rhs=cneg[:], start=False, stop=True, r=[bcid, bcneg], w=[pb])

            qcnt = [0]

            def fin(h, I):
                slot = qcnt[0] % 2
                qcnt[0] += 1
                for pe_ in [p_ for p_ in pending if p_[3] == slot]:
                    pending.remove(pe_)
                    fin2(pe_[0], pe_[1], pe_[3])
                ot, ob = psO[I % 2]
                o_, bo_ = osb[slot]
                rd, brd = rden[slot]
                P.op("dve", "reciprocal", out=rd[64:65, :], in_=ot[64:65, :], r=[ob], w=[brd])
                P.op("dve", "tensor_copy", out=o_[:], in_=ot[0:64, :], r=[ob], w=[bo_])
                pending.append((h, I, ndone[0] + 8, slot))

            def fin2(h, I, slot):
                o_, bo_ = osb[slot]
                rd, brd = rden[slot]
                y_, by_ = yst[slot]
                P.op("pe", "matmul", out=psX[0][0:64, :], lhsT=conesf[64:65, 0:64], rhs=rd[64:65, :], start=True, stop=True, r=[brd, bconesf], w=[psX[1]])
                P.op("dve", "tensor_tensor", out=y_[:], in0=o_[:], in1=psX[0][0:64, :], op=ALU.mult, r=[bo_, psX[1]], w=[by_])
                TPC = (S // 4) // 512
                P.dma("sp", Yw(slice(h * 64, (h + 1) * 64), I), y_[:], r=[by_], w=[dYp[(h, I // TPC)]])
                if ycb is not None and I % TPC == TPC - 1:
                    ycb(P, h, I // TPC, dYp[(h, I // TPC)])

            def exp_pv(n):
                h, I, jb, nk = allblocks[n]
                d = jb - 4 * I
                qlo = 128 * d if d > 0 else 0
                pt, pb = psS[n % NS]
                t, b = pT[n % NP_]
                P.op("act", "activation", out=t[:, qlo:512], in_=pt[:, qlo:512], func=AF.Exp, r=[pb], w=[b])
                ot, ob = psO[I % 2]
                P.op("pe", "matmul", out=ot[0:65, qlo:512], lhsT=vv[:, jb, 0:65], rhs=t[:, qlo:512], start=(jb == 0), stop=(jb == nk - 1),
                     r=[bvvp[(jb * 128) // CH], b], w=[ob])
                if jb == nk - 1:
                    fin(h, I)

            def run_stage(i, si):
                if i not in stage_cache:
                    stage_cache[i] = hgrn_stages(i)
                stage_cache[i][si]()

            LOOK = 2
            for n in range(nblk):
                hcur = allblocks[n][0]
                if n == 0 or allblocks[n - 1][0] != hcur:
                    load_head(hcur)
                    for m in range(n, min(n + LOOK, nblk)):
                        s_mm(m)
                if n + LOOK < nblk and allblocks[n + LOOK][0] == hcur:
                    s_mm(n + LOOK)
                exp_pv(n)
                ndone[0] += 1
                while pending and pending[0][2] <= ndone[0]:
                    h_, I_, _, sl_ = pending.pop(0)
                    fin2(h_, I_, sl_)
                for (i, si) in sched.pop(n, []):
                    run_stage(i, si)
            while pending:
                h_, I_, _, sl_ = pending.pop(0)
                fin2(h_, I_, sl_)
            for n in sorted(sched.keys()):
                for (i, si) in sched[n]:
                    run_stage(i, si)
            P.finish()


def alloc_scratch_A(nc, S, tag):
    def dt(name, shape, dty=BF16):
        return nc.dram_tensor(f"{tag}_{name}", shape, dty, kind="Internal").ap()
    return dict(QT=dt("QT", [2, 70, S]), KT=dt("KT", [2, 70, S]), VA=dt("VA", [S, 128]),
                QE=dt("QE", [128, S]), KE=dt("KE", [128, S]), KD=dt("KD", [S, 128]),
                VB=dt("VB", [S, 128]), GS=dt("GS", [128, S]))


def build_A(S, x_f32=True):
    nc = bass.Bass("TRN2", target_bir_lowering=False)
    R4 = S // 4
    xg = nc.dram_tensor("xg", [4, 1024, R4], F32 if x_f32 else BF16, kind="ExternalInput").ap()
    wA = nc.dram_tensor("wA", [1024, WA_COLS], F32, kind="ExternalInput").ap()
    vecA = nc.dram_tensor("vecA", [128, 8], F32, kind="ExternalInput").ap()
    cst = nc.dram_tensor("cst", [128, CST_COLS], F32, kind="ExternalInput").ap()
    Y = nc.dram_tensor("Y", [256, S], BF16, kind="ExternalOutput").ap()
    scr = alloc_scratch_A(nc, S, "a")
    with contextlib.ExitStack() as gst:
        GSEM[0] = gst
        GD.clear()
        phase_A(nc, S, xg, wA, vecA, cst, lambda rows, i: Y[rows, i * 512:(i + 1) * 512], scr, "A")
    return nc


def ln_alloc(TA, nb=2):
    return dict(hb=[TA.sb([128, 512], BF16, "hb") for _ in range(nb)], hq=[TA.sb([128, 512], BF16, "hq") for _ in range(nb)],
                mean=TA.sb([128, 512], F32, "mean"), rs=TA.sb([128, 512], F32, "rs"))


def layer_norm_steps(P, LT, nextP, h, bh, gcol, bcol, bvecB, conesK, bconesK, out_writer):
    hb, hq = LT["hb"], LT["hq"]
    mean, bmean = LT["mean"]
    rs, brs = LT["rs"]

    def stats():
        pm, pmb = nextP()
        pq, pqb = nextP()
        for c in range(8):
            t, b = hb[c % len(hb)]
            q, bq = hq[c % len(hq)]
            P.op("act", "copy", out=t[:], in_=h[:, c, :], r=[bh[c]], w=[b])
            P.op("dve", "tensor_tensor", out=q[:], in0=h[:, c, :], in1=h[:, c, :], op=ALU.mult, r=[bh[c]], w=[bq])
            P.op("pe", "matmul", out=pm[:], lhsT=conesK[:], rhs=t[:], start=(c == 0), stop=(c == 7), r=[bconesK, b], w=[pmb])
            P.op("pe", "matmul", out=pq[:], lhsT=conesK[:], rhs=q[:], start=(c == 0), stop=(c == 7), r=[bconesK, bq], w=[pqb])
        P.op("act", "copy", out=mean[:], in_=pm[:], r=[pmb], w=[bmean])
        P.op("dve", "tensor_tensor", out=rs[:], in0=mean[:], in1=mean[:], op=ALU.mult, r=[bmean], w=[brs])
        P.op("dve", "tensor_tensor", out=rs[:], in0=pq[:], in1=rs[:], op=ALU.subtract, r=[pqb, brs], w=[brs])
        P.op("dve", "tensor_scalar", out=rs[:], in0=rs[:], scalar1=LN_EPS, scalar2=None, op0=ALU.add, r=[brs], w=[brs])
        P.op("act", "activation", out=rs[:], in_=rs[:], func=AF.Sqrt, r=[brs], w=[brs])
        P.op("dve", "reciprocal", out=rs[:], in_=rs[:], r=[brs], w=[brs])

    def apply(c):
        def f():
            eng = "pool"
            P.op(eng, "tensor_tensor", out=h[:, c, :], in0=h[:, c, :], in1=mean[:], op=ALU.subtract, r=[bh[c], bmean], w=[bh[c]])
            P.op(eng, "tensor_tensor", out=h[:, c, :], in0=h[:, c, :], in1=rs[:], op=ALU.mult, r=[bh[c], brs], w=[bh[c]])
            P.op(eng, "tensor_scalar", out=h[:, c, :], in0=h[:, c, :], scalar1=gcol(c), scalar2=bcol(c), op0=ALU.mult, op1=ALU.add, r=[bh[c], bvecB], w=[bh[c]])
            if c == 7:
                out_writer()
        return f
    return [stats] + [apply(c) for c in range(8)]


def phase_B(nc, R4, Yg, xres, wg, wab, wo, wfi, wfo, vecB, cst, X1, XO, XOB, tag, xcb=None):
    NT = R4 // 512
    with contextlib.ExitStack() as st0:
        TA0 = TileAlloc(nc, st0, tag + "g")
        vB, bvB = TA0.sb([128, 32], F32, "vecB")
        conesK, bconesK = TA0.sb([128, 128], BF16, "onesK")
        onesf, bonesf = TA0.sb([128, 128], F32, "onesf")
        dX1 = [Buf() for _ in range(NT)]

        with contextlib.ExitStack() as st:
            P = Prog(nc, st, tag + "1")
            TA = TileAlloc(nc, st, tag + "1")
            wg_sb, bwg = TA.sb([128, 8, 2048], BF16, "wg")
            wab_sb, bwab = TA.sb([128, 8, 1024], BF16, "wab")
            wo_sb, bwo = TA.sb([128, 8, 1024], BF16, "wo")
            xr = [TA.sb([128, 8, 512], F32, "xr") for _ in range(2)]
            xb = [TA.sb([128, 8, 512], BF16, "xb") for _ in range(2)]
            yt = [TA.sb([128, 8, 512], BF16, "yt") for _ in range(2)]
            mg, bmg = TA.sb([128, 8, 512], BF16, "mg")
            sga = [TA.sb([128, 512], F32, "sga") for _ in range(2)]
            sgb = [TA.sb([128, 512], F32, "sgb") for _ in range(2)]
            t1 = [TA.sb([128, 512], F32, "t1") for _ in range(2)]
            t2 = [TA.sb([128, 512], F32, "t2") for _ in range(2)]
            hh = [(TA.sb([128, 8, 512], F32, "hh")[0], [Buf() for _ in range(8)]) for _ in range(2)]
            psl = [TA.ps([128, 512], F32, "ps") for _ in range(8)]
            LT = ln_alloc(TA, 4)
            npz = [0]

            def nextP():
                npz[0] += 1
                return psl[npz[0] % 8]
            P.dma("sp", vB[:], vecB, w=[bvB])
            P.dma("sp", onesf[:], cst[:, K_ONES:K_ONES + 128], w=[bonesf])
            P.op("dve", "tensor_scalar", out=conesK[:], in0=onesf[:], scalar1=1.0 / 1024.0, scalar2=None, op0=ALU.mult, r=[bonesf], w=[bconesK])
            wgv = wg.rearrange("(kc p) n -> p kc n", p=128)
            bwg_m = [Buf() for _ in range(8)]
            PRE_LOAD0 = True

            def load(i):
                tok = slice(i * 512, (i + 1) * 512)
                xv = xres.rearrange("(kc p) t -> p kc t", p=128)[:, :, tok]
                yv = Yg.rearrange("(kc p) t -> p kc t", p=128)[:, :, tok]
                P.dma("sp", xr[i % 2][0][:], xv, w=[xr[i % 2][1]])
                P.dma("pool", xb[i % 2][0][:], xv, w=[xb[i % 2][1]])
                P.dma("sp", yt[i % 2][0][:], yv, w=[yt[i % 2][1]])

            load(0)
            deferred = []
            for m0 in range(0, 8, 2):
                for n0 in (0, 1024):
                    P.dma("pool", wg_sb[:, :, n0 + m0 * 128:n0 + (m0 + 2) * 128], wgv[:, :, n0 + m0 * 128:n0 + (m0 + 2) * 128], w=[bwg_m[m0], bwg_m[m0 + 1]])
                if m0 == 0:
                    for kc0 in range(0, 8, 4):
                        P.dma("pool", wab_sb[:, kc0:kc0 + 4, :], wab.rearrange("(kc p) n -> p kc n", p=128)[:, kc0:kc0 + 4, :], w=[bwab])
            for kc0 in range(0, 8, 4):
                P.dma("pool", wo_sb[:, kc0:kc0 + 4, :], wo.rearrange("(kc p) n -> p kc n", p=128)[:, kc0:kc0 + 4, :], w=[bwo])
            for i in range(NT):
                if i + 1 < NT:
                    load(i + 1)
                tok = slice(i * 512, (i + 1) * 512)
                xr_t, xr_b = xr[i % 2]
                xb_t, xb_b = xb[i % 2]
                yt_t, yt_b = yt[i % 2]
                h_t, h_b = hh[i % 2]
                for m in range(8):
                    if m >= 1 and deferred:
                        deferred.pop(0)()
                    ms = slice(m * 128, (m + 1) * 128)
                    pga, pgab = nextP()
                    for kc in range(8):
                        P.op("pe", "matmul", out=pga[:], lhsT=wg_sb[:, kc, ms], rhs=xb_t[:, kc, :], start=(kc == 0), stop=(kc == 7), r=[bwg_m[m], xb_b], w=[pgab])
                    pgb, pgbb = nextP()
                    for kc in range(8):
                        P.op("pe", "matmul", out=pgb[:], lhsT=wg_sb[:, kc, 1024 + m * 128:1024 + (m + 1) * 128], rhs=xb_t[:, kc, :], start=(kc == 0), stop=(kc == 7), r=[bwg_m[m], xb_b], w=[pgbb])
                    ppa, ppab = nextP()
                    for r_ in range(4):
                        P.op("pe", "matmul", out=ppa[:], lhsT=wab_sb[:, 2 * r_, ms], rhs=yt_t[:, 2 * r_, :], start=(r_ == 0), stop=(r_ == 3), r=[bwab, yt_b], w=[ppab])
                    ppb, ppbb = nextP()
                    for r_ in range(4):
                        P.op("pe", "matmul", out=ppb[:], lhsT=wab_sb[:, 2 * r_ + 1, ms], rhs=yt_t[:, 2 * r_ + 1, :], start=(r_ == 0), stop=(r_ == 3), r=[bwab, yt_b], w=[ppbb])
                    sa, bsa = sga[m % 2]
                    sb_, bsb = sgb[m % 2]
                    a1, ba1 = t1[m % 2]
                    a2, ba2 = t2[m % 2]
                    P.op("act", "activation", out=sa[:], in_=pga[:], func=AF.Sigmoid, r=[pgab], w=[bsa])
                    P.op("act", "activation", out=sb_[:], in_=pgb[:], func=AF.Sigmoid, r=[pgbb], w=[bsb])
                    P.op("dve", "tensor_tensor", out=a1[:], in0=sa[:], in1=ppa[:], op=ALU.mult, r=[bsa, ppab], w=[ba1])
                    P.op("dve", "tensor_tensor", out=a2[:], in0=sb_[:], in1=ppb[:], op=ALU.mult, r=[bsb, ppbb], w=[ba2])
                    P.op("pool", "tensor_tensor", out=mg[:, m, :], in0=a1[:], in1=a2[:], op=ALU.add, r=[ba1, ba2], w=[bmg])
                for mo in range(8):
                    if deferred:
                        deferred.pop(0)()
                    pw, pwb = nextP()
                    for m in range(8):
                        P.op("pe", "matmul", out=pw[:], lhsT=wo_sb[:, m, mo * 128:(mo + 1) * 128], rhs=mg[:, m, :], start=(m == 0), stop=(m == 7), r=[bwo, bmg], w=[pwb])
                    P.op("dve", "scalar_tensor_tensor", out=h_t[:, mo, :], in0=xr_t[:, mo, :], scalar=float(ALPHA), in1=pw[:], op0=ALU.mult, op1=ALU.add,
                         r=[xr_b, pwb], w=[h_b[mo]])

                def wr(i=i, h_t=h_t, h_b=h_b, tok=tok):
                    P.dma("sp", X1.rearrange("(kc p) t -> p kc t", p=128)[:, :, tok], h_t[:], r=h_b, w=[dX1[i]])
                deferred.extend(layer_norm_steps(P, LT, nextP, h_t, h_b, lambda c: vB[:, c:c + 1], lambda c: vB[:, 8 + c:9 + c], bvB, conesK, bconesK, wr))
            while deferred:
                deferred.pop(0)()
            P.finish()

        with contextlib.ExitStack() as st:
            P = Prog(nc, st, tag + "2")
            TA = TileAlloc(nc, st, tag + "2")
            NM = FFH // 128
            wfi_sb, bwfi = TA.sb([128, 8, 2 * FFH], BF16, "wfi")
            wfo_sb, bwfo = TA.sb([128, NM, 1024], BF16, "wfo")
            xb = [TA.sb([128, 8, 512], BF16, "xb") for _ in range(2)]
            xrc = [TA.sb([128, 512], F32, "xrc") for _ in range(2)]
            aa, baa = TA.sb([128, NM, 512], BF16, "aa")
            sg = [TA.sb([128, 512], F32, "sg") for _ in range(2)]
            hh = TA.sb([128, 8, 512], F32, "hh")[0]
            bhh = [Buf() for _ in range(8)]
            psl = [TA.ps([128, 512], F32, "ps") for _ in range(8)]
            LT = ln_alloc(TA)
            npz = [0]

            def nextP():
                npz[0] += 1
                return psl[npz[0] % 8]
            wfv = wfi.rearrange("(kc p) n -> p kc n", p=128)
            X1v = X1.rearrange("(kc p) t -> p kc t", p=128)

            def load(i):
                tok = slice(i * 512, (i + 1) * 512)
                P.dma("pool", xb[i % 2][0][:], X1v[:, :, tok], r=[dX1[i]], w=[xb[i % 2][1]])

            deferred = []
            bwfi_m = [Buf() for _ in range(NM)]
            for m0 in range(0, NM, 2):
                for n0 in (0, FFH):
                    P.dma("pool", wfi_sb[:, :, n0 + m0 * 128:n0 + (m0 + 2) * 128], wfv[:, :, n0 + m0 * 128:n0 + (m0 + 2) * 128], w=[bwfi_m[m0], bwfi_m[m0 + 1]])
                if m0 == 0:
                    load(0)
            wov = wfo.rearrange("(kc p) n -> p kc n", p=128)
            for k0 in range(0, NM, 2):
                P.dma("pool", wfo_sb[:, k0:k0 + 2, :], wov[:, k0:k0 + 2, :], w=[bwfo])
            for i in range(NT):
                if i + 1 < NT:
                    load(i + 1)
                tok = slice(i * 512, (i + 1) * 512)
                xb_t, xb_b = xb[i % 2]
                for m in range(NM):
                    if m >= 2 and deferred:
                        deferred.pop(0)()
                    pu, pub = nextP()
                    for kc in range(8):
                        P.op("pe", "matmul", out=pu[:], lhsT=wfi_sb[:, kc, m * 128:(m + 1) * 128], rhs=xb_t[:, kc, :], start=(kc == 0), stop=(kc == 7), r=[bwfi_m[m], xb_b], w=[pub])
                    pg, pgb = nextP()
                    for kc in range(8):
                        P.op("pe", "matmul", out=pg[:], lhsT=wfi_sb[:, kc, FFH + m * 128:FFH + (m + 1) * 128], rhs=xb_t[:, kc, :], start=(kc == 0), stop=(kc == 7), r=[bwfi_m[m], xb_b], w=[pgb])
                    s_, bs_ = sg[m % 2]
                    P.op("act", "activation", out=s_[:], in_=pg[:], func=AF.Silu, r=[pgb], w=[bs_])
                    P.op("dve", "tensor_tensor", out=aa[:, m, :], in0=s_[:], in1=pu[:], op=ALU.mult, r=[bs_, pub], w=[baa])
                for mo in range(8):
                    xc, bxc = xrc[mo % 2]
                    P.dma("sp", xc[:], X1v[:, mo, tok], r=[dX1[i]], w=[bxc])
                    po, pob = nextP()
                    for m in range(NM):
                        P.op("pe", "matmul", out=po[:], lhsT=wfo_sb[:, m, mo * 128:(mo + 1) * 128], rhs=aa[:, m, :], start=(m == 0), stop=(m == NM - 1), r=[bwfo, baa], w=[pob])
                    P.op("dve", "scalar_tensor_tensor", out=hh[:, mo, :], in0=xc[:], scalar=float(ALPHA), in1=po[:], op0=ALU.mult, op1=ALU.add, r=[bxc, pob], w=[bhh[mo]])

                def wr(tok=tok, i=i):
                    P.dma("sp", XO.rearrange("(kc p) t -> p kc t", p=128)[:, :, tok], hh[:], r=bhh, w=[Buf()])
                    if XOB is not None:
                        bx_ = Buf()
                        if xcb is not None:
                            P.dma("pool", XOB[i].rearrange("(kc p) t -> p kc t", p=128), hh[:], r=bhh, w=[bx_])
                            xcb(P, i, bx_)
                        else:
                            P.dma("pool", XOB.rearrange("(kc p) t -> p kc t", p=128)[:, :, tok], hh[:], r=bhh, w=[bx_])
                deferred.extend(layer_norm_steps(P, LT, nextP, hh, bhh, lambda c: vB[:, 16 + c:17 + c], lambda c: vB[:, 24 + c:25 + c], bvB, conesK, bconesK, wr))
            while deferred:
                deferred.pop(0)()
            P.finish()


def build_B(R4, with_bf16_out=False):
    nc = bass.Bass("TRN2", target_bir_lowering=False)
    Yg = nc.dram_tensor("Yg", [1024, R4], BF16, kind="ExternalInput").ap()
    xres = nc.dram_tensor("xres", [1024, R4], F32, kind="ExternalInput").ap()
    wg = nc.dram_tensor("wg", [1024, 2048], F32, kind="ExternalInput").ap()
    wab = nc.dram_tensor("wab", [1024, 1024], F32, kind="ExternalInput").ap()
    wo = nc.dram_tensor("wo", [1024, 1024], F32, kind="ExternalInput").ap()
    wfi = nc.dram_tensor("wfi", [1024, 2 * FFH], F32, kind="ExternalInput").ap()
    wfo = nc.dram_tensor("wfo", [FFH, 1024], F32, kind="ExternalInput").ap()
    vecB = nc.dram_tensor("vecB", [128, 32], F32, kind="ExternalInput").ap()
    cst = nc.dram_tensor("cst", [128, CST_COLS], F32, kind="ExternalInput").ap()
    XO = nc.dram_tensor("XO", [1024, R4], F32, kind="ExternalOutput").ap()
    XOB = nc.dram_tensor("XOB", [1024, R4], BF16, kind="ExternalOutput").ap() if with_bf16_out else None
    X1 = nc.dram_tensor("b_X1", [1024, R4], F32, kind="Internal").ap()
    with contextlib.ExitStack() as gst:
        GSEM[0] = gst
        GD.clear()
        phase_B(nc, R4, Yg, xres, wg, wab, wo, wfi, wfo, vecB, cst, X1, XO, XOB, "B")
    return nc


def make_cst():
    c = np.zeros((128, CST_COLS), np.float32)
    p = np.arange(128)[:, None]
    f = np.arange(128)[None, :]
    c[:, K_ID:K_ID + 128] = (p == f)
    c[:, K_NEG:K_NEG + 128] = np.where(f < p, -30000.0, 0.0)
    c[:, K_BD:K_BD + 128] = (f >= p) & ((p // 64) == (f // 64))
    r = np.ones(512, np.float32)
    r[::64] = 0.0
    c[:, K_RESET:K_RESET + 512] = r[None, :]
    c[:, K_ONES:K_ONES + 512] = 1.0
    return c


def make_wA(w_in_l, j):
    B0 = 3 * 512 + 8
    sl = lambda base: w_in_l[:, base + 128 * j: base + 128 * (j + 1)]
    qa, ka, va = sl(0), sl(512), sl(1024)
    fa = w_in_l[:, 1536 + 2 * j: 1536 + 2 * (j + 1)]
    qb, fb, ib, gb = sl(B0), sl(B0 + 512), sl(B0 + 1024), sl(B0 + 1536)
    return np.ascontiguousarray(np.concatenate([qa, ka, qb, fb, gb, fa, va, ib], axis=1))


def make_vecA(l, j, b_fgate, hgrn_lb_logits, hgrn_norm_g):
    v = np.zeros((128, 8), np.float32)
    v[:, 0] = hgrn_lb_logits[0, 128 * j:128 * (j + 1)]
    v[:, 1] = hgrn_lb_logits[1, 128 * j:128 * (j + 1)]
    v[:, 2] = hgrn_norm_g[l]
    v[0:2, 3] = b_fgate[l, 2 * j:2 * j + 2]
    v[:, 4] = float(l)
    return v


def make_wab(wa_l, wb_l):
    out = np.empty((1024, 1024), np.float32)
    for r in range(4):
        out[256 * r:256 * r + 128] = wa_l[128 * r:128 * (r + 1)]
        out[256 * r + 128:256 * r + 256] = wb_l[128 * r:128 * (r + 1)]
    return out


def make_vecB(l, ln1_g, ln1_b, ln2_g, ln2_b):
    v = np.empty((128, 32), np.float32)
    for k, a in enumerate((ln1_g[l], ln1_b[l], ln2_g[l], ln2_b[l])):
        v[:, 8 * k:8 * k + 8] = a.reshape(8, 128).T
    return v


I32 = mybir.dt.int32
GROUPS = [[0, 1, 2, 3], [4, 5, 6, 7]]


def make_ycb(Ysc, GA, GB, GH):
    def ycb(P, part, r_, buf):
        if part < 2:
            src = Ysc[r_, part * 64:(part + 1) * 64, :]
            dst = (GA, GB)[part][r_].rearrange("s p t -> (s p) t")
        else:
            src = Ysc[r_, 128:256, :]
            dst = GH[r_].rearrange("s p t -> (s p) t")
        P.coll("AllGather", [src], [dst], GROUPS, r=[buf], w=[Buf()])
    return ycb


def exchange_Y(nc, R4, GA, GB, GH, Ygl, jidx, jt, bjt, jr, tag):
    with contextlib.ExitStack() as st:
        P = Prog(nc, st, tag)
        P.dma("sp", jt[:], jidx, w=[bjt])

        def src(Gt):
            def f(e):
                e.reg_load(jr, jt[0:1, 0:1])
                val = e.snap(jr, min_val=0, max_val=3)
                return Gt[bass.ds(val, 1)].rearrange("o s p t -> (o s) p t")
            return f
        Yv = Ygl.rearrange("(s p) t -> s p t", s=4)
        P.dma("sp", Yv[:, 0:64, :], src(GA), r=[bjt], w=[Buf()])
        P.dma("sp", Yv[:, 64:128, :], src(GB), r=[bjt], w=[Buf()])
        P.dma("sp", Yv[:, 128:256, :], src(GH), r=[bjt], w=[Buf()])
        P.finish()


def build_fused(S):
    nc = bass.Bass("TRN2", target_bir_lowering=False)
    R4 = S // 4
    ext = lambda name, shape, dt=F32: nc.dram_tensor(name, shape, dt, kind="ExternalInput").ap()
    itn = lambda name, shape, dt=BF16: nc.dram_tensor(name, shape, dt, kind="Internal").ap()
    xg = ext("xg", [4, 1024, R4])
    xres = ext("xres", [1024, R4])
    cst = ext("cst", [128, CST_COLS])
    jidx = ext("jidx", [1, 4], I32)
    W = []
    for l in range(DEPTH):
        W.append(dict(wA=ext(f"wA{l}", [1024, WA_COLS]), vecA=ext(f"vecA{l}", [128, 8]), wg=ext(f"wg{l}", [1024, 2048]),
                      wab=ext(f"wab{l}", [1024, 1024]), wo=ext(f"wo{l}", [1024, 1024]), wfi=ext(f"wfi{l}", [1024, 2 * FFH]),
                      wfo=ext(f"wfo{l}", [FFH, 1024]), vecB=ext(f"vecB{l}", [128, 32])))
    OUT = nc.dram_tensor("OUT", [1024, R4], F32, kind="ExternalOutput").ap()
    scr = alloc_scratch_A(nc, S, "a")
    Ysc = itn("Ysc", [4, 256, R4])
    GA = itn("GA", [4, 4, 64, R4])
    GB = itn("GB", [4, 4, 64, R4])
    GH = itn("GH", [4, 4, 128, R4])
    NTB = R4 // 512
    XOBt = itn("XOBt", [NTB, 1024, 512])
    XG1t = itn("XG1t", [NTB, 4, 1024, 512])
    Ygl = itn("Ygl", [1024, R4])
    X1 = itn("X1", [1024, R4], F32)
    XO0 = itn("XO0", [1024, R4], F32)
    xsrc1 = lambda r_, c0: XG1t[c0 // 512, r_].rearrange("(kc p) t -> p kc t", p=128)
    Yw = lambda rows, i: Ysc[(i * 512) // R4, rows, (i * 512) % R4:(i * 512) % R4 + 512]
    with contextlib.ExitStack() as gst:
        GSEM[0] = gst
        GD.clear()
        jt = gst.enter_context(nc.sbuf_tensor("jt", [1, 4], I32))
        bjt = Buf()
        jr = gst.enter_context(nc.sync.register("jr"))
        ycb = make_ycb(Ysc, GA, GB, GH)

        def xcb(P, i, buf):
            P.coll("AllGather", [XOBt[i]], [XG1t[i].rearrange("s f t -> (s f) t")], GROUPS, r=[buf], w=[Buf()])
        for l in range(DEPTH):
            w = W[l]
            phase_A(nc, S, xg if l == 0 else xsrc1, w["wA"], w["vecA"], cst, Yw, scr, f"A{l}", ycb=ycb)
            exchange_Y(nc, R4, GA, GB, GH, Ygl, jidx, jt, bjt, jr, f"E{l}")
            if os.environ.get("FUSE_STOP") == "1":
                break
            last = (l == DEPTH - 1)
            phase_B(nc, R4, Ygl, xres if l == 0 else XO0, w["wg"], w["wab"], w["wo"], w["wfi"], w["wfo"], w["vecB"], cst, X1,
                    OUT if last else XO0, None if last else XOBt, f"B{l}", xcb=None if last else xcb)
    return nc


def kernel(x, w_in, b_fgate, hgrn_lb_logits, hgrn_norm_g, w_branch_a, w_branch_b, w_out,
           ln1_g, ln1_b, w_ff_in, w_ff_out, ln2_g, ln2_b):
    x = np.asarray(x, np.float32)
    Bn, S, _ = x.shape
    R4 = S // 4
    f = lambda a: np.asarray(a, np.float32)
    w_in, b_fgate, hgrn_lb_logits, hgrn_norm_g = f(w_in), f(b_fgate), f(hgrn_lb_logits), f(hgrn_norm_g)
    w_branch_a, w_branch_b, w_out = f(w_branch_a), f(w_branch_b), f(w_out)
    ln1_g, ln1_b, w_ff_in, w_ff_out, ln2_g, ln2_b = f(ln1_g), f(ln1_b), f(w_ff_in), f(w_ff_out), f(ln2_g), f(ln2_b)
    cst = make_cst()
    cores = list(range(8))
    xg = [np.ascontiguousarray(x[b].reshape(4, R4, D).transpose(0, 2, 1)) for b in range(Bn)]
    B0 = 3 * 512 + 8 + 4 * 512
    shared = {}
    for l in range(DEPTH):
        shared[f"wg{l}"] = np.ascontiguousarray(w_in[l][:, B0:B0 + 2048])
        shared[f"wab{l}"] = make_wab(w_branch_a[l], w_branch_b[l])
        shared[f"wo{l}"] = w_out[l]
        shared[f"wfi{l}"] = w_ff_in[l]
        shared[f"wfo{l}"] = w_ff_out[l]
        shared[f"vecB{l}"] = make_vecB(l, ln1_g, ln1_b, ln2_g, ln2_b)
    wAs = {(l, j): make_wA(w_in[l], j) for l in range(DEPTH) for j in range(4)}
    in_maps = []
    for c in cores:
        b, j = divmod(c, 4)
        m = dict(xg=xg[b], xres=np.ascontiguousarray(xg[b][j]), cst=cst, jidx=np.array([[j, 0, 0, 0]], np.int32))
        for l in range(DEPTH):
            m[f"wA{l}"] = wAs[(l, j)]
            m[f"vecA{l}"] = make_vecA(l, j, b_fgate, hgrn_lb_logits, hgrn_norm_g)
        m.update(shared)
        in_maps.append(m)
    nc = build_fused(S)
    res = run_bass_kernel_spmd(nc, in_maps, core_ids=cores)
    out = np.empty((Bn, S, D), np.float32)
    for c in cores:
        b, j = divmod(c, 4)
        out[b, j * R4:(j + 1) * R4, :] = np.asarray(res.results[c]["OUT"]).T
    return out
```

```python
import contextlib
import os
import numpy as np
import ml_dtypes
import concourse.bass as bass
import concourse.mybir as mybir
from concourse.bass_utils import run_bass_kernel_spmd

F32 = mybir.dt.float32
BF16 = mybir.dt.bfloat16
AF = mybir.ActivationFunctionType
ALU = mybir.AluOpType
NPBF = ml_dtypes.bfloat16

D = 1024
SEQ = 16384
DEPTH = 2
FFH = 2816
ALPHA = (2 * DEPTH) ** 0.25
LN_EPS = 1e-5
RMS_EPS = 1e-6
NDMA = 6
GSEM = [None]
GD = {}


class Buf:
    __slots__ = ("name", "writers", "readers")

    def __init__(self, name=""):
        self.name = name
        self.writers = []
        self.readers = []


class Op:
    __slots__ = ("eng", "fn", "deps", "dma", "is_ms", "ms", "extra_waits", "prog")

    def __init__(self, eng, fn):
        self.eng = eng
        self.fn = fn
        self.deps = []
        self.dma = None
        self.is_ms = False
        self.ms = 0
        self.extra_waits = []


class Prog:
    ENGS = ("pe", "act", "dve", "pool", "sp")

    def __init__(self, nc, stack, tag):
        self.nc = nc
        self.tag = tag
        self.q = {e: [] for e in self.ENGS}
        stack = GSEM[0]
        self.psem = {e: stack.enter_context(nc.semaphore(f"{tag}_p_{e}")) for e in self.ENGS}
        if "dsem" not in GD:
            GD["dsem"] = {e: [stack.enter_context(nc.semaphore(f"gd_{e}{i}")) for i in range(NDMA)] for e in ("sp", "pool")}
            GD["dcount"] = {"sp": 0, "pool": 0}
            GD["cc"] = stack.enter_context(nc.semaphore("gcc"))
            GD["ccn"] = 0
        self.dsem = GD["dsem"]
        self.dcount = GD["dcount"]

    def _track(self, op, r, w):
        deps = []
        for b in r:
            deps += b.writers
        for b in w:
            deps += b.writers
            deps += b.readers
        op.deps = [d for d in deps if d.prog is self]
        for b in r:
            if op.dma is None:
                b.readers = [x for x in b.readers if not (x.eng == op.eng and x.dma is None)]
            b.readers.append(op)
        for b in w:
            b.writers = [op]
            b.readers = []

    def op(self, eng, name, r=(), w=(), **kw):
        o = Op(eng, (name, kw))
        o.prog = self
        self._track(o, r, w)
        self.q[eng].append(o)
        return o

    def dma(self, eng, out, in_, r=(), w=()):
        o = Op(eng, ("dma_start", dict(out=out, in_=in_)))
        o.prog = self
        n = self.dcount[eng]
        self.dcount[eng] += 1
        sem = self.dsem[eng][n % NDMA]
        gen = n // NDMA
        o.dma = (sem, 16 * (gen + 1), 16)
        if gen > 0:
            o.extra_waits.append((sem, 16 * gen))
        self._track(o, r, w)
        self.q[eng].append(o)
        return o

    def coll(self, kind, ins, outs, groups, r=(), w=()):
        o = Op("pool", ("collective_compute", dict(kind=kind, op=ALU.bypass, replica_groups=groups, ins=ins, outs=outs)))
        o.prog = self
        GD["ccn"] += 1
        o.dma = (GD["cc"], GD["ccn"], 1)
        self._track(o, r, w)
        self.q["pool"].append(o)
        return o

    def finish(self):
        nc = self.nc
        lasts = []
        for e in self.ENGS:
            if self.q[e]:
                for o in reversed(self.q[e]):
                    if o.dma is None:
                        lasts.append(o)
                        break
        dma_final = []
        for e in ("sp", "pool"):
            n = self.dcount[e]
            for i in range(min(n, NDMA)):
                cnt = (n - 1 - i) // NDMA + 1
                dma_final.append((self.dsem[e][i], 16 * cnt))
        for e in self.ENGS:
            for o in self.q[e]:
                for d in o.deps:
                    if d.dma is None and (d.eng != e or e != "pe"):
                        d.is_ms = True
        for o in lasts:
            o.is_ms = True
        for e in self.ENGS:
            c = 0
            for o in self.q[e]:
                if o.dma is None and o.is_ms:
                    c += 1
                    o.ms = c
        psem = self.psem

        def mk(eng):
            def body(e):
                waited = {}

                def wait(sem, v):
                    k = id(sem)
                    if waited.get(k, 0) >= v:
                        return
                    waited[k] = v
                    e.wait_ge(sem, v)

                for o in self.q[eng]:
                    need = {}
                    for d in o.deps:
                        if d.dma is not None:
                            s, v = d.dma[0], d.dma[1]
                        elif d.eng == eng and eng == "pe":
                            continue
                        else:
                            s, v = psem[d.eng], d.ms
                        k = id(s)
                        if k not in need or need[k][1] < v:
                            need[k] = (s, v)
                    for s, v in o.extra_waits:
                        k = id(s)
                        if k not in need or need[k][1] < v:
                            need[k] = (s, v)
                    for s, v in need.values():
                        wait(s, v)
                    kw = {k: (v(e) if callable(v) else v) for k, v in o.fn[1].items()}
                    ins = getattr(e, o.fn[0])(**kw)
                    if o.dma is not None:
                        ins.then_inc(o.dma[0], o.dma[2])
                    elif o.is_ms:
                        ins.then_inc(psem[eng], 1)
                for o in lasts:
                    if o.eng != eng:
                        wait(psem[o.eng], o.ms)
                for s, v in dma_final:
                    wait(s, v)
                if GD["ccn"] > 0:
                    wait(GD["cc"], GD["ccn"])

            return body

        with nc.Block() as block:
            block.tensor(mk("pe"))
            block.scalar(mk("act"))
            block.vector(mk("dve"))
            block.gpsimd(mk("pool"))
            block.sync(mk("sp"))


class TileAlloc:
    def __init__(self, nc, stack, tag):
        self.nc = nc
        self.stack = stack
        self.tag = tag
        self.n = 0

    def sb(self, shape, dt, name="t"):
        self.n += 1
        t = self.stack.enter_context(self.nc.sbuf_tensor(f"{self.tag}_{name}{self.n}", list(shape), dt))
        return t, Buf(name)

    def ps(self, shape, dt, name="p"):
        self.n += 1
        t = self.stack.enter_context(self.nc.psum_tensor(f"{self.tag}_{name}{self.n}", list(shape), dt))
        return t, Buf(name)


WA_COLS = 898
C_QA, C_KA, C_QB, C_FB, C_GB, C_FA, C_VA = 0, 128, 256, 384, 512, 640, 642
CST_COLS = 1408
K_ID, K_NEG, K_BD, K_RESET, K_ONES = 0, 128, 256, 384, 896


def phase_A(nc, S, xg, wA, vecA, cst, Yw, scr, tag, ycb=None):
    NT = S // 512
    R4 = S // 4
    QT, KT, VA, QE, KE, KD, VB, GS = (scr[k] for k in ("QT", "KT", "VA", "QE", "KE", "KD", "VB", "GS"))

    with contextlib.ExitStack() as st0:
        TA0 = TileAlloc(nc, st0, tag + "g")
        A_sb, bA = TA0.sb([128, S // 64], F32, "Adec")
        cid, bcid = TA0.sb([128, 128], BF16, "ident")
        cneg, bcneg = TA0.sb([128, 128], BF16, "neg")
        cbd, bcbd = TA0.sb([128, 128], BF16, "bdtri")
        cones, bcones = TA0.sb([128, 128], BF16, "ones")
        conesf, bconesf = TA0.sb([128, 64], F32, "onesf")
        vec, bvec = TA0.sb([128, 8], F32, "vec")
        lbv, blbv = TA0.sb([128, 8], F32, "lbv")

        with contextlib.ExitStack() as st:
            P = Prog(nc, st, tag + "1")
            TA = TileAlloc(nc, st, tag + "1")
            dQT = [[Buf() for _ in range(NT)] for _ in range(2)]
            dKT = [[Buf() for _ in range(NT)] for _ in range(2)]
            dVA = [Buf() for _ in range(NT)]
            dH = [Buf() for _ in range(NT)]
            wsb, bw = TA.sb([128, 8, WA_COLS], BF16, "wA")
            creset, bcreset = TA.sb([128, 512], F32, "reset")
            xt = [TA.sb([128, 8, 512], BF16, "xt") for _ in range(2)]
            psF = [TA.ps([128, 512], F32, "psF") for _ in range(5)]
            psT = [TA.ps([128, 512], F32, "psT") for _ in range(2)]
            psX = TA.ps([128, 512], BF16, "psX")
            nF = [0]

            def nextF():
                nF[0] += 1
                return psF[nF[0] % 5]
            stq = [TA.sb([128, 512], BF16, "stq") for _ in range(2)]
            stk = [TA.sb([128, 512], BF16, "stk") for _ in range(2)]
            stv = [TA.sb([128, 4, 256], BF16, "stv") for _ in range(2)]
            sqe = [TA.sb([128, 512], BF16, "sqe") for _ in range(2)]
            ske = [TA.sb([128, 512], BF16, "ske") for _ in range(2)]
            skdT = [TA.sb([128, 512], BF16, "skdT") for _ in range(2)]
            skd = [TA.sb([128, 4, 128], BF16, "skd") for _ in range(2)]
            sgs = [TA.sb([128, 512], BF16, "sgs") for _ in range(2)]
            wf = [TA.sb([128, 512], F32, "wf") for _ in range(24)]
            fz = [TA.sb([2, 512], F32, "fz") for _ in range(6)]
            Fc = [TA.sb([2, 512], F32, "Fc") for _ in range(2)]
            fst = [TA.sb([2, 3, 512], BF16, "fst") for _ in range(2)]
            negrow, bnegrow = TA.sb([2, 3, 512], BF16, "negrow")
            fr = [TA.sb([2, 512], F32, "fr") for _ in range(4)]
            onesrow, bonesrow = TA.sb([2, 3, 512], BF16, "onesrow")

            wv = wA.rearrange("(kc p) n -> p kc n", p=128)
            for kc0 in range(0, 8, 2):
                P.dma("pool", wsb[:, kc0:kc0 + 2, :], wv[:, kc0:kc0 + 2, :], w=[bw])
            P.dma("pool", cid[:], cst[:, K_ID:K_ID + 128], w=[bcid])
            P.dma("pool", cneg[:], cst[:, K_NEG:K_NEG + 128], w=[bcneg])
            P.dma("pool", cbd[:], cst[:, K_BD:K_BD + 128], w=[bcbd])
            P.dma("sp", creset[:], cst[:, K_RESET:K_RESET + 512], w=[bcreset])
            P.dma("pool", cones[:], cst[:, K_ONES:K_ONES + 128], w=[bcones])
            P.dma("sp", conesf[:], cst[:, K_ONES:K_ONES + 64], w=[bconesf])
            P.dma("sp", vec[:], vecA, w=[bvec])
            P.op("pool", "memset", ap=onesrow[:], constant=1.0, w=[bonesrow])
            P.op("pool", "memset", ap=negrow[:], constant=-1.0, w=[bnegrow])
            c_ = lambda i: lbv[:, i:i + 1]
            rw = dict(r=[blbv, bvec], w=[blbv])
            P.op("dve", "tensor_tensor", out=c_(0), in0=vec[:, 0:1], in1=vec[:, 1:2], op=ALU.max, **rw)
            P.op("dve", "tensor_tensor", out=c_(1), in0=vec[:, 0:1], in1=c_(0), op=ALU.subtract, **rw)
            P.op("dve", "tensor_tensor", out=c_(2), in0=vec[:, 1:2], in1=c_(0), op=ALU.subtract, **rw)
            P.op("act", "activation", out=lbv[:, 1:3], in_=lbv[:, 1:3], func=AF.Exp, **rw)
            P.op("dve", "tensor_tensor", out=c_(3), in0=c_(1), in1=c_(2), op=ALU.add, **rw)
            P.op("dve", "reciprocal", out=c_(4), in_=c_(3), **rw)
            P.op("dve", "tensor_tensor", out=c_(5), in0=c_(1), in1=c_(4), op=ALU.mult, **rw)
            P.op("dve", "tensor_tensor", out=c_(6), in0=c_(2), in1=c_(4), op=ALU.mult, **rw)
            P.op("dve", "scalar_tensor_tensor", out=c_(7), in0=c_(6), scalar=vec[:, 4:5], in1=c_(5), op0=ALU.mult, op1=ALU.add, **rw)
            P.op("dve", "tensor_tensor", out=c_(0), in0=c_(7), in1=c_(5), op=ALU.subtract, **rw)
            P.op("dve", "tensor_scalar", out=c_(1), in0=c_(0), scalar1=-1.0, scalar2=1.0, op0=ALU.mult, op1=ALU.add, **rw)
            P.op("dve", "tensor_scalar", out=c_(2), in0=vec[:, 3:4], scalar1=-1.0, scalar2=None, op0=ALU.mult, **rw)
            LB, OML, NBF = lbv[:, 0:1], lbv[:, 1:2], lbv[0:2, 2:3]

            for h in range(2):
                for i in range(NT):
                    tok = slice(i * 512, (i + 1) * 512)
                    P.dma("sp", QT[h:h + 1, 67:70, tok], negrow[0:1, :, :], r=[bnegrow], w=[dQT[h][i]])
                    P.dma("sp", KT[h:h + 1, 64:67, tok], onesrow[0:1, :, :], r=[bonesrow], w=[dKT[h][i]])

            def load_x(i):
                t, b = xt[i % 2]
                r_, c0 = divmod(i * 512, R4)
                src = xg(r_, c0) if callable(xg) else xg[r_].rearrange("(kc p) t -> p kc t", p=128)[:, :, c0:c0 + 512]
                for kc0 in range(0, 8, 4):
                    P.dma("pool", t[:, kc0:kc0 + 4, :], src[:, kc0:kc0 + 4, :], w=[b])

            load_x(0)
            Fprev = [None]

            def tiles(i):
                s3 = i % 3
                return wf[8 * s3:8 * s3 + 8], fz[2 * s3:2 * s3 + 2]

            def stA(i):
                if i + 1 < NT:
                    load_x(i + 1)
                xs, bx = xt[i % 2]
                tok = slice(i * 512, (i + 1) * 512)
                s2 = i % 2
                ((sig, bsig), (g_, bg), (bb, bbb), (eb, beb), (enb, benb), (ebr, bebr), (kk, bkk), (qs, bqs)), ((z0, bz0), (z1, bz1)) = tiles(i)

                def fgroup(col, M):
                    pt, pb = nextF()
                    for kc in range(8):
                        P.op("pe", "matmul", out=pt[0:M, :], lhsT=wsb[:, kc, col:col + M], rhs=xs[:, kc, :], start=(kc == 0), stop=(kc == 7),
                             r=[bw, bx], w=[pb])
                    return pt, pb
                pt, pb = fgroup(C_QA, 128)
                t, b = stq[s2]
                P.op("dve", "tensor_scalar", out=t[:], in0=pt[:], scalar1=0.125, scalar2=None, op0=ALU.mult, r=[pb], w=[b])
                for h in range(2):
                    P.dma("sp", QT[h, 0:64, tok], t[h * 64:(h + 1) * 64, :], r=[b], w=[dQT[h][i]])
                pt, pb = fgroup(C_KA, 128)
                t, b = stk[s2]
                P.op("dve", "tensor_copy", out=t[:], in_=pt[:], r=[pb], w=[b])
                for h in range(2):
                    P.dma("sp", KT[h, 0:64, tok], t[h * 64:(h + 1) * 64, :], r=[b], w=[dKT[h][i]])
                pt, pb = fgroup(C_FB, 128)
                P.op("act", "activation", out=sig[:], in_=pt[:], func=AF.Sigmoid, r=[pb], w=[bsig])
                pt, pb = fgroup(C_GB, 128)
                t, b = sgs[s2]
                P.op("act", "activation", out=t[:], in_=pt[:], func=AF.Sigmoid, r=[pb], w=[b])
                P.dma("sp", GS[:, tok], t[:], r=[b], w=[dH[i]])
                pt, pb = fgroup(C_QB, 128)
                P.op("act", "activation", out=qs[:], in_=pt[:], func=AF.Silu, r=[pb], w=[bqs])
                pt, pb = fgroup(C_FA, 2)
                P.op("act", "activation", out=z0[:], in_=pt[0:2, :], func=AF.Exp, scale=-1.0, bias=NBF, r=[pb, blbv], w=[bz0])
                t, b = stv[s2]
                for sb_ in range(4):
                    pt, pb = psT[sb_ % 2]
                    for kc in range(8):
                        P.op("pe", "matmul", out=pt[:, 0:256], lhsT=xs[:, kc, sb_ * 128:(sb_ + 1) * 128], rhs=wsb[:, kc, C_VA:C_VA + 256],
                             start=(kc == 0), stop=(kc == 7), r=[bw, bx], w=[pb])
                    P.op("act", "copy", out=t[:, sb_, :], in_=pt[:, 0:256], r=[pb], w=[b])
                P.dma("sp", VA[tok, :].rearrange("(a p) d -> p a d", p=128), t[:, :, 0:128], r=[b], w=[dVA[i]])
                P.dma("sp", VB[tok, :].rearrange("(a p) d -> p a d", p=128), t[:, :, 128:256], r=[b], w=[dH[i]])
                P.op("dve", "tensor_scalar", out=sig[:], in0=sig[:], scalar1=OML, scalar2=LB, op0=ALU.mult, op1=ALU.add, r=[bsig, blbv], w=[bsig])

            def stB1(i):
                tok = slice(i * 512, (i + 1) * 512)
                s2 = i % 2
                ((sig, bsig), (g_, bg), (bb, bbb), (eb, beb), (enb, benb), (ebr, bebr), (kk, bkk), (qs, bqs)), ((z0, bz0), (z1, bz1)) = tiles(i)
                P.op("act", "activation", out=g_[:], in_=sig[:], func=AF.Ln, r=[bsig], w=[bg])
                P.op("act", "activation", out=z1[:], in_=z0[:], func=AF.Ln, bias=1.0, r=[bz0], w=[bz1])
                P.op("pool", "tensor_scalar", out=kk[:], in0=sig[:], scalar1=-1.0, scalar2=1.0, op0=ALU.mult, op1=ALU.add, r=[bsig], w=[bkk])
                for c in range(8):
                    cs = slice(c * 64, (c + 1) * 64)
                    P.op("dve", "tensor_tensor_scan", out=bb[:, cs], data0=g_[:, cs], data1=g_[:, cs], initial=0.0, op0=ALU.add, op1=ALU.bypass, r=[bg], w=[bbb])
                Fc_t, Fc_b = Fc[s2]
                Fp = Fprev[0]
                init = 0.0 if Fp is None else Fp[0][:, 511:512]
                rr = [bz1] + ([] if Fp is None else [Fp[1]])
                P.op("dve", "tensor_scalar", out=z1[:], in0=z1[:], scalar1=-1.0, scalar2=None, op0=ALU.mult, r=[bz1], w=[bz1])
                P.op("dve", "tensor_tensor_scan", out=Fc_t[:], data0=z1[:], data1=z1[:], initial=init, op0=ALU.add, op1=ALU.bypass, r=rr, w=[Fc_b])
                Fprev[0] = (Fc_t, Fc_b)
                ft, fb = fst[s2]
                (r1, br1), (r2, br2) = fr[2 * s2:2 * s2 + 2]
                P.op("dve", "tensor_copy", out=ft[:, 0, :], in_=Fc_t[:], r=[Fc_b], w=[fb])
                P.op("dve", "tensor_tensor", out=r1[:], in0=Fc_t[:], in1=ft[:, 0, :], op=ALU.subtract, r=[Fc_b, fb], w=[br1])
                P.op("dve", "tensor_copy", out=ft[:, 1, :], in_=r1[:], r=[br1], w=[fb])
                P.op("dve", "tensor_tensor", out=r2[:], in0=r1[:], in1=ft[:, 1, :], op=ALU.subtract, r=[br1, fb], w=[br2])
                P.op("dve", "tensor_copy", out=ft[:, 2, :], in_=r2[:], r=[br2], w=[fb])
                for h in range(2):
                    P.dma("sp", QT[h:h + 1, 64:67, tok], ft[h:h + 1, 0:3, :], r=[fb], w=[dQT[h][i]])
                    P.dma("sp", KT[h:h + 1, 67:70, tok], ft[h:h + 1, 0:3, :], r=[fb], w=[dKT[h][i]])

            def stB2(i):
                tok = slice(i * 512, (i + 1) * 512)
                s2 = i % 2
                ((sig, bsig), (g_, bg), (bb, bbb), (eb, beb), (enb, benb), (ebr, bebr), (kk, bkk), (qs, bqs)), _ = tiles(i)
                P.op("act", "activation", out=eb[:], in_=bb[:], func=AF.Exp, r=[bbb], w=[beb])
                P.op("act", "activation", out=enb[:], in_=bb[:], func=AF.Exp, scale=-1.0, r=[bbb], w=[benb])
                for c in range(8):
                    cs = slice(c * 64, (c + 1) * 64)
                    last = slice(c * 64 + 63, c * 64 + 64)
                    P.op("act", "activation", out=ebr[:, cs], in_=bb[:, cs], func=AF.Exp, scale=-1.0, bias=bb[:, last], r=[bbb], w=[bebr])
                    P.op("pool", "tensor_copy", out=A_sb[:, i * 8 + c:i * 8 + c + 1], in_=eb[:, last], r=[beb], w=[bA])
                t, b = ske[s2]
                P.op("dve", "tensor_tensor", out=t[:], in0=kk[:], in1=enb[:], op=ALU.mult, r=[bkk, benb], w=[b])
                P.dma("sp", KE[:, tok], t[:], r=[b], w=[dH[i]])
                t, b = sqe[s2]
                P.op("pool", "tensor_tensor", out=t[:], in0=qs[:], in1=eb[:], op=ALU.mult, r=[bqs, beb], w=[b])
                P.dma("sp", QE[:, tok], t[:], r=[b], w=[dH[i]])
                tT, bT = skdT[s2]
                P.op("dve", "tensor_tensor", out=tT[:], in0=kk[:], in1=ebr[:], op=ALU.mult, r=[bkk, bebr], w=[bT])

            def stB3(i):
                tok = slice(i * 512, (i + 1) * 512)
                s2 = i % 2
                tT, bT = skdT[s2]
                for sb_ in range(4):
                    bs = slice(sb_ * 128, (sb_ + 1) * 128)
                    P.op("pe", "transpose", out=psX[0][:, bs], in_=tT[:, bs], identity=cid[:], r=[bT, bcid], w=[psX[1]])
                t, b = skd[s2]
                P.op("dve", "tensor_copy", out=t[:].rearrange("p a d -> p (a d)"), in_=psX[0][:], r=[psX[1]], w=[b])
                P.dma("sp", KD[tok, :].rearrange("(a p) d -> p a d", p=128), t[:], r=[b], w=[dH[i]])

            for t_ in range(NT + 3):
                if t_ < NT:
                    stA(t_)
                if 0 <= t_ - 1 < NT:
                    stB1(t_ - 1)
                if 0 <= t_ - 2 < NT:
                    stB2(t_ - 2)
                if 0 <= t_ - 3 < NT:
                    stB3(t_ - 3)
            P.finish()
        if os.environ.get("STOP_AFTER") == "1":
            return

        with contextlib.ExitStack() as st:
            P = Prog(nc, st, tag + "2")
            TA = TileAlloc(nc, st, tag + "2")
            NB = S // 128
            dYp = {(p_, r_): Buf() for p_ in range(3) for r_ in range(4)}
            qt, bqt = TA.sb([70, S], BF16, "qt")
            kt, bkt = TA.sb([70, S], BF16, "kt")
            vv, bvv = TA.sb([128, NB, 65], BF16, "vv")
            NS, NP_ = 3, 4
            psS = [TA.ps([128, 512], F32, "psS") for _ in range(NS)]
            psO = [TA.ps([128, 512], F32, "psO") for _ in range(2)]
            psX = TA.ps([128, 512], F32, "psX")
            psU = TA.ps([128, 512], F32, "psU")
            psOo = TA.ps([128, 512], F32, "psOo")
            pT = [TA.sb([128, 512], BF16, "pT") for _ in range(NP_)]
            osb = [TA.sb([64, 512], F32, "osb") for _ in range(2)]
            rden = [TA.sb([128, 512], F32, "rden") for _ in range(2)]
            yst = [TA.sb([64, 512], BF16, "yst") for _ in range(2)]
            SEG = min(2048, S)
            NSEG = S // SEG
            seg = []
            for k in range(2):
                seg.append(dict(
                    qe=TA.sb([128, SEG], BF16, "qe"), ke=TA.sb([128, SEG], BF16, "ke"),
                    kd=TA.sb([64, SEG // 64, 128], BF16, "kd"), vb=TA.sb([64, SEG // 64, 128], BF16, "vb"),
                    gs=TA.sb([128, SEG], BF16, "gs")))
            state = [TA.sb([128, 128], F32, "state") for _ in range(2)]
            sbf = [TA.sb([128, 8, 128], BF16, "sbf") for _ in range(2)]
            at = [TA.sb([128, 512], BF16, "at") for _ in range(2)]
            o_sb = [TA.sb([128, 512], F32, "o_sb") for _ in range(2)]
            sq = [TA.sb([128, 512], BF16, "sq") for _ in range(2)]
            rstd = [TA.sb([128, 512], F32, "rstd") for _ in range(2)]
            yb = [TA.sb([128, 512], BF16, "yb") for _ in range(2)]
            P.op("dve", "memset", ap=state[0][0][:], constant=0.0, w=[state[0][1]])
            P.op("dve", "memset", ap=sbf[0][0][:, 0, :], constant=0.0, w=[sbf[0][1]])

            def load_seg(k):
                sgm = seg[k % 2]
                c0 = k * SEG
                P.dma("sp", sgm["qe"][0][:], QE[:, c0:c0 + SEG], w=[sgm["qe"][1]])
                P.dma("pool", sgm["ke"][0][:], KE[:, c0:c0 + SEG], w=[sgm["ke"][1]])
                P.dma("sp", sgm["gs"][0][:], GS[:, c0:c0 + SEG], w=[sgm["gs"][1]])
                P.dma("pool", sgm["kd"][0][:], KD[c0:c0 + SEG, :].rearrange("(a p) d -> p a d", p=64), w=[sgm["kd"][1]])
                P.dma("sp", sgm["vb"][0][:], VB[c0:c0 + SEG, :].rearrange("(a p) d -> p a d", p=64), w=[sgm["vb"][1]])

            cur = [0]

            def hgrn_stages(i):
                k, ti = divmod(i * 512, SEG)
                sgm = seg[k % 2]
                qe, bqe = sgm["qe"]
                ke, bke = sgm["ke"]
                kd, bkd = sgm["kd"]
                vb, bvb = sgm["vb"]
                gs, bgs = sgm["gs"]
                s2 = i % 2
                sb_t, sb_b = sbf[s2]
                nsb_t, nsb_b = sbf[1 - s2]
                pu, pub = psU
                pa, pab = psU
                po, pob = psOo
                at_t, at_b = at[s2]
                o_t, o_b = o_sb[s2]
                sq_t, sq_b = sq[s2]
                rs_t, rs_b = rstd[s2]
                yb_t, yb_b = yb[s2]

                def st_load():
                    if ti == 0:
                        load_seg(k)

                def st_u(c0):
                    def f():
                        for c in range(c0, c0 + 4):
                            ch = (ti + c * 64) // 64
                            P.op("pe", "matmul", out=pu[:, (c % 4) * 128:(c % 4 + 1) * 128], lhsT=kd[:, ch, :], rhs=vb[:, ch, :],
                                 start=True, stop=True, r=[bkd, bvb], w=[pub])
                    return f

                def st_state(c0):
                    def f():
                        for c in range(c0, c0 + 4):
                            so_t, so_b = state[cur[0]]
                            sn_t, sn_b = state[1 - cur[0]]
                            P.op("dve", "scalar_tensor_tensor", out=sn_t[:], in0=so_t[:], scalar=A_sb[:, i * 8 + c:i * 8 + c + 1],
                                 in1=pu[:, (c % 4) * 128:(c % 4 + 1) * 128], op0=ALU.mult, op1=ALU.add, r=[so_b, bA, pub], w=[sn_b])
                            if c < 7:
                                P.op("pool", "tensor_copy", out=sb_t[:, c + 1, :], in_=sn_t[:], r=[sn_b], w=[sb_b])
                            else:
                                P.op("pool", "tensor_copy", out=nsb_t[:, 0, :], in_=sn_t[:], r=[sn_b], w=[nsb_b])
                            cur[0] = 1 - cur[0]
                    return f

                def st_attn():
                    for c in range(8):
                        tk = slice(ti + c * 64, ti + (c + 1) * 64)
                        P.op("pe", "matmul", out=pa[0:64, c * 64:(c + 1) * 64], lhsT=ke[:, tk], rhs=qe[:, tk], start=True, stop=True, r=[bke, bqe], w=[pab])

                def st_mask():
                    for c in range(8):
                        cs = slice(c * 64, (c + 1) * 64)
                        P.op("dve", "tensor_tensor", out=at_t[0:64, cs], in0=pa[0:64, cs], in1=cbd[0:64, 0:64], op=ALU.mult, r=[pab, bcbd], w=[at_b])

                def st_o():
                    for c in range(8):
                        ch = (ti + c * 64) // 64
                        cs = slice(c * 64, (c + 1) * 64)
                        P.op("pe", "matmul", out=po[:, cs], lhsT=vb[:, ch, :], rhs=at_t[0:64, cs], start=True, stop=False, r=[bvb, at_b], w=[pob])
                        P.op("pe", "matmul", out=po[:, cs], lhsT=sb_t[:, c, :], rhs=qe[:, ti + c * 64:ti + (c + 1) * 64],
                             start=False, stop=True, r=[sb_b, bqe], w=[pob])

                def st_sq():
                    P.op("dve", "tensor_copy", out=o_t[:], in_=po[:], r=[pob], w=[o_b])
                    P.op("pool", "tensor_tensor", out=sq_t[:], in0=o_t[:], in1=o_t[:], op=ALU.mult, r=[o_b], w=[sq_b])

                def st_m():
                    P.op("pe", "matmul", out=pa[:], lhsT=cones[:], rhs=sq_t[:], start=True, stop=True, r=[bcones, sq_b], w=[pab])

                def st_rs():
                    P.op("dve", "tensor_scalar", out=rs_t[:], in0=pa[:], scalar1=1.0 / 128.0, scalar2=RMS_EPS, op0=ALU.mult, op1=ALU.add, r=[pab], w=[rs_b])

                def st_sqrt():
                    P.op("act", "activation", out=rs_t[:], in_=rs_t[:], func=AF.Sqrt, r=[rs_b], w=[rs_b])

                def st_fin():
                    P.op("dve", "reciprocal", out=rs_t[:], in_=rs_t[:], r=[rs_b], w=[rs_b])
                    P.op("pool", "tensor_tensor", out=o_t[:], in0=o_t[:], in1=rs_t[:], op=ALU.mult, r=[o_b, rs_b], w=[o_b])
                    P.op("dve", "scalar_tensor_tensor", out=yb_t[:], in0=o_t[:], scalar=vec[:, 2:3], in1=gs[:, ti:ti + 512], op0=ALU.mult, op1=ALU.mult,
                         r=[o_b, bvec, bgs], w=[yb_b])
                    TPC = (S // 4) // 512
                    P.dma("sp", Yw(slice(128, 256), i), yb_t[:], r=[yb_b], w=[dYp[(2, i // TPC)]])
                    if ycb is not None and i % TPC == TPC - 1:
                        ycb(P, 2, i // TPC, dYp[(2, i // TPC)])
                return [st_load, st_u(0), st_state(0), st_u(4), st_state(4), st_attn, st_mask, st_o, st_sq, st_m, st_rs, st_sqrt, st_fin]

            allblocks = []
            for h in range(2):
                for I in range(NT):
                    nk = 4 * I + 4
                    for jb in range(nk):
                        allblocks.append((h, I, jb, nk))
            nblk = len(allblocks)
            per_tile = max(1, int(nblk * 0.85) // NT)
            GAP = max(1, min(6, (per_tile - 2) // 13))
            sched = {}
            for i in range(NT):
                for si in range(13):
                    sched.setdefault(i * per_tile + 2 + si * GAP, []).append((i, si))
            stage_cache = {}
            pending = []
            ndone = [0]
            CH = min(2048, S)

            NPC = S // CH
            bqtp = [Buf() for _ in range(NPC)]
            bktp = [Buf() for _ in range(NPC)]
            bvvp = [Buf() for _ in range(NPC)]

            def load_head(h):
                for c0 in range(0, S, CH):
                    pc = c0 // CH
                    P.dma("pool", kt[:, c0:c0 + CH], KT[h, :, c0:c0 + CH], w=[bktp[pc]])
                    P.dma("sp", qt[:, c0:c0 + CH], QT[h, :, c0:c0 + CH], w=[bqtp[pc]])
                    P.dma("sp", vv[:, c0 // 128:(c0 + CH) // 128, 0:64],
                          VA[c0:c0 + CH, h * 64:(h + 1) * 64].rearrange("(a p) d -> p a d", p=128), w=[bvvp[pc]])

            P.op("pool", "memset", ap=vv[:, :, 64:65], constant=1.0, w=bvvp)
            loaded = set()

            def s_mm(n):
                h, I, jb, nk = allblocks[n]
                d = jb - 4 * I
                qlo = 128 * d if d > 0 else 0
                pt, pb = psS[n % NS]
                P.op("pe", "matmul", out=pt[:, qlo:512], lhsT=kt[0:70, jb * 128:(jb + 1) * 128], rhs=qt[0:70, I * 512 + qlo:I * 512 + 512],
                     start=True, stop=(d < 0), r=[bqtp[(I * 512) // CH], bktp[(jb * 128) // CH]], w=[pb])
                if d >= 0:
                    P.op("pe", "matmul", out=pt[:, qlo:qlo + 128], lhsT=cid[:], rhs=cneg[:], start=False, stop=True, r=[bcid, bcneg], w=[pb])

            qcnt = [0]

            def fin(h, I):
                slot = qcnt[0] % 2
                qcnt[0] += 1
                for pe_ in [p_ for p_ in pending if p_[3] == slot]:
                    pending.remove(pe_)
                    fin2(pe_[0], pe_[1], pe_[3])
                ot, ob = psO[I % 2]
                o_, bo_ = osb[slot]
                rd, brd = rden[slot]
                P.op("dve", "reciprocal", out=rd[64:65, :], in_=ot[64:65, :], r=[ob], w=[brd])
                P.op("dve", "tensor_copy", out=o_[:], in_=ot[0:64, :], r=[ob], w=[bo_])
                pending.append((h, I, ndone[0] + 8, slot))

            def fin2(h, I, slot):
                o_, bo_ = osb[slot]
                rd, brd = rden[slot]
                y_, by_ = yst[slot]
                P.op("pe", "matmul", out=psX[0][0:64, :], lhsT=conesf[64:65, 0:64], rhs=rd[64:65, :], start=True, stop=True, r=[brd, bconesf], w=[psX[1]])
                P.op("dve", "tensor_tensor", out=y_[:], in0=o_[:], in1=psX[0][0:64, :], op=ALU.mult, r=[bo_, psX[1]], w=[by_])
                TPC = (S // 4) // 512
                P.dma("sp", Yw(slice(h * 64, (h + 1) * 64), I), y_[:], r=[by_], w=[dYp[(h, I // TPC)]])
                if ycb is not None and I % TPC == TPC - 1:
                    ycb(P, h, I // TPC, dYp[(h, I // TPC)])

            def exp_pv(n):
                h, I, jb, nk = allblocks[n]
                d = jb - 4 * I
                qlo = 128 * d if d > 0 else 0
                pt, pb = psS[n % NS]
                t, b = pT[n % NP_]
                P.op("act", "activation", out=t[:, qlo:512], in_=pt[:, qlo:512], func=AF.Exp, r=[pb], w=[b])
                ot, ob = psO[I % 2]
                P.op("pe", "matmul", out=ot[0:65, qlo:512], lhsT=vv[:, jb, 0:65], rhs=t[:, qlo:512], start=(jb == 0), stop=(jb == nk - 1),
                     r=[bvvp[(jb * 128) // CH], b], w=[ob])
                if jb == nk - 1:
                    fin(h, I)

            def run_stage(i, si):
                if i not in stage_cache:
                    stage_cache[i] = hgrn_stages(i)
                stage_cache[i][si]()

            LOOK = 2
            for n in range(nblk):
                hcur = allblocks[n][0]
                if n == 0 or allblocks[n - 1][0] != hcur:
                    load_head(hcur)
                    for m in range(n, min(n + LOOK, nblk)):
                        s_mm(m)
                if n + LOOK < nblk and allblocks[n + LOOK][0] == hcur:
                    s_mm(n + LOOK)
                exp_pv(n)
                ndone[0] += 1
                while pending and pending[0][2] <= ndone[0]:
                    h_, I_, _, sl_ = pending.pop(0)
                    fin2(h_, I_, sl_)
                for (i, si) in sched.pop(n, []):
                    run_stage(i, si)
            while pending:
                h_, I_, _, sl_ = pending.pop(0)
                fin2(h_, I_, sl_)
            for n in sorted(sched.keys()):
                for (i, si) in sched[n]:
                    run_stage(i, si)
            P.finish()


def alloc_scratch_A(nc, S, tag):
    def dt(name, shape, dty=BF16):
        return nc.dram_tensor(f"{tag}_{name}", shape, dty, kind="Internal").ap()
    return dict(QT=dt("QT", [2, 70, S]), KT=dt("KT", [2, 70, S]), VA=dt("VA", [S, 128]),
                QE=dt("QE", [128, S]), KE=dt("KE", [128, S]), KD=dt("KD", [S, 128]),
                VB=dt("VB", [S, 128]), GS=dt("GS", [128, S]))


def build_A(S, x_f32=True):
    nc = bass.Bass("TRN2", target_bir_lowering=False)
    R4 = S // 4
    xg = nc.dram_tensor("xg", [4, 1024, R4], F32 if x_f32 else BF16, kind="ExternalInput").ap()
    wA = nc.dram_tensor("wA", [1024, WA_COLS], F32, kind="ExternalInput").ap()
    vecA = nc.dram_tensor("vecA", [128, 8], F32, kind="ExternalInput").ap()
    cst = nc.dram_tensor("cst", [128, CST_COLS], F32, kind="ExternalInput").ap()
    Y = nc.dram_tensor("Y", [256, S], BF16, kind="ExternalOutput").ap()
    scr = alloc_scratch_A(nc, S, "a")
    with contextlib.ExitStack() as gst:
        GSEM[0] = gst
        GD.clear()
        phase_A(nc, S, xg, wA, vecA, cst, lambda rows, i: Y[rows, i * 512:(i + 1) * 512], scr, "A")
    return nc


def ln_alloc(TA, nb=2):
    return dict(hb=[TA.sb([128, 512], BF16, "hb") for _ in range(nb)], hq=[TA.sb([128, 512], BF16, "hq") for _ in range(nb)],
                mean=TA.sb([128, 512], F32, "mean"), rs=TA.sb([128, 512], F32, "rs"))


def layer_norm_steps(P, LT, nextP, h, bh, gcol, bcol, bvecB, conesK, bconesK, out_writer):
    hb, hq = LT["hb"], LT["hq"]
    mean, bmean = LT["mean"]
    rs, brs = LT["rs"]

    def stats():
        pm, pmb = nextP()
        pq, pqb = nextP()
        for c in range(8):
            t, b = hb[c % len(hb)]
            q, bq = hq[c % len(hq)]
            P.op("act", "copy", out=t[:], in_=h[:, c, :], r=[bh[c]], w=[b])
            P.op("dve", "tensor_tensor", out=q[:], in0=h[:, c, :], in1=h[:, c, :], op=ALU.mult, r=[bh[c]], w=[bq])
            P.op("pe", "matmul", out=pm[:], lhsT=conesK[:], rhs=t[:], start=(c == 0), stop=(c == 7), r=[bconesK, b], w=[pmb])
            P.op("pe", "matmul", out=pq[:], lhsT=conesK[:], rhs=q[:], start=(c == 0), stop=(c == 7), r=[bconesK, bq], w=[pqb])
        P.op("act", "copy", out=mean[:], in_=pm[:], r=[pmb], w=[bmean])
        P.op("dve", "tensor_tensor", out=rs[:], in0=mean[:], in1=mean[:], op=ALU.mult, r=[bmean], w=[brs])
        P.op("dve", "tensor_tensor", out=rs[:], in0=pq[:], in1=rs[:], op=ALU.subtract, r=[pqb, brs], w=[brs])
        P.op("dve", "tensor_scalar", out=rs[:], in0=rs[:], scalar1=LN_EPS, scalar2=None, op0=ALU.add, r=[brs], w=[brs])
        P.op("act", "activation", out=rs[:], in_=rs[:], func=AF.Sqrt, r=[brs], w=[brs])
        P.op("dve", "reciprocal", out=rs[:], in_=rs[:], r=[brs], w=[brs])

    def apply(c):
        def f():
            eng = "dve" if (LT.get("flush") and c % 2 == 0) else "pool"
            P.op(eng, "tensor_tensor", out=h[:, c, :], in0=h[:, c, :], in1=mean[:], op=ALU.subtract, r=[bh[c], bmean], w=[bh[c]])
            P.op(eng, "tensor_tensor", out=h[:, c, :], in0=h[:, c, :], in1=rs[:], op=ALU.mult, r=[bh[c], brs], w=[bh[c]])
            P.op(eng, "tensor_scalar", out=h[:, c, :], in0=h[:, c, :], scalar1=gcol(c), scalar2=bcol(c), op0=ALU.mult, op1=ALU.add, r=[bh[c], bvecB], w=[bh[c]])
            if c == 7:
                out_writer()
        return f
    return [stats] + [apply(c) for c in range(8)]


def phase_B(nc, R4, Yg, xres, wg, wab, wo, wfi, wfo, vecB, cst, X1, XO, XOB, tag, xcb=None):
    NT = R4 // 512
    with contextlib.ExitStack() as st0:
        TA0 = TileAlloc(nc, st0, tag + "g")
        vB, bvB = TA0.sb([128, 32], F32, "vecB")
        conesK, bconesK = TA0.sb([128, 128], BF16, "onesK")
        onesf, bonesf = TA0.sb([128, 128], F32, "onesf")
        dX1 = [Buf() for _ in range(NT)]

        with contextlib.ExitStack() as st:
            P = Prog(nc, st, tag + "1")
            TA = TileAlloc(nc, st, tag + "1")
            wg_sb, bwg = TA.sb([128, 8, 2048], BF16, "wg")
            wab_sb, bwab = TA.sb([128, 8, 1024], BF16, "wab")
            wo_sb, bwo = TA.sb([128, 8, 1024], BF16, "wo")
            xr = [TA.sb([128, 8, 512], F32, "xr") for _ in range(2)]
            xb = [TA.sb([128, 8, 512], BF16, "xb") for _ in range(2)]
            yt = [TA.sb([128, 8, 512], BF16, "yt") for _ in range(2)]
            mg, bmg = TA.sb([128, 8, 512], BF16, "mg")
            sga = [TA.sb([128, 512], F32, "sga") for _ in range(2)]
            sgb = [TA.sb([128, 512], F32, "sgb") for _ in range(2)]
            t1 = [TA.sb([128, 512], F32, "t1") for _ in range(2)]
            t2 = [TA.sb([128, 512], F32, "t2") for _ in range(2)]
            hh = [(TA.sb([128, 8, 512], F32, "hh")[0], [Buf() for _ in range(8)]) for _ in range(2)]
            psl = [TA.ps([128, 512], F32, "ps") for _ in range(8)]
            LT = ln_alloc(TA, 4)
            npz = [0]

            def nextP():
                npz[0] += 1
                return psl[npz[0] % 8]
            P.dma("sp", vB[:], vecB, w=[bvB])
            P.dma("sp", onesf[:], cst[:, K_ONES:K_ONES + 128], w=[bonesf])
            P.op("dve", "tensor_scalar", out=conesK[:], in0=onesf[:], scalar1=1.0 / 1024.0, scalar2=None, op0=ALU.mult, r=[bonesf], w=[bconesK])
            wgv = wg.rearrange("(kc p) n -> p kc n", p=128)
            bwg_m = [Buf() for _ in range(8)]
            PRE_LOAD0 = True

            def load(i):
                tok = slice(i * 512, (i + 1) * 512)
                xv = xres.rearrange("(kc p) t -> p kc t", p=128)[:, :, tok]
                yv = Yg.rearrange("(kc p) t -> p kc t", p=128)[:, :, tok]
                P.dma("sp", xr[i % 2][0][:], xv, w=[xr[i % 2][1]])
                P.dma("pool", xb[i % 2][0][:], xv, w=[xb[i % 2][1]])
                P.dma("sp", yt[i % 2][0][:], yv, w=[yt[i % 2][1]])

            load(0)
            deferred = []
            for m0 in range(0, 8, 2):
                for n0 in (0, 1024):
                    P.dma("pool", wg_sb[:, :, n0 + m0 * 128:n0 + (m0 + 2) * 128], wgv[:, :, n0 + m0 * 128:n0 + (m0 + 2) * 128], w=[bwg_m[m0], bwg_m[m0 + 1]])
                if m0 == 0:
                    for kc0 in range(0, 8, 4):
                        P.dma("pool", wab_sb[:, kc0:kc0 + 4, :], wab.rearrange("(kc p) n -> p kc n", p=128)[:, kc0:kc0 + 4, :], w=[bwab])
            for kc0 in range(0, 8, 4):
                P.dma("pool", wo_sb[:, kc0:kc0 + 4, :], wo.rearrange("(kc p) n -> p kc n", p=128)[:, kc0:kc0 + 4, :], w=[bwo])
            for i in range(NT):
                if i + 1 < NT:
                    load(i + 1)
                tok = slice(i * 512, (i + 1) * 512)
                xr_t, xr_b = xr[i % 2]
                xb_t, xb_b = xb[i % 2]
                yt_t, yt_b = yt[i % 2]
                h_t, h_b = hh[i % 2]
                for m in range(8):
                    if m >= 1 and deferred:
                        deferred.pop(0)()
                    ms = slice(m * 128, (m + 1) * 128)
                    pga, pgab = nextP()
                    for kc in range(8):
                        P.op("pe", "matmul", out=pga[:], lhsT=wg_sb[:, kc, ms], rhs=xb_t[:, kc, :], start=(kc == 0), stop=(kc == 7), r=[bwg_m[m], xb_b], w=[pgab])
                    pgb, pgbb = nextP()
                    for kc in range(8):
                        P.op("pe", "matmul", out=pgb[:], lhsT=wg_sb[:, kc, 1024 + m * 128:1024 + (m + 1) * 128], rhs=xb_t[:, kc, :], start=(kc == 0), stop=(kc == 7), r=[bwg_m[m], xb_b], w=[pgbb])
                    ppa, ppab = nextP()
                    for r_ in range(4):
                        P.op("pe", "matmul", out=ppa[:], lhsT=wab_sb[:, 2 * r_, ms], rhs=yt_t[:, 2 * r_, :], start=(r_ == 0), stop=(r_ == 3), r=[bwab, yt_b], w=[ppab])
                    ppb, ppbb = nextP()
                    for r_ in range(4):
                        P.op("pe", "matmul", out=ppb[:], lhsT=wab_sb[:, 2 * r_ + 1, ms], rhs=yt_t[:, 2 * r_ + 1, :], start=(r_ == 0), stop=(r_ == 3), r=[bwab, yt_b], w=[ppbb])
                    sa, bsa = sga[m % 2]
                    sb_, bsb = sgb[m % 2]
                    a1, ba1 = t1[m % 2]
                    a2, ba2 = t2[m % 2]
                    P.op("act", "activation", out=sa[:], in_=pga[:], func=AF.Sigmoid, r=[pgab], w=[bsa])
                    P.op("act", "activation", out=sb_[:], in_=pgb[:], func=AF.Sigmoid, r=[pgbb], w=[bsb])
                    P.op("dve", "tensor_tensor", out=a1[:], in0=sa[:], in1=ppa[:], op=ALU.mult, r=[bsa, ppab], w=[ba1])
                    P.op("dve", "tensor_tensor", out=a2[:], in0=sb_[:], in1=ppb[:], op=ALU.mult, r=[bsb, ppbb], w=[ba2])
                    P.op("pool", "tensor_tensor", out=mg[:, m, :], in0=a1[:], in1=a2[:], op=ALU.add, r=[ba1, ba2], w=[bmg])
                for mo in range(8):
                    if deferred:
                        deferred.pop(0)()
                    pw, pwb = nextP()
                    for m in range(8):
                        P.op("pe", "matmul", out=pw[:], lhsT=wo_sb[:, m, mo * 128:(mo + 1) * 128], rhs=mg[:, m, :], start=(m == 0), stop=(m == 7), r=[bwo, bmg], w=[pwb])
                    P.op("dve", "scalar_tensor_tensor", out=h_t[:, mo, :], in0=xr_t[:, mo, :], scalar=float(ALPHA), in1=pw[:], op0=ALU.mult, op1=ALU.add,
                         r=[xr_b, pwb], w=[h_b[mo]])

                def wr(i=i, h_t=h_t, h_b=h_b, tok=tok):
                    P.dma("sp", X1.rearrange("(kc p) t -> p kc t", p=128)[:, :, tok], h_t[:], r=h_b, w=[dX1[i]])
                deferred.extend(layer_norm_steps(P, LT, nextP, h_t, h_b, lambda c: vB[:, c:c + 1], lambda c: vB[:, 8 + c:9 + c], bvB, conesK, bconesK, wr))
            LT["flush"] = True
            while deferred:
                deferred.pop(0)()
            P.finish()

        with contextlib.ExitStack() as st:
            P = Prog(nc, st, tag + "2")
            TA = TileAlloc(nc, st, tag + "2")
            NM = FFH // 128
            wfi_sb, bwfi = TA.sb([128, 8, 2 * FFH], BF16, "wfi")
            wfo_sb, bwfo = TA.sb([128, NM, 1024], BF16, "wfo")
            xb = [TA.sb([128, 8, 512], BF16, "xb") for _ in range(2)]
            xrc = [TA.sb([128, 512], F32, "xrc") for _ in range(2)]
            aa, baa = TA.sb([128, NM, 512], BF16, "aa")
            sg = [TA.sb([128, 512], F32, "sg") for _ in range(2)]
            hh = TA.sb([128, 8, 512], F32, "hh")[0]
            bhh = [Buf() for _ in range(8)]
            psl = [TA.ps([128, 512], F32, "ps") for _ in range(8)]
            LT = ln_alloc(TA)
            npz = [0]

            def nextP():
                npz[0] += 1
                return psl[npz[0] % 8]
            wfv = wfi.rearrange("(kc p) n -> p kc n", p=128)
            X1v = X1.rearrange("(kc p) t -> p kc t", p=128)

            def load(i):
                tok = slice(i * 512, (i + 1) * 512)
                P.dma("pool", xb[i % 2][0][:], X1v[:, :, tok], r=[dX1[i]], w=[xb[i % 2][1]])

            deferred = []
            bwfi_m = [Buf() for _ in range(NM)]
            for m0 in range(0, NM, 2):
                for n0 in (0, FFH):
                    P.dma("pool", wfi_sb[:, :, n0 + m0 * 128:n0 + (m0 + 2) * 128], wfv[:, :, n0 + m0 * 128:n0 + (m0 + 2) * 128], w=[bwfi_m[m0], bwfi_m[m0 + 1]])
                if m0 == 0:
                    load(0)
            wov = wfo.rearrange("(kc p) n -> p kc n", p=128)
            for k0 in range(0, NM, 2):
                P.dma("pool", wfo_sb[:, k0:k0 + 2, :], wov[:, k0:k0 + 2, :], w=[bwfo])
            for i in range(NT):
                if i + 1 < NT:
                    load(i + 1)
                tok = slice(i * 512, (i + 1) * 512)
                xb_t, xb_b = xb[i % 2]
                for m in range(NM):
                    if m >= 2 and deferred:
                        deferred.pop(0)()
                    pu, pub = nextP()
                    for kc in range(8):
                        P.op("pe", "matmul", out=pu[:], lhsT=wfi_sb[:, kc, m * 128:(m + 1) * 128], rhs=xb_t[:, kc, :], start=(kc == 0), stop=(kc == 7), r=[bwfi_m[m], xb_b], w=[pub])
                    pg, pgb = nextP()
                    for kc in range(8):
                        P.op("pe", "matmul", out=pg[:], lhsT=wfi_sb[:, kc, FFH + m * 128:FFH + (m + 1) * 128], rhs=xb_t[:, kc, :], start=(kc == 0), stop=(kc == 7), r=[bwfi_m[m], xb_b], w=[pgb])
                    s_, bs_ = sg[m % 2]
                    P.op("act", "activation", out=s_[:], in_=pg[:], func=AF.Silu, r=[pgb], w=[bs_])
                    P.op("dve", "tensor_tensor", out=aa[:, m, :], in0=s_[:], in1=pu[:], op=ALU.mult, r=[bs_, pub], w=[baa])
                for mo in range(8):
                    xc, bxc = xrc[mo % 2]
                    P.dma("sp", xc[:], X1v[:, mo, tok], r=[dX1[i]], w=[bxc])
                    po, pob = nextP()
                    for m in range(NM):
                        P.op("pe", "matmul", out=po[:], lhsT=wfo_sb[:, m, mo * 128:(mo + 1) * 128], rhs=aa[:, m, :], start=(m == 0), stop=(m == NM - 1), r=[bwfo, baa], w=[pob])
                    P.op("dve", "scalar_tensor_tensor", out=hh[:, mo, :], in0=xc[:], scalar=float(ALPHA), in1=po[:], op0=ALU.mult, op1=ALU.add, r=[bxc, pob], w=[bhh[mo]])

                def wr(tok=tok, i=i):
                    P.dma("sp", XO.rearrange("(kc p) t -> p kc t", p=128)[:, :, tok], hh[:], r=bhh, w=[Buf()])
                    if XOB is not None:
                        bx_ = Buf()
                        if xcb is not None:
                            P.dma("pool", XOB[i].rearrange("(kc p) t -> p kc t", p=128), hh[:], r=bhh, w=[bx_])
                            xcb(P, i, bx_)
                        else:
                            P.dma("pool", XOB.rearrange("(kc p) t -> p kc t", p=128)[:, :, tok], hh[:], r=bhh, w=[bx_])
                deferred.extend(layer_norm_steps(P, LT, nextP, hh, bhh, lambda c: vB[:, 16 + c:17 + c], lambda c: vB[:, 24 + c:25 + c], bvB, conesK, bconesK, wr))
            LT["flush"] = True
            while deferred:
                deferred.pop(0)()
            P.finish()


def build_B(R4, with_bf16_out=False):
    nc = bass.Bass("TRN2", target_bir_lowering=False)
    Yg = nc.dram_tensor("Yg", [1024, R4], BF16, kind="ExternalInput").ap()
    xres = nc.dram_tensor("xres", [1024, R4], F32, kind="ExternalInput").ap()
    wg = nc.dram_tensor("wg", [1024, 2048], F32, kind="ExternalInput").ap()
    wab = nc.dram_tensor("wab", [1024, 1024], F32, kind="ExternalInput").ap()
    wo = nc.dram_tensor("wo", [1024, 1024], F32, kind="ExternalInput").ap()
    wfi = nc.dram_tensor("wfi", [1024, 2 * FFH], F32, kind="ExternalInput").ap()
    wfo = nc.dram_tensor("wfo", [FFH, 1024], F32, kind="ExternalInput").ap()
    vecB = nc.dram_tensor("vecB", [128, 32], F32, kind="ExternalInput").ap()
    cst = nc.dram_tensor("cst", [128, CST_COLS], F32, kind="ExternalInput").ap()
    XO = nc.dram_tensor("XO", [1024, R4], F32, kind="ExternalOutput").ap()
    XOB = nc.dram_tensor("XOB", [1024, R4], BF16, kind="ExternalOutput").ap() if with_bf16_out else None
    X1 = nc.dram_tensor("b_X1", [1024, R4], F32, kind="Internal").ap()
    with contextlib.ExitStack() as gst:
        GSEM[0] = gst
        GD.clear()
        phase_B(nc, R4, Yg, xres, wg, wab, wo, wfi, wfo, vecB, cst, X1, XO, XOB, "B")
    return nc


def make_cst():
    c = np.zeros((128, CST_COLS), np.float32)
    p = np.arange(128)[:, None]
    f = np.arange(128)[None, :]
    c[:, K_ID:K_ID + 128] = (p == f)
    c[:, K_NEG:K_NEG + 128] = np.where(f < p, -30000.0, 0.0)
    c[:, K_BD:K_BD + 128] = (f >= p) & ((p // 64) == (f // 64))
    r = np.ones(512, np.float32)
    r[::64] = 0.0
    c[:, K_RESET:K_RESET + 512] = r[None, :]
    c[:, K_ONES:K_ONES + 512] = 1.0
    return c


def make_wA(w_in_l, j):
    B0 = 3 * 512 + 8
    sl = lambda base: w_in_l[:, base + 128 * j: base + 128 * (j + 1)]
    qa, ka, va = sl(0), sl(512), sl(1024)
    fa = w_in_l[:, 1536 + 2 * j: 1536 + 2 * (j + 1)]
    qb, fb, ib, gb = sl(B0), sl(B0 + 512), sl(B0 + 1024), sl(B0 + 1536)
    return np.ascontiguousarray(np.concatenate([qa, ka, qb, fb, gb, fa, va, ib], axis=1))


def make_vecA(l, j, b_fgate, hgrn_lb_logits, hgrn_norm_g):
    v = np.zeros((128, 8), np.float32)
    v[:, 0] = hgrn_lb_logits[0, 128 * j:128 * (j + 1)]
    v[:, 1] = hgrn_lb_logits[1, 128 * j:128 * (j + 1)]
    v[:, 2] = hgrn_norm_g[l]
    v[0:2, 3] = b_fgate[l, 2 * j:2 * j + 2]
    v[:, 4] = float(l)
    return v


def make_wab(wa_l, wb_l):
    out = np.empty((1024, 1024), np.float32)
    for r in range(4):
        out[256 * r:256 * r + 128] = wa_l[128 * r:128 * (r + 1)]
        out[256 * r + 128:256 * r + 256] = wb_l[128 * r:128 * (r + 1)]
    return out


def make_vecB(l, ln1_g, ln1_b, ln2_g, ln2_b):
    v = np.empty((128, 32), np.float32)
    for k, a in enumerate((ln1_g[l], ln1_b[l], ln2_g[l], ln2_b[l])):
        v[:, 8 * k:8 * k + 8] = a.reshape(8, 128).T
    return v


I32 = mybir.dt.int32
GROUPS = [[0, 1, 2, 3], [4, 5, 6, 7]]


def make_ycb(Ysc, GA, GB, GH):
    def ycb(P, part, r_, buf):
        if part < 2:
            src = Ysc[r_, part * 64:(part + 1) * 64, :]
            dst = (GA, GB)[part][r_].rearrange("s p t -> (s p) t")
        else:
            src = Ysc[r_, 128:256, :]
            dst = GH[r_].rearrange("s p t -> (s p) t")
        P.coll("AllGather", [src], [dst], GROUPS, r=[buf], w=[Buf()])
    return ycb


def exchange_Y(nc, R4, GA, GB, GH, Ygl, jidx, jt, bjt, jr, tag):
    with contextlib.ExitStack() as st:
        P = Prog(nc, st, tag)
        P.dma("sp", jt[:], jidx, w=[bjt])

        def src(Gt):
            def f(e):
                e.reg_load(jr, jt[0:1, 0:1])
                val = e.snap(jr, min_val=0, max_val=3)
                return Gt[bass.ds(val, 1)].rearrange("o s p t -> (o s) p t")
            return f
        Yv = Ygl.rearrange("(s p) t -> s p t", s=4)
        P.dma("sp", Yv[:, 0:64, :], src(GA), r=[bjt], w=[Buf()])
        P.dma("sp", Yv[:, 64:128, :], src(GB), r=[bjt], w=[Buf()])
        P.dma("sp", Yv[:, 128:256, :], src(GH), r=[bjt], w=[Buf()])
        P.finish()


def build_fused(S):
    nc = bass.Bass("TRN2", target_bir_lowering=False)
    R4 = S // 4
    ext = lambda name, shape, dt=F32: nc.dram_tensor(name, shape, dt, kind="ExternalInput").ap()
    itn = lambda name, shape, dt=BF16: nc.dram_tensor(name, shape, dt, kind="Internal").ap()
    xg = ext("xg", [4, 1024, R4])
    xres = ext("xres", [1024, R4])
    cst = ext("cst", [128, CST_COLS])
    jidx = ext("jidx", [1, 4], I32)
    W = []
    for l in range(DEPTH):
        W.append(dict(wA=ext(f"wA{l}", [1024, WA_COLS]), vecA=ext(f"vecA{l}", [128, 8]), wg=ext(f"wg{l}", [1024, 2048]),
                      wab=ext(f"wab{l}", [1024, 1024]), wo=ext(f"wo{l}", [1024, 1024]), wfi=ext(f"wfi{l}", [1024, 2 * FFH]),
                      wfo=ext(f"wfo{l}", [FFH, 1024]), vecB=ext(f"vecB{l}", [128, 32])))
    OUT = nc.dram_tensor("OUT", [1024, R4], F32, kind="ExternalOutput").ap()
    scr = alloc_scratch_A(nc, S, "a")
    Ysc = itn("Ysc", [4, 256, R4])
    GA = itn("GA", [4, 4, 64, R4])
    GB = itn("GB", [4, 4, 64, R4])
    GH = itn("GH", [4, 4, 128, R4])
    NTB = R4 // 512
    XOBt = itn("XOBt", [NTB, 1024, 512])
    XG1t = itn("XG1t", [NTB, 4, 1024, 512])
    Ygl = itn("Ygl", [1024, R4])
    X1 = itn("X1", [1024, R4], F32)
    XO0 = itn("XO0", [1024, R4], F32)
    xsrc1 = lambda r_, c0: XG1t[c0 // 512, r_].rearrange("(kc p) t -> p kc t", p=128)
    Yw = lambda rows, i: Ysc[(i * 512) // R4, rows, (i * 512) % R4:(i * 512) % R4 + 512]
    with contextlib.ExitStack() as gst:
        GSEM[0] = gst
        GD.clear()
        jt = gst.enter_context(nc.sbuf_tensor("jt", [1, 4], I32))
        bjt = Buf()
        jr = gst.enter_context(nc.sync.register("jr"))
        ycb = make_ycb(Ysc, GA, GB, GH)

        def xcb(P, i, buf):
            P.coll("AllGather", [XOBt[i]], [XG1t[i].rearrange("s f t -> (s f) t")], GROUPS, r=[buf], w=[Buf()])
        for l in range(DEPTH):
            w = W[l]
            phase_A(nc, S, xg if l == 0 else xsrc1, w["wA"], w["vecA"], cst, Yw, scr, f"A{l}", ycb=ycb)
            exchange_Y(nc, R4, GA, GB, GH, Ygl, jidx, jt, bjt, jr, f"E{l}")
            if os.environ.get("FUSE_STOP") == "1":
                break
            last = (l == DEPTH - 1)
            phase_B(nc, R4, Ygl, xres if l == 0 else XO0, w["wg"], w["wab"], w["wo"], w["wfi"], w["wfo"], w["vecB"], cst, X1,
                    OUT if last else XO0, None if last else XOBt, f"B{l}", xcb=None if last else xcb)
    return nc


def kernel(x, w_in, b_fgate, hgrn_lb_logits, hgrn_norm_g, w_branch_a, w_branch_b, w_out,
           ln1_g, ln1_b, w_ff_in, w_ff_out, ln2_g, ln2_b):
    x = np.asarray(x, np.float32)
    Bn, S, _ = x.shape
    R4 = S // 4
    f = lambda a: np.asarray(a, np.float32)
    w_in, b_fgate, hgrn_lb_logits, hgrn_norm_g = f(w_in), f(b_fgate), f(hgrn_lb_logits), f(hgrn_norm_g)
    w_branch_a, w_branch_b, w_out = f(w_branch_a), f(w_branch_b), f(w_out)
    ln1_g, ln1_b, w_ff_in, w_ff_out, ln2_g, ln2_b = f(ln1_g), f(ln1_b), f(w_ff_in), f(w_ff_out), f(ln2_g), f(ln2_b)
    cst = make_cst()
    cores = list(range(8))
    xg = [np.ascontiguousarray(x[b].reshape(4, R4, D).transpose(0, 2, 1)) for b in range(Bn)]
    B0 = 3 * 512 + 8 + 4 * 512
    shared = {}
    for l in range(DEPTH):
        shared[f"wg{l}"] = np.ascontiguousarray(w_in[l][:, B0:B0 + 2048])
        shared[f"wab{l}"] = make_wab(w_branch_a[l], w_branch_b[l])
        shared[f"wo{l}"] = w_out[l]
        shared[f"wfi{l}"] = w_ff_in[l]
        shared[f"wfo{l}"] = w_ff_out[l]
        shared[f"vecB{l}"] = make_vecB(l, ln1_g, ln1_b, ln2_g, ln2_b)
    wAs = {(l, j): make_wA(w_in[l], j) for l in range(DEPTH) for j in range(4)}
    in_maps = []
    for c in cores:
        b, j = divmod(c, 4)
        m = dict(xg=xg[b], xres=np.ascontiguousarray(xg[b][j]), cst=cst, jidx=np.array([[j, 0, 0, 0]], np.int32))
        for l in range(DEPTH):
            m[f"wA{l}"] = wAs[(l, j)]
            m[f"vecA{l}"] = make_vecA(l, j, b_fgate, hgrn_lb_logits, hgrn_norm_g)
        m.update(shared)
        in_maps.append(m)
    nc = build_fused(S)
    res = run_bass_kernel_spmd(nc, in_maps, core_ids=cores)
    out = np.empty((Bn, S, D), np.float32)
    for c in cores:
        b, j = divmod(c, 4)
        out[b, j * R4:(j + 1) * R4, :] = np.asarray(res.results[c]["OUT"]).T
    return out
```

```python
import contextlib
import os
import numpy as np
import ml_dtypes
import concourse.bass as bass
import concourse.mybir as mybir
from concourse.bass_utils import run_bass_kernel_spmd

F32 = mybir.dt.float32
BF16 = mybir.dt.bfloat16
AF = mybir.ActivationFunctionType
ALU = mybir.AluOpType
NPBF = ml_dtypes.bfloat16

D = 1024
SEQ = 16384
DEPTH = 2
FFH = 2816
ALPHA = (2 * DEPTH) ** 0.25
LN_EPS = 1e-5
RMS_EPS = 1e-6
NDMA = 6
GSEM = [None]
GD = {}


class Buf:
    __slots__ = ("name", "writers", "readers")

    def __init__(self, name=""):
        self.name = name
        self.writers = []
        self.readers = []


class Op:
    __slots__ = ("eng", "fn", "deps", "dma", "is_ms", "ms", "extra_waits", "prog")

    def __init__(self, eng, fn):
        self.eng = eng
        self.fn = fn
        self.deps = []
        self.dma = None
        self.is_ms = False
        self.ms = 0
        self.extra_waits = []


class Prog:
    ENGS = ("pe", "act", "dve", "pool", "sp")

    def __init__(self, nc, stack, tag):
        self.nc = nc
        self.tag = tag
        self.q = {e: [] for e in self.ENGS}
        stack = GSEM[0]
        self.psem = {e: stack.enter_context(nc.semaphore(f"{tag}_p_{e}")) for e in self.ENGS}
        if "dsem" not in GD:
            GD["dsem"] = {e: [stack.enter_context(nc.semaphore(f"gd_{e}{i}")) for i in range(NDMA)] for e in ("sp", "pool")}
            GD["dcount"] = {"sp": 0, "pool": 0}
            GD["cc"] = stack.enter_context(nc.semaphore("gcc"))
            GD["ccn"] = 0
        self.dsem = GD["dsem"]
        self.dcount = GD["dcount"]

    def _track(self, op, r, w):
        deps = []
        for b in r:
            deps += b.writers
        for b in w:
            deps += b.writers
            deps += b.readers
        op.deps = [d for d in deps if d.prog is self]
        for b in r:
            if op.dma is None:
                b.readers = [x for x in b.readers if not (x.eng == op.eng and x.dma is None)]
            b.readers.append(op)
        for b in w:
            b.writers = [op]
            b.readers = []

    def op(self, eng, name, r=(), w=(), **kw):
        o = Op(eng, (name, kw))
        o.prog = self
        self._track(o, r, w)
        self.q[eng].append(o)
        return o

    def dma(self, eng, out, in_, r=(), w=()):
        o = Op(eng, ("dma_start", dict(out=out, in_=in_)))
        o.prog = self
        n = self.dcount[eng]
        self.dcount[eng] += 1
        sem = self.dsem[eng][n % NDMA]
        gen = n // NDMA
        o.dma = (sem, 16 * (gen + 1), 16)
        if gen > 0:
            o.extra_waits.append((sem, 16 * gen))
        self._track(o, r, w)
        self.q[eng].append(o)
        return o

    def coll(self, kind, ins, outs, groups, r=(), w=()):
        o = Op("pool", ("collective_compute", dict(kind=kind, op=ALU.bypass, replica_groups=groups, ins=ins, outs=outs)))
        o.prog = self
        GD["ccn"] += 1
        o.dma = (GD["cc"], GD["ccn"], 1)
        self._track(o, r, w)
        self.q["pool"].append(o)
        return o

    def finish(self):
        nc = self.nc
        lasts = []
        for e in self.ENGS:
            if self.q[e]:
                for o in reversed(self.q[e]):
                    if o.dma is None:
                        lasts.append(o)
                        break
        dma_final = []
        for e in ("sp", "pool"):
            n = self.dcount[e]
            for i in range(min(n, NDMA)):
                cnt = (n - 1 - i) // NDMA + 1
                dma_final.append((self.dsem[e][i], 16 * cnt))
        for e in self.ENGS:
            for o in self.q[e]:
                for d in o.deps:
                    if d.dma is None and (d.eng != e or e != "pe"):
                        d.is_ms = True
        for o in lasts:
            o.is_ms = True
        for e in self.ENGS:
            c = 0
            for o in self.q[e]:
                if o.dma is None and o.is_ms:
                    c += 1
                    o.ms = c
        psem = self.psem

        def mk(eng):
            def body(e):
                waited = {}

                def wait(sem, v):
                    k = id(sem)
                    if waited.get(k, 0) >= v:
                        return
                    waited[k] = v
                    e.wait_ge(sem, v)

                for o in self.q[eng]:
                    need = {}
                    for d in o.deps:
                        if d.dma is not None:
                            s, v = d.dma[0], d.dma[1]
                        elif d.eng == eng and eng == "pe":
                            continue
                        else:
                            s, v = psem[d.eng], d.ms
                        k = id(s)
                        if k not in need or need[k][1] < v:
                            need[k] = (s, v)
                    for s, v in o.extra_waits:
                        k = id(s)
                        if k not in need or need[k][1] < v:
                            need[k] = (s, v)
                    for s, v in need.values():
                        wait(s, v)
                    kw = {k: (v(e) if callable(v) else v) for k, v in o.fn[1].items()}
                    ins = getattr(e, o.fn[0])(**kw)
                    if o.dma is not None:
                        ins.then_inc(o.dma[0], o.dma[2])
                    elif o.is_ms:
                        ins.then_inc(psem[eng], 1)
                for o in lasts:
                    if o.eng != eng:
                        wait(psem[o.eng], o.ms)
                for s, v in dma_final:
                    wait(s, v)
                if GD["ccn"] > 0:
                    wait(GD["cc"], GD["ccn"])

            return body

        with nc.Block() as block:
            block.tensor(mk("pe"))
            block.scalar(mk("act"))
            block.vector(mk("dve"))
            block.gpsimd(mk("pool"))
            block.sync(mk("sp"))


class TileAlloc:
    def __init__(self, nc, stack, tag):
        self.nc = nc
        self.stack = stack
        self.tag = tag
        self.n = 0

    def sb(self, shape, dt, name="t"):
        self.n += 1
        t = self.stack.enter_context(self.nc.sbuf_tensor(f"{self.tag}_{name}{self.n}", list(shape), dt))
        return t, Buf(name)

    def ps(self, shape, dt, name="p"):
        self.n += 1
        t = self.stack.enter_context(self.nc.psum_tensor(f"{self.tag}_{name}{self.n}", list(shape), dt))
        return t, Buf(name)


WA_COLS = 898
C_QA, C_KA, C_QB, C_FB, C_GB, C_FA, C_VA = 0, 128, 256, 384, 512, 640, 642
CST_COLS = 1408
K_ID, K_NEG, K_BD, K_RESET, K_ONES = 0, 128, 256, 384, 896


def phase_A(nc, S, xg, wA, vecA, cst, Yw, scr, tag, ycb=None):
    NT = S // 512
    R4 = S // 4
    QT, KT, VA, QE, KE, KD, VB, GS = (scr[k] for k in ("QT", "KT", "VA", "QE", "KE", "KD", "VB", "GS"))

    with contextlib.ExitStack() as st0:
        TA0 = TileAlloc(nc, st0, tag + "g")
        A_sb, bA = TA0.sb([128, S // 64], F32, "Adec")
        cid, bcid = TA0.sb([128, 128], BF16, "ident")
        cneg, bcneg = TA0.sb([128, 128], BF16, "neg")
        cbd, bcbd = TA0.sb([128, 128], BF16, "bdtri")
        cones, bcones = TA0.sb([128, 128], BF16, "ones")
        conesf, bconesf = TA0.sb([128, 64], F32, "onesf")
        vec, bvec = TA0.sb([128, 8], F32, "vec")
        lbv, blbv = TA0.sb([128, 8], F32, "lbv")

        with contextlib.ExitStack() as st:
            P = Prog(nc, st, tag + "1")
            TA = TileAlloc(nc, st, tag + "1")
            dQT = [[Buf() for _ in range(NT)] for _ in range(2)]
            dKT = [[Buf() for _ in range(NT)] for _ in range(2)]
            dVA = [Buf() for _ in range(NT)]
            dH = [Buf() for _ in range(NT)]
            wsb, bw = TA.sb([128, 8, WA_COLS], BF16, "wA")
            creset, bcreset = TA.sb([128, 512], F32, "reset")
            xt = [TA.sb([128, 8, 512], BF16, "xt") for _ in range(2)]
            psF = [TA.ps([128, 512], F32, "psF") for _ in range(5)]
            psT = [TA.ps([128, 512], F32, "psT") for _ in range(2)]
            psX = TA.ps([128, 512], BF16, "psX")
            nF = [0]

            def nextF():
                nF[0] += 1
                return psF[nF[0] % 5]
            stq = [TA.sb([128, 512], BF16, "stq") for _ in range(2)]
            stk = [TA.sb([128, 512], BF16, "stk") for _ in range(2)]
            stv = [TA.sb([128, 4, 256], BF16, "stv") for _ in range(2)]
            sqe = [TA.sb([128, 512], BF16, "sqe") for _ in range(2)]
            ske = [TA.sb([128, 512], BF16, "ske") for _ in range(2)]
            skdT = [TA.sb([128, 512], BF16, "skdT") for _ in range(2)]
            skd = [TA.sb([128, 4, 128], BF16, "skd") for _ in range(2)]
            sgs = [TA.sb([128, 512], BF16, "sgs") for _ in range(2)]
            wf = [TA.sb([128, 512], F32, "wf") for _ in range(24)]
            fz = [TA.sb([2, 512], F32, "fz") for _ in range(6)]
            Fc = [TA.sb([2, 512], F32, "Fc") for _ in range(2)]
            fst = [TA.sb([2, 3, 512], BF16, "fst") for _ in range(2)]
            negrow, bnegrow = TA.sb([2, 3, 512], BF16, "negrow")
            fr = [TA.sb([2, 512], F32, "fr") for _ in range(4)]
            onesrow, bonesrow = TA.sb([2, 3, 512], BF16, "onesrow")

            wv = wA.rearrange("(kc p) n -> p kc n", p=128)
            for kc0 in range(0, 8, 2):
                P.dma("pool", wsb[:, kc0:kc0 + 2, :], wv[:, kc0:kc0 + 2, :], w=[bw])
            P.dma("pool", cid[:], cst[:, K_ID:K_ID + 128], w=[bcid])
            P.dma("pool", cneg[:], cst[:, K_NEG:K_NEG + 128], w=[bcneg])
            P.dma("pool", cbd[:], cst[:, K_BD:K_BD + 128], w=[bcbd])
            P.dma("sp", creset[:], cst[:, K_RESET:K_RESET + 512], w=[bcreset])
            P.dma("pool", cones[:], cst[:, K_ONES:K_ONES + 128], w=[bcones])
            P.dma("sp", conesf[:], cst[:, K_ONES:K_ONES + 64], w=[bconesf])
            P.dma("sp", vec[:], vecA, w=[bvec])
            P.op("pool", "memset", ap=onesrow[:], constant=1.0, w=[bonesrow])
            P.op("pool", "memset", ap=negrow[:], constant=-1.0, w=[bnegrow])
            c_ = lambda i: lbv[:, i:i + 1]
            rw = dict(r=[blbv, bvec], w=[blbv])
            P.op("dve", "tensor_tensor", out=c_(0), in0=vec[:, 0:1], in1=vec[:, 1:2], op=ALU.max, **rw)
            P.op("dve", "tensor_tensor", out=c_(1), in0=vec[:, 0:1], in1=c_(0), op=ALU.subtract, **rw)
            P.op("dve", "tensor_tensor", out=c_(2), in0=vec[:, 1:2], in1=c_(0), op=ALU.subtract, **rw)
            P.op("act", "activation", out=lbv[:, 1:3], in_=lbv[:, 1:3], func=AF.Exp, **rw)
            P.op("dve", "tensor_tensor", out=c_(3), in0=c_(1), in1=c_(2), op=ALU.add, **rw)
            P.op("dve", "reciprocal", out=c_(4), in_=c_(3), **rw)
            P.op("dve", "tensor_tensor", out=c_(5), in0=c_(1), in1=c_(4), op=ALU.mult, **rw)
            P.op("dve", "tensor_tensor", out=c_(6), in0=c_(2), in1=c_(4), op=ALU.mult, **rw)
            P.op("dve", "scalar_tensor_tensor", out=c_(7), in0=c_(6), scalar=vec[:, 4:5], in1=c_(5), op0=ALU.mult, op1=ALU.add, **rw)
            P.op("dve", "tensor_tensor", out=c_(0), in0=c_(7), in1=c_(5), op=ALU.subtract, **rw)
            P.op("dve", "tensor_scalar", out=c_(1), in0=c_(0), scalar1=-1.0, scalar2=1.0, op0=ALU.mult, op1=ALU.add, **rw)
            P.op("dve", "tensor_scalar", out=c_(2), in0=vec[:, 3:4], scalar1=-1.0, scalar2=None, op0=ALU.mult, **rw)
            LB, OML, NBF = lbv[:, 0:1], lbv[:, 1:2], lbv[0:2, 2:3]

            for h in range(2):
                for i in range(NT):
                    tok = slice(i * 512, (i + 1) * 512)
                    P.dma("sp", QT[h:h + 1, 67:70, tok], negrow[0:1, :, :], r=[bnegrow], w=[dQT[h][i]])
                    P.dma("sp", KT[h:h + 1, 64:67, tok], onesrow[0:1, :, :], r=[bonesrow], w=[dKT[h][i]])

            def load_x(i):
                t, b = xt[i % 2]
                r_, c0 = divmod(i * 512, R4)
                src = xg(r_, c0) if callable(xg) else xg[r_].rearrange("(kc p) t -> p kc t", p=128)[:, :, c0:c0 + 512]
                for kc0 in range(0, 8, 4):
                    P.dma("pool", t[:, kc0:kc0 + 4, :], src[:, kc0:kc0 + 4, :], w=[b])

            load_x(0)
            Fprev = [None]

            def tiles(i):
                s3 = i % 3
                return wf[8 * s3:8 * s3 + 8], fz[2 * s3:2 * s3 + 2]

            def stA(i):
                if i + 1 < NT:
                    load_x(i + 1)
                xs, bx = xt[i % 2]
                tok = slice(i * 512, (i + 1) * 512)
                s2 = i % 2
                ((sig, bsig), (g_, bg), (bb, bbb), (eb, beb), (enb, benb), (ebr, bebr), (kk, bkk), (qs, bqs)), ((z0, bz0), (z1, bz1)) = tiles(i)

                def fgroup(col, M):
                    pt, pb = nextF()
                    for kc in range(8):
                        P.op("pe", "matmul", out=pt[0:M, :], lhsT=wsb[:, kc, col:col + M], rhs=xs[:, kc, :], start=(kc == 0), stop=(kc == 7),
                             r=[bw, bx], w=[pb])
                    return pt, pb
                pt, pb = fgroup(C_QA, 128)
                t, b = stq[s2]
                P.op("dve", "tensor_scalar", out=t[:], in0=pt[:], scalar1=0.125, scalar2=None, op0=ALU.mult, r=[pb], w=[b])
                for h in range(2):
                    P.dma("sp", QT[h, 0:64, tok], t[h * 64:(h + 1) * 64, :], r=[b], w=[dQT[h][i]])
                pt, pb = fgroup(C_KA, 128)
                t, b = stk[s2]
                P.op("dve", "tensor_copy", out=t[:], in_=pt[:], r=[pb], w=[b])
                for h in range(2):
                    P.dma("sp", KT[h, 0:64, tok], t[h * 64:(h + 1) * 64, :], r=[b], w=[dKT[h][i]])
                pt, pb = fgroup(C_FB, 128)
                P.op("act", "activation", out=sig[:], in_=pt[:], func=AF.Sigmoid, r=[pb], w=[bsig])
                pt, pb = fgroup(C_GB, 128)
                t, b = sgs[s2]
                P.op("act", "activation", out=t[:], in_=pt[:], func=AF.Sigmoid, r=[pb], w=[b])
                P.dma("sp", GS[:, tok], t[:], r=[b], w=[dH[i]])
                pt, pb = fgroup(C_QB, 128)
                P.op("act", "activation", out=qs[:], in_=pt[:], func=AF.Silu, r=[pb], w=[bqs])
                pt, pb = fgroup(C_FA, 2)
                P.op("act", "activation", out=z0[:], in_=pt[0:2, :], func=AF.Exp, scale=-1.0, bias=NBF, r=[pb, blbv], w=[bz0])
                t, b = stv[s2]
                for sb_ in range(4):
                    pt, pb = psT[sb_ % 2]
                    for kc in range(8):
                        P.op("pe", "matmul", out=pt[:, 0:256], lhsT=xs[:, kc, sb_ * 128:(sb_ + 1) * 128], rhs=wsb[:, kc, C_VA:C_VA + 256],
                             start=(kc == 0), stop=(kc == 7), r=[bw, bx], w=[pb])
                    P.op("act", "copy", out=t[:, sb_, :], in_=pt[:, 0:256], r=[pb], w=[b])
                P.dma("sp", VA[tok, :].rearrange("(a p) d -> p a d", p=128), t[:, :, 0:128], r=[b], w=[dVA[i]])
                P.dma("sp", VB[tok, :].rearrange("(a p) d -> p a d", p=128), t[:, :, 128:256], r=[b], w=[dH[i]])
                P.op("dve", "tensor_scalar", out=sig[:], in0=sig[:], scalar1=OML, scalar2=LB, op0=ALU.mult, op1=ALU.add, r=[bsig, blbv], w=[bsig])

            def stB1(i):
                tok = slice(i * 512, (i + 1) * 512)
                s2 = i % 2
                ((sig, bsig), (g_, bg), (bb, bbb), (eb, beb), (enb, benb), (ebr, bebr), (kk, bkk), (qs, bqs)), ((z0, bz0), (z1, bz1)) = tiles(i)
                P.op("act", "activation", out=g_[:], in_=sig[:], func=AF.Ln, r=[bsig], w=[bg])
                P.op("act", "activation", out=z1[:], in_=z0[:], func=AF.Ln, bias=1.0, r=[bz0], w=[bz1])
                P.op("pool", "tensor_scalar", out=kk[:], in0=sig[:], scalar1=-1.0, scalar2=1.0, op0=ALU.mult, op1=ALU.add, r=[bsig], w=[bkk])
                for c in range(8):
                    cs = slice(c * 64, (c + 1) * 64)
                    P.op("dve", "tensor_tensor_scan", out=bb[:, cs], data0=g_[:, cs], data1=g_[:, cs], initial=0.0, op0=ALU.add, op1=ALU.bypass, r=[bg], w=[bbb])
                Fc_t, Fc_b = Fc[s2]
                Fp = Fprev[0]
                init = 0.0 if Fp is None else Fp[0][:, 511:512]
                rr = [bz1] + ([] if Fp is None else [Fp[1]])
                P.op("dve", "tensor_scalar", out=z1[:], in0=z1[:], scalar1=-1.0, scalar2=None, op0=ALU.mult, r=[bz1], w=[bz1])
                P.op("dve", "tensor_tensor_scan", out=Fc_t[:], data0=z1[:], data1=z1[:], initial=init, op0=ALU.add, op1=ALU.bypass, r=rr, w=[Fc_b])
                Fprev[0] = (Fc_t, Fc_b)
                ft, fb = fst[s2]
                (r1, br1), (r2, br2) = fr[2 * s2:2 * s2 + 2]
                P.op("dve", "tensor_copy", out=ft[:, 0, :], in_=Fc_t[:], r=[Fc_b], w=[fb])
                P.op("dve", "tensor_tensor", out=r1[:], in0=Fc_t[:], in1=ft[:, 0, :], op=ALU.subtract, r=[Fc_b, fb], w=[br1])
                P.op("dve", "tensor_copy", out=ft[:, 1, :], in_=r1[:], r=[br1], w=[fb])
                P.op("dve", "tensor_tensor", out=r2[:], in0=r1[:], in1=ft[:, 1, :], op=ALU.subtract, r=[br1, fb], w=[br2])
                P.op("dve", "tensor_copy", out=ft[:, 2, :], in_=r2[:], r=[br2], w=[fb])
                for h in range(2):
                    P.dma("sp", QT[h:h + 1, 64:67, tok], ft[h:h + 1, 0:3, :], r=[fb], w=[dQT[h][i]])
                    P.dma("sp", KT[h:h + 1, 67:70, tok], ft[h:h + 1, 0:3, :], r=[fb], w=[dKT[h][i]])

            def stB2(i):
                tok = slice(i * 512, (i + 1) * 512)
                s2 = i % 2
                ((sig, bsig), (g_, bg), (bb, bbb), (eb, beb), (enb, benb), (ebr, bebr), (kk, bkk), (qs, bqs)), _ = tiles(i)
                P.op("act", "activation", out=eb[:], in_=bb[:], func=AF.Exp, r=[bbb], w=[beb])
                P.op("act", "activation", out=enb[:], in_=bb[:], func=AF.Exp, scale=-1.0, r=[bbb], w=[benb])
                for c in range(8):
                    cs = slice(c * 64, (c + 1) * 64)
                    last = slice(c * 64 + 63, c * 64 + 64)
                    P.op("act", "activation", out=ebr[:, cs], in_=bb[:, cs], func=AF.Exp, scale=-1.0, bias=bb[:, last], r=[bbb], w=[bebr])
                    P.op("pool", "tensor_copy", out=A_sb[:, i * 8 + c:i * 8 + c + 1], in_=eb[:, last], r=[beb], w=[bA])
                t, b = ske[s2]
                P.op("dve", "tensor_tensor", out=t[:], in0=kk[:], in1=enb[:], op=ALU.mult, r=[bkk, benb], w=[b])
                P.dma("sp", KE[:, tok], t[:], r=[b], w=[dH[i]])
                t, b = sqe[s2]
                P.op("pool", "tensor_tensor", out=t[:], in0=qs[:], in1=eb[:], op=ALU.mult, r=[bqs, beb], w=[b])
                P.dma("sp", QE[:, tok], t[:], r=[b], w=[dH[i]])
                tT, bT = skdT[s2]
                P.op("dve", "tensor_tensor", out=tT[:], in0=kk[:], in1=ebr[:], op=ALU.mult, r=[bkk, bebr], w=[bT])

            def stB3(i):
                tok = slice(i * 512, (i + 1) * 512)
                s2 = i % 2
                tT, bT = skdT[s2]
                for sb_ in range(4):
                    bs = slice(sb_ * 128, (sb_ + 1) * 128)
                    P.op("pe", "transpose", out=psX[0][:, bs], in_=tT[:, bs], identity=cid[:], r=[bT, bcid], w=[psX[1]])
                t, b = skd[s2]
                P.op("dve", "tensor_copy", out=t[:].rearrange("p a d -> p (a d)"), in_=psX[0][:], r=[psX[1]], w=[b])
                P.dma("sp", KD[tok, :].rearrange("(a p) d -> p a d", p=128), t[:], r=[b], w=[dH[i]])

            for t_ in range(NT + 3):
                if t_ < NT:
                    stA(t_)
                if 0 <= t_ - 1 < NT:
                    stB1(t_ - 1)
                if 0 <= t_ - 2 < NT:
                    stB2(t_ - 2)
                if 0 <= t_ - 3 < NT:
                    stB3(t_ - 3)
            P.finish()
        if os.environ.get("STOP_AFTER") == "1":
            return

        with contextlib.ExitStack() as st:
            P = Prog(nc, st, tag + "2")
            TA = TileAlloc(nc, st, tag + "2")
            NB = S // 128
            dYp = {(p_, r_): Buf() for p_ in range(3) for r_ in range(4)}
            qt, bqt = TA.sb([70, S], BF16, "qt")
            kt, bkt = TA.sb([70, S], BF16, "kt")
            vv, bvv = TA.sb([128, NB, 65], BF16, "vv")
            NS, NP_ = 3, 4
            psS = [TA.ps([128, 512], F32, "psS") for _ in range(NS)]
            psO = [TA.ps([128, 512], F32, "psO") for _ in range(2)]
            psX = TA.ps([128, 512], F32, "psX")
            psU = TA.ps([128, 512], F32, "psU")
            psOo = TA.ps([128, 512], F32, "psOo")
            pT = [TA.sb([128, 512], BF16, "pT") for _ in range(NP_)]
            osb = [TA.sb([64, 512], F32, "osb") for _ in range(2)]
            rden = [TA.sb([128, 512], F32, "rden") for _ in range(2)]
            yst = [TA.sb([64, 512], BF16, "yst") for _ in range(2)]
            SEG = min(2048, S)
            NSEG = S // SEG
            seg = []
            for k in range(2):
                seg.append(dict(
                    qe=TA.sb([128, SEG], BF16, "qe"), ke=TA.sb([128, SEG], BF16, "ke"),
                    kd=TA.sb([64, SEG // 64, 128], BF16, "kd"), vb=TA.sb([64, SEG // 64, 128], BF16, "vb"),
                    gs=TA.sb([128, SEG], BF16, "gs")))
            state = [TA.sb([128, 128], F32, "state") for _ in range(2)]
            sbf = [TA.sb([128, 8, 128], BF16, "sbf") for _ in range(2)]
            at = [TA.sb([128, 512], BF16, "at") for _ in range(2)]
            o_sb = [TA.sb([128, 512], F32, "o_sb") for _ in range(2)]
            sq = [TA.sb([128, 512], BF16, "sq") for _ in range(2)]
            rstd = [TA.sb([128, 512], F32, "rstd") for _ in range(2)]
            yb = [TA.sb([128, 512], BF16, "yb") for _ in range(2)]
            P.op("dve", "memset", ap=state[0][0][:], constant=0.0, w=[state[0][1]])
            P.op("dve", "memset", ap=sbf[0][0][:, 0, :], constant=0.0, w=[sbf[0][1]])

            def load_seg(k):
                sgm = seg[k % 2]
                c0 = k * SEG
                P.dma("sp", sgm["qe"][0][:], QE[:, c0:c0 + SEG], w=[sgm["qe"][1]])
                P.dma("pool", sgm["ke"][0][:], KE[:, c0:c0 + SEG], w=[sgm["ke"][1]])
                P.dma("sp", sgm["gs"][0][:], GS[:, c0:c0 + SEG], w=[sgm["gs"][1]])
                P.dma("pool", sgm["kd"][0][:], KD[c0:c0 + SEG, :].rearrange("(a p) d -> p a d", p=64), w=[sgm["kd"][1]])
                P.dma("sp", sgm["vb"][0][:], VB[c0:c0 + SEG, :].rearrange("(a p) d -> p a d", p=64), w=[sgm["vb"][1]])

            cur = [0]

            def hgrn_stages(i):
                k, ti = divmod(i * 512, SEG)
                sgm = seg[k % 2]
                qe, bqe = sgm["qe"]
                ke, bke = sgm["ke"]
                kd, bkd = sgm["kd"]
                vb, bvb = sgm["vb"]
                gs, bgs = sgm["gs"]
                s2 = i % 2
                sb_t, sb_b = sbf[s2]
                nsb_t, nsb_b = sbf[1 - s2]
                pu, pub = psU
                pa, pab = psU
                po, pob = psOo
                at_t, at_b = at[s2]
                o_t, o_b = o_sb[s2]
                sq_t, sq_b = sq[s2]
                rs_t, rs_b = rstd[s2]
                yb_t, yb_b = yb[s2]

                def st_load():
                    if ti == 0:
                        load_seg(k)

                def st_u(c0):
                    def f():
                        for c in range(c0, c0 + 4):
                            ch = (ti + c * 64) // 64
                            P.op("pe", "matmul", out=pu[:, (c % 4) * 128:(c % 4 + 1) * 128], lhsT=kd[:, ch, :], rhs=vb[:, ch, :],
                                 start=True, stop=True, r=[bkd, bvb], w=[pub])
                    return f

                def st_state(c0):
                    def f():
                        for c in range(c0, c0 + 4):
                            so_t, so_b = state[cur[0]]
                            sn_t, sn_b = state[1 - cur[0]]
                            P.op("dve", "scalar_tensor_tensor", out=sn_t[:], in0=so_t[:], scalar=A_sb[:, i * 8 + c:i * 8 + c + 1],
                                 in1=pu[:, (c % 4) * 128:(c % 4 + 1) * 128], op0=ALU.mult, op1=ALU.add, r=[so_b, bA, pub], w=[sn_b])
                            if c < 7:
                                P.op("pool", "tensor_copy", out=sb_t[:, c + 1, :], in_=sn_t[:], r=[sn_b], w=[sb_b])
                            else:
                                P.op("pool", "tensor_copy", out=nsb_t[:, 0, :], in_=sn_t[:], r=[sn_b], w=[nsb_b])
                            cur[0] = 1 - cur[0]
                    return f

                def st_attn():
                    for c in range(8):
                        tk = slice(ti + c * 64, ti + (c + 1) * 64)
                        P.op("pe", "matmul", out=pa[0:64, c * 64:(c + 1) * 64], lhsT=ke[:, tk], rhs=qe[:, tk], start=True, stop=True, r=[bke, bqe], w=[pab])

                def st_mask():
                    for c in range(8):
                        cs = slice(c * 64, (c + 1) * 64)
                        P.op("dve", "tensor_tensor", out=at_t[0:64, cs], in0=pa[0:64, cs], in1=cbd[0:64, 0:64], op=ALU.mult, r=[pab, bcbd], w=[at_b])

                def st_o():
                    for c in range(8):
                        ch = (ti + c * 64) // 64
                        cs = slice(c * 64, (c + 1) * 64)
                        P.op("pe", "matmul", out=po[:, cs], lhsT=vb[:, ch, :], rhs=at_t[0:64, cs], start=True, stop=False, r=[bvb, at_b], w=[pob])
                        P.op("pe", "matmul", out=po[:, cs], lhsT=sb_t[:, c, :], rhs=qe[:, ti + c * 64:ti + (c + 1) * 64],
                             start=False, stop=True, r=[sb_b, bqe], w=[pob])

                def st_sq():
                    P.op("dve", "tensor_copy", out=o_t[:], in_=po[:], r=[pob], w=[o_b])
                    P.op("pool", "tensor_tensor", out=sq_t[:], in0=o_t[:], in1=o_t[:], op=ALU.mult, r=[o_b], w=[sq_b])

                def st_m():
                    P.op("pe", "matmul", out=pa[:], lhsT=cones[:], rhs=sq_t[:], start=True, stop=True, r=[bcones, sq_b], w=[pab])

                def st_rs():
                    P.op("dve", "tensor_scalar", out=rs_t[:], in0=pa[:], scalar1=1.0 / 128.0, scalar2=RMS_EPS, op0=ALU.mult, op1=ALU.add, r=[pab], w=[rs_b])

                def st_sqrt():
                    P.op("act", "activation", out=rs_t[:], in_=rs_t[:], func=AF.Sqrt, r=[rs_b], w=[rs_b])

                def st_fin():
                    P.op("dve", "reciprocal", out=rs_t[:], in_=rs_t[:], r=[rs_b], w=[rs_b])
                    P.op("pool", "tensor_tensor", out=o_t[:], in0=o_t[:], in1=rs_t[:], op=ALU.mult, r=[o_b, rs_b], w=[o_b])
                    P.op("dve", "scalar_tensor_tensor", out=yb_t[:], in0=o_t[:], scalar=vec[:, 2:3], in1=gs[:, ti:ti + 512], op0=ALU.mult, op1=ALU.mult,
                         r=[o_b, bvec, bgs], w=[yb_b])
                    TPC = (S // 4) // 512
                    P.dma("sp", Yw(slice(128, 256), i), yb_t[:], r=[yb_b], w=[dYp[(2, i // TPC)]])
                    if ycb is not None and i % TPC == TPC - 1:
                        ycb(P, 2, i // TPC, dYp[(2, i // TPC)])
                return [st_load, st_u(0), st_state(0), st_u(4), st_state(4), st_attn, st_mask, st_o, st_sq, st_m, st_rs, st_sqrt, st_fin]

            allblocks = []
            for h in range(2):
                for I in range(NT):
                    nk = 4 * I + 4
                    for jb in range(nk):
                        allblocks.append((h, I, jb, nk))
            nblk = len(allblocks)
            per_tile = max(1, int(nblk * 0.85) // NT)
            GAP = max(1, min(6, (per_tile - 2) // 13))
            sched = {}
            for i in range(NT):
                for si in range(13):
                    sched.setdefault(i * per_tile + 2 + si * GAP, []).append((i, si))
            stage_cache = {}
            pending = []
            ndone = [0]
            CH = min(2048, S)

            NPC = S // CH
            bqtp = [Buf() for _ in range(NPC)]
            bktp = [Buf() for _ in range(NPC)]
            bvvp = [Buf() for _ in range(NPC)]

            def load_head(h):
                for c0 in range(0, S, CH):
                    pc = c0 // CH
                    P.dma("pool", kt[:, c0:c0 + CH], KT[h, :, c0:c0 + CH], w=[bktp[pc]])
                    P.dma("sp", qt[:, c0:c0 + CH], QT[h, :, c0:c0 + CH], w=[bqtp[pc]])
                    P.dma("sp", vv[:, c0 // 128:(c0 + CH) // 128, 0:64],
                          VA[c0:c0 + CH, h * 64:(h + 1) * 64].rearrange("(a p) d -> p a d", p=128), w=[bvvp[pc]])

            P.op("pool", "memset", ap=vv[:, :, 64:65], constant=1.0, w=bvvp)
            loaded = set()

            def s_mm(n):
                h, I, jb, nk = allblocks[n]
                d = jb - 4 * I
                qlo = 128 * d if d > 0 else 0
                pt, pb = psS[n % NS]
                P.op("pe", "matmul", out=pt[:, qlo:512], lhsT=kt[0:70, jb * 128:(jb + 1) * 128], rhs=qt[0:70, I * 512 + qlo:I * 512 + 512],
                     start=True, stop=(d < 0), r=[bqtp[(I * 512) // CH], bktp[(jb * 128) // CH]], w=[pb])
                if d >= 0:
                    P.op("pe", "matmul", out=pt[:, qlo:qlo + 128], lhsT=cid[:], rhs=cneg[:], start=False, stop=True, r=[bcid, bcneg], w=[pb])

            qcnt = [0]

            def fin(h, I):
                slot = qcnt[0] % 2
                qcnt[0] += 1
                for pe_ in [p_ for p_ in pending if p_[3] == slot]:
                    pending.remove(pe_)
                    fin2(pe_[0], pe_[1], pe_[3])
                ot, ob = psO[I % 2]
                o_, bo_ = osb[slot]
                rd, brd = rden[slot]
                P.op("dve", "reciprocal", out=rd[64:65, :], in_=ot[64:65, :], r=[ob], w=[brd])
                P.op("dve", "tensor_copy", out=o_[:], in_=ot[0:64, :], r=[ob], w=[bo_])
                pending.append((h, I, ndone[0] + 8, slot))

            def fin2(h, I, slot):
                o_, bo_ = osb[slot]
                rd, brd = rden[slot]
                y_, by_ = yst[slot]
                P.op("pe", "matmul", out=psX[0][0:64, :], lhsT=conesf[64:65, 0:64], rhs=rd[64:65, :], start=True, stop=True, r=[brd, bconesf], w=[psX[1]])
                P.op("dve", "tensor_tensor", out=y_[:], in0=o_[:], in1=psX[0][0:64, :], op=ALU.mult, r=[bo_, psX[1]], w=[by_])
                TPC = (S // 4) // 512
                P.dma("sp", Yw(slice(h * 64, (h + 1) * 64), I), y_[:], r=[by_], w=[dYp[(h, I // TPC)]])
                if ycb is not None and I % TPC == TPC - 1:
                    ycb(P, h, I // TPC, dYp[(h, I // TPC)])

            def exp_pv(n):
                h, I, jb, nk = allblocks[n]
                d = jb - 4 * I
                qlo = 128 * d if d > 0 else 0
                pt, pb = psS[n % NS]
                t, b = pT[n % NP_]
                P.op("act", "activation", out=t[:, qlo:512], in_=pt[:, qlo:512], func=AF.Exp, r=[pb], w=[b])
                ot, ob = psO[I % 2]
                P.op("pe", "matmul", out=ot[0:65, qlo:512], lhsT=vv[:, jb, 0:65], rhs=t[:, qlo:512], start=(jb == 0), stop=(jb == nk - 1),
                     r=[bvvp[(jb * 128) // CH], b], w=[ob])
                if jb == nk - 1:
                    fin(h, I)

            def run_stage(i, si):
                if i not in stage_cache:
                    stage_cache[i] = hgrn_stages(i)
                stage_cache[i][si]()

            LOOK = 2
            for n in range(nblk):
                hcur = allblocks[n][0]
                if n == 0 or allblocks[n - 1][0] != hcur:
                    load_head(hcur)
                    for m in range(n, min(n + LOOK, nblk)):
                        s_mm(m)
                if n + LOOK < nblk and allblocks[n + LOOK][0] == hcur:
                    s_mm(n + LOOK)
                exp_pv(n)
                ndone[0] += 1
                while pending and pending[0][2] <= ndone[0]:
                    h_, I_, _, sl_ = pending.pop(0)
                    fin2(h_, I_, sl_)
                for (i, si) in sched.pop(n, []):
                    run_stage(i, si)
            while pending:
                h_, I_, _, sl_ = pending.pop(0)
                fin2(h_, I_, sl_)
            for n in sorted(sched.keys()):
                for (i, si) in sched[n]:
                    run_stage(i, si)
            P.finish()


def alloc_scratch_A(nc, S, tag):
    def dt(name, shape, dty=BF16):
        return nc.dram_tensor(f"{tag}_{name}", shape, dty, kind="Internal").ap()
    return dict(QT=dt("QT", [2, 70, S]), KT=dt("KT", [2, 70, S]), VA=dt("VA", [S, 128]),
                QE=dt("QE", [128, S]), KE=dt("KE", [128, S]), KD=dt("KD", [S, 128]),
                VB=dt("VB", [S, 128]), GS=dt("GS", [128, S]))


def build_A(S, x_f32=True):
    nc = bass.Bass("TRN2", target_bir_lowering=False)
    R4 = S // 4
    xg = nc.dram_tensor("xg", [4, 1024, R4], F32 if x_f32 else BF16, kind="ExternalInput").ap()
    wA = nc.dram_tensor("wA", [1024, WA_COLS], F32, kind="ExternalInput").ap()
    vecA = nc.dram_tensor("vecA", [128, 8], F32, kind="ExternalInput").ap()
    cst = nc.dram_tensor("cst", [128, CST_COLS], F32, kind="ExternalInput").ap()
    Y = nc.dram_tensor("Y", [256, S], BF16, kind="ExternalOutput").ap()
    scr = alloc_scratch_A(nc, S, "a")
    with contextlib.ExitStack() as gst:
        GSEM[0] = gst
        GD.clear()
        phase_A(nc, S, xg, wA, vecA, cst, lambda rows, i: Y[rows, i * 512:(i + 1) * 512], scr, "A")
    return nc


def ln_alloc(TA, nb=2):
    return dict(hb=[TA.sb([128, 512], BF16, "hb") for _ in range(nb)], hq=[TA.sb([128, 512], BF16, "hq") for _ in range(nb)],
                mean=TA.sb([128, 512], F32, "mean"), rs=TA.sb([128, 512], F32, "rs"))


def layer_norm_steps(P, LT, nextP, h, bh, gcol, bcol, bvecB, conesK, bconesK, out_writer):
    hb, hq = LT["hb"], LT["hq"]
    mean, bmean = LT["mean"]
    rs, brs = LT["rs"]

    def stats():
        pm, pmb = nextP()
        pq, pqb = nextP()
        for c in range(8):
            t, b = hb[c % len(hb)]
            q, bq = hq[c % len(hq)]
            P.op("act", "copy", out=t[:], in_=h[:, c, :], r=[bh[c]], w=[b])
            P.op("dve", "tensor_tensor", out=q[:], in0=h[:, c, :], in1=h[:, c, :], op=ALU.mult, r=[bh[c]], w=[bq])
            P.op("pe", "matmul", out=pm[:], lhsT=conesK[:], rhs=t[:], start=(c == 0), stop=(c == 7), r=[bconesK, b], w=[pmb])
            P.op("pe", "matmul", out=pq[:], lhsT=conesK[:], rhs=q[:], start=(c == 0), stop=(c == 7), r=[bconesK, bq], w=[pqb])
        P.op("act", "copy", out=mean[:], in_=pm[:], r=[pmb], w=[bmean])
        P.op("dve", "tensor_tensor", out=rs[:], in0=mean[:], in1=mean[:], op=ALU.mult, r=[bmean], w=[brs])
        P.op("dve", "tensor_tensor", out=rs[:], in0=pq[:], in1=rs[:], op=ALU.subtract, r=[pqb, brs], w=[brs])
        P.op("dve", "tensor_scalar", out=rs[:], in0=rs[:], scalar1=LN_EPS, scalar2=None, op0=ALU.add, r=[brs], w=[brs])
        P.op("act", "activation", out=rs[:], in_=rs[:], func=AF.Sqrt, r=[brs], w=[brs])
        P.op("dve", "reciprocal", out=rs[:], in_=rs[:], r=[brs], w=[brs])

    def apply(c):
        def f():
            eng = "dve" if (LT.get("flush") and c % 2 == 0) else "pool"
            P.op(eng, "tensor_tensor", out=h[:, c, :], in0=h[:, c, :], in1=mean[:], op=ALU.subtract, r=[bh[c], bmean], w=[bh[c]])
            P.op(eng, "tensor_tensor", out=h[:, c, :], in0=h[:, c, :], in1=rs[:], op=ALU.mult, r=[bh[c], brs], w=[bh[c]])
            P.op(eng, "tensor_scalar", out=h[:, c, :], in0=h[:, c, :], scalar1=gcol(c), scalar2=bcol(c), op0=ALU.mult, op1=ALU.add, r=[bh[c], bvecB], w=[bh[c]])
            if c == 7:
                out_writer()
        return f
    return [stats] + [apply(c) for c in range(8)]


def phase_B(nc, R4, Yg, xres, wg, wab, wo, wfi, wfo, vecB, cst, X1, XO, XOB, tag, xcb=None):
    NT = R4 // 512
    with contextlib.ExitStack() as st0:
        TA0 = TileAlloc(nc, st0, tag + "g")
        vB, bvB = TA0.sb([128, 32], F32, "vecB")
        conesK, bconesK = TA0.sb([128, 128], BF16, "onesK")
        onesf, bonesf = TA0.sb([128, 128], F32, "onesf")
        dX1 = [Buf() for _ in range(NT)]

        with contextlib.ExitStack() as st:
            P = Prog(nc, st, tag + "1")
            TA = TileAlloc(nc, st, tag + "1")
            wg_sb, bwg = TA.sb([128, 8, 2048], BF16, "wg")
            wab_sb, bwab = TA.sb([128, 8, 1024], BF16, "wab")
            wo_sb, bwo = TA.sb([128, 8, 1024], BF16, "wo")
            xr = [TA.sb([128, 8, 512], F32, "xr") for _ in range(2)]
            xb = [TA.sb([128, 8, 512], BF16, "xb") for _ in range(2)]
            yt = [TA.sb([128, 8, 512], BF16, "yt") for _ in range(2)]
            mg, bmg = TA.sb([128, 8, 512], BF16, "mg")
            sga = [TA.sb([128, 512], F32, "sga") for _ in range(2)]
            sgb = [TA.sb([128, 512], F32, "sgb") for _ in range(2)]
            t1 = [TA.sb([128, 512], F32, "t1") for _ in range(2)]
            t2 = [TA.sb([128, 512], F32, "t2") for _ in range(2)]
            hh = [(TA.sb([128, 8, 512], F32, "hh")[0], [Buf() for _ in range(8)]) for _ in range(2)]
            psl = [TA.ps([128, 512], F32, "ps") for _ in range(8)]
            LT = ln_alloc(TA, 4)
            npz = [0]

            def nextP():
                npz[0] += 1
                return psl[npz[0] % 8]
            P.dma("sp", vB[:], vecB, w=[bvB])
            bYg = [Buf(), Buf(), Buf()]
            if isinstance(Yg, dict):
                gd = Yg
                Yg = gd["Ygl"]
                P.dma("sp", gd["jt"][:], gd["jidx"], w=[gd["bjt"]])

                def gsrc(Gt):
                    def f(e):
                        e.reg_load(gd["jr"], gd["jt"][0:1, 0:1])
                        val = e.snap(gd["jr"], min_val=0, max_val=3)
                        return Gt[bass.ds(val, 1)].rearrange("o s p t -> (o s) p t")
                    return f
                Yv = Yg.rearrange("(s p) t -> s p t", s=4)
                P.dma("sp", Yv[:, 128:256, :], gsrc(gd["GH"]), r=[gd["bjt"]], w=[bYg[0]])
                P.dma("sp", Yv[:, 0:64, :], gsrc(gd["GA"]), r=[gd["bjt"]], w=[bYg[1]])
                P.dma("sp", Yv[:, 64:128, :], gsrc(gd["GB"]), r=[gd["bjt"]], w=[bYg[2]])
            P.dma("sp", onesf[:], cst[:, K_ONES:K_ONES + 128], w=[bonesf])
            P.op("dve", "tensor_scalar", out=conesK[:], in0=onesf[:], scalar1=1.0 / 1024.0, scalar2=None, op0=ALU.mult, r=[bonesf], w=[bconesK])
            wgv = wg.rearrange("(kc p) n -> p kc n", p=128)
            bwg_m = [Buf() for _ in range(8)]
            PRE_LOAD0 = True

            def load(i):
                tok = slice(i * 512, (i + 1) * 512)
                xv = xres.rearrange("(kc p) t -> p kc t", p=128)[:, :, tok]
                yv = Yg.rearrange("(kc p) t -> p kc t", p=128)[:, :, tok]
                P.dma("sp", xr[i % 2][0][:], xv, w=[xr[i % 2][1]])
                P.dma("pool", xb[i % 2][0][:], xv, w=[xb[i % 2][1]])
                P.dma("sp", yt[i % 2][0][:], yv, r=bYg, w=[yt[i % 2][1]])

            load(0)
            deferred = []
            for m0 in range(0, 8, 2):
                for n0 in (0, 1024):
                    P.dma("pool", wg_sb[:, :, n0 + m0 * 128:n0 + (m0 + 2) * 128], wgv[:, :, n0 + m0 * 128:n0 + (m0 + 2) * 128], w=[bwg_m[m0], bwg_m[m0 + 1]])
                if m0 == 0:
                    for kc0 in range(0, 8, 4):
                        P.dma("pool", wab_sb[:, kc0:kc0 + 4, :], wab.rearrange("(kc p) n -> p kc n", p=128)[:, kc0:kc0 + 4, :], w=[bwab])
            for kc0 in range(0, 8, 4):
                P.dma("pool", wo_sb[:, kc0:kc0 + 4, :], wo.rearrange("(kc p) n -> p kc n", p=128)[:, kc0:kc0 + 4, :], w=[bwo])
            for i in range(NT):
                if i + 1 < NT:
                    load(i + 1)
                tok = slice(i * 512, (i + 1) * 512)
                xr_t, xr_b = xr[i % 2]
                xb_t, xb_b = xb[i % 2]
                yt_t, yt_b = yt[i % 2]
                h_t, h_b = hh[i % 2]
                for m in range(8):
                    if m >= 1 and deferred:
                        deferred.pop(0)()
                    ms = slice(m * 128, (m + 1) * 128)
                    pga, pgab = nextP()
                    for kc in range(8):
                        P.op("pe", "matmul", out=pga[:], lhsT=wg_sb[:, kc, ms], rhs=xb_t[:, kc, :], start=(kc == 0), stop=(kc == 7), r=[bwg_m[m], xb_b], w=[pgab])
                    pgb, pgbb = nextP()
                    for kc in range(8):
                        P.op("pe", "matmul", out=pgb[:], lhsT=wg_sb[:, kc, 1024 + m * 128:1024 + (m + 1) * 128], rhs=xb_t[:, kc, :], start=(kc == 0), stop=(kc == 7), r=[bwg_m[m], xb_b], w=[pgbb])
                    ppa, ppab = nextP()
                    for r_ in range(4):
                        P.op("pe", "matmul", out=ppa[:], lhsT=wab_sb[:, 2 * r_, ms], rhs=yt_t[:, 2 * r_, :], start=(r_ == 0), stop=(r_ == 3), r=[bwab, yt_b], w=[ppab])
                    ppb, ppbb = nextP()
                    for r_ in range(4):
                        P.op("pe", "matmul", out=ppb[:], lhsT=wab_sb[:, 2 * r_ + 1, ms], rhs=yt_t[:, 2 * r_ + 1, :], start=(r_ == 0), stop=(r_ == 3), r=[bwab, yt_b], w=[ppbb])
                    sa, bsa = sga[m % 2]
                    sb_, bsb = sgb[m % 2]
                    a1, ba1 = t1[m % 2]
                    a2, ba2 = t2[m % 2]
                    P.op("act", "activation", out=sa[:], in_=pga[:], func=AF.Sigmoid, r=[pgab], w=[bsa])
                    P.op("act", "activation", out=sb_[:], in_=pgb[:], func=AF.Sigmoid, r=[pgbb], w=[bsb])
                    P.op("dve", "tensor_tensor", out=a1[:], in0=sa[:], in1=ppa[:], op=ALU.mult, r=[bsa, ppab], w=[ba1])
                    P.op("dve", "tensor_tensor", out=a2[:], in0=sb_[:], in1=ppb[:], op=ALU.mult, r=[bsb, ppbb], w=[ba2])
                    P.op("pool", "tensor_tensor", out=mg[:, m, :], in0=a1[:], in1=a2[:], op=ALU.add, r=[ba1, ba2], w=[bmg])
                for mo in range(8):
                    if deferred:
                        deferred.pop(0)()
                    pw, pwb = nextP()
                    for m in range(8):
                        P.op("pe", "matmul", out=pw[:], lhsT=wo_sb[:, m, mo * 128:(mo + 1) * 128], rhs=mg[:, m, :], start=(m == 0), stop=(m == 7), r=[bwo, bmg], w=[pwb])
                    P.op("dve", "scalar_tensor_tensor", out=h_t[:, mo, :], in0=xr_t[:, mo, :], scalar=float(ALPHA), in1=pw[:], op0=ALU.mult, op1=ALU.add,
                         r=[xr_b, pwb], w=[h_b[mo]])

                def wr(i=i, h_t=h_t, h_b=h_b, tok=tok):
                    P.dma("sp", X1.rearrange("(kc p) t -> p kc t", p=128)[:, :, tok], h_t[:], r=h_b, w=[dX1[i]])
                deferred.extend(layer_norm_steps(P, LT, nextP, h_t, h_b, lambda c: vB[:, c:c + 1], lambda c: vB[:, 8 + c:9 + c], bvB, conesK, bconesK, wr))
            LT["flush"] = True
            while deferred:
                deferred.pop(0)()
            P.finish()

        with contextlib.ExitStack() as st:
            P = Prog(nc, st, tag + "2")
            TA = TileAlloc(nc, st, tag + "2")
            NM = FFH // 128
            wfi_sb, bwfi = TA.sb([128, 8, 2 * FFH], BF16, "wfi")
            wfo_sb, bwfo = TA.sb([128, NM, 1024], BF16, "wfo")
            xb = [TA.sb([128, 8, 512], BF16, "xb") for _ in range(2)]
            xrc = [TA.sb([128, 512], F32, "xrc") for _ in range(2)]
            aa, baa = TA.sb([128, NM, 512], BF16, "aa")
            sg = [TA.sb([128, 512], F32, "sg") for _ in range(2)]
            hh = TA.sb([128, 8, 512], F32, "hh")[0]
            bhh = [Buf() for _ in range(8)]
            psl = [TA.ps([128, 512], F32, "ps") for _ in range(8)]
            LT = ln_alloc(TA)
            npz = [0]

            def nextP():
                npz[0] += 1
                return psl[npz[0] % 8]
            wfv = wfi.rearrange("(kc p) n -> p kc n", p=128)
            X1v = X1.rearrange("(kc p) t -> p kc t", p=128)

            def load(i):
                tok = slice(i * 512, (i + 1) * 512)
                P.dma("pool", xb[i % 2][0][:], X1v[:, :, tok], r=[dX1[i]], w=[xb[i % 2][1]])

            deferred = []
            bwfi_m = [Buf() for _ in range(NM)]
            for m0 in range(0, NM, 2):
                for n0 in (0, FFH):
                    P.dma("pool", wfi_sb[:, :, n0 + m0 * 128:n0 + (m0 + 2) * 128], wfv[:, :, n0 + m0 * 128:n0 + (m0 + 2) * 128], w=[bwfi_m[m0], bwfi_m[m0 + 1]])
                if m0 == 0:
                    load(0)
            wov = wfo.rearrange("(kc p) n -> p kc n", p=128)
            for k0 in range(0, NM, 2):
                P.dma("pool", wfo_sb[:, k0:k0 + 2, :], wov[:, k0:k0 + 2, :], w=[bwfo])
            for i in range(NT):
                if i + 1 < NT:
                    load(i + 1)
                tok = slice(i * 512, (i + 1) * 512)
                xb_t, xb_b = xb[i % 2]
                for m in range(NM):
                    if m >= 2 and deferred:
                        deferred.pop(0)()
                    pu, pub = nextP()
                    for kc in range(8):
                        P.op("pe", "matmul", out=pu[:], lhsT=wfi_sb[:, kc, m * 128:(m + 1) * 128], rhs=xb_t[:, kc, :], start=(kc == 0), stop=(kc == 7), r=[bwfi_m[m], xb_b], w=[pub])
                    pg, pgb = nextP()
                    for kc in range(8):
                        P.op("pe", "matmul", out=pg[:], lhsT=wfi_sb[:, kc, FFH + m * 128:FFH + (m + 1) * 128], rhs=xb_t[:, kc, :], start=(kc == 0), stop=(kc == 7), r=[bwfi_m[m], xb_b], w=[pgb])
                    s_, bs_ = sg[m % 2]
                    P.op("act", "activation", out=s_[:], in_=pg[:], func=AF.Silu, r=[pgb], w=[bs_])
                    P.op("dve", "tensor_tensor", out=aa[:, m, :], in0=s_[:], in1=pu[:], op=ALU.mult, r=[bs_, pub], w=[baa])
                for mo in range(8):
                    xc, bxc = xrc[mo % 2]
                    P.dma("sp", xc[:], X1v[:, mo, tok], r=[dX1[i]], w=[bxc])
                    po, pob = nextP()
                    for m in range(NM):
                        P.op("pe", "matmul", out=po[:], lhsT=wfo_sb[:, m, mo * 128:(mo + 1) * 128], rhs=aa[:, m, :], start=(m == 0), stop=(m == NM - 1), r=[bwfo, baa], w=[pob])
                    P.op("dve", "scalar_tensor_tensor", out=hh[:, mo, :], in0=xc[:], scalar=float(ALPHA), in1=po[:], op0=ALU.mult, op1=ALU.add, r=[bxc, pob], w=[bhh[mo]])

                def wr(tok=tok, i=i):
                    P.dma("sp", XO.rearrange("(kc p) t -> p kc t", p=128)[:, :, tok], hh[:], r=bhh, w=[Buf()])
                    if XOB is not None:
                        bx_ = Buf()
                        if xcb is not None:
                            P.dma("pool", XOB[i].rearrange("(kc p) t -> p kc t", p=128), hh[:], r=bhh, w=[bx_])
                            xcb(P, i, bx_)
                        else:
                            P.dma("pool", XOB.rearrange("(kc p) t -> p kc t", p=128)[:, :, tok], hh[:], r=bhh, w=[bx_])
                deferred.extend(layer_norm_steps(P, LT, nextP, hh, bhh, lambda c: vB[:, 16 + c:17 + c], lambda c: vB[:, 24 + c:25 + c], bvB, conesK, bconesK, wr))
            LT["flush"] = True
            while deferred:
                deferred.pop(0)()
            P.finish()


def build_B(R4, with_bf16_out=False):
    nc = bass.Bass("TRN2", target_bir_lowering=False)
    Yg = nc.dram_tensor("Yg", [1024, R4], BF16, kind="ExternalInput").ap()
    xres = nc.dram_tensor("xres", [1024, R4], F32, kind="ExternalInput").ap()
    wg = nc.dram_tensor("wg", [1024, 2048], F32, kind="ExternalInput").ap()
    wab = nc.dram_tensor("wab", [1024, 1024], F32, kind="ExternalInput").ap()
    wo = nc.dram_tensor("wo", [1024, 1024], F32, kind="ExternalInput").ap()
    wfi = nc.dram_tensor("wfi", [1024, 2 * FFH], F32, kind="ExternalInput").ap()
    wfo = nc.dram_tensor("wfo", [FFH, 1024], F32, kind="ExternalInput").ap()
    vecB = nc.dram_tensor("vecB", [128, 32], F32, kind="ExternalInput").ap()
    cst = nc.dram_tensor("cst", [128, CST_COLS], F32, kind="ExternalInput").ap()
    XO = nc.dram_tensor("XO", [1024, R4], F32, kind="ExternalOutput").ap()
    XOB = nc.dram_tensor("XOB", [1024, R4], BF16, kind="ExternalOutput").ap() if with_bf16_out else None
    X1 = nc.dram_tensor("b_X1", [1024, R4], F32, kind="Internal").ap()
    with contextlib.ExitStack() as gst:
        GSEM[0] = gst
        GD.clear()
        phase_B(nc, R4, Yg, xres, wg, wab, wo, wfi, wfo, vecB, cst, X1, XO, XOB, "B")
    return nc


def make_cst():
    c = np.zeros((128, CST_COLS), np.float32)
    p = np.arange(128)[:, None]
    f = np.arange(128)[None, :]
    c[:, K_ID:K_ID + 128] = (p == f)
    c[:, K_NEG:K_NEG + 128] = np.where(f < p, -30000.0, 0.0)
    c[:, K_BD:K_BD + 128] = (f >= p) & ((p // 64) == (f // 64))
    r = np.ones(512, np.float32)
    r[::64] = 0.0
    c[:, K_RESET:K_RESET + 512] = r[None, :]
    c[:, K_ONES:K_ONES + 512] = 1.0
    return c


def make_wA(w_in_l, j):
    B0 = 3 * 512 + 8
    sl = lambda base: w_in_l[:, base + 128 * j: base + 128 * (j + 1)]
    qa, ka, va = sl(0), sl(512), sl(1024)
    fa = w_in_l[:, 1536 + 2 * j: 1536 + 2 * (j + 1)]
    qb, fb, ib, gb = sl(B0), sl(B0 + 512), sl(B0 + 1024), sl(B0 + 1536)
    return np.ascontiguousarray(np.concatenate([qa, ka, qb, fb, gb, fa, va, ib], axis=1))


def make_vecA(l, j, b_fgate, hgrn_lb_logits, hgrn_norm_g):
    v = np.zeros((128, 8), np.float32)
    v[:, 0] = hgrn_lb_logits[0, 128 * j:128 * (j + 1)]
    v[:, 1] = hgrn_lb_logits[1, 128 * j:128 * (j + 1)]
    v[:, 2] = hgrn_norm_g[l]
    v[0:2, 3] = b_fgate[l, 2 * j:2 * j + 2]
    v[:, 4] = float(l)
    return v


def make_wab(wa_l, wb_l):
    out = np.empty((1024, 1024), np.float32)
    for r in range(4):
        out[256 * r:256 * r + 128] = wa_l[128 * r:128 * (r + 1)]
        out[256 * r + 128:256 * r + 256] = wb_l[128 * r:128 * (r + 1)]
    return out


def make_vecB(l, ln1_g, ln1_b, ln2_g, ln2_b):
    v = np.empty((128, 32), np.float32)
    for k, a in enumerate((ln1_g[l], ln1_b[l], ln2_g[l], ln2_b[l])):
        v[:, 8 * k:8 * k + 8] = a.reshape(8, 128).T
    return v


I32 = mybir.dt.int32
GROUPS = [[0, 1, 2, 3], [4, 5, 6, 7]]


def make_ycb(Ysc, GA, GB, GH):
    def ycb(P, part, r_, buf):
        if part < 2:
            src = Ysc[r_, part * 64:(part + 1) * 64, :]
            dst = (GA, GB)[part][r_].rearrange("s p t -> (s p) t")
        else:
            src = Ysc[r_, 128:256, :]
            dst = GH[r_].rearrange("s p t -> (s p) t")
        P.coll("AllGather", [src], [dst], GROUPS, r=[buf], w=[Buf()])
    return ycb


def exchange_Y(nc, R4, GA, GB, GH, Ygl, jidx, jt, bjt, jr, tag):
    with contextlib.ExitStack() as st:
        P = Prog(nc, st, tag)
        P.dma("sp", jt[:], jidx, w=[bjt])

        def src(Gt):
            def f(e):
                e.reg_load(jr, jt[0:1, 0:1])
                val = e.snap(jr, min_val=0, max_val=3)
                return Gt[bass.ds(val, 1)].rearrange("o s p t -> (o s) p t")
            return f
        Yv = Ygl.rearrange("(s p) t -> s p t", s=4)
        P.dma("sp", Yv[:, 0:64, :], src(GA), r=[bjt], w=[Buf()])
        P.dma("sp", Yv[:, 64:128, :], src(GB), r=[bjt], w=[Buf()])
        P.dma("sp", Yv[:, 128:256, :], src(GH), r=[bjt], w=[Buf()])
        P.finish()


def build_fused(S):
    nc = bass.Bass("TRN2", target_bir_lowering=False)
    R4 = S // 4
    ext = lambda name, shape, dt=F32: nc.dram_tensor(name, shape, dt, kind="ExternalInput").ap()
    itn = lambda name, shape, dt=BF16: nc.dram_tensor(name, shape, dt, kind="Internal").ap()
    xg = ext("xg", [4, 1024, R4])
    xres = ext("xres", [1024, R4])
    cst = ext("cst", [128, CST_COLS])
    jidx = ext("jidx", [1, 4], I32)
    W = []
    for l in range(DEPTH):
        W.append(dict(wA=ext(f"wA{l}", [1024, WA_COLS]), vecA=ext(f"vecA{l}", [128, 8]), wg=ext(f"wg{l}", [1024, 2048]),
                      wab=ext(f"wab{l}", [1024, 1024]), wo=ext(f"wo{l}", [1024, 1024]), wfi=ext(f"wfi{l}", [1024, 2 * FFH]),
                      wfo=ext(f"wfo{l}", [FFH, 1024]), vecB=ext(f"vecB{l}", [128, 32])))
    OUT = nc.dram_tensor("OUT", [1024, R4], F32, kind="ExternalOutput").ap()
    scr = alloc_scratch_A(nc, S, "a")
    Ysc = itn("Ysc", [4, 256, R4])
    GA = itn("GA", [4, 4, 64, R4])
    GB = itn("GB", [4, 4, 64, R4])
    GH = itn("GH", [4, 4, 128, R4])
    NTB = R4 // 512
    XOBt = itn("XOBt", [NTB, 1024, 512])
    XG1t = itn("XG1t", [NTB, 4, 1024, 512])
    Ygl = itn("Ygl", [1024, R4])
    X1 = itn("X1", [1024, R4], F32)
    XO0 = itn("XO0", [1024, R4], F32)
    xsrc1 = lambda r_, c0: XG1t[c0 // 512, r_].rearrange("(kc p) t -> p kc t", p=128)
    Yw = lambda rows, i: Ysc[(i * 512) // R4, rows, (i * 512) % R4:(i * 512) % R4 + 512]
    with contextlib.ExitStack() as gst:
        GSEM[0] = gst
        GD.clear()
        jt = gst.enter_context(nc.sbuf_tensor("jt", [1, 4], I32))
        bjt = Buf()
        jr = gst.enter_context(nc.sync.register("jr"))
        ycb = make_ycb(Ysc, GA, GB, GH)

        def xcb(P, i, buf):
            P.coll("AllGather", [XOBt[i]], [XG1t[i].rearrange("s f t -> (s f) t")], GROUPS, r=[buf], w=[Buf()])
        for l in range(DEPTH):
            w = W[l]
            phase_A(nc, S, xg if l == 0 else xsrc1, w["wA"], w["vecA"], cst, Yw, scr, f"A{l}", ycb=ycb)
            last = (l == DEPTH - 1)
            phase_B(nc, R4, dict(GA=GA, GB=GB, GH=GH, Ygl=Ygl, jr=jr, jt=jt, bjt=bjt, jidx=jidx), xres if l == 0 else XO0, w["wg"], w["wab"], w["wo"], w["wfi"], w["wfo"], w["vecB"], cst, X1,
                    OUT if last else XO0, None if last else XOBt, f"B{l}", xcb=None if last else xcb)
    return nc


def kernel(x, w_in, b_fgate, hgrn_lb_logits, hgrn_norm_g, w_branch_a, w_branch_b, w_out,
           ln1_g, ln1_b, w_ff_in, w_ff_out, ln2_g, ln2_b):
    x = np.asarray(x, np.float32)
    Bn, S, _ = x.shape
    R4 = S // 4
    f = lambda a: np.asarray(a, np.float32)
    w_in, b_fgate, hgrn_lb_logits, hgrn_norm_g = f(w_in), f(b_fgate), f(hgrn_lb_logits), f(hgrn_norm_g)
    w_branch_a, w_branch_b, w_out = f(w_branch_a), f(w_branch_b), f(w_out)
    ln1_g, ln1_b, w_ff_in, w_ff_out, ln2_g, ln2_b = f(ln1_g), f(ln1_b), f(w_ff_in), f(w_ff_out), f(ln2_g), f(ln2_b)
    cst = make_cst()
    cores = list(range(8))
    xg = [np.ascontiguousarray(x[b].reshape(4, R4, D).transpose(0, 2, 1)) for b in range(Bn)]
    B0 = 3 * 512 + 8 + 4 * 512
    shared = {}
    for l in range(DEPTH):
        shared[f"wg{l}"] = np.ascontiguousarray(w_in[l][:, B0:B0 + 2048])
        shared[f"wab{l}"] = make_wab(w_branch_a[l], w_branch_b[l])
        shared[f"wo{l}"] = w_out[l]
        shared[f"wfi{l}"] = w_ff_in[l]
        shared[f"wfo{l}"] = w_ff_out[l]
        shared[f"vecB{l}"] = make_vecB(l, ln1_g, ln1_b, ln2_g, ln2_b)
    wAs = {(l, j): make_wA(w_in[l], j) for l in range(DEPTH) for j in range(4)}
    in_maps = []
    for c in cores:
        b, j = divmod(c, 4)
        m = dict(xg=xg[b], xres=np.ascontiguousarray(xg[b][j]), cst=cst, jidx=np.array([[j, 0, 0, 0]], np.int32))
        for l in range(DEPTH):
            m[f"wA{l}"] = wAs[(l, j)]
            m[f"vecA{l}"] = make_vecA(l, j, b_fgate, hgrn_lb_logits, hgrn_norm_g)
        m.update(shared)
        in_maps.append(m)
    nc = build_fused(S)
    res = run_bass_kernel_spmd(nc, in_maps, core_ids=cores)
    out = np.empty((Bn, S, D), np.float32)
    for c in cores:
        b, j = divmod(c, 4)
        out[b, j * R4:(j + 1) * R4, :] = np.asarray(res.results[c]["OUT"]).T
    return out
```

```python
import contextlib
import os
import numpy as np
import ml_dtypes
import concourse.bass as bass
import concourse.mybir as mybir
from concourse.bass_utils import run_bass_kernel_spmd

F32 = mybir.dt.float32
BF16 = mybir.dt.bfloat16
AF = mybir.ActivationFunctionType
ALU = mybir.AluOpType
NPBF = ml_dtypes.bfloat16

D = 1024
SEQ = 16384
DEPTH = 2
FFH = 2816
ALPHA = (2 * DEPTH) ** 0.25
LN_EPS = 1e-5
RMS_EPS = 1e-6
NDMA = 6
GSEM = [None]
GD = {}


class Buf:
    __slots__ = ("name", "writers", "readers")

    def __init__(self, name=""):
        self.name = name
        self.writers = []
        self.readers = []


class Op:
    __slots__ = ("eng", "fn", "deps", "dma", "is_ms", "ms", "extra_waits", "prog")

    def __init__(self, eng, fn):
        self.eng = eng
        self.fn = fn
        self.deps = []
        self.dma = None
        self.is_ms = False
        self.ms = 0
        self.extra_waits = []


class Prog:
    ENGS = ("pe", "act", "dve", "pool", "sp")

    def __init__(self, nc, stack, tag):
        self.nc = nc
        self.tag = tag
        self.q = {e: [] for e in self.ENGS}
        stack = GSEM[0]
        self.psem = {e: stack.enter_context(nc.semaphore(f"{tag}_p_{e}")) for e in self.ENGS}
        if "dsem" not in GD:
            GD["dsem"] = {e: [stack.enter_context(nc.semaphore(f"gd_{e}{i}")) for i in range(NDMA)] for e in ("sp", "pool")}
            GD["dcount"] = {"sp": 0, "pool": 0}
            GD["cc"] = stack.enter_context(nc.semaphore("gcc"))
            GD["ccn"] = 0
        self.dsem = GD["dsem"]
        self.dcount = GD["dcount"]

    def _track(self, op, r, w):
        deps = []
        for b in r:
            deps += b.writers
        for b in w:
            deps += b.writers
            deps += b.readers
        op.deps = [d for d in deps if d.prog is self]
        for b in r:
            if op.dma is None:
                b.readers = [x for x in b.readers if not (x.eng == op.eng and x.dma is None)]
            b.readers.append(op)
        for b in w:
            b.writers = [op]
            b.readers = []

    def op(self, eng, name, r=(), w=(), **kw):
        o = Op(eng, (name, kw))
        o.prog = self
        self._track(o, r, w)
        self.q[eng].append(o)
        return o

    def dma(self, eng, out, in_, r=(), w=()):
        o = Op(eng, ("dma_start", dict(out=out, in_=in_)))
        o.prog = self
        n = self.dcount[eng]
        self.dcount[eng] += 1
        sem = self.dsem[eng][n % NDMA]
        gen = n // NDMA
        o.dma = (sem, 16 * (gen + 1), 16)
        if gen > 0:
            o.extra_waits.append((sem, 16 * gen))
        self._track(o, r, w)
        self.q[eng].append(o)
        return o

    def coll(self, kind, ins, outs, groups, r=(), w=()):
        o = Op("pool", ("collective_compute", dict(kind=kind, op=ALU.bypass, replica_groups=groups, ins=ins, outs=outs)))
        o.prog = self
        GD["ccn"] += 1
        o.dma = (GD["cc"], GD["ccn"], 1)
        self._track(o, r, w)
        self.q["pool"].append(o)
        return o

    def finish(self):
        nc = self.nc
        lasts = []
        for e in self.ENGS:
            if self.q[e]:
                for o in reversed(self.q[e]):
                    if o.dma is None:
                        lasts.append(o)
                        break
        dma_final = []
        for e in ("sp", "pool"):
            n = self.dcount[e]
            for i in range(min(n, NDMA)):
                cnt = (n - 1 - i) // NDMA + 1
                dma_final.append((self.dsem[e][i], 16 * cnt))
        for e in self.ENGS:
            for o in self.q[e]:
                for d in o.deps:
                    if d.dma is None and (d.eng != e or e != "pe"):
                        d.is_ms = True
        for o in lasts:
            o.is_ms = True
        for e in self.ENGS:
            c = 0
            for o in self.q[e]:
                if o.dma is None and o.is_ms:
                    c += 1
                    o.ms = c
        psem = self.psem

        def mk(eng):
            def body(e):
                waited = {}

                def wait(sem, v):
                    k = id(sem)
                    if waited.get(k, 0) >= v:
                        return
                    waited[k] = v
                    e.wait_ge(sem, v)

                for o in self.q[eng]:
                    need = {}
                    for d in o.deps:
                        if d.dma is not None:
                            s, v = d.dma[0], d.dma[1]
                        elif d.eng == eng and eng == "pe":
                            continue
                        else:
                            s, v = psem[d.eng], d.ms
                        k = id(s)
                        if k not in need or need[k][1] < v:
                            need[k] = (s, v)
                    for s, v in o.extra_waits:
                        k = id(s)
                        if k not in need or need[k][1] < v:
                            need[k] = (s, v)
                    for s, v in need.values():
                        wait(s, v)
                    kw = {k: (v(e) if callable(v) else v) for k, v in o.fn[1].items()}
                    ins = getattr(e, o.fn[0])(**kw)
                    if o.dma is not None:
                        ins.then_inc(o.dma[0], o.dma[2])
                    elif o.is_ms:
                        ins.then_inc(psem[eng], 1)
                for o in lasts:
                    if o.eng != eng:
                        wait(psem[o.eng], o.ms)
                for s, v in dma_final:
                    wait(s, v)
                if GD["ccn"] > 0 and not getattr(self, "defer_cc", False):
                    wait(GD["cc"], GD["ccn"])

            return body

        with nc.Block() as block:
            block.tensor(mk("pe"))
            block.scalar(mk("act"))
            block.vector(mk("dve"))
            block.gpsimd(mk("pool"))
            block.sync(mk("sp"))


class TileAlloc:
    def __init__(self, nc, stack, tag):
        self.nc = nc
        self.stack = stack
        self.tag = tag
        self.n = 0

    def sb(self, shape, dt, name="t"):
        self.n += 1
        t = self.stack.enter_context(self.nc.sbuf_tensor(f"{self.tag}_{name}{self.n}", list(shape), dt))
        return t, Buf(name)

    def ps(self, shape, dt, name="p"):
        self.n += 1
        t = self.stack.enter_context(self.nc.psum_tensor(f"{self.tag}_{name}{self.n}", list(shape), dt))
        return t, Buf(name)


WA_COLS = 898
C_QA, C_KA, C_QB, C_FB, C_GB, C_FA, C_VA = 0, 128, 256, 384, 512, 640, 642
CST_COLS = 1408
K_ID, K_NEG, K_BD, K_RESET, K_ONES = 0, 128, 256, 384, 896


def phase_A(nc, S, xg, wA, vecA, cst, Yw, scr, tag, ycb=None):
    NT = S // 512
    R4 = S // 4
    QT, KT, VA, QE, KE, KD, VB, GS = (scr[k] for k in ("QT", "KT", "VA", "QE", "KE", "KD", "VB", "GS"))

    with contextlib.ExitStack() as st0:
        TA0 = TileAlloc(nc, st0, tag + "g")
        A_sb, bA = TA0.sb([128, S // 64], F32, "Adec")
        cid, bcid = TA0.sb([128, 128], BF16, "ident")
        cneg, bcneg = TA0.sb([128, 128], BF16, "neg")
        cbd, bcbd = TA0.sb([128, 128], BF16, "bdtri")
        cones, bcones = TA0.sb([128, 128], BF16, "ones")
        conesf, bconesf = TA0.sb([128, 64], F32, "onesf")
        vec, bvec = TA0.sb([128, 8], F32, "vec")
        lbv, blbv = TA0.sb([128, 8], F32, "lbv")

        with contextlib.ExitStack() as st:
            P = Prog(nc, st, tag + "1")
            TA = TileAlloc(nc, st, tag + "1")
            dQT = [[Buf() for _ in range(NT)] for _ in range(2)]
            dKT = [[Buf() for _ in range(NT)] for _ in range(2)]
            dVA = [Buf() for _ in range(NT)]
            dH = [Buf() for _ in range(NT)]
            wsb, bw = TA.sb([128, 8, WA_COLS], BF16, "wA")
            creset, bcreset = TA.sb([128, 512], F32, "reset")
            xt = [TA.sb([128, 8, 512], BF16, "xt") for _ in range(2)]
            psF = [TA.ps([128, 512], F32, "psF") for _ in range(5)]
            psT = [TA.ps([128, 512], F32, "psT") for _ in range(2)]
            psX = TA.ps([128, 512], BF16, "psX")
            nF = [0]

            def nextF():
                nF[0] += 1
                return psF[nF[0] % 5]
            stq = [TA.sb([128, 512], BF16, "stq") for _ in range(2)]
            stk = [TA.sb([128, 512], BF16, "stk") for _ in range(2)]
            stv = [TA.sb([128, 4, 256], BF16, "stv") for _ in range(2)]
            sqe = [TA.sb([128, 512], BF16, "sqe") for _ in range(2)]
            ske = [TA.sb([128, 512], BF16, "ske") for _ in range(2)]
            skdT = [TA.sb([128, 512], BF16, "skdT") for _ in range(2)]
            skd = [TA.sb([128, 4, 128], BF16, "skd") for _ in range(2)]
            sgs = [TA.sb([128, 512], BF16, "sgs") for _ in range(2)]
            wf = [TA.sb([128, 512], F32, "wf") for _ in range(24)]
            fz = [TA.sb([2, 512], F32, "fz") for _ in range(6)]
            Fc = [TA.sb([2, 512], F32, "Fc") for _ in range(2)]
            fst = [TA.sb([2, 3, 512], BF16, "fst") for _ in range(2)]
            negrow, bnegrow = TA.sb([2, 3, 512], BF16, "negrow")
            fr = [TA.sb([2, 512], F32, "fr") for _ in range(4)]
            onesrow, bonesrow = TA.sb([2, 3, 512], BF16, "onesrow")

            wv = wA.rearrange("(kc p) n -> p kc n", p=128)
            for kc0 in range(0, 8, 2):
                P.dma("pool", wsb[:, kc0:kc0 + 2, :], wv[:, kc0:kc0 + 2, :], w=[bw])
            P.dma("pool", cid[:], cst[:, K_ID:K_ID + 128], w=[bcid])
            P.dma("pool", cneg[:], cst[:, K_NEG:K_NEG + 128], w=[bcneg])
            P.dma("pool", cbd[:], cst[:, K_BD:K_BD + 128], w=[bcbd])
            P.dma("sp", creset[:], cst[:, K_RESET:K_RESET + 512], w=[bcreset])
            P.dma("pool", cones[:], cst[:, K_ONES:K_ONES + 128], w=[bcones])
            P.dma("sp", conesf[:], cst[:, K_ONES:K_ONES + 64], w=[bconesf])
            P.dma("sp", vec[:], vecA, w=[bvec])
            P.op("pool", "memset", ap=onesrow[:], constant=1.0, w=[bonesrow])
            P.op("pool", "memset", ap=negrow[:], constant=-1.0, w=[bnegrow])
            c_ = lambda i: lbv[:, i:i + 1]
            rw = dict(r=[blbv, bvec], w=[blbv])
            P.op("dve", "tensor_tensor", out=c_(0), in0=vec[:, 0:1], in1=vec[:, 1:2], op=ALU.max, **rw)
            P.op("dve", "tensor_tensor", out=c_(1), in0=vec[:, 0:1], in1=c_(0), op=ALU.subtract, **rw)
            P.op("dve", "tensor_tensor", out=c_(2), in0=vec[:, 1:2], in1=c_(0), op=ALU.subtract, **rw)
            P.op("act", "activation", out=lbv[:, 1:3], in_=lbv[:, 1:3], func=AF.Exp, **rw)
            P.op("dve", "tensor_tensor", out=c_(3), in0=c_(1), in1=c_(2), op=ALU.add, **rw)
            P.op("dve", "reciprocal", out=c_(4), in_=c_(3), **rw)
            P.op("dve", "tensor_tensor", out=c_(5), in0=c_(1), in1=c_(4), op=ALU.mult, **rw)
            P.op("dve", "tensor_tensor", out=c_(6), in0=c_(2), in1=c_(4), op=ALU.mult, **rw)
            P.op("dve", "scalar_tensor_tensor", out=c_(7), in0=c_(6), scalar=vec[:, 4:5], in1=c_(5), op0=ALU.mult, op1=ALU.add, **rw)
            P.op("dve", "tensor_tensor", out=c_(0), in0=c_(7), in1=c_(5), op=ALU.subtract, **rw)
            P.op("dve", "tensor_scalar", out=c_(1), in0=c_(0), scalar1=-1.0, scalar2=1.0, op0=ALU.mult, op1=ALU.add, **rw)
            P.op("dve", "tensor_scalar", out=c_(2), in0=vec[:, 3:4], scalar1=-1.0, scalar2=None, op0=ALU.mult, **rw)
            LB, OML, NBF = lbv[:, 0:1], lbv[:, 1:2], lbv[0:2, 2:3]

            for h in range(2):
                for i in range(NT):
                    tok = slice(i * 512, (i + 1) * 512)
                    P.dma("sp", QT[h:h + 1, 67:70, tok], negrow[0:1, :, :], r=[bnegrow], w=[dQT[h][i]])
                    P.dma("sp", KT[h:h + 1, 64:67, tok], onesrow[0:1, :, :], r=[bonesrow], w=[dKT[h][i]])

            def load_x(i):
                t, b = xt[i % 2]
                r_, c0 = divmod(i * 512, R4)
                src = xg(r_, c0) if callable(xg) else xg[r_].rearrange("(kc p) t -> p kc t", p=128)[:, :, c0:c0 + 512]
                for kc0 in range(0, 8, 4):
                    P.dma("pool", t[:, kc0:kc0 + 4, :], src[:, kc0:kc0 + 4, :], w=[b])

            load_x(0)
            Fprev = [None]

            def tiles(i):
                s3 = i % 3
                return wf[8 * s3:8 * s3 + 8], fz[2 * s3:2 * s3 + 2]

            def stA(i):
                if i + 1 < NT:
                    load_x(i + 1)
                xs, bx = xt[i % 2]
                tok = slice(i * 512, (i + 1) * 512)
                s2 = i % 2
                ((sig, bsig), (g_, bg), (bb, bbb), (eb, beb), (enb, benb), (ebr, bebr), (kk, bkk), (qs, bqs)), ((z0, bz0), (z1, bz1)) = tiles(i)

                def fgroup(col, M):
                    pt, pb = nextF()
                    for kc in range(8):
                        P.op("pe", "matmul", out=pt[0:M, :], lhsT=wsb[:, kc, col:col + M], rhs=xs[:, kc, :], start=(kc == 0), stop=(kc == 7),
                             r=[bw, bx], w=[pb])
                    return pt, pb
                pt, pb = fgroup(C_QA, 128)
                t, b = stq[s2]
                P.op("dve", "tensor_scalar", out=t[:], in0=pt[:], scalar1=0.125, scalar2=None, op0=ALU.mult, r=[pb], w=[b])
                for h in range(2):
                    P.dma("sp", QT[h, 0:64, tok], t[h * 64:(h + 1) * 64, :], r=[b], w=[dQT[h][i]])
                pt, pb = fgroup(C_KA, 128)
                t, b = stk[s2]
                P.op("dve", "tensor_copy", out=t[:], in_=pt[:], r=[pb], w=[b])
                for h in range(2):
                    P.dma("sp", KT[h, 0:64, tok], t[h * 64:(h + 1) * 64, :], r=[b], w=[dKT[h][i]])
                pt, pb = fgroup(C_FB, 128)
                P.op("act", "activation", out=sig[:], in_=pt[:], func=AF.Sigmoid, r=[pb], w=[bsig])
                pt, pb = fgroup(C_GB, 128)
                t, b = sgs[s2]
                P.op("act", "activation", out=t[:], in_=pt[:], func=AF.Sigmoid, r=[pb], w=[b])
                P.dma("sp", GS[:, tok], t[:], r=[b], w=[dH[i]])
                pt, pb = fgroup(C_QB, 128)
                P.op("act", "activation", out=qs[:], in_=pt[:], func=AF.Silu, r=[pb], w=[bqs])
                pt, pb = fgroup(C_FA, 2)
                P.op("act", "activation", out=z0[:], in_=pt[0:2, :], func=AF.Exp, scale=-1.0, bias=NBF, r=[pb, blbv], w=[bz0])
                t, b = stv[s2]
                for sb_ in range(4):
                    pt, pb = psT[sb_ % 2]
                    for kc in range(8):
                        P.op("pe", "matmul", out=pt[:, 0:256], lhsT=xs[:, kc, sb_ * 128:(sb_ + 1) * 128], rhs=wsb[:, kc, C_VA:C_VA + 256],
                             start=(kc == 0), stop=(kc == 7), r=[bw, bx], w=[pb])
                    P.op("act", "copy", out=t[:, sb_, :], in_=pt[:, 0:256], r=[pb], w=[b])
                P.dma("sp", VA[tok, :].rearrange("(a p) d -> p a d", p=128), t[:, :, 0:128], r=[b], w=[dVA[i]])
                P.dma("sp", VB[tok, :].rearrange("(a p) d -> p a d", p=128), t[:, :, 128:256], r=[b], w=[dH[i]])
                P.op("dve", "tensor_scalar", out=sig[:], in0=sig[:], scalar1=OML, scalar2=LB, op0=ALU.mult, op1=ALU.add, r=[bsig, blbv], w=[bsig])

            def stB1(i):
                tok = slice(i * 512, (i + 1) * 512)
                s2 = i % 2
                ((sig, bsig), (g_, bg), (bb, bbb), (eb, beb), (enb, benb), (ebr, bebr), (kk, bkk), (qs, bqs)), ((z0, bz0), (z1, bz1)) = tiles(i)
                P.op("act", "activation", out=g_[:], in_=sig[:], func=AF.Ln, r=[bsig], w=[bg])
                P.op("act", "activation", out=z1[:], in_=z0[:], func=AF.Ln, bias=1.0, r=[bz0], w=[bz1])
                P.op("pool", "tensor_scalar", out=kk[:], in0=sig[:], scalar1=-1.0, scalar2=1.0, op0=ALU.mult, op1=ALU.add, r=[bsig], w=[bkk])
                for c in range(8):
                    cs = slice(c * 64, (c + 1) * 64)
                    P.op("dve", "tensor_tensor_scan", out=bb[:, cs], data0=g_[:, cs], data1=g_[:, cs], initial=0.0, op0=ALU.add, op1=ALU.bypass, r=[bg], w=[bbb])
                Fc_t, Fc_b = Fc[s2]
                Fp = Fprev[0]
                init = 0.0 if Fp is None else Fp[0][:, 511:512]
                rr = [bz1] + ([] if Fp is None else [Fp[1]])
                P.op("dve", "tensor_scalar", out=z1[:], in0=z1[:], scalar1=-1.0, scalar2=None, op0=ALU.mult, r=[bz1], w=[bz1])
                P.op("dve", "tensor_tensor_scan", out=Fc_t[:], data0=z1[:], data1=z1[:], initial=init, op0=ALU.add, op1=ALU.bypass, r=rr, w=[Fc_b])
                Fprev[0] = (Fc_t, Fc_b)
                ft, fb = fst[s2]
                (r1, br1), (r2, br2) = fr[2 * s2:2 * s2 + 2]
                P.op("dve", "tensor_copy", out=ft[:, 0, :], in_=Fc_t[:], r=[Fc_b], w=[fb])
                P.op("dve", "tensor_tensor", out=r1[:], in0=Fc_t[:], in1=ft[:, 0, :], op=ALU.subtract, r=[Fc_b, fb], w=[br1])
                P.op("dve", "tensor_copy", out=ft[:, 1, :], in_=r1[:], r=[br1], w=[fb])
                P.op("dve", "tensor_tensor", out=r2[:], in0=r1[:], in1=ft[:, 1, :], op=ALU.subtract, r=[br1, fb], w=[br2])
                P.op("dve", "tensor_copy", out=ft[:, 2, :], in_=r2[:], r=[br2], w=[fb])
                for h in range(2):
                    P.dma("sp", QT[h:h + 1, 64:67, tok], ft[h:h + 1, 0:3, :], r=[fb], w=[dQT[h][i]])
                    P.dma("sp", KT[h:h + 1, 67:70, tok], ft[h:h + 1, 0:3, :], r=[fb], w=[dKT[h][i]])

            def stB2(i):
                tok = slice(i * 512, (i + 1) * 512)
                s2 = i % 2
                ((sig, bsig), (g_, bg), (bb, bbb), (eb, beb), (enb, benb), (ebr, bebr), (kk, bkk), (qs, bqs)), _ = tiles(i)
                P.op("act", "activation", out=eb[:], in_=bb[:], func=AF.Exp, r=[bbb], w=[beb])
                P.op("act", "activation", out=enb[:], in_=bb[:], func=AF.Exp, scale=-1.0, r=[bbb], w=[benb])
                for c in range(8):
                    cs = slice(c * 64, (c + 1) * 64)
                    last = slice(c * 64 + 63, c * 64 + 64)
                    P.op("act", "activation", out=ebr[:, cs], in_=bb[:, cs], func=AF.Exp, scale=-1.0, bias=bb[:, last], r=[bbb], w=[bebr])
                    P.op("pool", "tensor_copy", out=A_sb[:, i * 8 + c:i * 8 + c + 1], in_=eb[:, last], r=[beb], w=[bA])
                t, b = ske[s2]
                P.op("dve", "tensor_tensor", out=t[:], in0=kk[:], in1=enb[:], op=ALU.mult, r=[bkk, benb], w=[b])
                P.dma("sp", KE[:, tok], t[:], r=[b], w=[dH[i]])
                t, b = sqe[s2]
                P.op("pool", "tensor_tensor", out=t[:], in0=qs[:], in1=eb[:], op=ALU.mult, r=[bqs, beb], w=[b])
                P.dma("sp", QE[:, tok], t[:], r=[b], w=[dH[i]])
                tT, bT = skdT[s2]
                P.op("dve", "tensor_tensor", out=tT[:], in0=kk[:], in1=ebr[:], op=ALU.mult, r=[bkk, bebr], w=[bT])

            def stB3(i):
                tok = slice(i * 512, (i + 1) * 512)
                s2 = i % 2
                tT, bT = skdT[s2]
                for sb_ in range(4):
                    bs = slice(sb_ * 128, (sb_ + 1) * 128)
                    P.op("pe", "transpose", out=psX[0][:, bs], in_=tT[:, bs], identity=cid[:], r=[bT, bcid], w=[psX[1]])
                t, b = skd[s2]
                P.op("dve", "tensor_copy", out=t[:].rearrange("p a d -> p (a d)"), in_=psX[0][:], r=[psX[1]], w=[b])
                P.dma("sp", KD[tok, :].rearrange("(a p) d -> p a d", p=128), t[:], r=[b], w=[dH[i]])

            for t_ in range(NT + 3):
                if t_ < NT:
                    stA(t_)
                if 0 <= t_ - 1 < NT:
                    stB1(t_ - 1)
                if 0 <= t_ - 2 < NT:
                    stB2(t_ - 2)
                if 0 <= t_ - 3 < NT:
                    stB3(t_ - 3)
            P.finish()
        if os.environ.get("STOP_AFTER") == "1":
            return

        with contextlib.ExitStack() as st:
            P = Prog(nc, st, tag + "2")
            P.defer_cc = ycb is not None
            TA = TileAlloc(nc, st, tag + "2")
            NB = S // 128
            dYp = {(p_, r_): Buf() for p_ in range(3) for r_ in range(4)}
            qt, bqt = TA.sb([70, S], BF16, "qt")
            kt, bkt = TA.sb([70, S], BF16, "kt")
            vv, bvv = TA.sb([128, NB, 65], BF16, "vv")
            NS, NP_ = 3, 4
            psS = [TA.ps([128, 512], F32, "psS") for _ in range(NS)]
            psO = [TA.ps([128, 512], F32, "psO") for _ in range(2)]
            psX = TA.ps([128, 512], F32, "psX")
            psU = TA.ps([128, 512], F32, "psU")
            psOo = TA.ps([128, 512], F32, "psOo")
            pT = [TA.sb([128, 512], BF16, "pT") for _ in range(NP_)]
            osb = [TA.sb([64, 512], F32, "osb") for _ in range(2)]
            rden = [TA.sb([128, 512], F32, "rden") for _ in range(2)]
            yst = [TA.sb([64, 512], BF16, "yst") for _ in range(2)]
            SEG = min(2048, S)
            NSEG = S // SEG
            seg = []
            for k in range(2):
                seg.append(dict(
                    qe=TA.sb([128, SEG], BF16, "qe"), ke=TA.sb([128, SEG], BF16, "ke"),
                    kd=TA.sb([64, SEG // 64, 128], BF16, "kd"), vb=TA.sb([64, SEG // 64, 128], BF16, "vb"),
                    gs=TA.sb([128, SEG], BF16, "gs")))
            state = [TA.sb([128, 128], F32, "state") for _ in range(2)]
            sbf = [TA.sb([128, 8, 128], BF16, "sbf") for _ in range(2)]
            at = [TA.sb([128, 512], BF16, "at") for _ in range(2)]
            o_sb = [TA.sb([128, 512], F32, "o_sb") for _ in range(2)]
            sq = [TA.sb([128, 512], BF16, "sq") for _ in range(2)]
            rstd = [TA.sb([128, 512], F32, "rstd") for _ in range(2)]
            yb = [TA.sb([128, 512], BF16, "yb") for _ in range(2)]
            P.op("dve", "memset", ap=state[0][0][:], constant=0.0, w=[state[0][1]])
            P.op("dve", "memset", ap=sbf[0][0][:, 0, :], constant=0.0, w=[sbf[0][1]])

            def load_seg(k):
                sgm = seg[k % 2]
                c0 = k * SEG
                P.dma("sp", sgm["qe"][0][:], QE[:, c0:c0 + SEG], w=[sgm["qe"][1]])
                P.dma("pool", sgm["ke"][0][:], KE[:, c0:c0 + SEG], w=[sgm["ke"][1]])
                P.dma("sp", sgm["gs"][0][:], GS[:, c0:c0 + SEG], w=[sgm["gs"][1]])
                P.dma("pool", sgm["kd"][0][:], KD[c0:c0 + SEG, :].rearrange("(a p) d -> p a d", p=64), w=[sgm["kd"][1]])
                P.dma("sp", sgm["vb"][0][:], VB[c0:c0 + SEG, :].rearrange("(a p) d -> p a d", p=64), w=[sgm["vb"][1]])

            cur = [0]

            def hgrn_stages(i):
                k, ti = divmod(i * 512, SEG)
                sgm = seg[k % 2]
                qe, bqe = sgm["qe"]
                ke, bke = sgm["ke"]
                kd, bkd = sgm["kd"]
                vb, bvb = sgm["vb"]
                gs, bgs = sgm["gs"]
                s2 = i % 2
                sb_t, sb_b = sbf[s2]
                nsb_t, nsb_b = sbf[1 - s2]
                pu, pub = psU
                pa, pab = psU
                po, pob = psOo
                at_t, at_b = at[s2]
                o_t, o_b = o_sb[s2]
                sq_t, sq_b = sq[s2]
                rs_t, rs_b = rstd[s2]
                yb_t, yb_b = yb[s2]

                def st_load():
                    if ti == 0:
                        load_seg(k)

                def st_u(c0):
                    def f():
                        for c in range(c0, c0 + 4):
                            ch = (ti + c * 64) // 64
                            P.op("pe", "matmul", out=pu[:, (c % 4) * 128:(c % 4 + 1) * 128], lhsT=kd[:, ch, :], rhs=vb[:, ch, :],
                                 start=True, stop=True, r=[bkd, bvb], w=[pub])
                    return f

                def st_state(c0):
                    def f():
                        for c in range(c0, c0 + 4):
                            so_t, so_b = state[cur[0]]
                            sn_t, sn_b = state[1 - cur[0]]
                            P.op("dve", "scalar_tensor_tensor", out=sn_t[:], in0=so_t[:], scalar=A_sb[:, i * 8 + c:i * 8 + c + 1],
                                 in1=pu[:, (c % 4) * 128:(c % 4 + 1) * 128], op0=ALU.mult, op1=ALU.add, r=[so_b, bA, pub], w=[sn_b])
                            if c < 7:
                                P.op("pool", "tensor_copy", out=sb_t[:, c + 1, :], in_=sn_t[:], r=[sn_b], w=[sb_b])
                            else:
                                P.op("pool", "tensor_copy", out=nsb_t[:, 0, :], in_=sn_t[:], r=[sn_b], w=[nsb_b])
                            cur[0] = 1 - cur[0]
                    return f

                def st_attn():
                    for c in range(8):
                        tk = slice(ti + c * 64, ti + (c + 1) * 64)
                        P.op("pe", "matmul", out=pa[0:64, c * 64:(c + 1) * 64], lhsT=ke[:, tk], rhs=qe[:, tk], start=True, stop=True, r=[bke, bqe], w=[pab])

                def st_mask():
                    for c in range(8):
                        cs = slice(c * 64, (c + 1) * 64)
                        P.op("dve", "tensor_tensor", out=at_t[0:64, cs], in0=pa[0:64, cs], in1=cbd[0:64, 0:64], op=ALU.mult, r=[pab, bcbd], w=[at_b])

                def st_o():
                    for c in range(8):
                        ch = (ti + c * 64) // 64
                        cs = slice(c * 64, (c + 1) * 64)
                        P.op("pe", "matmul", out=po[:, cs], lhsT=vb[:, ch, :], rhs=at_t[0:64, cs], start=True, stop=False, r=[bvb, at_b], w=[pob])
                        P.op("pe", "matmul", out=po[:, cs], lhsT=sb_t[:, c, :], rhs=qe[:, ti + c * 64:ti + (c + 1) * 64],
                             start=False, stop=True, r=[sb_b, bqe], w=[pob])

                def st_sq():
                    P.op("dve", "tensor_copy", out=o_t[:], in_=po[:], r=[pob], w=[o_b])
                    P.op("pool", "tensor_tensor", out=sq_t[:], in0=o_t[:], in1=o_t[:], op=ALU.mult, r=[o_b], w=[sq_b])

                def st_m():
                    P.op("pe", "matmul", out=pa[:], lhsT=cones[:], rhs=sq_t[:], start=True, stop=True, r=[bcones, sq_b], w=[pab])

                def st_rs():
                    P.op("dve", "tensor_scalar", out=rs_t[:], in0=pa[:], scalar1=1.0 / 128.0, scalar2=RMS_EPS, op0=ALU.mult, op1=ALU.add, r=[pab], w=[rs_b])

                def st_sqrt():
                    P.op("act", "activation", out=rs_t[:], in_=rs_t[:], func=AF.Sqrt, r=[rs_b], w=[rs_b])

                def st_fin():
                    P.op("dve", "reciprocal", out=rs_t[:], in_=rs_t[:], r=[rs_b], w=[rs_b])
                    P.op("pool", "tensor_tensor", out=o_t[:], in0=o_t[:], in1=rs_t[:], op=ALU.mult, r=[o_b, rs_b], w=[o_b])
                    P.op("dve", "scalar_tensor_tensor", out=yb_t[:], in0=o_t[:], scalar=vec[:, 2:3], in1=gs[:, ti:ti + 512], op0=ALU.mult, op1=ALU.mult,
                         r=[o_b, bvec, bgs], w=[yb_b])
                    TPC = (S // 4) // 512
                    P.dma("sp", Yw(slice(128, 256), i), yb_t[:], r=[yb_b], w=[dYp[(2, i // TPC)]])
                    if ycb is not None and i % TPC == TPC - 1:
                        ycb(P, 2, i // TPC, dYp[(2, i // TPC)])
                return [st_load, st_u(0), st_state(0), st_u(4), st_state(4), st_attn, st_mask, st_o, st_sq, st_m, st_rs, st_sqrt, st_fin]

            allblocks = []
            for h in range(2):
                for I in range(NT):
                    nk = 4 * I + 4
                    for jb in range(nk):
                        allblocks.append((h, I, jb, nk))
            nblk = len(allblocks)
            per_tile = max(1, int(nblk * 0.85) // NT)
            GAP = max(1, min(6, (per_tile - 2) // 13))
            sched = {}
            for i in range(NT):
                for si in range(13):
                    sched.setdefault(i * per_tile + 2 + si * GAP, []).append((i, si))
            stage_cache = {}
            pending = []
            ndone = [0]
            CH = min(2048, S)

            NPC = S // CH
            bqtp = [Buf() for _ in range(NPC)]
            bktp = [Buf() for _ in range(NPC)]
            bvvp = [Buf() for _ in range(NPC)]

            def load_head(h):
                for c0 in range(0, S, CH):
                    pc = c0 // CH
                    P.dma("pool", kt[:, c0:c0 + CH], KT[h, :, c0:c0 + CH], w=[bktp[pc]])
                    P.dma("sp", qt[:, c0:c0 + CH], QT[h, :, c0:c0 + CH], w=[bqtp[pc]])
                    P.dma("sp", vv[:, c0 // 128:(c0 + CH) // 128, 0:64],
                          VA[c0:c0 + CH, h * 64:(h + 1) * 64].rearrange("(a p) d -> p a d", p=128), w=[bvvp[pc]])

            P.op("pool", "memset", ap=vv[:, :, 64:65], constant=1.0, w=bvvp)
            loaded = set()

            def s_mm(n):
                h, I, jb, nk = allblocks[n]
                d = jb - 4 * I
                qlo = 128 * d if d > 0 else 0
                pt, pb = psS[n % NS]
                P.op("pe", "matmul", out=pt[:, qlo:512], lhsT=kt[0:70, jb * 128:(jb + 1) * 128], rhs=qt[0:70, I * 512 + qlo:I * 512 + 512],
                     start=True, stop=(d < 0), r=[bqtp[(I * 512) // CH], bktp[(jb * 128) // CH]], w=[pb])
                if d >= 0:
                    P.op("pe", "matmul", out=pt[:, qlo:qlo + 128], lhsT=cid[:], rhs=cneg[:], start=False, stop=True, r=[bcid, bcneg], w=[pb])

            qcnt = [0]

            def fin(h, I):
                slot = qcnt[0] % 2
                qcnt[0] += 1
                for pe_ in [p_ for p_ in pending if p_[3] == slot]:
                    pending.remove(pe_)
                    fin2(pe_[0], pe_[1], pe_[3])
                ot, ob = psO[I % 2]
                o_, bo_ = osb[slot]
                rd, brd = rden[slot]
                P.op("dve", "reciprocal", out=rd[64:65, :], in_=ot[64:65, :], r=[ob], w=[brd])
                P.op("dve", "tensor_copy", out=o_[:], in_=ot[0:64, :], r=[ob], w=[bo_])
                pending.append((h, I, ndone[0] + 8, slot))

            def fin2(h, I, slot):
                o_, bo_ = osb[slot]
                rd, brd = rden[slot]
                y_, by_ = yst[slot]
                P.op("pe", "matmul", out=psX[0][0:64, :], lhsT=conesf[64:65, 0:64], rhs=rd[64:65, :], start=True, stop=True, r=[brd, bconesf], w=[psX[1]])
                P.op("dve", "tensor_tensor", out=y_[:], in0=o_[:], in1=psX[0][0:64, :], op=ALU.mult, r=[bo_, psX[1]], w=[by_])
                TPC = (S // 4) // 512
                P.dma("sp", Yw(slice(h * 64, (h + 1) * 64), I), y_[:], r=[by_], w=[dYp[(h, I // TPC)]])
                if ycb is not None and I % TPC == TPC - 1:
                    ycb(P, h, I // TPC, dYp[(h, I // TPC)])

            def exp_pv(n):
                h, I, jb, nk = allblocks[n]
                d = jb - 4 * I
                qlo = 128 * d if d > 0 else 0
                pt, pb = psS[n % NS]
                t, b = pT[n % NP_]
                P.op("act", "activation", out=t[:, qlo:512], in_=pt[:, qlo:512], func=AF.Exp, r=[pb], w=[b])
                ot, ob = psO[I % 2]
                P.op("pe", "matmul", out=ot[0:65, qlo:512], lhsT=vv[:, jb, 0:65], rhs=t[:, qlo:512], start=(jb == 0), stop=(jb == nk - 1),
                     r=[bvvp[(jb * 128) // CH], b], w=[ob])
                if jb == nk - 1:
                    fin(h, I)

            def run_stage(i, si):
                if i not in stage_cache:
                    stage_cache[i] = hgrn_stages(i)
                stage_cache[i][si]()

            LOOK = 2
            for n in range(nblk):
                hcur = allblocks[n][0]
                if n == 0 or allblocks[n - 1][0] != hcur:
                    load_head(hcur)
                    for m in range(n, min(n + LOOK, nblk)):
                        s_mm(m)
                if n + LOOK < nblk and allblocks[n + LOOK][0] == hcur:
                    s_mm(n + LOOK)
                exp_pv(n)
                ndone[0] += 1
                while pending and pending[0][2] <= ndone[0]:
                    h_, I_, _, sl_ = pending.pop(0)
                    fin2(h_, I_, sl_)
                for (i, si) in sched.pop(n, []):
                    run_stage(i, si)
            while pending:
                h_, I_, _, sl_ = pending.pop(0)
                fin2(h_, I_, sl_)
            for n in sorted(sched.keys()):
                for (i, si) in sched[n]:
                    run_stage(i, si)
            P.finish()


def alloc_scratch_A(nc, S, tag):
    def dt(name, shape, dty=BF16):
        return nc.dram_tensor(f"{tag}_{name}", shape, dty, kind="Internal").ap()
    return dict(QT=dt("QT", [2, 70, S]), KT=dt("KT", [2, 70, S]), VA=dt("VA", [S, 128]),
                QE=dt("QE", [128, S]), KE=dt("KE", [128, S]), KD=dt("KD", [S, 128]),
                VB=dt("VB", [S, 128]), GS=dt("GS", [128, S]))


def build_A(S, x_f32=True):
    nc = bass.Bass("TRN2", target_bir_lowering=False)
    R4 = S // 4
    xg = nc.dram_tensor("xg", [4, 1024, R4], F32 if x_f32 else BF16, kind="ExternalInput").ap()
    wA = nc.dram_tensor("wA", [1024, WA_COLS], F32, kind="ExternalInput").ap()
    vecA = nc.dram_tensor("vecA", [128, 8], F32, kind="ExternalInput").ap()
    cst = nc.dram_tensor("cst", [128, CST_COLS], F32, kind="ExternalInput").ap()
    Y = nc.dram_tensor("Y", [256, S], BF16, kind="ExternalOutput").ap()
    scr = alloc_scratch_A(nc, S, "a")
    with contextlib.ExitStack() as gst:
        GSEM[0] = gst
        GD.clear()
        phase_A(nc, S, xg, wA, vecA, cst, lambda rows, i: Y[rows, i * 512:(i + 1) * 512], scr, "A")
    return nc


def ln_alloc(TA, nb=2):
    return dict(hb=[TA.sb([128, 512], BF16, "hb") for _ in range(nb)], hq=[TA.sb([128, 512], BF16, "hq") for _ in range(nb)],
                mean=TA.sb([128, 512], F32, "mean"), rs=TA.sb([128, 512], F32, "rs"))


def layer_norm_steps(P, LT, nextP, h, bh, gcol, bcol, bvecB, conesK, bconesK, out_writer):
    hb, hq = LT["hb"], LT["hq"]
    mean, bmean = LT["mean"]
    rs, brs = LT["rs"]

    def stats():
        pm, pmb = nextP()
        pq, pqb = nextP()
        for c in range(8):
            t, b = hb[c % len(hb)]
            q, bq = hq[c % len(hq)]
            P.op("act", "copy", out=t[:], in_=h[:, c, :], r=[bh[c]], w=[b])
            P.op("dve", "tensor_tensor", out=q[:], in0=h[:, c, :], in1=h[:, c, :], op=ALU.mult, r=[bh[c]], w=[bq])
            P.op("pe", "matmul", out=pm[:], lhsT=conesK[:], rhs=t[:], start=(c == 0), stop=(c == 7), r=[bconesK, b], w=[pmb])
            P.op("pe", "matmul", out=pq[:], lhsT=conesK[:], rhs=q[:], start=(c == 0), stop=(c == 7), r=[bconesK, bq], w=[pqb])
        P.op("act", "copy", out=mean[:], in_=pm[:], r=[pmb], w=[bmean])
        P.op("dve", "tensor_tensor", out=rs[:], in0=mean[:], in1=mean[:], op=ALU.mult, r=[bmean], w=[brs])
        P.op("dve", "tensor_tensor", out=rs[:], in0=pq[:], in1=rs[:], op=ALU.subtract, r=[pqb, brs], w=[brs])
        P.op("dve", "tensor_scalar", out=rs[:], in0=rs[:], scalar1=LN_EPS, scalar2=None, op0=ALU.add, r=[brs], w=[brs])
        P.op("act", "activation", out=rs[:], in_=rs[:], func=AF.Sqrt, r=[brs], w=[brs])
        P.op("dve", "reciprocal", out=rs[:], in_=rs[:], r=[brs], w=[brs])

    def apply(c):
        def f():
            eng = "dve" if (LT.get("flush") and c % 2 == 0) else "pool"
            P.op(eng, "tensor_tensor", out=h[:, c, :], in0=h[:, c, :], in1=mean[:], op=ALU.subtract, r=[bh[c], bmean], w=[bh[c]])
            P.op(eng, "tensor_tensor", out=h[:, c, :], in0=h[:, c, :], in1=rs[:], op=ALU.mult, r=[bh[c], brs], w=[bh[c]])
            P.op(eng, "tensor_scalar", out=h[:, c, :], in0=h[:, c, :], scalar1=gcol(c), scalar2=bcol(c), op0=ALU.mult, op1=ALU.add, r=[bh[c], bvecB], w=[bh[c]])
            if c == 7:
                out_writer()
        return f
    return [stats] + [apply(c) for c in range(8)]


def phase_B(nc, R4, Yg, xres, wg, wab, wo, wfi, wfo, vecB, cst, X1, XO, XOB, tag, xcb=None):
    NT = R4 // 512
    with contextlib.ExitStack() as st0:
        TA0 = TileAlloc(nc, st0, tag + "g")
        vB, bvB = TA0.sb([128, 32], F32, "vecB")
        conesK, bconesK = TA0.sb([128, 128], BF16, "onesK")
        onesf, bonesf = TA0.sb([128, 128], F32, "onesf")
        dX1 = [Buf() for _ in range(NT)]

        with contextlib.ExitStack() as st:
            P = Prog(nc, st, tag + "1")
            TA = TileAlloc(nc, st, tag + "1")
            wg_sb, bwg = TA.sb([128, 8, 2048], BF16, "wg")
            wab_sb, bwab = TA.sb([128, 8, 1024], BF16, "wab")
            wo_sb, bwo = TA.sb([128, 8, 1024], BF16, "wo")
            xr = [TA.sb([128, 8, 512], F32, "xr") for _ in range(2)]
            xb = [TA.sb([128, 8, 512], BF16, "xb") for _ in range(2)]
            yt = [TA.sb([128, 8, 512], BF16, "yt") for _ in range(2)]
            mg, bmg = TA.sb([128, 8, 512], BF16, "mg")
            sga = [TA.sb([128, 512], F32, "sga") for _ in range(2)]
            sgb = [TA.sb([128, 512], F32, "sgb") for _ in range(2)]
            t1 = [TA.sb([128, 512], F32, "t1") for _ in range(2)]
            t2 = [TA.sb([128, 512], F32, "t2") for _ in range(2)]
            hh = [(TA.sb([128, 8, 512], F32, "hh")[0], [Buf() for _ in range(8)]) for _ in range(2)]
            psl = [TA.ps([128, 512], F32, "ps") for _ in range(8)]
            LT = ln_alloc(TA, 4)
            npz = [0]

            def nextP():
                npz[0] += 1
                return psl[npz[0] % 8]
            P.dma("sp", vB[:], vecB, w=[bvB])
            bYg = [Buf(), Buf(), Buf()]
            if isinstance(Yg, dict):
                gd = Yg
                Yg = gd["Ygl"]
                P.dma("sp", gd["jt"][:], gd["jidx"], w=[gd["bjt"]])

                def gsrc(Gt):
                    def f(e):
                        e.reg_load(gd["jr"], gd["jt"][0:1, 0:1])
                        val = e.snap(gd["jr"], min_val=0, max_val=3)
                        return Gt[bass.ds(val, 1)].rearrange("o s p t -> (o s) p t")
                    return f
                Yv = Yg.rearrange("(s p) t -> s p t", s=4)
                for o_ in (P.dma("sp", Yv[:, 128:256, :], gsrc(gd["GH"]), r=[gd["bjt"]], w=[bYg[0]]),
                           P.dma("sp", Yv[:, 0:64, :], gsrc(gd["GA"]), r=[gd["bjt"]], w=[bYg[1]]),
                           P.dma("sp", Yv[:, 64:128, :], gsrc(gd["GB"]), r=[gd["bjt"]], w=[bYg[2]])):
                    o_.extra_waits.append((GD["cc"], GD["ccn"]))
            P.dma("sp", onesf[:], cst[:, K_ONES:K_ONES + 128], w=[bonesf])
            P.op("dve", "tensor_scalar", out=conesK[:], in0=onesf[:], scalar1=1.0 / 1024.0, scalar2=None, op0=ALU.mult, r=[bonesf], w=[bconesK])
            wgv = wg.rearrange("(kc p) n -> p kc n", p=128)
            bwg_m = [Buf() for _ in range(8)]
            PRE_LOAD0 = True

            def load(i):
                tok = slice(i * 512, (i + 1) * 512)
                xv = xres.rearrange("(kc p) t -> p kc t", p=128)[:, :, tok]
                yv = Yg.rearrange("(kc p) t -> p kc t", p=128)[:, :, tok]
                P.dma("sp", xr[i % 2][0][:], xv, w=[xr[i % 2][1]])
                P.dma("pool", xb[i % 2][0][:], xv, w=[xb[i % 2][1]])
                P.dma("sp", yt[i % 2][0][:], yv, r=bYg, w=[yt[i % 2][1]])

            load(0)
            deferred = []
            for m0 in range(0, 8, 2):
                for n0 in (0, 1024):
                    P.dma("pool", wg_sb[:, :, n0 + m0 * 128:n0 + (m0 + 2) * 128], wgv[:, :, n0 + m0 * 128:n0 + (m0 + 2) * 128], w=[bwg_m[m0], bwg_m[m0 + 1]])
                if m0 == 0:
                    for kc0 in range(0, 8, 4):
                        P.dma("pool", wab_sb[:, kc0:kc0 + 4, :], wab.rearrange("(kc p) n -> p kc n", p=128)[:, kc0:kc0 + 4, :], w=[bwab])
            for kc0 in range(0, 8, 4):
                P.dma("pool", wo_sb[:, kc0:kc0 + 4, :], wo.rearrange("(kc p) n -> p kc n", p=128)[:, kc0:kc0 + 4, :], w=[bwo])
            for i in range(NT):
                if i + 1 < NT:
                    load(i + 1)
                tok = slice(i * 512, (i + 1) * 512)
                xr_t, xr_b = xr[i % 2]
                xb_t, xb_b = xb[i % 2]
                yt_t, yt_b = yt[i % 2]
                h_t, h_b = hh[i % 2]
                for m in range(8):
                    if m >= 1 and deferred:
                        deferred.pop(0)()
                    ms = slice(m * 128, (m + 1) * 128)
                    pga, pgab = nextP()
                    for kc in range(8):
                        P.op("pe", "matmul", out=pga[:], lhsT=wg_sb[:, kc, ms], rhs=xb_t[:, kc, :], start=(kc == 0), stop=(kc == 7), r=[bwg_m[m], xb_b], w=[pgab])
                    pgb, pgbb = nextP()
                    for kc in range(8):
                        P.op("pe", "matmul", out=pgb[:], lhsT=wg_sb[:, kc, 1024 + m * 128:1024 + (m + 1) * 128], rhs=xb_t[:, kc, :], start=(kc == 0), stop=(kc == 7), r=[bwg_m[m], xb_b], w=[pgbb])
                    ppa, ppab = nextP()
                    for r_ in range(4):
                        P.op("pe", "matmul", out=ppa[:], lhsT=wab_sb[:, 2 * r_, ms], rhs=yt_t[:, 2 * r_, :], start=(r_ == 0), stop=(r_ == 3), r=[bwab, yt_b], w=[ppab])
                    ppb, ppbb = nextP()
                    for r_ in range(4):
                        P.op("pe", "matmul", out=ppb[:], lhsT=wab_sb[:, 2 * r_ + 1, ms], rhs=yt_t[:, 2 * r_ + 1, :], start=(r_ == 0), stop=(r_ == 3), r=[bwab, yt_b], w=[ppbb])
                    sa, bsa = sga[m % 2]
                    sb_, bsb = sgb[m % 2]
                    a1, ba1 = t1[m % 2]
                    a2, ba2 = t2[m % 2]
                    P.op("act", "activation", out=sa[:], in_=pga[:], func=AF.Sigmoid, r=[pgab], w=[bsa])
                    P.op("act", "activation", out=sb_[:], in_=pgb[:], func=AF.Sigmoid, r=[pgbb], w=[bsb])
                    P.op("dve", "tensor_tensor", out=a1[:], in0=sa[:], in1=ppa[:], op=ALU.mult, r=[bsa, ppab], w=[ba1])
                    P.op("dve", "tensor_tensor", out=a2[:], in0=sb_[:], in1=ppb[:], op=ALU.mult, r=[bsb, ppbb], w=[ba2])
                    P.op("pool", "tensor_tensor", out=mg[:, m, :], in0=a1[:], in1=a2[:], op=ALU.add, r=[ba1, ba2], w=[bmg])
                for mo in range(8):
                    if deferred:
                        deferred.pop(0)()
                    pw, pwb = nextP()
                    for m in range(8):
                        P.op("pe", "matmul", out=pw[:], lhsT=wo_sb[:, m, mo * 128:(mo + 1) * 128], rhs=mg[:, m, :], start=(m == 0), stop=(m == 7), r=[bwo, bmg], w=[pwb])
                    P.op("dve", "scalar_tensor_tensor", out=h_t[:, mo, :], in0=xr_t[:, mo, :], scalar=float(ALPHA), in1=pw[:], op0=ALU.mult, op1=ALU.add,
                         r=[xr_b, pwb], w=[h_b[mo]])

                def wr(i=i, h_t=h_t, h_b=h_b, tok=tok):
                    P.dma("sp", X1.rearrange("(kc p) t -> p kc t", p=128)[:, :, tok], h_t[:], r=h_b, w=[dX1[i]])
                deferred.extend(layer_norm_steps(P, LT, nextP, h_t, h_b, lambda c: vB[:, c:c + 1], lambda c: vB[:, 8 + c:9 + c], bvB, conesK, bconesK, wr))
            LT["flush"] = True
            while deferred:
                deferred.pop(0)()
            P.finish()

        with contextlib.ExitStack() as st:
            P = Prog(nc, st, tag + "2")
            TA = TileAlloc(nc, st, tag + "2")
            NM = FFH // 128
            wfi_sb, bwfi = TA.sb([128, 8, 2 * FFH], BF16, "wfi")
            wfo_sb, bwfo = TA.sb([128, NM, 1024], BF16, "wfo")
            xb = [TA.sb([128, 8, 512], BF16, "xb") for _ in range(2)]
            xrc = [TA.sb([128, 512], F32, "xrc") for _ in range(2)]
            aa, baa = TA.sb([128, NM, 512], BF16, "aa")
            sg = [TA.sb([128, 512], F32, "sg") for _ in range(2)]
            hh = TA.sb([128, 8, 512], F32, "hh")[0]
            bhh = [Buf() for _ in range(8)]
            psl = [TA.ps([128, 512], F32, "ps") for _ in range(8)]
            LT = ln_alloc(TA)
            npz = [0]

            def nextP():
                npz[0] += 1
                return psl[npz[0] % 8]
            wfv = wfi.rearrange("(kc p) n -> p kc n", p=128)
            X1v = X1.rearrange("(kc p) t -> p kc t", p=128)

            def load(i):
                tok = slice(i * 512, (i + 1) * 512)
                P.dma("pool", xb[i % 2][0][:], X1v[:, :, tok], r=[dX1[i]], w=[xb[i % 2][1]])

            deferred = []
            bwfi_m = [Buf() for _ in range(NM)]
            for m0 in range(0, NM, 2):
                for n0 in (0, FFH):
                    P.dma("pool", wfi_sb[:, :, n0 + m0 * 128:n0 + (m0 + 2) * 128], wfv[:, :, n0 + m0 * 128:n0 + (m0 + 2) * 128], w=[bwfi_m[m0], bwfi_m[m0 + 1]])
                if m0 == 0:
                    load(0)
            wov = wfo.rearrange("(kc p) n -> p kc n", p=128)
            for k0 in range(0, NM, 2):
                P.dma("pool", wfo_sb[:, k0:k0 + 2, :], wov[:, k0:k0 + 2, :], w=[bwfo])
            for i in range(NT):
                if i + 1 < NT:
                    load(i + 1)
                tok = slice(i * 512, (i + 1) * 512)
                xb_t, xb_b = xb[i % 2]
                for m in range(NM):
                    if m >= 2 and deferred:
                        deferred.pop(0)()
                    pu, pub = nextP()
                    for kc in range(8):
                        P.op("pe", "matmul", out=pu[:], lhsT=wfi_sb[:, kc, m * 128:(m + 1) * 128], rhs=xb_t[:, kc, :], start=(kc == 0), stop=(kc == 7), r=[bwfi_m[m], xb_b], w=[pub])
                    pg, pgb = nextP()
                    for kc in range(8):
                        P.op("pe", "matmul", out=pg[:], lhsT=wfi_sb[:, kc, FFH + m * 128:FFH + (m + 1) * 128], rhs=xb_t[:, kc, :], start=(kc == 0), stop=(kc == 7), r=[bwfi_m[m], xb_b], w=[pgb])
                    s_, bs_ = sg[m % 2]
                    P.op("act", "activation", out=s_[:], in_=pg[:], func=AF.Silu, r=[pgb], w=[bs_])
                    P.op("dve", "tensor_tensor", out=aa[:, m, :], in0=s_[:], in1=pu[:], op=ALU.mult, r=[bs_, pub], w=[baa])
                for mo in range(8):
                    xc, bxc = xrc[mo % 2]
                    P.dma("sp", xc[:], X1v[:, mo, tok], r=[dX1[i]], w=[bxc])
                    po, pob = nextP()
                    for m in range(NM):
                        P.op("pe", "matmul", out=po[:], lhsT=wfo_sb[:, m, mo * 128:(mo + 1) * 128], rhs=aa[:, m, :], start=(m == 0), stop=(m == NM - 1), r=[bwfo, baa], w=[pob])
                    P.op("dve", "scalar_tensor_tensor", out=hh[:, mo, :], in0=xc[:], scalar=float(ALPHA), in1=po[:], op0=ALU.mult, op1=ALU.add, r=[bxc, pob], w=[bhh[mo]])

                def wr(tok=tok, i=i):
                    P.dma("sp", XO.rearrange("(kc p) t -> p kc t", p=128)[:, :, tok], hh[:], r=bhh, w=[Buf()])
                    if XOB is not None:
                        bx_ = Buf()
                        if xcb is not None:
                            P.dma("pool", XOB[i].rearrange("(kc p) t -> p kc t", p=128), hh[:], r=bhh, w=[bx_])
                            xcb(P, i, bx_)
                        else:
                            P.dma("pool", XOB.rearrange("(kc p) t -> p kc t", p=128)[:, :, tok], hh[:], r=bhh, w=[bx_])
                deferred.extend(layer_norm_steps(P, LT, nextP, hh, bhh, lambda c: vB[:, 16 + c:17 + c], lambda c: vB[:, 24 + c:25 + c], bvB, conesK, bconesK, wr))
            LT["flush"] = True
            while deferred:
                deferred.pop(0)()
            P.finish()


def build_B(R4, with_bf16_out=False):
    nc = bass.Bass("TRN2", target_bir_lowering=False)
    Yg = nc.dram_tensor("Yg", [1024, R4], BF16, kind="ExternalInput").ap()
    xres = nc.dram_tensor("xres", [1024, R4], F32, kind="ExternalInput").ap()
    wg = nc.dram_tensor("wg", [1024, 2048], F32, kind="ExternalInput").ap()
    wab = nc.dram_tensor("wab", [1024, 1024], F32, kind="ExternalInput").ap()
    wo = nc.dram_tensor("wo", [1024, 1024], F32, kind="ExternalInput").ap()
    wfi = nc.dram_tensor("wfi", [1024, 2 * FFH], F32, kind="ExternalInput").ap()
    wfo = nc.dram_tensor("wfo", [FFH, 1024], F32, kind="ExternalInput").ap()
    vecB = nc.dram_tensor("vecB", [128, 32], F32, kind="ExternalInput").ap()
    cst = nc.dram_tensor("cst", [128, CST_COLS], F32, kind="ExternalInput").ap()
    XO = nc.dram_tensor("XO", [1024, R4], F32, kind="ExternalOutput").ap()
    XOB = nc.dram_tensor("XOB", [1024, R4], BF16, kind="ExternalOutput").ap() if with_bf16_out else None
    X1 = nc.dram_tensor("b_X1", [1024, R4], F32, kind="Internal").ap()
    with contextlib.ExitStack() as gst:
        GSEM[0] = gst
        GD.clear()
        phase_B(nc, R4, Yg, xres, wg, wab, wo, wfi, wfo, vecB, cst, X1, XO, XOB, "B")
    return nc


def make_cst():
    c = np.zeros((128, CST_COLS), np.float32)
    p = np.arange(128)[:, None]
    f = np.arange(128)[None, :]
    c[:, K_ID:K_ID + 128] = (p == f)
    c[:, K_NEG:K_NEG + 128] = np.where(f < p, -30000.0, 0.0)
    c[:, K_BD:K_BD + 128] = (f >= p) & ((p // 64) == (f // 64))
    r = np.ones(512, np.float32)
    r[::64] = 0.0
    c[:, K_RESET:K_RESET + 512] = r[None, :]
    c[:, K_ONES:K_ONES + 512] = 1.0
    return c


def make_wA(w_in_l, j):
    B0 = 3 * 512 + 8
    sl = lambda base: w_in_l[:, base + 128 * j: base + 128 * (j + 1)]
    qa, ka, va = sl(0), sl(512), sl(1024)
    fa = w_in_l[:, 1536 + 2 * j: 1536 + 2 * (j + 1)]
    qb, fb, ib, gb = sl(B0), sl(B0 + 512), sl(B0 + 1024), sl(B0 + 1536)
    return np.ascontiguousarray(np.concatenate([qa, ka, qb, fb, gb, fa, va, ib], axis=1))


def make_vecA(l, j, b_fgate, hgrn_lb_logits, hgrn_norm_g):
    v = np.zeros((128, 8), np.float32)
    v[:, 0] = hgrn_lb_logits[0, 128 * j:128 * (j + 1)]
    v[:, 1] = hgrn_lb_logits[1, 128 * j:128 * (j + 1)]
    v[:, 2] = hgrn_norm_g[l]
    v[0:2, 3] = b_fgate[l, 2 * j:2 * j + 2]
    v[:, 4] = float(l)
    return v


def make_wab(wa_l, wb_l):
    out = np.empty((1024, 1024), np.float32)
    for r in range(4):
        out[256 * r:256 * r + 128] = wa_l[128 * r:128 * (r + 1)]
        out[256 * r + 128:256 * r + 256] = wb_l[128 * r:128 * (r + 1)]
    return out


def make_vecB(l, ln1_g, ln1_b, ln2_g, ln2_b):
    v = np.empty((128, 32), np.float32)
    for k, a in enumerate((ln1_g[l], ln1_b[l], ln2_g[l], ln2_b[l])):
        v[:, 8 * k:8 * k + 8] = a.reshape(8, 128).T
    return v


I32 = mybir.dt.int32
GROUPS = [[0, 1, 2, 3], [4, 5, 6, 7]]


def make_ycb(Ysc, GA, GB, GH):
    def ycb(P, part, r_, buf):
        if part < 2:
            src = Ysc[r_, part * 64:(part + 1) * 64, :]
            dst = (GA, GB)[part][r_].rearrange("s p t -> (s p) t")
        else:
            src = Ysc[r_, 128:256, :]
            dst = GH[r_].rearrange("s p t -> (s p) t")
        P.coll("AllGather", [src], [dst], GROUPS, r=[buf], w=[Buf()])
    return ycb


def exchange_Y(nc, R4, GA, GB, GH, Ygl, jidx, jt, bjt, jr, tag):
    with contextlib.ExitStack() as st:
        P = Prog(nc, st, tag)
        P.dma("sp", jt[:], jidx, w=[bjt])

        def src(Gt):
            def f(e):
                e.reg_load(jr, jt[0:1, 0:1])
                val = e.snap(jr, min_val=0, max_val=3)
                return Gt[bass.ds(val, 1)].rearrange("o s p t -> (o s) p t")
            return f
        Yv = Ygl.rearrange("(s p) t -> s p t", s=4)
        P.dma("sp", Yv[:, 0:64, :], src(GA), r=[bjt], w=[Buf()])
        P.dma("sp", Yv[:, 64:128, :], src(GB), r=[bjt], w=[Buf()])
        P.dma("sp", Yv[:, 128:256, :], src(GH), r=[bjt], w=[Buf()])
        P.finish()


def build_fused(S):
    nc = bass.Bass("TRN2", target_bir_lowering=False)
    R4 = S // 4
    ext = lambda name, shape, dt=F32: nc.dram_tensor(name, shape, dt, kind="ExternalInput").ap()
    itn = lambda name, shape, dt=BF16: nc.dram_tensor(name, shape, dt, kind="Internal").ap()
    xg = ext("xg", [4, 1024, R4])
    xres = ext("xres", [1024, R4])
    cst = ext("cst", [128, CST_COLS])
    jidx = ext("jidx", [1, 4], I32)
    W = []
    for l in range(DEPTH):
        W.append(dict(wA=ext(f"wA{l}", [1024, WA_COLS]), vecA=ext(f"vecA{l}", [128, 8]), wg=ext(f"wg{l}", [1024, 2048]),
                      wab=ext(f"wab{l}", [1024, 1024]), wo=ext(f"wo{l}", [1024, 1024]), wfi=ext(f"wfi{l}", [1024, 2 * FFH]),
                      wfo=ext(f"wfo{l}", [FFH, 1024]), vecB=ext(f"vecB{l}", [128, 32])))
    OUT = nc.dram_tensor("OUT", [1024, R4], F32, kind="ExternalOutput").ap()
    scr = alloc_scratch_A(nc, S, "a")
    Ysc = itn("Ysc", [4, 256, R4])
    GA = itn("GA", [4, 4, 64, R4])
    GB = itn("GB", [4, 4, 64, R4])
    GH = itn("GH", [4, 4, 128, R4])
    NTB = R4 // 512
    XOBt = itn("XOBt", [NTB, 1024, 512])
    XG1t = itn("XG1t", [NTB, 4, 1024, 512])
    Ygl = itn("Ygl", [1024, R4])
    X1 = itn("X1", [1024, R4], F32)
    XO0 = itn("XO0", [1024, R4], F32)
    xsrc1 = lambda r_, c0: XG1t[c0 // 512, r_].rearrange("(kc p) t -> p kc t", p=128)
    Yw = lambda rows, i: Ysc[(i * 512) // R4, rows, (i * 512) % R4:(i * 512) % R4 + 512]
    with contextlib.ExitStack() as gst:
        GSEM[0] = gst
        GD.clear()
        jt = gst.enter_context(nc.sbuf_tensor("jt", [1, 4], I32))
        bjt = Buf()
        jr = gst.enter_context(nc.sync.register("jr"))
        ycb = make_ycb(Ysc, GA, GB, GH)

        def xcb(P, i, buf):
            P.coll("AllGather", [XOBt[i]], [XG1t[i].rearrange("s f t -> (s f) t")], GROUPS, r=[buf], w=[Buf()])
        for l in range(DEPTH):
            w = W[l]
            phase_A(nc, S, xg if l == 0 else xsrc1, w["wA"], w["vecA"], cst, Yw, scr, f"A{l}", ycb=ycb)
            last = (l == DEPTH - 1)
            phase_B(nc, R4, dict(GA=GA, GB=GB, GH=GH, Ygl=Ygl, jr=jr, jt=jt, bjt=bjt, jidx=jidx), xres if l == 0 else XO0, w["wg"], w["wab"], w["wo"], w["wfi"], w["wfo"], w["vecB"], cst, X1,
                    OUT if last else XO0, None if last else XOBt, f"B{l}", xcb=None if last else xcb)
    return nc


def kernel(x, w_in, b_fgate, hgrn_lb_logits, hgrn_norm_g, w_branch_a, w_branch_b, w_out,
           ln1_g, ln1_b, w_ff_in, w_ff_out, ln2_g, ln2_b):
    x = np.asarray(x, np.float32)
    Bn, S, _ = x.shape
    R4 = S // 4
    f = lambda a: np.asarray(a, np.float32)
    w_in, b_fgate, hgrn_lb_logits, hgrn_norm_g = f(w_in), f(b_fgate), f(hgrn_lb_logits), f(hgrn_norm_g)
    w_branch_a, w_branch_b, w_out = f(w_branch_a), f(w_branch_b), f(w_out)
    ln1_g, ln1_b, w_ff_in, w_ff_out, ln2_g, ln2_b = f(ln1_g), f(ln1_b), f(w_ff_in), f(w_ff_out), f(ln2_g), f(ln2_b)
    cst = make_cst()
    cores = list(range(8))
    xg = [np.ascontiguousarray(x[b].reshape(4, R4, D).transpose(0, 2, 1)) for b in range(Bn)]
    B0 = 3 * 512 + 8 + 4 * 512
    shared = {}
    for l in range(DEPTH):
        shared[f"wg{l}"] = np.ascontiguousarray(w_in[l][:, B0:B0 + 2048])
        shared[f"wab{l}"] = make_wab(w_branch_a[l], w_branch_b[l])
        shared[f"wo{l}"] = w_out[l]
        shared[f"wfi{l}"] = w_ff_in[l]
        shared[f"wfo{l}"] = w_ff_out[l]
        shared[f"vecB{l}"] = make_vecB(l, ln1_g, ln1_b, ln2_g, ln2_b)
    wAs = {(l, j): make_wA(w_in[l], j) for l in range(DEPTH) for j in range(4)}
    in_maps = []
    for c in cores:
        b, j = divmod(c, 4)
        m = dict(xg=xg[b], xres=np.ascontiguousarray(xg[b][j]), cst=cst, jidx=np.array([[j, 0, 0, 0]], np.int32))
        for l in range(DEPTH):
            m[f"wA{l}"] = wAs[(l, j)]
            m[f"vecA{l}"] = make_vecA(l, j, b_fgate, hgrn_lb_logits, hgrn_norm_g)
        m.update(shared)
        in_maps.append(m)
    nc = build_fused(S)
    res = run_bass_kernel_spmd(nc, in_maps, core_ids=cores)
    out = np.empty((Bn, S, D), np.float32)
    for c in cores:
        b, j = divmod(c, 4)
        out[b, j * R4:(j + 1) * R4, :] = np.asarray(res.results[c]["OUT"]).T
    return out
```

```python
import contextlib
import os
import numpy as np
import ml_dtypes
import concourse.bass as bass
import concourse.mybir as mybir
from concourse.bass_utils import run_bass_kernel_spmd

F32 = mybir.dt.float32
BF16 = mybir.dt.bfloat16
AF = mybir.ActivationFunctionType
ALU = mybir.AluOpType
NPBF = ml_dtypes.bfloat16

D = 1024
SEQ = 16384
DEPTH = 2
FFH = 2816
ALPHA = (2 * DEPTH) ** 0.25
LN_EPS = 1e-5
RMS_EPS = 1e-6
NDMA = 6
GSEM = [None]
GD = {}


class Buf:
    __slots__ = ("name", "writers", "readers")

    def __init__(self, name=""):
        self.name = name
        self.writers = []
        self.readers = []


class Op:
    __slots__ = ("eng", "fn", "deps", "dma", "is_ms", "ms", "extra_waits", "prog")

    def __init__(self, eng, fn):
        self.eng = eng
        self.fn = fn
        self.deps = []
        self.dma = None
        self.is_ms = False
        self.ms = 0
        self.extra_waits = []


class Prog:
    ENGS = ("pe", "act", "dve", "pool", "sp")

    def __init__(self, nc, stack, tag):
        self.nc = nc
        self.tag = tag
        self.q = {e: [] for e in self.ENGS}
        stack = GSEM[0]
        self.psem = {e: stack.enter_context(nc.semaphore(f"{tag}_p_{e}")) for e in self.ENGS}
        if "dsem" not in GD:
            GD["dsem"] = {e: [stack.enter_context(nc.semaphore(f"gd_{e}{i}")) for i in range(NDMA)] for e in ("sp", "pool")}
            GD["dcount"] = {"sp": 0, "pool": 0}
            GD["cc"] = stack.enter_context(nc.semaphore("gcc"))
            GD["ccn"] = 0
        self.dsem = GD["dsem"]
        self.dcount = GD["dcount"]

    def _track(self, op, r, w):
        deps = []
        for b in r:
            deps += b.writers
        for b in w:
            deps += b.writers
            deps += b.readers
        op.deps = [d for d in deps if d.prog is self]
        for b in r:
            if op.dma is None:
                b.readers = [x for x in b.readers if not (x.eng == op.eng and x.dma is None)]
            b.readers.append(op)
        for b in w:
            b.writers = [op]
            b.readers = []

    def op(self, eng, name, r=(), w=(), **kw):
        o = Op(eng, (name, kw))
        o.prog = self
        self._track(o, r, w)
        self.q[eng].append(o)
        return o

    def dma(self, eng, out, in_, r=(), w=()):
        o = Op(eng, ("dma_start", dict(out=out, in_=in_)))
        o.prog = self
        n = self.dcount[eng]
        self.dcount[eng] += 1
        sem = self.dsem[eng][n % NDMA]
        gen = n // NDMA
        o.dma = (sem, 16 * (gen + 1), 16)
        if gen > 0:
            o.extra_waits.append((sem, 16 * gen))
        self._track(o, r, w)
        self.q[eng].append(o)
        return o

    def coll(self, kind, ins, outs, groups, r=(), w=()):
        o = Op("pool", ("collective_compute", dict(kind=kind, op=ALU.bypass, replica_groups=groups, ins=ins, outs=outs)))
        o.prog = self
        GD["ccn"] += 1
        o.dma = (GD["cc"], GD["ccn"], 1)
        self._track(o, r, w)
        self.q["pool"].append(o)
        return o

    def finish(self):
        nc = self.nc
        lasts = []
        for e in self.ENGS:
            if self.q[e]:
                for o in reversed(self.q[e]):
                    if o.dma is None:
                        lasts.append(o)
                        break
        dma_final = []
        for e in ("sp", "pool"):
            n = self.dcount[e]
            for i in range(min(n, NDMA)):
                cnt = (n - 1 - i) // NDMA + 1
                dma_final.append((self.dsem[e][i], 16 * cnt))
        for e in self.ENGS:
            for o in self.q[e]:
                for d in o.deps:
                    if d.dma is None and (d.eng != e or e != "pe"):
                        d.is_ms = True
        for o in lasts:
            o.is_ms = True
        for e in self.ENGS:
            c = 0
            for o in self.q[e]:
                if o.dma is None and o.is_ms:
                    c += 1
                    o.ms = c
        psem = self.psem

        def mk(eng):
            def body(e):
                waited = {}

                def wait(sem, v):
                    k = id(sem)
                    if waited.get(k, 0) >= v:
                        return
                    waited[k] = v
                    e.wait_ge(sem, v)

                for o in self.q[eng]:
                    need = {}
                    for d in o.deps:
                        if d.dma is not None:
                            s, v = d.dma[0], d.dma[1]
                        elif d.eng == eng and eng == "pe":
                            continue
                        else:
                            s, v = psem[d.eng], d.ms
                        k = id(s)
                        if k not in need or need[k][1] < v:
                            need[k] = (s, v)
                    for s, v in o.extra_waits:
                        k = id(s)
                        if k not in need or need[k][1] < v:
                            need[k] = (s, v)
                    for s, v in need.values():
                        wait(s, v)
                    kw = {k: (v(e) if callable(v) else v) for k, v in o.fn[1].items()}
                    ins = getattr(e, o.fn[0])(**kw)
                    if o.dma is not None:
                        ins.then_inc(o.dma[0], o.dma[2])
                    elif o.is_ms:
                        ins.then_inc(psem[eng], 1)
                for o in lasts:
                    if o.eng != eng:
                        wait(psem[o.eng], o.ms)
                for s, v in dma_final:
                    wait(s, v)
                if GD["ccn"] > 0 and not getattr(self, "defer_cc", False):
                    wait(GD["cc"], GD["ccn"])

            return body

        with nc.Block() as block:
            block.tensor(mk("pe"))
            block.scalar(mk("act"))
            block.vector(mk("dve"))
            block.gpsimd(mk("pool"))
            block.sync(mk("sp"))


class TileAlloc:
    def __init__(self, nc, stack, tag):
        self.nc = nc
        self.stack = stack
        self.tag = tag
        self.n = 0

    def sb(self, shape, dt, name="t"):
        self.n += 1
        t = self.stack.enter_context(self.nc.sbuf_tensor(f"{self.tag}_{name}{self.n}", list(shape), dt))
        return t, Buf(name)

    def ps(self, shape, dt, name="p"):
        self.n += 1
        t = self.stack.enter_context(self.nc.psum_tensor(f"{self.tag}_{name}{self.n}", list(shape), dt))
        return t, Buf(name)


WA_COLS = 898
C_QA, C_KA, C_QB, C_FB, C_GB, C_FA, C_VA = 0, 128, 256, 384, 512, 640, 642
CST_COLS = 1408
K_ID, K_NEG, K_BD, K_RESET, K_ONES = 0, 128, 256, 384, 896


def phase_A(nc, S, xg, wA, vecA, cst, Yw, scr, tag, ycb=None):
    NT = S // 512
    R4 = S // 4
    QT, KT, VA, QE, KE, KD, VB, GS = (scr[k] for k in ("QT", "KT", "VA", "QE", "KE", "KD", "VB", "GS"))

    with contextlib.ExitStack() as st0:
        TA0 = TileAlloc(nc, st0, tag + "g")
        A_sb, bA = TA0.sb([128, S // 64], F32, "Adec")
        cid, bcid = TA0.sb([128, 128], BF16, "ident")
        cneg, bcneg = TA0.sb([128, 128], BF16, "neg")
        cbd, bcbd = TA0.sb([128, 128], BF16, "bdtri")
        cones, bcones = TA0.sb([128, 128], BF16, "ones")
        conesf, bconesf = TA0.sb([128, 64], F32, "onesf")
        vec, bvec = TA0.sb([128, 8], F32, "vec")
        lbv, blbv = TA0.sb([128, 8], F32, "lbv")

        with contextlib.ExitStack() as st:
            P = Prog(nc, st, tag + "1")
            TA = TileAlloc(nc, st, tag + "1")
            dQT = [[Buf() for _ in range(NT)] for _ in range(2)]
            dKT = [[Buf() for _ in range(NT)] for _ in range(2)]
            dVA = [Buf() for _ in range(NT)]
            dH = [Buf() for _ in range(NT)]
            wsb, bw = TA.sb([128, 8, WA_COLS], BF16, "wA")
            creset, bcreset = TA.sb([128, 512], F32, "reset")
            xt = [TA.sb([128, 8, 512], BF16, "xt") for _ in range(2)]
            psF = [TA.ps([128, 512], F32, "psF") for _ in range(5)]
            psT = [TA.ps([128, 512], F32, "psT") for _ in range(2)]
            psX = TA.ps([128, 512], BF16, "psX")
            nF = [0]

            def nextF():
                nF[0] += 1
                return psF[nF[0] % 5]
            stq = [TA.sb([128, 512], BF16, "stq") for _ in range(2)]
            stk = [TA.sb([128, 512], BF16, "stk") for _ in range(2)]
            stv = [TA.sb([128, 4, 256], BF16, "stv") for _ in range(2)]
            sqe = [TA.sb([128, 512], BF16, "sqe") for _ in range(2)]
            ske = [TA.sb([128, 512], BF16, "ske") for _ in range(2)]
            skdT = [TA.sb([128, 512], BF16, "skdT") for _ in range(2)]
            skd = [TA.sb([128, 4, 128], BF16, "skd") for _ in range(2)]
            sgs = [TA.sb([128, 512], BF16, "sgs") for _ in range(2)]
            wf = [TA.sb([128, 512], F32, "wf") for _ in range(24)]
            fz = [TA.sb([2, 512], F32, "fz") for _ in range(6)]
            Fc = [TA.sb([2, 512], F32, "Fc") for _ in range(2)]
            fst = [TA.sb([2, 3, 512], BF16, "fst") for _ in range(2)]
            negrow, bnegrow = TA.sb([2, 3, 512], BF16, "negrow")
            fr = [TA.sb([2, 512], F32, "fr") for _ in range(4)]
            onesrow, bonesrow = TA.sb([2, 3, 512], BF16, "onesrow")

            wv = wA.rearrange("(kc p) n -> p kc n", p=128)
            for kc0 in range(0, 8, 2):
                P.dma("pool", wsb[:, kc0:kc0 + 2, :], wv[:, kc0:kc0 + 2, :], w=[bw])
            P.dma("pool", cid[:], cst[:, K_ID:K_ID + 128], w=[bcid])
            P.dma("pool", cneg[:], cst[:, K_NEG:K_NEG + 128], w=[bcneg])
            P.dma("pool", cbd[:], cst[:, K_BD:K_BD + 128], w=[bcbd])
            P.dma("sp", creset[:], cst[:, K_RESET:K_RESET + 512], w=[bcreset])
            P.dma("pool", cones[:], cst[:, K_ONES:K_ONES + 128], w=[bcones])
            P.dma("sp", conesf[:], cst[:, K_ONES:K_ONES + 64], w=[bconesf])
            P.dma("sp", vec[:], vecA, w=[bvec])
            P.op("pool", "memset", ap=onesrow[:], constant=1.0, w=[bonesrow])
            P.op("pool", "memset", ap=negrow[:], constant=-1.0, w=[bnegrow])
            c_ = lambda i: lbv[:, i:i + 1]
            rw = dict(r=[blbv, bvec], w=[blbv])
            P.op("dve", "tensor_tensor", out=c_(0), in0=vec[:, 0:1], in1=vec[:, 1:2], op=ALU.max, **rw)
            P.op("dve", "tensor_tensor", out=c_(1), in0=vec[:, 0:1], in1=c_(0), op=ALU.subtract, **rw)
            P.op("dve", "tensor_tensor", out=c_(2), in0=vec[:, 1:2], in1=c_(0), op=ALU.subtract, **rw)
            P.op("act", "activation", out=lbv[:, 1:3], in_=lbv[:, 1:3], func=AF.Exp, **rw)
            P.op("dve", "tensor_tensor", out=c_(3), in0=c_(1), in1=c_(2), op=ALU.add, **rw)
            P.op("dve", "reciprocal", out=c_(4), in_=c_(3), **rw)
            P.op("dve", "tensor_tensor", out=c_(5), in0=c_(1), in1=c_(4), op=ALU.mult, **rw)
            P.op("dve", "tensor_tensor", out=c_(6), in0=c_(2), in1=c_(4), op=ALU.mult, **rw)
            P.op("dve", "scalar_tensor_tensor", out=c_(7), in0=c_(6), scalar=vec[:, 4:5], in1=c_(5), op0=ALU.mult, op1=ALU.add, **rw)
            P.op("dve", "tensor_tensor", out=c_(0), in0=c_(7), in1=c_(5), op=ALU.subtract, **rw)
            P.op("dve", "tensor_scalar", out=c_(1), in0=c_(0), scalar1=-1.0, scalar2=1.0, op0=ALU.mult, op1=ALU.add, **rw)
            P.op("dve", "tensor_scalar", out=c_(2), in0=vec[:, 3:4], scalar1=-1.0, scalar2=None, op0=ALU.mult, **rw)
            LB, OML, NBF = lbv[:, 0:1], lbv[:, 1:2], lbv[0:2, 2:3]

            for h in range(2):
                for i in range(NT):
                    tok = slice(i * 512, (i + 1) * 512)
                    P.dma("sp", QT[h:h + 1, 67:70, tok], negrow[0:1, :, :], r=[bnegrow], w=[dQT[h][i]])
                    P.dma("sp", KT[h:h + 1, 64:67, tok], onesrow[0:1, :, :], r=[bonesrow], w=[dKT[h][i]])

            def load_x(i):
                t, b = xt[i % 2]
                r_, c0 = divmod(i * 512, R4)
                src = xg(r_, c0) if callable(xg) else xg[r_].rearrange("(kc p) t -> p kc t", p=128)[:, :, c0:c0 + 512]
                for kc0 in range(0, 8, 4):
                    o_ = P.dma("pool", t[:, kc0:kc0 + 4, :], src[:, kc0:kc0 + 4, :], w=[b])
                    if callable(xg) and hasattr(xg, "ccwait"):
                        o_.extra_waits.append(xg.ccwait(c0 // 512))

            load_x(0)
            Fprev = [None]

            def tiles(i):
                s3 = i % 3
                return wf[8 * s3:8 * s3 + 8], fz[2 * s3:2 * s3 + 2]

            def stA(i):
                if i + 1 < NT:
                    load_x(i + 1)
                xs, bx = xt[i % 2]
                tok = slice(i * 512, (i + 1) * 512)
                s2 = i % 2
                ((sig, bsig), (g_, bg), (bb, bbb), (eb, beb), (enb, benb), (ebr, bebr), (kk, bkk), (qs, bqs)), ((z0, bz0), (z1, bz1)) = tiles(i)

                def fgroup(col, M):
                    pt, pb = nextF()
                    for kc in range(8):
                        P.op("pe", "matmul", out=pt[0:M, :], lhsT=wsb[:, kc, col:col + M], rhs=xs[:, kc, :], start=(kc == 0), stop=(kc == 7),
                             r=[bw, bx], w=[pb])
                    return pt, pb
                pt, pb = fgroup(C_QA, 128)
                t, b = stq[s2]
                P.op("dve", "tensor_scalar", out=t[:], in0=pt[:], scalar1=0.125, scalar2=None, op0=ALU.mult, r=[pb], w=[b])
                for h in range(2):
                    P.dma("sp", QT[h, 0:64, tok], t[h * 64:(h + 1) * 64, :], r=[b], w=[dQT[h][i]])
                pt, pb = fgroup(C_KA, 128)
                t, b = stk[s2]
                P.op("dve", "tensor_copy", out=t[:], in_=pt[:], r=[pb], w=[b])
                for h in range(2):
                    P.dma("sp", KT[h, 0:64, tok], t[h * 64:(h + 1) * 64, :], r=[b], w=[dKT[h][i]])
                pt, pb = fgroup(C_FB, 128)
                P.op("act", "activation", out=sig[:], in_=pt[:], func=AF.Sigmoid, r=[pb], w=[bsig])
                pt, pb = fgroup(C_GB, 128)
                t, b = sgs[s2]
                P.op("act", "activation", out=t[:], in_=pt[:], func=AF.Sigmoid, r=[pb], w=[b])
                P.dma("sp", GS[:, tok], t[:], r=[b], w=[dH[i]])
                pt, pb = fgroup(C_QB, 128)
                P.op("act", "activation", out=qs[:], in_=pt[:], func=AF.Silu, r=[pb], w=[bqs])
                pt, pb = fgroup(C_FA, 2)
                P.op("act", "activation", out=z0[:], in_=pt[0:2, :], func=AF.Exp, scale=-1.0, bias=NBF, r=[pb, blbv], w=[bz0])
                t, b = stv[s2]
                for sb_ in range(4):
                    pt, pb = psT[sb_ % 2]
                    for kc in range(8):
                        P.op("pe", "matmul", out=pt[:, 0:256], lhsT=xs[:, kc, sb_ * 128:(sb_ + 1) * 128], rhs=wsb[:, kc, C_VA:C_VA + 256],
                             start=(kc == 0), stop=(kc == 7), r=[bw, bx], w=[pb])
                    P.op("act", "copy", out=t[:, sb_, :], in_=pt[:, 0:256], r=[pb], w=[b])
                P.dma("sp", VA[tok, :].rearrange("(a p) d -> p a d", p=128), t[:, :, 0:128], r=[b], w=[dVA[i]])
                P.dma("sp", VB[tok, :].rearrange("(a p) d -> p a d", p=128), t[:, :, 128:256], r=[b], w=[dH[i]])
                P.op("dve", "tensor_scalar", out=sig[:], in0=sig[:], scalar1=OML, scalar2=LB, op0=ALU.mult, op1=ALU.add, r=[bsig, blbv], w=[bsig])

            def stB1(i):
                tok = slice(i * 512, (i + 1) * 512)
                s2 = i % 2
                ((sig, bsig), (g_, bg), (bb, bbb), (eb, beb), (enb, benb), (ebr, bebr), (kk, bkk), (qs, bqs)), ((z0, bz0), (z1, bz1)) = tiles(i)
                P.op("act", "activation", out=g_[:], in_=sig[:], func=AF.Ln, r=[bsig], w=[bg])
                P.op("act", "activation", out=z1[:], in_=z0[:], func=AF.Ln, bias=1.0, r=[bz0], w=[bz1])
                P.op("pool", "tensor_scalar", out=kk[:], in0=sig[:], scalar1=-1.0, scalar2=1.0, op0=ALU.mult, op1=ALU.add, r=[bsig], w=[bkk])
                for c in range(8):
                    cs = slice(c * 64, (c + 1) * 64)
                    P.op("dve", "tensor_tensor_scan", out=bb[:, cs], data0=g_[:, cs], data1=g_[:, cs], initial=0.0, op0=ALU.add, op1=ALU.bypass, r=[bg], w=[bbb])
                Fc_t, Fc_b = Fc[s2]
                Fp = Fprev[0]
                init = 0.0 if Fp is None else Fp[0][:, 511:512]
                rr = [bz1] + ([] if Fp is None else [Fp[1]])
                P.op("dve", "tensor_scalar", out=z1[:], in0=z1[:], scalar1=-1.0, scalar2=None, op0=ALU.mult, r=[bz1], w=[bz1])
                P.op("dve", "tensor_tensor_scan", out=Fc_t[:], data0=z1[:], data1=z1[:], initial=init, op0=ALU.add, op1=ALU.bypass, r=rr, w=[Fc_b])
                Fprev[0] = (Fc_t, Fc_b)
                ft, fb = fst[s2]
                (r1, br1), (r2, br2) = fr[2 * s2:2 * s2 + 2]
                P.op("dve", "tensor_copy", out=ft[:, 0, :], in_=Fc_t[:], r=[Fc_b], w=[fb])
                P.op("dve", "tensor_tensor", out=r1[:], in0=Fc_t[:], in1=ft[:, 0, :], op=ALU.subtract, r=[Fc_b, fb], w=[br1])
                P.op("dve", "tensor_copy", out=ft[:, 1, :], in_=r1[:], r=[br1], w=[fb])
                P.op("dve", "tensor_tensor", out=r2[:], in0=r1[:], in1=ft[:, 1, :], op=ALU.subtract, r=[br1, fb], w=[br2])
                P.op("dve", "tensor_copy", out=ft[:, 2, :], in_=r2[:], r=[br2], w=[fb])
                for h in range(2):
                    P.dma("sp", QT[h:h + 1, 64:67, tok], ft[h:h + 1, 0:3, :], r=[fb], w=[dQT[h][i]])
                    P.dma("sp", KT[h:h + 1, 67:70, tok], ft[h:h + 1, 0:3, :], r=[fb], w=[dKT[h][i]])

            def stB2(i):
                tok = slice(i * 512, (i + 1) * 512)
                s2 = i % 2
                ((sig, bsig), (g_, bg), (bb, bbb), (eb, beb), (enb, benb), (ebr, bebr), (kk, bkk), (qs, bqs)), _ = tiles(i)
                P.op("act", "activation", out=eb[:], in_=bb[:], func=AF.Exp, r=[bbb], w=[beb])
                P.op("act", "activation", out=enb[:], in_=bb[:], func=AF.Exp, scale=-1.0, r=[bbb], w=[benb])
                for c in range(8):
                    cs = slice(c * 64, (c + 1) * 64)
                    last = slice(c * 64 + 63, c * 64 + 64)
                    P.op("act", "activation", out=ebr[:, cs], in_=bb[:, cs], func=AF.Exp, scale=-1.0, bias=bb[:, last], r=[bbb], w=[bebr])
                    P.op("pool", "tensor_copy", out=A_sb[:, i * 8 + c:i * 8 + c + 1], in_=eb[:, last], r=[beb], w=[bA])
                t, b = ske[s2]
                P.op("dve", "tensor_tensor", out=t[:], in0=kk[:], in1=enb[:], op=ALU.mult, r=[bkk, benb], w=[b])
                P.dma("sp", KE[:, tok], t[:], r=[b], w=[dH[i]])
                t, b = sqe[s2]
                P.op("pool", "tensor_tensor", out=t[:], in0=qs[:], in1=eb[:], op=ALU.mult, r=[bqs, beb], w=[b])
                P.dma("sp", QE[:, tok], t[:], r=[b], w=[dH[i]])
                tT, bT = skdT[s2]
                P.op("dve", "tensor_tensor", out=tT[:], in0=kk[:], in1=ebr[:], op=ALU.mult, r=[bkk, bebr], w=[bT])

            def stB3(i):
                tok = slice(i * 512, (i + 1) * 512)
                s2 = i % 2
                tT, bT = skdT[s2]
                for sb_ in range(4):
                    bs = slice(sb_ * 128, (sb_ + 1) * 128)
                    P.op("pe", "transpose", out=psX[0][:, bs], in_=tT[:, bs], identity=cid[:], r=[bT, bcid], w=[psX[1]])
                t, b = skd[s2]
                P.op("dve", "tensor_copy", out=t[:].rearrange("p a d -> p (a d)"), in_=psX[0][:], r=[psX[1]], w=[b])
                P.dma("sp", KD[tok, :].rearrange("(a p) d -> p a d", p=128), t[:], r=[b], w=[dH[i]])

            for t_ in range(NT + 3):
                if t_ < NT:
                    stA(t_)
                if 0 <= t_ - 1 < NT:
                    stB1(t_ - 1)
                if 0 <= t_ - 2 < NT:
                    stB2(t_ - 2)
                if 0 <= t_ - 3 < NT:
                    stB3(t_ - 3)
            P.finish()
        if os.environ.get("STOP_AFTER") == "1":
            return

        with contextlib.ExitStack() as st:
            P = Prog(nc, st, tag + "2")
            P.defer_cc = ycb is not None
            TA = TileAlloc(nc, st, tag + "2")
            NB = S // 128
            dYp = {(p_, r_): Buf() for p_ in range(3) for r_ in range(4)}
            qt, bqt = TA.sb([70, S], BF16, "qt")
            kt, bkt = TA.sb([70, S], BF16, "kt")
            vv, bvv = TA.sb([128, NB, 65], BF16, "vv")
            NS, NP_ = 3, 4
            psS = [TA.ps([128, 512], F32, "psS") for _ in range(NS)]
            psO = [TA.ps([128, 512], F32, "psO") for _ in range(2)]
            psX = TA.ps([128, 512], F32, "psX")
            psU = TA.ps([128, 512], F32, "psU")
            psOo = TA.ps([128, 512], F32, "psOo")
            pT = [TA.sb([128, 512], BF16, "pT") for _ in range(NP_)]
            osb = [TA.sb([64, 512], F32, "osb") for _ in range(2)]
            rden = [TA.sb([128, 512], F32, "rden") for _ in range(2)]
            yst = [TA.sb([64, 512], BF16, "yst") for _ in range(2)]
            SEG = min(2048, S)
            NSEG = S // SEG
            seg = []
            for k in range(2):
                seg.append(dict(
                    qe=TA.sb([128, SEG], BF16, "qe"), ke=TA.sb([128, SEG], BF16, "ke"),
                    kd=TA.sb([64, SEG // 64, 128], BF16, "kd"), vb=TA.sb([64, SEG // 64, 128], BF16, "vb"),
                    gs=TA.sb([128, SEG], BF16, "gs")))
            state = [TA.sb([128, 128], F32, "state") for _ in range(2)]
            sbf = [TA.sb([128, 8, 128], BF16, "sbf") for _ in range(2)]
            at = [TA.sb([128, 512], BF16, "at") for _ in range(2)]
            o_sb = [TA.sb([128, 512], F32, "o_sb") for _ in range(2)]
            sq = [TA.sb([128, 512], BF16, "sq") for _ in range(2)]
            rstd = [TA.sb([128, 512], F32, "rstd") for _ in range(2)]
            yb = [TA.sb([128, 512], BF16, "yb") for _ in range(2)]
            P.op("dve", "memset", ap=state[0][0][:], constant=0.0, w=[state[0][1]])
            P.op("dve", "memset", ap=sbf[0][0][:, 0, :], constant=0.0, w=[sbf[0][1]])

            def load_seg(k):
                sgm = seg[k % 2]
                c0 = k * SEG
                P.dma("sp", sgm["qe"][0][:], QE[:, c0:c0 + SEG], w=[sgm["qe"][1]])
                P.dma("pool", sgm["ke"][0][:], KE[:, c0:c0 + SEG], w=[sgm["ke"][1]])
                P.dma("sp", sgm["gs"][0][:], GS[:, c0:c0 + SEG], w=[sgm["gs"][1]])
                P.dma("pool", sgm["kd"][0][:], KD[c0:c0 + SEG, :].rearrange("(a p) d -> p a d", p=64), w=[sgm["kd"][1]])
                P.dma("sp", sgm["vb"][0][:], VB[c0:c0 + SEG, :].rearrange("(a p) d -> p a d", p=64), w=[sgm["vb"][1]])

            cur = [0]

            def hgrn_stages(i):
                k, ti = divmod(i * 512, SEG)
                sgm = seg[k % 2]
                qe, bqe = sgm["qe"]
                ke, bke = sgm["ke"]
                kd, bkd = sgm["kd"]
                vb, bvb = sgm["vb"]
                gs, bgs = sgm["gs"]
                s2 = i % 2
                sb_t, sb_b = sbf[s2]
                nsb_t, nsb_b = sbf[1 - s2]
                pu, pub = psU
                pa, pab = psU
                po, pob = psOo
                at_t, at_b = at[s2]
                o_t, o_b = o_sb[s2]
                sq_t, sq_b = sq[s2]
                rs_t, rs_b = rstd[s2]
                yb_t, yb_b = yb[s2]

                def st_load():
                    if ti == 0:
                        load_seg(k)

                def st_u(c0):
                    def f():
                        for c in range(c0, c0 + 4):
                            ch = (ti + c * 64) // 64
                            P.op("pe", "matmul", out=pu[:, (c % 4) * 128:(c % 4 + 1) * 128], lhsT=kd[:, ch, :], rhs=vb[:, ch, :],
                                 start=True, stop=True, r=[bkd, bvb], w=[pub])
                    return f

                def st_state(c0):
                    def f():
                        for c in range(c0, c0 + 4):
                            so_t, so_b = state[cur[0]]
                            sn_t, sn_b = state[1 - cur[0]]
                            P.op("dve", "scalar_tensor_tensor", out=sn_t[:], in0=so_t[:], scalar=A_sb[:, i * 8 + c:i * 8 + c + 1],
                                 in1=pu[:, (c % 4) * 128:(c % 4 + 1) * 128], op0=ALU.mult, op1=ALU.add, r=[so_b, bA, pub], w=[sn_b])
                            if c < 7:
                                P.op("pool", "tensor_copy", out=sb_t[:, c + 1, :], in_=sn_t[:], r=[sn_b], w=[sb_b])
                            else:
                                P.op("pool", "tensor_copy", out=nsb_t[:, 0, :], in_=sn_t[:], r=[sn_b], w=[nsb_b])
                            cur[0] = 1 - cur[0]
                    return f

                def st_attn():
                    for c in range(8):
                        tk = slice(ti + c * 64, ti + (c + 1) * 64)
                        P.op("pe", "matmul", out=pa[0:64, c * 64:(c + 1) * 64], lhsT=ke[:, tk], rhs=qe[:, tk], start=True, stop=True, r=[bke, bqe], w=[pab])

                def st_mask():
                    for c in range(8):
                        cs = slice(c * 64, (c + 1) * 64)
                        P.op("dve", "tensor_tensor", out=at_t[0:64, cs], in0=pa[0:64, cs], in1=cbd[0:64, 0:64], op=ALU.mult, r=[pab, bcbd], w=[at_b])

                def st_o():
                    for c in range(8):
                        ch = (ti + c * 64) // 64
                        cs = slice(c * 64, (c + 1) * 64)
                        P.op("pe", "matmul", out=po[:, cs], lhsT=vb[:, ch, :], rhs=at_t[0:64, cs], start=True, stop=False, r=[bvb, at_b], w=[pob])
                        P.op("pe", "matmul", out=po[:, cs], lhsT=sb_t[:, c, :], rhs=qe[:, ti + c * 64:ti + (c + 1) * 64],
                             start=False, stop=True, r=[sb_b, bqe], w=[pob])

                def st_sq():
                    P.op("dve", "tensor_copy", out=o_t[:], in_=po[:], r=[pob], w=[o_b])
                    P.op("pool", "tensor_tensor", out=sq_t[:], in0=o_t[:], in1=o_t[:], op=ALU.mult, r=[o_b], w=[sq_b])

                def st_m():
                    P.op("pe", "matmul", out=pa[:], lhsT=cones[:], rhs=sq_t[:], start=True, stop=True, r=[bcones, sq_b], w=[pab])

                def st_rs():
                    P.op("dve", "tensor_scalar", out=rs_t[:], in0=pa[:], scalar1=1.0 / 128.0, scalar2=RMS_EPS, op0=ALU.mult, op1=ALU.add, r=[pab], w=[rs_b])

                def st_sqrt():
                    P.op("act", "activation", out=rs_t[:], in_=rs_t[:], func=AF.Sqrt, r=[rs_b], w=[rs_b])

                def st_fin():
                    P.op("dve", "reciprocal", out=rs_t[:], in_=rs_t[:], r=[rs_b], w=[rs_b])
                    P.op("pool", "tensor_tensor", out=o_t[:], in0=o_t[:], in1=rs_t[:], op=ALU.mult, r=[o_b, rs_b], w=[o_b])
                    P.op("dve", "scalar_tensor_tensor", out=yb_t[:], in0=o_t[:], scalar=vec[:, 2:3], in1=gs[:, ti:ti + 512], op0=ALU.mult, op1=ALU.mult,
                         r=[o_b, bvec, bgs], w=[yb_b])
                    TPC = (S // 4) // 512
                    P.dma("sp", Yw(slice(128, 256), i), yb_t[:], r=[yb_b], w=[dYp[(2, i // TPC)]])
                    if ycb is not None and i % TPC == TPC - 1:
                        ycb(P, 2, i // TPC, dYp[(2, i // TPC)])
                return [st_load, st_u(0), st_state(0), st_u(4), st_state(4), st_attn, st_mask, st_o, st_sq, st_m, st_rs, st_sqrt, st_fin]

            allblocks = []
            for h in range(2):
                for I in range(NT):
                    nk = 4 * I + 4
                    for jb in range(nk):
                        allblocks.append((h, I, jb, nk))
            nblk = len(allblocks)
            per_tile = max(1, int(nblk * 0.85) // NT)
            GAP = max(1, min(6, (per_tile - 2) // 13))
            sched = {}
            for i in range(NT):
                for si in range(13):
                    sched.setdefault(i * per_tile + 2 + si * GAP, []).append((i, si))
            stage_cache = {}
            pending = []
            ndone = [0]
            CH = min(2048, S)

            NPC = S // CH
            bqtp = [Buf() for _ in range(NPC)]
            bktp = [Buf() for _ in range(NPC)]
            bvvp = [Buf() for _ in range(NPC)]

            def load_head(h):
                for c0 in range(0, S, CH):
                    pc = c0 // CH
                    P.dma("pool", kt[:, c0:c0 + CH], KT[h, :, c0:c0 + CH], w=[bktp[pc]])
                    P.dma("sp", qt[:, c0:c0 + CH], QT[h, :, c0:c0 + CH], w=[bqtp[pc]])
                    P.dma("sp", vv[:, c0 // 128:(c0 + CH) // 128, 0:64],
                          VA[c0:c0 + CH, h * 64:(h + 1) * 64].rearrange("(a p) d -> p a d", p=128), w=[bvvp[pc]])

            P.op("pool", "memset", ap=vv[:, :, 64:65], constant=1.0, w=bvvp)
            loaded = set()

            def s_mm(n):
                h, I, jb, nk = allblocks[n]
                d = jb - 4 * I
                qlo = 128 * d if d > 0 else 0
                pt, pb = psS[n % NS]
                P.op("pe", "matmul", out=pt[:, qlo:512], lhsT=kt[0:70, jb * 128:(jb + 1) * 128], rhs=qt[0:70, I * 512 + qlo:I * 512 + 512],
                     start=True, stop=(d < 0), r=[bqtp[(I * 512) // CH], bktp[(jb * 128) // CH]], w=[pb])
                if d >= 0:
                    P.op("pe", "matmul", out=pt[:, qlo:qlo + 128], lhsT=cid[:], rhs=cneg[:], start=False, stop=True, r=[bcid, bcneg], w=[pb])

            qcnt = [0]

            def fin(h, I):
                slot = qcnt[0] % 2
                qcnt[0] += 1
                for pe_ in [p_ for p_ in pending if p_[3] == slot]:
                    pending.remove(pe_)
                    fin2(pe_[0], pe_[1], pe_[3])
                ot, ob = psO[I % 2]
                o_, bo_ = osb[slot]
                rd, brd = rden[slot]
                P.op("dve", "reciprocal", out=rd[64:65, :], in_=ot[64:65, :], r=[ob], w=[brd])
                P.op("dve", "tensor_copy", out=o_[:], in_=ot[0:64, :], r=[ob], w=[bo_])
                pending.append((h, I, ndone[0] + 8, slot))

            def fin2(h, I, slot):
                o_, bo_ = osb[slot]
                rd, brd = rden[slot]
                y_, by_ = yst[slot]
                P.op("pe", "matmul", out=psX[0][0:64, :], lhsT=conesf[64:65, 0:64], rhs=rd[64:65, :], start=True, stop=True, r=[brd, bconesf], w=[psX[1]])
                P.op("dve", "tensor_tensor", out=y_[:], in0=o_[:], in1=psX[0][0:64, :], op=ALU.mult, r=[bo_, psX[1]], w=[by_])
                TPC = (S // 4) // 512
                P.dma("sp", Yw(slice(h * 64, (h + 1) * 64), I), y_[:], r=[by_], w=[dYp[(h, I // TPC)]])
                if ycb is not None and I % TPC == TPC - 1:
                    ycb(P, h, I // TPC, dYp[(h, I // TPC)])

            def exp_pv(n):
                h, I, jb, nk = allblocks[n]
                d = jb - 4 * I
                qlo = 128 * d if d > 0 else 0
                pt, pb = psS[n % NS]
                t, b = pT[n % NP_]
                P.op("act", "activation", out=t[:, qlo:512], in_=pt[:, qlo:512], func=AF.Exp, r=[pb], w=[b])
                ot, ob = psO[I % 2]
                P.op("pe", "matmul", out=ot[0:65, qlo:512], lhsT=vv[:, jb, 0:65], rhs=t[:, qlo:512], start=(jb == 0), stop=(jb == nk - 1),
                     r=[bvvp[(jb * 128) // CH], b], w=[ob])
                if jb == nk - 1:
                    fin(h, I)

            def run_stage(i, si):
                if i not in stage_cache:
                    stage_cache[i] = hgrn_stages(i)
                stage_cache[i][si]()

            LOOK = 2
            for n in range(nblk):
                hcur = allblocks[n][0]
                if n == 0 or allblocks[n - 1][0] != hcur:
                    load_head(hcur)
                    for m in range(n, min(n + LOOK, nblk)):
                        s_mm(m)
                if n + LOOK < nblk and allblocks[n + LOOK][0] == hcur:
                    s_mm(n + LOOK)
                exp_pv(n)
                ndone[0] += 1
                while pending and pending[0][2] <= ndone[0]:
                    h_, I_, _, sl_ = pending.pop(0)
                    fin2(h_, I_, sl_)
                for (i, si) in sched.pop(n, []):
                    run_stage(i, si)
            while pending:
                h_, I_, _, sl_ = pending.pop(0)
                fin2(h_, I_, sl_)
            for n in sorted(sched.keys()):
                for (i, si) in sched[n]:
                    run_stage(i, si)
            P.finish()


def alloc_scratch_A(nc, S, tag):
    def dt(name, shape, dty=BF16):
        return nc.dram_tensor(f"{tag}_{name}", shape, dty, kind="Internal").ap()
    return dict(QT=dt("QT", [2, 70, S]), KT=dt("KT", [2, 70, S]), VA=dt("VA", [S, 128]),
                QE=dt("QE", [128, S]), KE=dt("KE", [128, S]), KD=dt("KD", [S, 128]),
                VB=dt("VB", [S, 128]), GS=dt("GS", [128, S]))


def build_A(S, x_f32=True):
    nc = bass.Bass("TRN2", target_bir_lowering=False)
    R4 = S // 4
    xg = nc.dram_tensor("xg", [4, 1024, R4], F32 if x_f32 else BF16, kind="ExternalInput").ap()
    wA = nc.dram_tensor("wA", [1024, WA_COLS], F32, kind="ExternalInput").ap()
    vecA = nc.dram_tensor("vecA", [128, 8], F32, kind="ExternalInput").ap()
    cst = nc.dram_tensor("cst", [128, CST_COLS], F32, kind="ExternalInput").ap()
    Y = nc.dram_tensor("Y", [256, S], BF16, kind="ExternalOutput").ap()
    scr = alloc_scratch_A(nc, S, "a")
    with contextlib.ExitStack() as gst:
        GSEM[0] = gst
        GD.clear()
        phase_A(nc, S, xg, wA, vecA, cst, lambda rows, i: Y[rows, i * 512:(i + 1) * 512], scr, "A")
    return nc


def ln_alloc(TA, nb=2):
    return dict(hb=[TA.sb([128, 512], BF16, "hb") for _ in range(nb)], hq=[TA.sb([128, 512], BF16, "hq") for _ in range(nb)],
                mean=TA.sb([128, 512], F32, "mean"), rs=TA.sb([128, 512], F32, "rs"))


def layer_norm_steps(P, LT, nextP, h, bh, gcol, bcol, bvecB, conesK, bconesK, out_writer):
    hb, hq = LT["hb"], LT["hq"]
    mean, bmean = LT["mean"]
    rs, brs = LT["rs"]

    def stats():
        pm, pmb = nextP()
        pq, pqb = nextP()
        for c in range(8):
            t, b = hb[c % len(hb)]
            q, bq = hq[c % len(hq)]
            P.op("act", "copy", out=t[:], in_=h[:, c, :], r=[bh[c]], w=[b])
            P.op("dve", "tensor_tensor", out=q[:], in0=h[:, c, :], in1=h[:, c, :], op=ALU.mult, r=[bh[c]], w=[bq])
            P.op("pe", "matmul", out=pm[:], lhsT=conesK[:], rhs=t[:], start=(c == 0), stop=(c == 7), r=[bconesK, b], w=[pmb])
            P.op("pe", "matmul", out=pq[:], lhsT=conesK[:], rhs=q[:], start=(c == 0), stop=(c == 7), r=[bconesK, bq], w=[pqb])
        P.op("act", "copy", out=mean[:], in_=pm[:], r=[pmb], w=[bmean])
        P.op("dve", "tensor_tensor", out=rs[:], in0=mean[:], in1=mean[:], op=ALU.mult, r=[bmean], w=[brs])
        P.op("dve", "tensor_tensor", out=rs[:], in0=pq[:], in1=rs[:], op=ALU.subtract, r=[pqb, brs], w=[brs])
        P.op("dve", "tensor_scalar", out=rs[:], in0=rs[:], scalar1=LN_EPS, scalar2=None, op0=ALU.add, r=[brs], w=[brs])
        P.op("act", "activation", out=rs[:], in_=rs[:], func=AF.Sqrt, r=[brs], w=[brs])
        P.op("dve", "reciprocal", out=rs[:], in_=rs[:], r=[brs], w=[brs])

    def apply(c):
        def f():
            eng = "dve" if (LT.get("flush") and c % 2 == 0) else "pool"
            P.op(eng, "tensor_tensor", out=h[:, c, :], in0=h[:, c, :], in1=mean[:], op=ALU.subtract, r=[bh[c], bmean], w=[bh[c]])
            P.op(eng, "tensor_tensor", out=h[:, c, :], in0=h[:, c, :], in1=rs[:], op=ALU.mult, r=[bh[c], brs], w=[bh[c]])
            P.op(eng, "tensor_scalar", out=h[:, c, :], in0=h[:, c, :], scalar1=gcol(c), scalar2=bcol(c), op0=ALU.mult, op1=ALU.add, r=[bh[c], bvecB], w=[bh[c]])
            if c == 7:
                out_writer()
        return f
    return [stats] + [apply(c) for c in range(8)]


def phase_B(nc, R4, Yg, xres, wg, wab, wo, wfi, wfo, vecB, cst, X1, XO, XOB, tag, xcb=None):
    NT = R4 // 512
    with contextlib.ExitStack() as st0:
        TA0 = TileAlloc(nc, st0, tag + "g")
        vB, bvB = TA0.sb([128, 32], F32, "vecB")
        conesK, bconesK = TA0.sb([128, 128], BF16, "onesK")
        onesf, bonesf = TA0.sb([128, 128], F32, "onesf")
        dX1 = [Buf() for _ in range(NT)]

        with contextlib.ExitStack() as st:
            P = Prog(nc, st, tag + "1")
            TA = TileAlloc(nc, st, tag + "1")
            wg_sb, bwg = TA.sb([128, 8, 2048], BF16, "wg")
            wab_sb, bwab = TA.sb([128, 8, 1024], BF16, "wab")
            wo_sb, bwo = TA.sb([128, 8, 1024], BF16, "wo")
            xr = [TA.sb([128, 8, 512], F32, "xr") for _ in range(2)]
            xb = [TA.sb([128, 8, 512], BF16, "xb") for _ in range(2)]
            yt = [TA.sb([128, 8, 512], BF16, "yt") for _ in range(2)]
            mg, bmg = TA.sb([128, 8, 512], BF16, "mg")
            sga = [TA.sb([128, 512], F32, "sga") for _ in range(2)]
            sgb = [TA.sb([128, 512], F32, "sgb") for _ in range(2)]
            t1 = [TA.sb([128, 512], F32, "t1") for _ in range(2)]
            t2 = [TA.sb([128, 512], F32, "t2") for _ in range(2)]
            hh = [(TA.sb([128, 8, 512], F32, "hh")[0], [Buf() for _ in range(8)]) for _ in range(2)]
            psl = [TA.ps([128, 512], F32, "ps") for _ in range(8)]
            LT = ln_alloc(TA, 4)
            npz = [0]

            def nextP():
                npz[0] += 1
                return psl[npz[0] % 8]
            P.dma("sp", vB[:], vecB, w=[bvB])
            bYg = [Buf(), Buf(), Buf()]
            if isinstance(Yg, dict):
                gd = Yg
                Yg = gd["Ygl"]
                P.dma("sp", gd["jt"][:], gd["jidx"], w=[gd["bjt"]])

                def gsrc(Gt):
                    def f(e):
                        e.reg_load(gd["jr"], gd["jt"][0:1, 0:1])
                        val = e.snap(gd["jr"], min_val=0, max_val=3)
                        return Gt[bass.ds(val, 1)].rearrange("o s p t -> (o s) p t")
                    return f
                Yv = Yg.rearrange("(s p) t -> s p t", s=4)
                for o_ in (P.dma("sp", Yv[:, 128:256, :], gsrc(gd["GH"]), r=[gd["bjt"]], w=[bYg[0]]),
                           P.dma("sp", Yv[:, 0:64, :], gsrc(gd["GA"]), r=[gd["bjt"]], w=[bYg[1]]),
                           P.dma("sp", Yv[:, 64:128, :], gsrc(gd["GB"]), r=[gd["bjt"]], w=[bYg[2]])):
                    o_.extra_waits.append((GD["cc"], GD["ccn"]))
            P.dma("sp", onesf[:], cst[:, K_ONES:K_ONES + 128], w=[bonesf])
            P.op("dve", "tensor_scalar", out=conesK[:], in0=onesf[:], scalar1=1.0 / 1024.0, scalar2=None, op0=ALU.mult, r=[bonesf], w=[bconesK])
            wgv = wg.rearrange("(kc p) n -> p kc n", p=128)
            bwg_m = [Buf() for _ in range(8)]
            PRE_LOAD0 = True

            def load(i):
                tok = slice(i * 512, (i + 1) * 512)
                xv = xres.rearrange("(kc p) t -> p kc t", p=128)[:, :, tok]
                yv = Yg.rearrange("(kc p) t -> p kc t", p=128)[:, :, tok]
                P.dma("sp", xr[i % 2][0][:], xv, w=[xr[i % 2][1]])
                P.dma("pool", xb[i % 2][0][:], xv, w=[xb[i % 2][1]])
                P.dma("sp", yt[i % 2][0][:], yv, r=bYg, w=[yt[i % 2][1]])

            load(0)
            deferred = []
            for m0 in range(0, 8, 2):
                for n0 in (0, 1024):
                    P.dma("pool", wg_sb[:, :, n0 + m0 * 128:n0 + (m0 + 2) * 128], wgv[:, :, n0 + m0 * 128:n0 + (m0 + 2) * 128], w=[bwg_m[m0], bwg_m[m0 + 1]])
                if m0 == 0:
                    for kc0 in range(0, 8, 4):
                        P.dma("pool", wab_sb[:, kc0:kc0 + 4, :], wab.rearrange("(kc p) n -> p kc n", p=128)[:, kc0:kc0 + 4, :], w=[bwab])
            for kc0 in range(0, 8, 4):
                P.dma("pool", wo_sb[:, kc0:kc0 + 4, :], wo.rearrange("(kc p) n -> p kc n", p=128)[:, kc0:kc0 + 4, :], w=[bwo])
            for i in range(NT):
                if i + 1 < NT:
                    load(i + 1)
                tok = slice(i * 512, (i + 1) * 512)
                xr_t, xr_b = xr[i % 2]
                xb_t, xb_b = xb[i % 2]
                yt_t, yt_b = yt[i % 2]
                h_t, h_b = hh[i % 2]
                for m in range(8):
                    if m >= 1 and deferred:
                        deferred.pop(0)()
                    ms = slice(m * 128, (m + 1) * 128)
                    pga, pgab = nextP()
                    for kc in range(8):
                        P.op("pe", "matmul", out=pga[:], lhsT=wg_sb[:, kc, ms], rhs=xb_t[:, kc, :], start=(kc == 0), stop=(kc == 7), r=[bwg_m[m], xb_b], w=[pgab])
                    pgb, pgbb = nextP()
                    for kc in range(8):
                        P.op("pe", "matmul", out=pgb[:], lhsT=wg_sb[:, kc, 1024 + m * 128:1024 + (m + 1) * 128], rhs=xb_t[:, kc, :], start=(kc == 0), stop=(kc == 7), r=[bwg_m[m], xb_b], w=[pgbb])
                    ppa, ppab = nextP()
                    for r_ in range(4):
                        P.op("pe", "matmul", out=ppa[:], lhsT=wab_sb[:, 2 * r_, ms], rhs=yt_t[:, 2 * r_, :], start=(r_ == 0), stop=(r_ == 3), r=[bwab, yt_b], w=[ppab])
                    ppb, ppbb = nextP()
                    for r_ in range(4):
                        P.op("pe", "matmul", out=ppb[:], lhsT=wab_sb[:, 2 * r_ + 1, ms], rhs=yt_t[:, 2 * r_ + 1, :], start=(r_ == 0), stop=(r_ == 3), r=[bwab, yt_b], w=[ppbb])
                    sa, bsa = sga[m % 2]
                    sb_, bsb = sgb[m % 2]
                    a1, ba1 = t1[m % 2]
                    a2, ba2 = t2[m % 2]
                    P.op("act", "activation", out=sa[:], in_=pga[:], func=AF.Sigmoid, r=[pgab], w=[bsa])
                    P.op("act", "activation", out=sb_[:], in_=pgb[:], func=AF.Sigmoid, r=[pgbb], w=[bsb])
                    P.op("dve", "tensor_tensor", out=a1[:], in0=sa[:], in1=ppa[:], op=ALU.mult, r=[bsa, ppab], w=[ba1])
                    P.op("dve", "tensor_tensor", out=a2[:], in0=sb_[:], in1=ppb[:], op=ALU.mult, r=[bsb, ppbb], w=[ba2])
                    P.op("pool", "tensor_tensor", out=mg[:, m, :], in0=a1[:], in1=a2[:], op=ALU.add, r=[ba1, ba2], w=[bmg])
                for mo in range(8):
                    if deferred:
                        deferred.pop(0)()
                    pw, pwb = nextP()
                    for m in range(8):
                        P.op("pe", "matmul", out=pw[:], lhsT=wo_sb[:, m, mo * 128:(mo + 1) * 128], rhs=mg[:, m, :], start=(m == 0), stop=(m == 7), r=[bwo, bmg], w=[pwb])
                    P.op("dve", "scalar_tensor_tensor", out=h_t[:, mo, :], in0=xr_t[:, mo, :], scalar=float(ALPHA), in1=pw[:], op0=ALU.mult, op1=ALU.add,
                         r=[xr_b, pwb], w=[h_b[mo]])

                def wr(i=i, h_t=h_t, h_b=h_b, tok=tok):
                    P.dma("sp", X1.rearrange("(kc p) t -> p kc t", p=128)[:, :, tok], h_t[:], r=h_b, w=[dX1[i]])
                deferred.extend(layer_norm_steps(P, LT, nextP, h_t, h_b, lambda c: vB[:, c:c + 1], lambda c: vB[:, 8 + c:9 + c], bvB, conesK, bconesK, wr))
            LT["flush"] = True
            while deferred:
                deferred.pop(0)()
            P.finish()

        with contextlib.ExitStack() as st:
            P = Prog(nc, st, tag + "2")
            P.defer_cc = xcb is not None
            TA = TileAlloc(nc, st, tag + "2")
            NM = FFH // 128
            wfi_sb, bwfi = TA.sb([128, 8, 2 * FFH], BF16, "wfi")
            wfo_sb, bwfo = TA.sb([128, NM, 1024], BF16, "wfo")
            xb = [TA.sb([128, 8, 512], BF16, "xb") for _ in range(2)]
            xrc = [TA.sb([128, 512], F32, "xrc") for _ in range(2)]
            aa, baa = TA.sb([128, NM, 512], BF16, "aa")
            sg = [TA.sb([128, 512], F32, "sg") for _ in range(2)]
            hh = TA.sb([128, 8, 512], F32, "hh")[0]
            bhh = [Buf() for _ in range(8)]
            psl = [TA.ps([128, 512], F32, "ps") for _ in range(8)]
            LT = ln_alloc(TA)
            npz = [0]

            def nextP():
                npz[0] += 1
                return psl[npz[0] % 8]
            wfv = wfi.rearrange("(kc p) n -> p kc n", p=128)
            X1v = X1.rearrange("(kc p) t -> p kc t", p=128)

            def load(i):
                tok = slice(i * 512, (i + 1) * 512)
                P.dma("pool", xb[i % 2][0][:], X1v[:, :, tok], r=[dX1[i]], w=[xb[i % 2][1]])

            deferred = []
            bwfi_m = [Buf() for _ in range(NM)]
            for m0 in range(0, NM, 2):
                for n0 in (0, FFH):
                    P.dma("pool", wfi_sb[:, :, n0 + m0 * 128:n0 + (m0 + 2) * 128], wfv[:, :, n0 + m0 * 128:n0 + (m0 + 2) * 128], w=[bwfi_m[m0], bwfi_m[m0 + 1]])
                if m0 == 0:
                    load(0)
            wov = wfo.rearrange("(kc p) n -> p kc n", p=128)
            for k0 in range(0, NM, 2):
                P.dma("pool", wfo_sb[:, k0:k0 + 2, :], wov[:, k0:k0 + 2, :], w=[bwfo])
            for i in range(NT):
                if i + 1 < NT:
                    load(i + 1)
                tok = slice(i * 512, (i + 1) * 512)
                xb_t, xb_b = xb[i % 2]
                for m in range(NM):
                    if m >= 2 and deferred:
                        deferred.pop(0)()
                    pu, pub = nextP()
                    for kc in range(8):
                        P.op("pe", "matmul", out=pu[:], lhsT=wfi_sb[:, kc, m * 128:(m + 1) * 128], rhs=xb_t[:, kc, :], start=(kc == 0), stop=(kc == 7), r=[bwfi_m[m], xb_b], w=[pub])
                    pg, pgb = nextP()
                    for kc in range(8):
                        P.op("pe", "matmul", out=pg[:], lhsT=wfi_sb[:, kc, FFH + m * 128:FFH + (m + 1) * 128], rhs=xb_t[:, kc, :], start=(kc == 0), stop=(kc == 7), r=[bwfi_m[m], xb_b], w=[pgb])
                    s_, bs_ = sg[m % 2]
                    P.op("act", "activation", out=s_[:], in_=pg[:], func=AF.Silu, r=[pgb], w=[bs_])
                    P.op("dve", "tensor_tensor", out=aa[:, m, :], in0=s_[:], in1=pu[:], op=ALU.mult, r=[bs_, pub], w=[baa])
                for mo in range(8):
                    xc, bxc = xrc[mo % 2]
                    P.dma("sp", xc[:], X1v[:, mo, tok], r=[dX1[i]], w=[bxc])
                    po, pob = nextP()
                    for m in range(NM):
                        P.op("pe", "matmul", out=po[:], lhsT=wfo_sb[:, m, mo * 128:(mo + 1) * 128], rhs=aa[:, m, :], start=(m == 0), stop=(m == NM - 1), r=[bwfo, baa], w=[pob])
                    P.op("dve", "scalar_tensor_tensor", out=hh[:, mo, :], in0=xc[:], scalar=float(ALPHA), in1=po[:], op0=ALU.mult, op1=ALU.add, r=[bxc, pob], w=[bhh[mo]])

                def wr(tok=tok, i=i):
                    P.dma("sp", XO.rearrange("(kc p) t -> p kc t", p=128)[:, :, tok], hh[:], r=bhh, w=[Buf()])
                    if XOB is not None:
                        bx_ = Buf()
                        if xcb is not None:
                            P.dma("pool", XOB[i].rearrange("(kc p) t -> p kc t", p=128), hh[:], r=bhh, w=[bx_])
                            xcb(P, i, bx_)
                        else:
                            P.dma("pool", XOB.rearrange("(kc p) t -> p kc t", p=128)[:, :, tok], hh[:], r=bhh, w=[bx_])
                deferred.extend(layer_norm_steps(P, LT, nextP, hh, bhh, lambda c: vB[:, 16 + c:17 + c], lambda c: vB[:, 24 + c:25 + c], bvB, conesK, bconesK, wr))
            LT["flush"] = True
            while deferred:
                deferred.pop(0)()
            P.finish()


def build_B(R4, with_bf16_out=False):
    nc = bass.Bass("TRN2", target_bir_lowering=False)
    Yg = nc.dram_tensor("Yg", [1024, R4], BF16, kind="ExternalInput").ap()
    xres = nc.dram_tensor("xres", [1024, R4], F32, kind="ExternalInput").ap()
    wg = nc.dram_tensor("wg", [1024, 2048], F32, kind="ExternalInput").ap()
    wab = nc.dram_tensor("wab", [1024, 1024], F32, kind="ExternalInput").ap()
    wo = nc.dram_tensor("wo", [1024, 1024], F32, kind="ExternalInput").ap()
    wfi = nc.dram_tensor("wfi", [1024, 2 * FFH], F32, kind="ExternalInput").ap()
    wfo = nc.dram_tensor("wfo", [FFH, 1024], F32, kind="ExternalInput").ap()
    vecB = nc.dram_tensor("vecB", [128, 32], F32, kind="ExternalInput").ap()
    cst = nc.dram_tensor("cst", [128, CST_COLS], F32, kind="ExternalInput").ap()
    XO = nc.dram_tensor("XO", [1024, R4], F32, kind="ExternalOutput").ap()
    XOB = nc.dram_tensor("XOB", [1024, R4], BF16, kind="ExternalOutput").ap() if with_bf16_out else None
    X1 = nc.dram_tensor("b_X1", [1024, R4], F32, kind="Internal").ap()
    with contextlib.ExitStack() as gst:
        GSEM[0] = gst
        GD.clear()
        phase_B(nc, R4, Yg, xres, wg, wab, wo, wfi, wfo, vecB, cst, X1, XO, XOB, "B")
    return nc


def make_cst():
    c = np.zeros((128, CST_COLS), np.float32)
    p = np.arange(128)[:, None]
    f = np.arange(128)[None, :]
    c[:, K_ID:K_ID + 128] = (p == f)
    c[:, K_NEG:K_NEG + 128] = np.where(f < p, -30000.0, 0.0)
    c[:, K_BD:K_BD + 128] = (f >= p) & ((p // 64) == (f // 64))
    r = np.ones(512, np.float32)
    r[::64] = 0.0
    c[:, K_RESET:K_RESET + 512] = r[None, :]
    c[:, K_ONES:K_ONES + 512] = 1.0
    return c


def make_wA(w_in_l, j):
    B0 = 3 * 512 + 8
    sl = lambda base: w_in_l[:, base + 128 * j: base + 128 * (j + 1)]
    qa, ka, va = sl(0), sl(512), sl(1024)
    fa = w_in_l[:, 1536 + 2 * j: 1536 + 2 * (j + 1)]
    qb, fb, ib, gb = sl(B0), sl(B0 + 512), sl(B0 + 1024), sl(B0 + 1536)
    return np.ascontiguousarray(np.concatenate([qa, ka, qb, fb, gb, fa, va, ib], axis=1))


def make_vecA(l, j, b_fgate, hgrn_lb_logits, hgrn_norm_g):
    v = np.zeros((128, 8), np.float32)
    v[:, 0] = hgrn_lb_logits[0, 128 * j:128 * (j + 1)]
    v[:, 1] = hgrn_lb_logits[1, 128 * j:128 * (j + 1)]
    v[:, 2] = hgrn_norm_g[l]
    v[0:2, 3] = b_fgate[l, 2 * j:2 * j + 2]
    v[:, 4] = float(l)
    return v


def make_wab(wa_l, wb_l):
    out = np.empty((1024, 1024), np.float32)
    for r in range(4):
        out[256 * r:256 * r + 128] = wa_l[128 * r:128 * (r + 1)]
        out[256 * r + 128:256 * r + 256] = wb_l[128 * r:128 * (r + 1)]
    return out


def make_vecB(l, ln1_g, ln1_b, ln2_g, ln2_b):
    v = np.empty((128, 32), np.float32)
    for k, a in enumerate((ln1_g[l], ln1_b[l], ln2_g[l], ln2_b[l])):
        v[:, 8 * k:8 * k + 8] = a.reshape(8, 128).T
    return v


I32 = mybir.dt.int32
GROUPS = [[0, 1, 2, 3], [4, 5, 6, 7]]


def make_ycb(Ysc, GA, GB, GH):
    def ycb(P, part, r_, buf):
        if part < 2:
            src = Ysc[r_, part * 64:(part + 1) * 64, :]
            dst = (GA, GB)[part][r_].rearrange("s p t -> (s p) t")
        else:
            src = Ysc[r_, 128:256, :]
            dst = GH[r_].rearrange("s p t -> (s p) t")
        P.coll("AllGather", [src], [dst], GROUPS, r=[buf], w=[Buf()])
    return ycb


def exchange_Y(nc, R4, GA, GB, GH, Ygl, jidx, jt, bjt, jr, tag):
    with contextlib.ExitStack() as st:
        P = Prog(nc, st, tag)
        P.dma("sp", jt[:], jidx, w=[bjt])

        def src(Gt):
            def f(e):
                e.reg_load(jr, jt[0:1, 0:1])
                val = e.snap(jr, min_val=0, max_val=3)
                return Gt[bass.ds(val, 1)].rearrange("o s p t -> (o s) p t")
            return f
        Yv = Ygl.rearrange("(s p) t -> s p t", s=4)
        P.dma("sp", Yv[:, 0:64, :], src(GA), r=[bjt], w=[Buf()])
        P.dma("sp", Yv[:, 64:128, :], src(GB), r=[bjt], w=[Buf()])
        P.dma("sp", Yv[:, 128:256, :], src(GH), r=[bjt], w=[Buf()])
        P.finish()


def build_fused(S):
    nc = bass.Bass("TRN2", target_bir_lowering=False)
    R4 = S // 4
    ext = lambda name, shape, dt=F32: nc.dram_tensor(name, shape, dt, kind="ExternalInput").ap()
    itn = lambda name, shape, dt=BF16: nc.dram_tensor(name, shape, dt, kind="Internal").ap()
    xg = ext("xg", [4, 1024, R4])
    xres = ext("xres", [1024, R4])
    cst = ext("cst", [128, CST_COLS])
    jidx = ext("jidx", [1, 4], I32)
    W = []
    for l in range(DEPTH):
        W.append(dict(wA=ext(f"wA{l}", [1024, WA_COLS]), vecA=ext(f"vecA{l}", [128, 8]), wg=ext(f"wg{l}", [1024, 2048]),
                      wab=ext(f"wab{l}", [1024, 1024]), wo=ext(f"wo{l}", [1024, 1024]), wfi=ext(f"wfi{l}", [1024, 2 * FFH]),
                      wfo=ext(f"wfo{l}", [FFH, 1024]), vecB=ext(f"vecB{l}", [128, 32])))
    OUT = nc.dram_tensor("OUT", [1024, R4], F32, kind="ExternalOutput").ap()
    scr = alloc_scratch_A(nc, S, "a")
    Ysc = itn("Ysc", [4, 256, R4])
    GA = itn("GA", [4, 4, 64, R4])
    GB = itn("GB", [4, 4, 64, R4])
    GH = itn("GH", [4, 4, 128, R4])
    NTB = R4 // 512
    XOBt = itn("XOBt", [NTB, 1024, 512])
    XG1t = itn("XG1t", [NTB, 4, 1024, 512])
    Ygl = itn("Ygl", [1024, R4])
    X1 = itn("X1", [1024, R4], F32)
    XO0 = itn("XO0", [1024, R4], F32)
    xsrc1 = lambda r_, c0: XG1t[c0 // 512, r_].rearrange("(kc p) t -> p kc t", p=128)
    Yw = lambda rows, i: Ysc[(i * 512) // R4, rows, (i * 512) % R4:(i * 512) % R4 + 512]
    with contextlib.ExitStack() as gst:
        GSEM[0] = gst
        GD.clear()
        jt = gst.enter_context(nc.sbuf_tensor("jt", [1, 4], I32))
        bjt = Buf()
        jr = gst.enter_context(nc.sync.register("jr"))
        ycb = make_ycb(Ysc, GA, GB, GH)

        def xcb(P, i, buf):
            P.coll("AllGather", [XOBt[i]], [XG1t[i].rearrange("s f t -> (s f) t")], GROUPS, r=[buf], w=[Buf()])
        for l in range(DEPTH):
            w = W[l]
            phase_A(nc, S, xg if l == 0 else xsrc1, w["wA"], w["vecA"], cst, Yw, scr, f"A{l}", ycb=ycb)
            last = (l == DEPTH - 1)
            if not last:
                xbase = GD["ccn"]
                xsrc1.ccwait = lambda ti, xbase=xbase: (GD["cc"], xbase + min(ti + 2, NTB))
            phase_B(nc, R4, dict(GA=GA, GB=GB, GH=GH, Ygl=Ygl, jr=jr, jt=jt, bjt=bjt, jidx=jidx), xres if l == 0 else XO0, w["wg"], w["wab"], w["wo"], w["wfi"], w["wfo"], w["vecB"], cst, X1,
                    OUT if last else XO0, None if last else XOBt, f"B{l}", xcb=None if last else xcb)
    return nc


def kernel(x, w_in, b_fgate, hgrn_lb_logits, hgrn_norm_g, w_branch_a, w_branch_b, w_out,
           ln1_g, ln1_b, w_ff_in, w_ff_out, ln2_g, ln2_b):
    x = np.asarray(x, np.float32)
    Bn, S, _ = x.shape
    R4 = S // 4
    f = lambda a: np.asarray(a, np.float32)
    w_in, b_fgate, hgrn_lb_logits, hgrn_norm_g = f(w_in), f(b_fgate), f(hgrn_lb_logits), f(hgrn_norm_g)
    w_branch_a, w_branch_b, w_out = f(w_branch_a), f(w_branch_b), f(w_out)
    ln1_g, ln1_b, w_ff_in, w_ff_out, ln2_g, ln2_b = f(ln1_g), f(ln1_b), f(w_ff_in), f(w_ff_out), f(ln2_g), f(ln2_b)
    cst = make_cst()
    cores = list(range(8))
    xg = [np.ascontiguousarray(x[b].reshape(4, R4, D).transpose(0, 2, 1)) for b in range(Bn)]
    B0 = 3 * 512 + 8 + 4 * 512
    shared = {}
    for l in range(DEPTH):
        shared[f"wg{l}"] = np.ascontiguousarray(w_in[l][:, B0:B0 + 2048])
        shared[f"wab{l}"] = make_wab(w_branch_a[l], w_branch_b[l])
        shared[f"wo{l}"] = w_out[l]
        shared[f"wfi{l}"] = w_ff_in[l]
        shared[f"wfo{l}"] = w_ff_out[l]
        shared[f"vecB{l}"] = make_vecB(l, ln1_g, ln1_b, ln2_g, ln2_b)
    wAs = {(l, j): make_wA(w_in[l], j) for l in range(DEPTH) for j in range(4)}
    in_maps = []
    for c in cores:
        b, j = divmod(c, 4)
        m = dict(xg=xg[b], xres=np.ascontiguousarray(xg[b][j]), cst=cst, jidx=np.array([[j, 0, 0, 0]], np.int32))
        for l in range(DEPTH):
            m[f"wA{l}"] = wAs[(l, j)]
            m[f"vecA{l}"] = make_vecA(l, j, b_fgate, hgrn_lb_logits, hgrn_norm_g)
        m.update(shared)
        in_maps.append(m)
    nc = build_fused(S)
    res = run_bass_kernel_spmd(nc, in_maps, core_ids=cores)
    out = np.empty((Bn, S, D), np.float32)
    for c in cores:
        b, j = divmod(c, 4)
        out[b, j * R4:(j + 1) * R4, :] = np.asarray(res.results[c]["OUT"]).T
    return out
```

```python
import contextlib
import os
import numpy as np
import ml_dtypes
import concourse.bass as bass
import concourse.mybir as mybir
from concourse.bass_utils import run_bass_kernel_spmd

F32 = mybir.dt.float32
BF16 = mybir.dt.bfloat16
AF = mybir.ActivationFunctionType
ALU = mybir.AluOpType
NPBF = ml_dtypes.bfloat16

D = 1024
SEQ = 16384
DEPTH = 2
FFH = 2816
ALPHA = (2 * DEPTH) ** 0.25
LN_EPS = 1e-5
RMS_EPS = 1e-6
NDMA = 6
GSEM = [None]
GD = {}


class Buf:
    __slots__ = ("name", "writers", "readers")

    def __init__(self, name=""):
        self.name = name
        self.writers = []
        self.readers = []


class Op:
    __slots__ = ("eng", "fn", "deps", "dma", "is_ms", "ms", "extra_waits", "prog")

    def __init__(self, eng, fn):
        self.eng = eng
        self.fn = fn
        self.deps = []
        self.dma = None
        self.is_ms = False
        self.ms = 0
        self.extra_waits = []


class Prog:
    ENGS = ("pe", "act", "dve", "pool", "sp")

    def __init__(self, nc, stack, tag):
        self.nc = nc
        self.tag = tag
        self.q = {e: [] for e in self.ENGS}
        stack = GSEM[0]
        self.psem = {e: stack.enter_context(nc.semaphore(f"{tag}_p_{e}")) for e in self.ENGS}
        if "dsem" not in GD:
            GD["dsem"] = {e: [stack.enter_context(nc.semaphore(f"gd_{e}{i}")) for i in range(NDMA)] for e in ("sp", "pool")}
            GD["dcount"] = {"sp": 0, "pool": 0}
            GD["cc"] = stack.enter_context(nc.semaphore("gcc"))
            GD["ccn"] = 0
        self.dsem = GD["dsem"]
        self.dcount = GD["dcount"]

    def _track(self, op, r, w):
        deps = []
        for b in r:
            deps += b.writers
        for b in w:
            deps += b.writers
            deps += b.readers
        op.deps = [d for d in deps if d.prog is self]
        for b in r:
            if op.dma is None:
                b.readers = [x for x in b.readers if not (x.eng == op.eng and x.dma is None)]
            b.readers.append(op)
        for b in w:
            b.writers = [op]
            b.readers = []

    def op(self, eng, name, r=(), w=(), **kw):
        o = Op(eng, (name, kw))
        o.prog = self
        self._track(o, r, w)
        self.q[eng].append(o)
        return o

    def dma(self, eng, out, in_, r=(), w=()):
        o = Op(eng, ("dma_start", dict(out=out, in_=in_)))
        o.prog = self
        n = self.dcount[eng]
        self.dcount[eng] += 1
        sem = self.dsem[eng][n % NDMA]
        gen = n // NDMA
        o.dma = (sem, 16 * (gen + 1), 16)
        if gen > 0:
            o.extra_waits.append((sem, 16 * gen))
        self._track(o, r, w)
        self.q[eng].append(o)
        return o

    def coll(self, kind, ins, outs, groups, r=(), w=()):
        o = Op("pool", ("collective_compute", dict(kind=kind, op=ALU.bypass, replica_groups=groups, ins=ins, outs=outs)))
        o.prog = self
        GD["ccn"] += 1
        o.dma = (GD["cc"], GD["ccn"], 1)
        self._track(o, r, w)
        self.q["pool"].append(o)
        return o

    def finish(self):
        nc = self.nc
        lasts = []
        for e in self.ENGS:
            if self.q[e]:
                for o in reversed(self.q[e]):
                    if o.dma is None:
                        lasts.append(o)
                        break
        dma_final = []
        for e in ("sp", "pool"):
            n = self.dcount[e]
            for i in range(min(n, NDMA)):
                cnt = (n - 1 - i) // NDMA + 1
                dma_final.append((self.dsem[e][i], 16 * cnt))
        for e in self.ENGS:
            for o in self.q[e]:
                for d in o.deps:
                    if d.dma is None and (d.eng != e or e != "pe"):
                        d.is_ms = True
        for o in lasts:
            o.is_ms = True
        for e in self.ENGS:
            c = 0
            for o in self.q[e]:
                if o.dma is None and o.is_ms:
                    c += 1
                    o.ms = c
        psem = self.psem

        def mk(eng):
            def body(e):
                waited = {}

                def wait(sem, v):
                    k = id(sem)
                    if waited.get(k, 0) >= v:
                        return
                    waited[k] = v
                    e.wait_ge(sem, v)

                for o in self.q[eng]:
                    need = {}
                    for d in o.deps:
                        if d.dma is not None:
                            s, v = d.dma[0], d.dma[1]
                        elif d.eng == eng and eng == "pe":
                            continue
                        else:
                            s, v = psem[d.eng], d.ms
                        k = id(s)
                        if k not in need or need[k][1] < v:
                            need[k] = (s, v)
                    for s, v in o.extra_waits:
                        k = id(s)
                        if k not in need or need[k][1] < v:
                            need[k] = (s, v)
                    for s, v in need.values():
                        wait(s, v)
                    kw = {k: (v(e) if callable(v) else v) for k, v in o.fn[1].items()}
                    ins = getattr(e, o.fn[0])(**kw)
                    if o.dma is not None:
                        ins.then_inc(o.dma[0], o.dma[2])
                    elif o.is_ms:
                        ins.then_inc(psem[eng], 1)
                for o in lasts:
                    if o.eng != eng:
                        wait(psem[o.eng], o.ms)
                for s, v in dma_final:
                    wait(s, v)
                if GD["ccn"] > 0 and not getattr(self, "defer_cc", False):
                    wait(GD["cc"], GD["ccn"])

            return body

        with nc.Block() as block:
            block.tensor(mk("pe"))
            block.scalar(mk("act"))
            block.vector(mk("dve"))
            block.gpsimd(mk("pool"))
            block.sync(mk("sp"))


class TileAlloc:
    def __init__(self, nc, stack, tag):
        self.nc = nc
        self.stack = stack
        self.tag = tag
        self.n = 0

    def sb(self, shape, dt, name="t"):
        self.n += 1
        t = self.stack.enter_context(self.nc.sbuf_tensor(f"{self.tag}_{name}{self.n}", list(shape), dt))
        return t, Buf(name)

    def ps(self, shape, dt, name="p"):
        self.n += 1
        t = self.stack.enter_context(self.nc.psum_tensor(f"{self.tag}_{name}{self.n}", list(shape), dt))
        return t, Buf(name)


WA_COLS = 898
C_QA, C_KA, C_QB, C_FB, C_GB, C_FA, C_VA = 0, 128, 256, 384, 512, 640, 642
CST_COLS = 1408
K_ID, K_NEG, K_BD, K_RESET, K_ONES = 0, 128, 256, 384, 896


def phase_A(nc, S, xg, wA, vecA, cst, Yw, scr, tag, ycb=None):
    NT = S // 512
    R4 = S // 4
    QT, KT, VA, QE, KE, KD, VB, GS = (scr[k] for k in ("QT", "KT", "VA", "QE", "KE", "KD", "VB", "GS"))

    with contextlib.ExitStack() as st0:
        TA0 = TileAlloc(nc, st0, tag + "g")
        A_sb, bA = TA0.sb([128, S // 64], F32, "Adec")
        cid, bcid = TA0.sb([128, 128], BF16, "ident")
        cneg, bcneg = TA0.sb([128, 128], BF16, "neg")
        cbd, bcbd = TA0.sb([128, 128], BF16, "bdtri")
        cones, bcones = TA0.sb([128, 128], BF16, "ones")
        conesf, bconesf = TA0.sb([128, 64], F32, "onesf")
        vec, bvec = TA0.sb([128, 8], F32, "vec")
        lbv, blbv = TA0.sb([128, 8], F32, "lbv")

        with contextlib.ExitStack() as st:
            P = Prog(nc, st, tag + "1")
            TA = TileAlloc(nc, st, tag + "1")
            dQT = [[Buf() for _ in range(NT)] for _ in range(2)]
            dKT = [[Buf() for _ in range(NT)] for _ in range(2)]
            dVA = [Buf() for _ in range(NT)]
            dH = [Buf() for _ in range(NT)]
            wsb, bw = TA.sb([128, 8, WA_COLS], BF16, "wA")
            creset, bcreset = TA.sb([128, 512], F32, "reset")
            xt = [TA.sb([128, 8, 512], BF16, "xt") for _ in range(2)]
            psF = [TA.ps([128, 512], F32, "psF") for _ in range(5)]
            psT = [TA.ps([128, 512], F32, "psT") for _ in range(2)]
            psX = TA.ps([128, 512], BF16, "psX")
            nF = [0]

            def nextF():
                nF[0] += 1
                return psF[nF[0] % 5]
            stq = [TA.sb([128, 512], BF16, "stq") for _ in range(2)]
            stk = [TA.sb([128, 512], BF16, "stk") for _ in range(2)]
            stv = [TA.sb([128, 4, 256], BF16, "stv") for _ in range(2)]
            sqe = [TA.sb([128, 512], BF16, "sqe") for _ in range(2)]
            ske = [TA.sb([128, 512], BF16, "ske") for _ in range(2)]
            skdT = [TA.sb([128, 512], BF16, "skdT") for _ in range(2)]
            skd = [TA.sb([128, 4, 128], BF16, "skd") for _ in range(2)]
            sgs = [TA.sb([128, 512], BF16, "sgs") for _ in range(2)]
            wf = [TA.sb([128, 512], F32, "wf") for _ in range(24)]
            fz = [TA.sb([2, 512], F32, "fz") for _ in range(6)]
            Fc = [TA.sb([2, 512], F32, "Fc") for _ in range(2)]
            fst = [TA.sb([2, 3, 512], BF16, "fst") for _ in range(2)]
            negrow, bnegrow = TA.sb([2, 3, 512], BF16, "negrow")
            fr = [TA.sb([2, 512], F32, "fr") for _ in range(4)]
            onesrow, bonesrow = TA.sb([2, 3, 512], BF16, "onesrow")

            wv = wA.rearrange("(kc p) n -> p kc n", p=128)
            for kc0 in range(0, 8, 2):
                P.dma("pool", wsb[:, kc0:kc0 + 2, :], wv[:, kc0:kc0 + 2, :], w=[bw])
            P.dma("pool", cid[:], cst[:, K_ID:K_ID + 128], w=[bcid])
            P.dma("pool", cneg[:], cst[:, K_NEG:K_NEG + 128], w=[bcneg])
            P.dma("pool", cbd[:], cst[:, K_BD:K_BD + 128], w=[bcbd])
            P.dma("sp", creset[:], cst[:, K_RESET:K_RESET + 512], w=[bcreset])
            P.dma("pool", cones[:], cst[:, K_ONES:K_ONES + 128], w=[bcones])
            P.dma("sp", conesf[:], cst[:, K_ONES:K_ONES + 64], w=[bconesf])
            P.dma("sp", vec[:], vecA, w=[bvec])
            P.op("pool", "memset", ap=onesrow[:], constant=1.0, w=[bonesrow])
            P.op("pool", "memset", ap=negrow[:], constant=-1.0, w=[bnegrow])
            c_ = lambda i: lbv[:, i:i + 1]
            rw = dict(r=[blbv, bvec], w=[blbv])
            P.op("dve", "tensor_tensor", out=c_(0), in0=vec[:, 0:1], in1=vec[:, 1:2], op=ALU.max, **rw)
            P.op("dve", "tensor_tensor", out=c_(1), in0=vec[:, 0:1], in1=c_(0), op=ALU.subtract, **rw)
            P.op("dve", "tensor_tensor", out=c_(2), in0=vec[:, 1:2], in1=c_(0), op=ALU.subtract, **rw)
            P.op("act", "activation", out=lbv[:, 1:3], in_=lbv[:, 1:3], func=AF.Exp, **rw)
            P.op("dve", "tensor_tensor", out=c_(3), in0=c_(1), in1=c_(2), op=ALU.add, **rw)
            P.op("dve", "reciprocal", out=c_(4), in_=c_(3), **rw)
            P.op("dve", "tensor_tensor", out=c_(5), in0=c_(1), in1=c_(4), op=ALU.mult, **rw)
            P.op("dve", "tensor_tensor", out=c_(6), in0=c_(2), in1=c_(4), op=ALU.mult, **rw)
            P.op("dve", "scalar_tensor_tensor", out=c_(7), in0=c_(6), scalar=vec[:, 4:5], in1=c_(5), op0=ALU.mult, op1=ALU.add, **rw)
            P.op("dve", "tensor_tensor", out=c_(0), in0=c_(7), in1=c_(5), op=ALU.subtract, **rw)
            P.op("dve", "tensor_scalar", out=c_(1), in0=c_(0), scalar1=-1.0, scalar2=1.0, op0=ALU.mult, op1=ALU.add, **rw)
            P.op("dve", "tensor_scalar", out=c_(2), in0=vec[:, 3:4], scalar1=-1.0, scalar2=None, op0=ALU.mult, **rw)
            LB, OML, NBF = lbv[:, 0:1], lbv[:, 1:2], lbv[0:2, 2:3]

            for h in range(2):
                for i in range(NT):
                    tok = slice(i * 512, (i + 1) * 512)
                    P.dma("sp", QT[h:h + 1, 67:70, tok], negrow[0:1, :, :], r=[bnegrow], w=[dQT[h][i]])
                    P.dma("sp", KT[h:h + 1, 64:67, tok], onesrow[0:1, :, :], r=[bonesrow], w=[dKT[h][i]])

            def load_x(i):
                t, b = xt[i % 2]
                r_, c0 = divmod(i * 512, R4)
                src = xg(r_, c0) if callable(xg) else xg[r_].rearrange("(kc p) t -> p kc t", p=128)[:, :, c0:c0 + 512]
                for kc0 in range(0, 8, 4):
                    o_ = P.dma("pool", t[:, kc0:kc0 + 4, :], src[:, kc0:kc0 + 4, :], w=[b])
                    if callable(xg) and hasattr(xg, "ccwait"):
                        o_.extra_waits.append(xg.ccwait(c0 // 512))

            load_x(0)
            Fprev = [None]

            def tiles(i):
                s3 = i % 3
                return wf[8 * s3:8 * s3 + 8], fz[2 * s3:2 * s3 + 2]

            def stA(i):
                if i + 1 < NT:
                    load_x(i + 1)
                xs, bx = xt[i % 2]
                tok = slice(i * 512, (i + 1) * 512)
                s2 = i % 2
                ((sig, bsig), (g_, bg), (bb, bbb), (eb, beb), (enb, benb), (ebr, bebr), (kk, bkk), (qs, bqs)), ((z0, bz0), (z1, bz1)) = tiles(i)

                def fgroup(col, M):
                    pt, pb = nextF()
                    for kc in range(8):
                        P.op("pe", "matmul", out=pt[0:M, :], lhsT=wsb[:, kc, col:col + M], rhs=xs[:, kc, :], start=(kc == 0), stop=(kc == 7),
                             r=[bw, bx], w=[pb])
                    return pt, pb
                pt, pb = fgroup(C_QA, 128)
                t, b = stq[s2]
                P.op("dve", "tensor_scalar", out=t[:], in0=pt[:], scalar1=0.125, scalar2=None, op0=ALU.mult, r=[pb], w=[b])
                for h in range(2):
                    P.dma("sp", QT[h, 0:64, tok], t[h * 64:(h + 1) * 64, :], r=[b], w=[dQT[h][i]])
                pt, pb = fgroup(C_KA, 128)
                t, b = stk[s2]
                P.op("dve", "tensor_copy", out=t[:], in_=pt[:], r=[pb], w=[b])
                for h in range(2):
                    P.dma("sp", KT[h, 0:64, tok], t[h * 64:(h + 1) * 64, :], r=[b], w=[dKT[h][i]])
                pt, pb = fgroup(C_FB, 128)
                P.op("act", "activation", out=sig[:], in_=pt[:], func=AF.Sigmoid, r=[pb], w=[bsig])
                pt, pb = fgroup(C_GB, 128)
                t, b = sgs[s2]
                P.op("act", "activation", out=t[:], in_=pt[:], func=AF.Sigmoid, r=[pb], w=[b])
                P.dma("sp", GS[:, tok], t[:], r=[b], w=[dH[i]])
                pt, pb = fgroup(C_QB, 128)
                P.op("act", "activation", out=qs[:], in_=pt[:], func=AF.Silu, r=[pb], w=[bqs])
                pt, pb = fgroup(C_FA, 2)
                P.op("act", "activation", out=z0[:], in_=pt[0:2, :], func=AF.Exp, scale=-1.0, bias=NBF, r=[pb, blbv], w=[bz0])
                t, b = stv[s2]
                for sb_ in range(4):
                    pt, pb = psT[sb_ % 2]
                    for kc in range(8):
                        P.op("pe", "matmul", out=pt[:, 0:256], lhsT=xs[:, kc, sb_ * 128:(sb_ + 1) * 128], rhs=wsb[:, kc, C_VA:C_VA + 256],
                             start=(kc == 0), stop=(kc == 7), r=[bw, bx], w=[pb])
                    P.op("act", "copy", out=t[:, sb_, :], in_=pt[:, 0:256], r=[pb], w=[b])
                P.dma("sp", VA[tok, :].rearrange("(a p) d -> p a d", p=128), t[:, :, 0:128], r=[b], w=[dVA[i]])
                P.dma("sp", VB[tok, :].rearrange("(a p) d -> p a d", p=128), t[:, :, 128:256], r=[b], w=[dH[i]])
                P.op("dve", "tensor_scalar", out=sig[:], in0=sig[:], scalar1=OML, scalar2=LB, op0=ALU.mult, op1=ALU.add, r=[bsig, blbv], w=[bsig])

            def stB1(i):
                tok = slice(i * 512, (i + 1) * 512)
                s2 = i % 2
                ((sig, bsig), (g_, bg), (bb, bbb), (eb, beb), (enb, benb), (ebr, bebr), (kk, bkk), (qs, bqs)), ((z0, bz0), (z1, bz1)) = tiles(i)
                P.op("act", "activation", out=g_[:], in_=sig[:], func=AF.Ln, r=[bsig], w=[bg])
                P.op("act", "activation", out=z1[:], in_=z0[:], func=AF.Ln, bias=1.0, r=[bz0], w=[bz1])
                P.op("pool", "tensor_scalar", out=kk[:], in0=sig[:], scalar1=-1.0, scalar2=1.0, op0=ALU.mult, op1=ALU.add, r=[bsig], w=[bkk])
                for c in range(8):
                    cs = slice(c * 64, (c + 1) * 64)
                    P.op("dve", "tensor_tensor_scan", out=bb[:, cs], data0=g_[:, cs], data1=g_[:, cs], initial=0.0, op0=ALU.add, op1=ALU.bypass, r=[bg], w=[bbb])
                Fc_t, Fc_b = Fc[s2]
                Fp = Fprev[0]
                init = 0.0 if Fp is None else Fp[0][:, 511:512]
                rr = [bz1] + ([] if Fp is None else [Fp[1]])
                P.op("dve", "tensor_scalar", out=z1[:], in0=z1[:], scalar1=-1.0, scalar2=None, op0=ALU.mult, r=[bz1], w=[bz1])
                P.op("dve", "tensor_tensor_scan", out=Fc_t[:], data0=z1[:], data1=z1[:], initial=init, op0=ALU.add, op1=ALU.bypass, r=rr, w=[Fc_b])
                Fprev[0] = (Fc_t, Fc_b)
                ft, fb = fst[s2]
                (r1, br1), (r2, br2) = fr[2 * s2:2 * s2 + 2]
                P.op("dve", "tensor_copy", out=ft[:, 0, :], in_=Fc_t[:], r=[Fc_b], w=[fb])
                P.op("dve", "tensor_tensor", out=r1[:], in0=Fc_t[:], in1=ft[:, 0, :], op=ALU.subtract, r=[Fc_b, fb], w=[br1])
                P.op("dve", "tensor_copy", out=ft[:, 1, :], in_=r1[:], r=[br1], w=[fb])
                P.op("dve", "tensor_tensor", out=r2[:], in0=r1[:], in1=ft[:, 1, :], op=ALU.subtract, r=[br1, fb], w=[br2])
                P.op("dve", "tensor_copy", out=ft[:, 2, :], in_=r2[:], r=[br2], w=[fb])
                for h in range(2):
                    P.dma("sp", QT[h:h + 1, 64:67, tok], ft[h:h + 1, 0:3, :], r=[fb], w=[dQT[h][i]])
                    P.dma("sp", KT[h:h + 1, 67:70, tok], ft[h:h + 1, 0:3, :], r=[fb], w=[dKT[h][i]])

            def stB2(i):
                tok = slice(i * 512, (i + 1) * 512)
                s2 = i % 2
                ((sig, bsig), (g_, bg), (bb, bbb), (eb, beb), (enb, benb), (ebr, bebr), (kk, bkk), (qs, bqs)), _ = tiles(i)
                P.op("act", "activation", out=eb[:], in_=bb[:], func=AF.Exp, r=[bbb], w=[beb])
                P.op("act", "activation", out=enb[:], in_=bb[:], func=AF.Exp, scale=-1.0, r=[bbb], w=[benb])
                for c in range(8):
                    cs = slice(c * 64, (c + 1) * 64)
                    last = slice(c * 64 + 63, c * 64 + 64)
                    P.op("act", "activation", out=ebr[:, cs], in_=bb[:, cs], func=AF.Exp, scale=-1.0, bias=bb[:, last], r=[bbb], w=[bebr])
                    P.op("pool", "tensor_copy", out=A_sb[:, i * 8 + c:i * 8 + c + 1], in_=eb[:, last], r=[beb], w=[bA])
                t, b = ske[s2]
                P.op("dve", "tensor_tensor", out=t[:], in0=kk[:], in1=enb[:], op=ALU.mult, r=[bkk, benb], w=[b])
                P.dma("sp", KE[:, tok], t[:], r=[b], w=[dH[i]])
                t, b = sqe[s2]
                P.op("pool", "tensor_tensor", out=t[:], in0=qs[:], in1=eb[:], op=ALU.mult, r=[bqs, beb], w=[b])
                P.dma("sp", QE[:, tok], t[:], r=[b], w=[dH[i]])
                tT, bT = skdT[s2]
                P.op("dve", "tensor_tensor", out=tT[:], in0=kk[:], in1=ebr[:], op=ALU.mult, r=[bkk, bebr], w=[bT])

            def stB3(i):
                tok = slice(i * 512, (i + 1) * 512)
                s2 = i % 2
                tT, bT = skdT[s2]
                for sb_ in range(4):
                    bs = slice(sb_ * 128, (sb_ + 1) * 128)
                    P.op("pe", "transpose", out=psX[0][:, bs], in_=tT[:, bs], identity=cid[:], r=[bT, bcid], w=[psX[1]])
                t, b = skd[s2]
                P.op("dve", "tensor_copy", out=t[:].rearrange("p a d -> p (a d)"), in_=psX[0][:], r=[psX[1]], w=[b])
                P.dma("sp", KD[tok, :].rearrange("(a p) d -> p a d", p=128), t[:], r=[b], w=[dH[i]])

            for t_ in range(NT + 3):
                if t_ < NT:
                    stA(t_)
                if 0 <= t_ - 1 < NT:
                    stB1(t_ - 1)
                if 0 <= t_ - 2 < NT:
                    stB2(t_ - 2)
                if 0 <= t_ - 3 < NT:
                    stB3(t_ - 3)
            P.finish()
        if os.environ.get("STOP_AFTER") == "1":
            return

        with contextlib.ExitStack() as st:
            P = Prog(nc, st, tag + "2")
            P.defer_cc = ycb is not None
            TA = TileAlloc(nc, st, tag + "2")
            NB = S // 128
            dYp = {(p_, r_): Buf() for p_ in range(3) for r_ in range(4)}
            qt, bqt = TA.sb([70, S], BF16, "qt")
            kt, bkt = TA.sb([70, S], BF16, "kt")
            vv, bvv = TA.sb([128, NB, 65], BF16, "vv")
            NS, NP_ = 4, 5
            psS = [TA.ps([128, 512], F32, "psS") for _ in range(NS)]
            psO = [TA.ps([128, 512], F32, "psO") for _ in range(2)]
            psX = TA.ps([128, 512], F32, "psX")
            psU = TA.ps([128, 512], F32, "psU")
            psOo = psU
            pT = [TA.sb([128, 512], BF16, "pT") for _ in range(NP_)]
            osb = [TA.sb([64, 512], F32, "osb") for _ in range(2)]
            rden = [TA.sb([128, 512], F32, "rden") for _ in range(2)]
            yst = [TA.sb([64, 512], BF16, "yst") for _ in range(2)]
            SEG = min(2048, S)
            NSEG = S // SEG
            seg = []
            for k in range(2):
                seg.append(dict(
                    qe=TA.sb([128, SEG], BF16, "qe"), ke=TA.sb([128, SEG], BF16, "ke"),
                    kd=TA.sb([64, SEG // 64, 128], BF16, "kd"), vb=TA.sb([64, SEG // 64, 128], BF16, "vb"),
                    gs=TA.sb([128, SEG], BF16, "gs")))
            state = [TA.sb([128, 128], F32, "state") for _ in range(2)]
            sbf = [TA.sb([128, 8, 128], BF16, "sbf") for _ in range(2)]
            at = [TA.sb([128, 512], BF16, "at") for _ in range(2)]
            o_sb = [TA.sb([128, 512], F32, "o_sb") for _ in range(2)]
            sq = [TA.sb([128, 512], BF16, "sq") for _ in range(2)]
            rstd = [TA.sb([128, 512], F32, "rstd") for _ in range(2)]
            yb = [TA.sb([128, 512], BF16, "yb") for _ in range(2)]
            P.op("dve", "memset", ap=state[0][0][:], constant=0.0, w=[state[0][1]])
            P.op("dve", "memset", ap=sbf[0][0][:, 0, :], constant=0.0, w=[sbf[0][1]])

            def load_seg(k):
                sgm = seg[k % 2]
                c0 = k * SEG
                P.dma("sp", sgm["qe"][0][:], QE[:, c0:c0 + SEG], w=[sgm["qe"][1]])
                P.dma("pool", sgm["ke"][0][:], KE[:, c0:c0 + SEG], w=[sgm["ke"][1]])
                P.dma("sp", sgm["gs"][0][:], GS[:, c0:c0 + SEG], w=[sgm["gs"][1]])
                P.dma("pool", sgm["kd"][0][:], KD[c0:c0 + SEG, :].rearrange("(a p) d -> p a d", p=64), w=[sgm["kd"][1]])
                P.dma("sp", sgm["vb"][0][:], VB[c0:c0 + SEG, :].rearrange("(a p) d -> p a d", p=64), w=[sgm["vb"][1]])

            cur = [0]

            def hgrn_stages(i):
                k, ti = divmod(i * 512, SEG)
                sgm = seg[k % 2]
                qe, bqe = sgm["qe"]
                ke, bke = sgm["ke"]
                kd, bkd = sgm["kd"]
                vb, bvb = sgm["vb"]
                gs, bgs = sgm["gs"]
                s2 = i % 2
                sb_t, sb_b = sbf[s2]
                nsb_t, nsb_b = sbf[1 - s2]
                pu, pub = psU
                pa, pab = psU
                po, pob = psOo
                at_t, at_b = at[s2]
                o_t, o_b = o_sb[s2]
                sq_t, sq_b = sq[s2]
                rs_t, rs_b = rstd[s2]
                yb_t, yb_b = yb[s2]

                def st_load():
                    if ti == 0:
                        load_seg(k)

                def st_u(c0):
                    def f():
                        for c in range(c0, c0 + 4):
                            ch = (ti + c * 64) // 64
                            P.op("pe", "matmul", out=pu[:, (c % 4) * 128:(c % 4 + 1) * 128], lhsT=kd[:, ch, :], rhs=vb[:, ch, :],
                                 start=True, stop=True, r=[bkd, bvb], w=[pub])
                    return f

                def st_state(c0):
                    def f():
                        for c in range(c0, c0 + 4):
                            so_t, so_b = state[cur[0]]
                            sn_t, sn_b = state[1 - cur[0]]
                            P.op("dve", "scalar_tensor_tensor", out=sn_t[:], in0=so_t[:], scalar=A_sb[:, i * 8 + c:i * 8 + c + 1],
                                 in1=pu[:, (c % 4) * 128:(c % 4 + 1) * 128], op0=ALU.mult, op1=ALU.add, r=[so_b, bA, pub], w=[sn_b])
                            if c < 7:
                                P.op("pool", "tensor_copy", out=sb_t[:, c + 1, :], in_=sn_t[:], r=[sn_b], w=[sb_b])
                            else:
                                P.op("pool", "tensor_copy", out=nsb_t[:, 0, :], in_=sn_t[:], r=[sn_b], w=[nsb_b])
                            cur[0] = 1 - cur[0]
                    return f

                def st_attn():
                    for c in range(8):
                        tk = slice(ti + c * 64, ti + (c + 1) * 64)
                        P.op("pe", "matmul", out=pa[0:64, c * 64:(c + 1) * 64], lhsT=ke[:, tk], rhs=qe[:, tk], start=True, stop=True, r=[bke, bqe], w=[pab])

                def st_mask():
                    for c in range(8):
                        cs = slice(c * 64, (c + 1) * 64)
                        P.op("dve", "tensor_tensor", out=at_t[0:64, cs], in0=pa[0:64, cs], in1=cbd[0:64, 0:64], op=ALU.mult, r=[pab, bcbd], w=[at_b])

                def st_o():
                    for c in range(8):
                        ch = (ti + c * 64) // 64
                        cs = slice(c * 64, (c + 1) * 64)
                        P.op("pe", "matmul", out=po[:, cs], lhsT=vb[:, ch, :], rhs=at_t[0:64, cs], start=True, stop=False, r=[bvb, at_b], w=[pob])
                        P.op("pe", "matmul", out=po[:, cs], lhsT=sb_t[:, c, :], rhs=qe[:, ti + c * 64:ti + (c + 1) * 64],
                             start=False, stop=True, r=[sb_b, bqe], w=[pob])

                def st_sq():
                    P.op("dve", "tensor_copy", out=o_t[:], in_=po[:], r=[pob], w=[o_b])
                    P.op("pool", "tensor_tensor", out=sq_t[:], in0=o_t[:], in1=o_t[:], op=ALU.mult, r=[o_b], w=[sq_b])

                def st_m():
                    P.op("pe", "matmul", out=pa[:], lhsT=cones[:], rhs=sq_t[:], start=True, stop=True, r=[bcones, sq_b], w=[pab])

                def st_rs():
                    P.op("dve", "tensor_scalar", out=rs_t[:], in0=pa[:], scalar1=1.0 / 128.0, scalar2=RMS_EPS, op0=ALU.mult, op1=ALU.add, r=[pab], w=[rs_b])

                def st_sqrt():
                    P.op("act", "activation", out=rs_t[:], in_=rs_t[:], func=AF.Sqrt, r=[rs_b], w=[rs_b])

                def st_fin():
                    P.op("dve", "reciprocal", out=rs_t[:], in_=rs_t[:], r=[rs_b], w=[rs_b])
                    P.op("pool", "tensor_tensor", out=o_t[:], in0=o_t[:], in1=rs_t[:], op=ALU.mult, r=[o_b, rs_b], w=[o_b])
                    P.op("dve", "scalar_tensor_tensor", out=yb_t[:], in0=o_t[:], scalar=vec[:, 2:3], in1=gs[:, ti:ti + 512], op0=ALU.mult, op1=ALU.mult,
                         r=[o_b, bvec, bgs], w=[yb_b])
                    TPC = (S // 4) // 512
                    P.dma("sp", Yw(slice(128, 256), i), yb_t[:], r=[yb_b], w=[dYp[(2, i // TPC)]])
                    if ycb is not None and i % TPC == TPC - 1:
                        ycb(P, 2, i // TPC, dYp[(2, i // TPC)])
                return [st_load, st_u(0), st_state(0), st_u(4), st_state(4), st_attn, st_mask, st_o, st_sq, st_m, st_rs, st_sqrt, st_fin]

            allblocks = []
            for h in range(2):
                for I in range(NT):
                    nk = 4 * I + 4
                    for jb in range(nk):
                        allblocks.append((h, I, jb, nk))
            nblk = len(allblocks)
            per_tile = max(1, int(nblk * 0.85) // NT)
            GAP = max(1, min(6, (per_tile - 2) // 13))
            sched = {}
            for i in range(NT):
                for si in range(13):
                    sched.setdefault(i * per_tile + 2 + si * GAP, []).append((i, si))
            stage_cache = {}
            pending = []
            ndone = [0]
            CH = min(2048, S)

            NPC = S // CH
            bqtp = [Buf() for _ in range(NPC)]
            bktp = [Buf() for _ in range(NPC)]
            bvvp = [Buf() for _ in range(NPC)]

            def load_head(h):
                for c0 in range(0, S, CH):
                    pc = c0 // CH
                    P.dma("pool", kt[:, c0:c0 + CH], KT[h, :, c0:c0 + CH], w=[bktp[pc]])
                    P.dma("sp", qt[:, c0:c0 + CH], QT[h, :, c0:c0 + CH], w=[bqtp[pc]])
                    P.dma("sp", vv[:, c0 // 128:(c0 + CH) // 128, 0:64],
                          VA[c0:c0 + CH, h * 64:(h + 1) * 64].rearrange("(a p) d -> p a d", p=128), w=[bvvp[pc]])

            P.op("pool", "memset", ap=vv[:, :, 64:65], constant=1.0, w=bvvp)
            loaded = set()

            def s_mm(n):
                h, I, jb, nk = allblocks[n]
                d = jb - 4 * I
                qlo = 128 * d if d > 0 else 0
                pt, pb = psS[n % NS]
                P.op("pe", "matmul", out=pt[:, qlo:512], lhsT=kt[0:70, jb * 128:(jb + 1) * 128], rhs=qt[0:70, I * 512 + qlo:I * 512 + 512],
                     start=True, stop=(d < 0), r=[bqtp[(I * 512) // CH], bktp[(jb * 128) // CH]], w=[pb])
                if d >= 0:
                    P.op("pe", "matmul", out=pt[:, qlo:qlo + 128], lhsT=cid[:], rhs=cneg[:], start=False, stop=True, r=[bcid, bcneg], w=[pb])

            qcnt = [0]

            def fin(h, I):
                slot = qcnt[0] % 2
                qcnt[0] += 1
                for pe_ in [p_ for p_ in pending if p_[3] == slot]:
                    pending.remove(pe_)
                    fin2(pe_[0], pe_[1], pe_[3])
                ot, ob = psO[I % 2]
                o_, bo_ = osb[slot]
                rd, brd = rden[slot]
                P.op("dve", "reciprocal", out=rd[64:65, :], in_=ot[64:65, :], r=[ob], w=[brd])
                P.op("dve", "tensor_copy", out=o_[:], in_=ot[0:64, :], r=[ob], w=[bo_])
                pending.append((h, I, ndone[0] + 8, slot))

            def fin2(h, I, slot):
                o_, bo_ = osb[slot]
                rd, brd = rden[slot]
                y_, by_ = yst[slot]
                P.op("pe", "matmul", out=psX[0][0:64, :], lhsT=conesf[64:65, 0:64], rhs=rd[64:65, :], start=True, stop=True, r=[brd, bconesf], w=[psX[1]])
                P.op("dve", "tensor_tensor", out=y_[:], in0=o_[:], in1=psX[0][0:64, :], op=ALU.mult, r=[bo_, psX[1]], w=[by_])
                TPC = (S // 4) // 512
                P.dma("sp", Yw(slice(h * 64, (h + 1) * 64), I), y_[:], r=[by_], w=[dYp[(h, I // TPC)]])
                if ycb is not None and I % TPC == TPC - 1:
                    ycb(P, h, I // TPC, dYp[(h, I // TPC)])

            def exp_pv(n):
                h, I, jb, nk = allblocks[n]
                d = jb - 4 * I
                qlo = 128 * d if d > 0 else 0
                pt, pb = psS[n % NS]
                t, b = pT[n % NP_]
                P.op("act", "activation", out=t[:, qlo:512], in_=pt[:, qlo:512], func=AF.Exp, r=[pb], w=[b])
                ot, ob = psO[I % 2]
                P.op("pe", "matmul", out=ot[0:65, qlo:512], lhsT=vv[:, jb, 0:65], rhs=t[:, qlo:512], start=(jb == 0), stop=(jb == nk - 1),
                     r=[bvvp[(jb * 128) // CH], b], w=[ob])
                if jb == nk - 1:
                    fin(h, I)

            def run_stage(i, si):
                if i not in stage_cache:
                    stage_cache[i] = hgrn_stages(i)
                stage_cache[i][si]()

            LOOK = 3
            for n in range(nblk):
                hcur = allblocks[n][0]
                if n == 0 or allblocks[n - 1][0] != hcur:
                    load_head(hcur)
                    for m in range(n, min(n + LOOK, nblk)):
                        s_mm(m)
                if n + LOOK < nblk and allblocks[n + LOOK][0] == hcur:
                    s_mm(n + LOOK)
                exp_pv(n)
                ndone[0] += 1
                while pending and pending[0][2] <= ndone[0]:
                    h_, I_, _, sl_ = pending.pop(0)
                    fin2(h_, I_, sl_)
                for (i, si) in sched.pop(n, []):
                    run_stage(i, si)
            while pending:
                h_, I_, _, sl_ = pending.pop(0)
                fin2(h_, I_, sl_)
            for n in sorted(sched.keys()):
                for (i, si) in sched[n]:
                    run_stage(i, si)
            P.finish()


def alloc_scratch_A(nc, S, tag):
    def dt(name, shape, dty=BF16):
        return nc.dram_tensor(f"{tag}_{name}", shape, dty, kind="Internal").ap()
    return dict(QT=dt("QT", [2, 70, S]), KT=dt("KT", [2, 70, S]), VA=dt("VA", [S, 128]),
                QE=dt("QE", [128, S]), KE=dt("KE", [128, S]), KD=dt("KD", [S, 128]),
                VB=dt("VB", [S, 128]), GS=dt("GS", [128, S]))


def build_A(S, x_f32=True):
    nc = bass.Bass("TRN2", target_bir_lowering=False)
    R4 = S // 4
    xg = nc.dram_tensor("xg", [4, 1024, R4], F32 if x_f32 else BF16, kind="ExternalInput").ap()
    wA = nc.dram_tensor("wA", [1024, WA_COLS], F32, kind="ExternalInput").ap()
    vecA = nc.dram_tensor("vecA", [128, 8], F32, kind="ExternalInput").ap()
    cst = nc.dram_tensor("cst", [128, CST_COLS], F32, kind="ExternalInput").ap()
    Y = nc.dram_tensor("Y", [256, S], BF16, kind="ExternalOutput").ap()
    scr = alloc_scratch_A(nc, S, "a")
    with contextlib.ExitStack() as gst:
        GSEM[0] = gst
        GD.clear()
        phase_A(nc, S, xg, wA, vecA, cst, lambda rows, i: Y[rows, i * 512:(i + 1) * 512], scr, "A")
    return nc


def ln_alloc(TA, nb=2):
    return dict(hb=[TA.sb([128, 512], BF16, "hb") for _ in range(nb)], hq=[TA.sb([128, 512], BF16, "hq") for _ in range(nb)],
                mean=TA.sb([128, 512], F32, "mean"), rs=TA.sb([128, 512], F32, "rs"))


def layer_norm_steps(P, LT, nextP, h, bh, gcol, bcol, bvecB, conesK, bconesK, out_writer):
    hb, hq = LT["hb"], LT["hq"]
    mean, bmean = LT["mean"]
    rs, brs = LT["rs"]

    def stats():
        pm, pmb = nextP()
        pq, pqb = nextP()
        for c in range(8):
            t, b = hb[c % len(hb)]
            q, bq = hq[c % len(hq)]
            P.op("act", "copy", out=t[:], in_=h[:, c, :], r=[bh[c]], w=[b])
            P.op("dve", "tensor_tensor", out=q[:], in0=h[:, c, :], in1=h[:, c, :], op=ALU.mult, r=[bh[c]], w=[bq])
            P.op("pe", "matmul", out=pm[:], lhsT=conesK[:], rhs=t[:], start=(c == 0), stop=(c == 7), r=[bconesK, b], w=[pmb])
            P.op("pe", "matmul", out=pq[:], lhsT=conesK[:], rhs=q[:], start=(c == 0), stop=(c == 7), r=[bconesK, bq], w=[pqb])
        P.op("act", "copy", out=mean[:], in_=pm[:], r=[pmb], w=[bmean])
        P.op("dve", "tensor_tensor", out=rs[:], in0=mean[:], in1=mean[:], op=ALU.mult, r=[bmean], w=[brs])
        P.op("dve", "tensor_tensor", out=rs[:], in0=pq[:], in1=rs[:], op=ALU.subtract, r=[pqb, brs], w=[brs])
        P.op("dve", "tensor_scalar", out=rs[:], in0=rs[:], scalar1=LN_EPS, scalar2=None, op0=ALU.add, r=[brs], w=[brs])
        P.op("act", "activation", out=rs[:], in_=rs[:], func=AF.Sqrt, r=[brs], w=[brs])
        P.op("dve", "reciprocal", out=rs[:], in_=rs[:], r=[brs], w=[brs])

    def apply(c):
        def f():
            eng = "dve" if (LT.get("flush") and c % 2 == 0) else "pool"
            P.op(eng, "tensor_tensor", out=h[:, c, :], in0=h[:, c, :], in1=mean[:], op=ALU.subtract, r=[bh[c], bmean], w=[bh[c]])
            P.op(eng, "tensor_tensor", out=h[:, c, :], in0=h[:, c, :], in1=rs[:], op=ALU.mult, r=[bh[c], brs], w=[bh[c]])
            P.op(eng, "tensor_scalar", out=h[:, c, :], in0=h[:, c, :], scalar1=gcol(c), scalar2=bcol(c), op0=ALU.mult, op1=ALU.add, r=[bh[c], bvecB], w=[bh[c]])
            if c == 7:
                out_writer()
        return f
    return [stats] + [apply(c) for c in range(8)]


def phase_B(nc, R4, Yg, xres, wg, wab, wo, wfi, wfo, vecB, cst, X1, XO, XOB, tag, xcb=None):
    NT = R4 // 512
    with contextlib.ExitStack() as st0:
        TA0 = TileAlloc(nc, st0, tag + "g")
        vB, bvB = TA0.sb([128, 32], F32, "vecB")
        conesK, bconesK = TA0.sb([128, 128], BF16, "onesK")
        onesf, bonesf = TA0.sb([128, 128], F32, "onesf")
        dX1 = [Buf() for _ in range(NT)]

        with contextlib.ExitStack() as st:
            P = Prog(nc, st, tag + "1")
            TA = TileAlloc(nc, st, tag + "1")
            wg_sb, bwg = TA.sb([128, 8, 2048], BF16, "wg")
            wab_sb, bwab = TA.sb([128, 8, 1024], BF16, "wab")
            wo_sb, bwo = TA.sb([128, 8, 1024], BF16, "wo")
            xr = [TA.sb([128, 8, 512], F32, "xr") for _ in range(2)]
            xb = [TA.sb([128, 8, 512], BF16, "xb") for _ in range(2)]
            yt = [TA.sb([128, 8, 512], BF16, "yt") for _ in range(2)]
            mg, bmg = TA.sb([128, 8, 512], BF16, "mg")
            sga = [TA.sb([128, 512], F32, "sga") for _ in range(2)]
            sgb = [TA.sb([128, 512], F32, "sgb") for _ in range(2)]
            t1 = [TA.sb([128, 512], F32, "t1") for _ in range(2)]
            t2 = [TA.sb([128, 512], F32, "t2") for _ in range(2)]
            hh = [(TA.sb([128, 8, 512], F32, "hh")[0], [Buf() for _ in range(8)]) for _ in range(2)]
            psl = [TA.ps([128, 512], F32, "ps") for _ in range(8)]
            LT = ln_alloc(TA, 4)
            npz = [0]

            def nextP():
                npz[0] += 1
                return psl[npz[0] % 8]
            P.dma("sp", vB[:], vecB, w=[bvB])
            bYg = [Buf(), Buf(), Buf()]
            if isinstance(Yg, dict):
                gd = Yg
                Yg = gd["Ygl"]
                P.dma("sp", gd["jt"][:], gd["jidx"], w=[gd["bjt"]])

                def gsrc(Gt):
                    def f(e):
                        e.reg_load(gd["jr"], gd["jt"][0:1, 0:1])
                        val = e.snap(gd["jr"], min_val=0, max_val=3)
                        return Gt[bass.ds(val, 1)].rearrange("o s p t -> (o s) p t")
                    return f
                Yv = Yg.rearrange("(s p) t -> s p t", s=4)
                for o_ in (P.dma("sp", Yv[:, 128:256, :], gsrc(gd["GH"]), r=[gd["bjt"]], w=[bYg[0]]),
                           P.dma("sp", Yv[:, 0:64, :], gsrc(gd["GA"]), r=[gd["bjt"]], w=[bYg[1]]),
                           P.dma("sp", Yv[:, 64:128, :], gsrc(gd["GB"]), r=[gd["bjt"]], w=[bYg[2]])):
                    o_.extra_waits.append((GD["cc"], GD["ccn"]))
            P.dma("sp", onesf[:], cst[:, K_ONES:K_ONES + 128], w=[bonesf])
            P.op("dve", "tensor_scalar", out=conesK[:], in0=onesf[:], scalar1=1.0 / 1024.0, scalar2=None, op0=ALU.mult, r=[bonesf], w=[bconesK])
            wgv = wg.rearrange("(kc p) n -> p kc n", p=128)
            bwg_m = [Buf() for _ in range(8)]
            PRE_LOAD0 = True

            def load(i):
                tok = slice(i * 512, (i + 1) * 512)
                xv = xres.rearrange("(kc p) t -> p kc t", p=128)[:, :, tok]
                yv = Yg.rearrange("(kc p) t -> p kc t", p=128)[:, :, tok]
                P.dma("sp", xr[i % 2][0][:], xv, w=[xr[i % 2][1]])
                P.dma("pool", xb[i % 2][0][:], xv, w=[xb[i % 2][1]])
                P.dma("sp", yt[i % 2][0][:], yv, r=bYg, w=[yt[i % 2][1]])

            load(0)
            deferred = []
            for m0 in range(0, 8, 2):
                for n0 in (0, 1024):
                    P.dma("pool", wg_sb[:, :, n0 + m0 * 128:n0 + (m0 + 2) * 128], wgv[:, :, n0 + m0 * 128:n0 + (m0 + 2) * 128], w=[bwg_m[m0], bwg_m[m0 + 1]])
                if m0 == 0:
                    for kc0 in range(0, 8, 4):
                        P.dma("pool", wab_sb[:, kc0:kc0 + 4, :], wab.rearrange("(kc p) n -> p kc n", p=128)[:, kc0:kc0 + 4, :], w=[bwab])
            for kc0 in range(0, 8, 4):
                P.dma("pool", wo_sb[:, kc0:kc0 + 4, :], wo.rearrange("(kc p) n -> p kc n", p=128)[:, kc0:kc0 + 4, :], w=[bwo])
            for i in range(NT):
                if i + 1 < NT:
                    load(i + 1)
                tok = slice(i * 512, (i + 1) * 512)
                xr_t, xr_b = xr[i % 2]
                xb_t, xb_b = xb[i % 2]
                yt_t, yt_b = yt[i % 2]
                h_t, h_b = hh[i % 2]
                for m in range(8):
                    if m >= 1 and deferred:
                        deferred.pop(0)()
                    ms = slice(m * 128, (m + 1) * 128)
                    pga, pgab = nextP()
                    for kc in range(8):
                        P.op("pe", "matmul", out=pga[:], lhsT=wg_sb[:, kc, ms], rhs=xb_t[:, kc, :], start=(kc == 0), stop=(kc == 7), r=[bwg_m[m], xb_b], w=[pgab])
                    pgb, pgbb = nextP()
                    for kc in range(8):
                        P.op("pe", "matmul", out=pgb[:], lhsT=wg_sb[:, kc, 1024 + m * 128:1024 + (m + 1) * 128], rhs=xb_t[:, kc, :], start=(kc == 0), stop=(kc == 7), r=[bwg_m[m], xb_b], w=[pgbb])
                    ppa, ppab = nextP()
                    for r_ in range(4):
                        P.op("pe", "matmul", out=ppa[:], lhsT=wab_sb[:, 2 * r_, ms], rhs=yt_t[:, 2 * r_, :], start=(r_ == 0), stop=(r_ == 3), r=[bwab, yt_b], w=[ppab])
                    ppb, ppbb = nextP()
                    for r_ in range(4):
                        P.op("pe", "matmul", out=ppb[:], lhsT=wab_sb[:, 2 * r_ + 1, ms], rhs=yt_t[:, 2 * r_ + 1, :], start=(r_ == 0), stop=(r_ == 3), r=[bwab, yt_b], w=[ppbb])
                    sa, bsa = sga[m % 2]
                    sb_, bsb = sgb[m % 2]
                    a1, ba1 = t1[m % 2]
                    a2, ba2 = t2[m % 2]
                    P.op("act", "activation", out=sa[:], in_=pga[:], func=AF.Sigmoid, r=[pgab], w=[bsa])
                    P.op("act", "activation", out=sb_[:], in_=pgb[:], func=AF.Sigmoid, r=[pgbb], w=[bsb])
                    P.op("dve", "tensor_tensor", out=a1[:], in0=sa[:], in1=ppa[:], op=ALU.mult, r=[bsa, ppab], w=[ba1])
                    P.op("dve", "tensor_tensor", out=a2[:], in0=sb_[:], in1=ppb[:], op=ALU.mult, r=[bsb, ppbb], w=[ba2])
                    P.op("pool", "tensor_tensor", out=mg[:, m, :], in0=a1[:], in1=a2[:], op=ALU.add, r=[ba1, ba2], w=[bmg])
                for mo in range(8):
                    if deferred:
                        deferred.pop(0)()
                    pw, pwb = nextP()
                    for m in range(8):
                        P.op("pe", "matmul", out=pw[:], lhsT=wo_sb[:, m, mo * 128:(mo + 1) * 128], rhs=mg[:, m, :], start=(m == 0), stop=(m == 7), r=[bwo, bmg], w=[pwb])
                    P.op("dve", "scalar_tensor_tensor", out=h_t[:, mo, :], in0=xr_t[:, mo, :], scalar=float(ALPHA), in1=pw[:], op0=ALU.mult, op1=ALU.add,
                         r=[xr_b, pwb], w=[h_b[mo]])

                def wr(i=i, h_t=h_t, h_b=h_b, tok=tok):
                    P.dma("sp", X1.rearrange("(kc p) t -> p kc t", p=128)[:, :, tok], h_t[:], r=h_b, w=[dX1[i]])
                deferred.extend(layer_norm_steps(P, LT, nextP, h_t, h_b, lambda c: vB[:, c:c + 1], lambda c: vB[:, 8 + c:9 + c], bvB, conesK, bconesK, wr))
            LT["flush"] = True
            while deferred:
                deferred.pop(0)()
            P.finish()

        with contextlib.ExitStack() as st:
            P = Prog(nc, st, tag + "2")
            P.defer_cc = xcb is not None
            TA = TileAlloc(nc, st, tag + "2")
            NM = FFH // 128
            wfi_sb, bwfi = TA.sb([128, 8, 2 * FFH], BF16, "wfi")
            wfo_sb, bwfo = TA.sb([128, NM, 1024], BF16, "wfo")
            xb = [TA.sb([128, 8, 512], BF16, "xb") for _ in range(2)]
            xrc = [TA.sb([128, 512], F32, "xrc") for _ in range(2)]
            aa, baa = TA.sb([128, NM, 512], BF16, "aa")
            sg = [TA.sb([128, 512], F32, "sg") for _ in range(2)]
            hh = TA.sb([128, 8, 512], F32, "hh")[0]
            bhh = [Buf() for _ in range(8)]
            psl = [TA.ps([128, 512], F32, "ps") for _ in range(8)]
            LT = ln_alloc(TA)
            npz = [0]

            def nextP():
                npz[0] += 1
                return psl[npz[0] % 8]
            wfv = wfi.rearrange("(kc p) n -> p kc n", p=128)
            X1v = X1.rearrange("(kc p) t -> p kc t", p=128)

            def load(i):
                tok = slice(i * 512, (i + 1) * 512)
                P.dma("pool", xb[i % 2][0][:], X1v[:, :, tok], r=[dX1[i]], w=[xb[i % 2][1]])

            deferred = []
            bwfi_m = [Buf() for _ in range(NM)]
            for m0 in range(0, NM, 2):
                for n0 in (0, FFH):
                    P.dma("pool", wfi_sb[:, :, n0 + m0 * 128:n0 + (m0 + 2) * 128], wfv[:, :, n0 + m0 * 128:n0 + (m0 + 2) * 128], w=[bwfi_m[m0], bwfi_m[m0 + 1]])
                if m0 == 0:
                    load(0)
            wov = wfo.rearrange("(kc p) n -> p kc n", p=128)
            for k0 in range(0, NM, 2):
                P.dma("pool", wfo_sb[:, k0:k0 + 2, :], wov[:, k0:k0 + 2, :], w=[bwfo])
            for i in range(NT):
                if i + 1 < NT:
                    load(i + 1)
                tok = slice(i * 512, (i + 1) * 512)
                xb_t, xb_b = xb[i % 2]
                for m in range(NM):
                    if m >= 2 and deferred:
                        deferred.pop(0)()
                    pu, pub = nextP()
                    for kc in range(8):
                        P.op("pe", "matmul", out=pu[:], lhsT=wfi_sb[:, kc, m * 128:(m + 1) * 128], rhs=xb_t[:, kc, :], start=(kc == 0), stop=(kc == 7), r=[bwfi_m[m], xb_b], w=[pub])
                    pg, pgb = nextP()
                    for kc in range(8):
                        P.op("pe", "matmul", out=pg[:], lhsT=wfi_sb[:, kc, FFH + m * 128:FFH + (m + 1) * 128], rhs=xb_t[:, kc, :], start=(kc == 0), stop=(kc == 7), r=[bwfi_m[m], xb_b], w=[pgb])
                    s_, bs_ = sg[m % 2]
                    P.op("act", "activation", out=s_[:], in_=pg[:], func=AF.Silu, r=[pgb], w=[bs_])
                    P.op("dve", "tensor_tensor", out=aa[:, m, :], in0=s_[:], in1=pu[:], op=ALU.mult, r=[bs_, pub], w=[baa])
                for mo in range(8):
                    xc, bxc = xrc[mo % 2]
                    P.dma("sp", xc[:], X1v[:, mo, tok], r=[dX1[i]], w=[bxc])
                    po, pob = nextP()
                    for m in range(NM):
                        P.op("pe", "matmul", out=po[:], lhsT=wfo_sb[:, m, mo * 128:(mo + 1) * 128], rhs=aa[:, m, :], start=(m == 0), stop=(m == NM - 1), r=[bwfo, baa], w=[pob])
                    P.op("dve", "scalar_tensor_tensor", out=hh[:, mo, :], in0=xc[:], scalar=float(ALPHA), in1=po[:], op0=ALU.mult, op1=ALU.add, r=[bxc, pob], w=[bhh[mo]])

                def wr(tok=tok, i=i):
                    P.dma("sp", XO.rearrange("(kc p) t -> p kc t", p=128)[:, :, tok], hh[:], r=bhh, w=[Buf()])
                    if XOB is not None:
                        bx_ = Buf()
                        if xcb is not None:
                            P.dma("pool", XOB[i].rearrange("(kc p) t -> p kc t", p=128), hh[:], r=bhh, w=[bx_])
                            xcb(P, i, bx_)
                        else:
                            P.dma("pool", XOB.rearrange("(kc p) t -> p kc t", p=128)[:, :, tok], hh[:], r=bhh, w=[bx_])
                deferred.extend(layer_norm_steps(P, LT, nextP, hh, bhh, lambda c: vB[:, 16 + c:17 + c], lambda c: vB[:, 24 + c:25 + c], bvB, conesK, bconesK, wr))
            LT["flush"] = True
            while deferred:
                deferred.pop(0)()
            P.finish()


def build_B(R4, with_bf16_out=False):
    nc = bass.Bass("TRN2", target_bir_lowering=False)
    Yg = nc.dram_tensor("Yg", [1024, R4], BF16, kind="ExternalInput").ap()
    xres = nc.dram_tensor("xres", [1024, R4], F32, kind="ExternalInput").ap()
    wg = nc.dram_tensor("wg", [1024, 2048], F32, kind="ExternalInput").ap()
    wab = nc.dram_tensor("wab", [1024, 1024], F32, kind="ExternalInput").ap()
    wo = nc.dram_tensor("wo", [1024, 1024], F32, kind="ExternalInput").ap()
    wfi = nc.dram_tensor("wfi", [1024, 2 * FFH], F32, kind="ExternalInput").ap()
    wfo = nc.dram_tensor("wfo", [FFH, 1024], F32, kind="ExternalInput").ap()
    vecB = nc.dram_tensor("vecB", [128, 32], F32, kind="ExternalInput").ap()
    cst = nc.dram_tensor("cst", [128, CST_COLS], F32, kind="ExternalInput").ap()
    XO = nc.dram_tensor("XO", [1024, R4], F32, kind="ExternalOutput").ap()
    XOB = nc.dram_tensor("XOB", [1024, R4], BF16, kind="ExternalOutput").ap() if with_bf16_out else None
    X1 = nc.dram_tensor("b_X1", [1024, R4], F32, kind="Internal").ap()
    with contextlib.ExitStack() as gst:
        GSEM[0] = gst
        GD.clear()
        phase_B(nc, R4, Yg, xres, wg, wab, wo, wfi, wfo, vecB, cst, X1, XO, XOB, "B")
    return nc


def make_cst():
    c = np.zeros((128, CST_COLS), np.float32)
    p = np.arange(128)[:, None]
    f = np.arange(128)[None, :]
    c[:, K_ID:K_ID + 128] = (p == f)
    c[:, K_NEG:K_NEG + 128] = np.where(f < p, -30000.0, 0.0)
    c[:, K_BD:K_BD + 128] = (f >= p) & ((p // 64) == (f // 64))
    r = np.ones(512, np.float32)
    r[::64] = 0.0
    c[:, K_RESET:K_RESET + 512] = r[None, :]
    c[:, K_ONES:K_ONES + 512] = 1.0
    return c


def make_wA(w_in_l, j):
    B0 = 3 * 512 + 8
    sl = lambda base: w_in_l[:, base + 128 * j: base + 128 * (j + 1)]
    qa, ka, va = sl(0), sl(512), sl(1024)
    fa = w_in_l[:, 1536 + 2 * j: 1536 + 2 * (j + 1)]
    qb, fb, ib, gb = sl(B0), sl(B0 + 512), sl(B0 + 1024), sl(B0 + 1536)
    return np.ascontiguousarray(np.concatenate([qa, ka, qb, fb, gb, fa, va, ib], axis=1))


def make_vecA(l, j, b_fgate, hgrn_lb_logits, hgrn_norm_g):
    v = np.zeros((128, 8), np.float32)
    v[:, 0] = hgrn_lb_logits[0, 128 * j:128 * (j + 1)]
    v[:, 1] = hgrn_lb_logits[1, 128 * j:128 * (j + 1)]
    v[:, 2] = hgrn_norm_g[l]
    v[0:2, 3] = b_fgate[l, 2 * j:2 * j + 2]
    v[:, 4] = float(l)
    return v


def make_wab(wa_l, wb_l):
    out = np.empty((1024, 1024), np.float32)
    for r in range(4):
        out[256 * r:256 * r + 128] = wa_l[128 * r:128 * (r + 1)]
        out[256 * r + 128:256 * r + 256] = wb_l[128 * r:128 * (r + 1)]
    return out


def make_vecB(l, ln1_g, ln1_b, ln2_g, ln2_b):
    v = np.empty((128, 32), np.float32)
    for k, a in enumerate((ln1_g[l], ln1_b[l], ln2_g[l], ln2_b[l])):
        v[:, 8 * k:8 * k + 8] = a.reshape(8, 128).T
    return v


I32 = mybir.dt.int32
GROUPS = [[0, 1, 2, 3], [4, 5, 6, 7]]


def make_ycb(Ysc, GA, GB, GH):
    def ycb(P, part, r_, buf):
        if part < 2:
            src = Ysc[r_, part * 64:(part + 1) * 64, :]
            dst = (GA, GB)[part][r_].rearrange("s p t -> (s p) t")
        else:
            src = Ysc[r_, 128:256, :]
            dst = GH[r_].rearrange("s p t -> (s p) t")
        P.coll("AllGather", [src], [dst], GROUPS, r=[buf], w=[Buf()])
    return ycb


def exchange_Y(nc, R4, GA, GB, GH, Ygl, jidx, jt, bjt, jr, tag):
    with contextlib.ExitStack() as st:
        P = Prog(nc, st, tag)
        P.dma("sp", jt[:], jidx, w=[bjt])

        def src(Gt):
            def f(e):
                e.reg_load(jr, jt[0:1, 0:1])
                val = e.snap(jr, min_val=0, max_val=3)
                return Gt[bass.ds(val, 1)].rearrange("o s p t -> (o s) p t")
            return f
        Yv = Ygl.rearrange("(s p) t -> s p t", s=4)
        P.dma("sp", Yv[:, 0:64, :], src(GA), r=[bjt], w=[Buf()])
        P.dma("sp", Yv[:, 64:128, :], src(GB), r=[bjt], w=[Buf()])
        P.dma("sp", Yv[:, 128:256, :], src(GH), r=[bjt], w=[Buf()])
        P.finish()


def build_fused(S):
    nc = bass.Bass("TRN2", target_bir_lowering=False)
    R4 = S // 4
    ext = lambda name, shape, dt=F32: nc.dram_tensor(name, shape, dt, kind="ExternalInput").ap()
    itn = lambda name, shape, dt=BF16: nc.dram_tensor(name, shape, dt, kind="Internal").ap()
    xg = ext("xg", [4, 1024, R4])
    xres = ext("xres", [1024, R4])
    cst = ext("cst", [128, CST_COLS])
    jidx = ext("jidx", [1, 4], I32)
    W = []
    for l in range(DEPTH):
        W.append(dict(wA=ext(f"wA{l}", [1024, WA_COLS]), vecA=ext(f"vecA{l}", [128, 8]), wg=ext(f"wg{l}", [1024, 2048]),
                      wab=ext(f"wab{l}", [1024, 1024]), wo=ext(f"wo{l}", [1024, 1024]), wfi=ext(f"wfi{l}", [1024, 2 * FFH]),
                      wfo=ext(f"wfo{l}", [FFH, 1024]), vecB=ext(f"vecB{l}", [128, 32])))
    OUT = nc.dram_tensor("OUT", [1024, R4], F32, kind="ExternalOutput").ap()
    scr = alloc_scratch_A(nc, S, "a")
    Ysc = itn("Ysc", [4, 256, R4])
    GA = itn("GA", [4, 4, 64, R4])
    GB = itn("GB", [4, 4, 64, R4])
    GH = itn("GH", [4, 4, 128, R4])
    NTB = R4 // 512
    XOBt = itn("XOBt", [NTB, 1024, 512])
    XG1t = itn("XG1t", [NTB, 4, 1024, 512])
    Ygl = itn("Ygl", [1024, R4])
    X1 = itn("X1", [1024, R4], F32)
    XO0 = itn("XO0", [1024, R4], F32)
    xsrc1 = lambda r_, c0: XG1t[c0 // 512, r_].rearrange("(kc p) t -> p kc t", p=128)
    Yw = lambda rows, i: Ysc[(i * 512) // R4, rows, (i * 512) % R4:(i * 512) % R4 + 512]
    with contextlib.ExitStack() as gst:
        GSEM[0] = gst
        GD.clear()
        jt = gst.enter_context(nc.sbuf_tensor("jt", [1, 4], I32))
        bjt = Buf()
        jr = gst.enter_context(nc.sync.register("jr"))
        ycb = make_ycb(Ysc, GA, GB, GH)

        def xcb(P, i, buf):
            P.coll("AllGather", [XOBt[i]], [XG1t[i].rearrange("s f t -> (s f) t")], GROUPS, r=[buf], w=[Buf()])
        for l in range(DEPTH):
            w = W[l]
            phase_A(nc, S, xg if l == 0 else xsrc1, w["wA"], w["vecA"], cst, Yw, scr, f"A{l}", ycb=ycb)
            last = (l == DEPTH - 1)
            if not last:
                xbase = GD["ccn"]
                xsrc1.ccwait = lambda ti, xbase=xbase: (GD["cc"], xbase + min(ti + 2, NTB))
            phase_B(nc, R4, dict(GA=GA, GB=GB, GH=GH, Ygl=Ygl, jr=jr, jt=jt, bjt=bjt, jidx=jidx), xres if l == 0 else XO0, w["wg"], w["wab"], w["wo"], w["wfi"], w["wfo"], w["vecB"], cst, X1,
                    OUT if last else XO0, None if last else XOBt, f"B{l}", xcb=None if last else xcb)
    return nc


def kernel(x, w_in, b_fgate, hgrn_lb_logits, hgrn_norm_g, w_branch_a, w_branch_b, w_out,
           ln1_g, ln1_b, w_ff_in, w_ff_out, ln2_g, ln2_b):
    x = np.asarray(x, np.float32)
    Bn, S, _ = x.shape
    R4 = S // 4
    f = lambda a: np.asarray(a, np.float32)
    w_in, b_fgate, hgrn_lb_logits, hgrn_norm_g = f(w_in), f(b_fgate), f(hgrn_lb_logits), f(hgrn_norm_g)
    w_branch_a, w_branch_b, w_out = f(w_branch_a), f(w_branch_b), f(w_out)
    ln1_g, ln1_b, w_ff_in, w_ff_out, ln2_g, ln2_b = f(ln1_g), f(ln1_b), f(w_ff_in), f(w_ff_out), f(ln2_g), f(ln2_b)
    cst = make_cst()
    cores = list(range(8))
    xg = [np.ascontiguousarray(x[b].reshape(4, R4, D).transpose(0, 2, 1)) for b in range(Bn)]
    B0 = 3 * 512 + 8 + 4 * 512
    shared = {}
    for l in range(DEPTH):
        shared[f"wg{l}"] = np.ascontiguousarray(w_in[l][:, B0:B0 + 2048])
        shared[f"wab{l}"] = make_wab(w_branch_a[l], w_branch_b[l])
        shared[f"wo{l}"] = w_out[l]
        shared[f"wfi{l}"] = w_ff_in[l]
        shared[f"wfo{l}"] = w_ff_out[l]
        shared[f"vecB{l}"] = make_vecB(l, ln1_g, ln1_b, ln2_g, ln2_b)
    wAs = {(l, j): make_wA(w_in[l], j) for l in range(DEPTH) for j in range(4)}
    in_maps = []
    for c in cores:
        b, j = divmod(c, 4)
        m = dict(xg=xg[b], xres=np.ascontiguousarray(xg[b][j]), cst=cst, jidx=np.array([[j, 0, 0, 0]], np.int32))
        for l in range(DEPTH):
            m[f"wA{l}"] = wAs[(l, j)]
            m[f"vecA{l}"] = make_vecA(l, j, b_fgate, hgrn_lb_logits, hgrn_norm_g)
        m.update(shared)
        in_maps.append(m)
    nc = build_fused(S)
    res = run_bass_kernel_spmd(nc, in_maps, core_ids=cores)
    out = np.empty((Bn, S, D), np.float32)
    for c in cores:
        b, j = divmod(c, 4)
        out[b, j * R4:(j + 1) * R4, :] = np.asarray(res.results[c]["OUT"]).T
    return out
```
